# Optimizing a Trainium2 kernel written in Bass

```python
import jax, jax.numpy as jnp
from jax import lax
import numpy as np

D_MODEL = 1024
BATCH = 4
SEQ = 4096
DEPTH = 2
DEC_BATCH = 32
DEC_SEQ = 1
PAST_LEN = 8192
PAGE_SIZE = 128

HEAD_DIM = 64
R_HEADS = 8
R_WIDTH = R_HEADS * HEAD_DIM
DECAY_LORA = 64
ICLR_LORA = 64
GATE_LORA = 128
R_COLS = 3 * R_WIDTH + DECAY_LORA + ICLR_LORA + GATE_LORA
LNX_EPS = 64e-5
CONV_CH = 512
CONV_K = 31
SWA_GROUPS = ((128, 1), (512, 4), (2048, 16))
N_GROUPS = len(SWA_GROUPS)
G_HEADS = 4
A_WIDTH = N_GROUPS * G_HEADS * HEAD_DIM
Q_BLOCK = 128
ATTN_SCALE = HEAD_DIM ** -0.5
N_BRANCH = 3
D_FF = 2816
FFN_K = 3
OFF_R = 0
OFF_C = OFF_R + R_COLS
OFF_Q = OFF_C + 2 * CONV_CH
OFF_K = OFF_Q + A_WIDTH
OFF_V = OFF_K + A_WIDTH
OFF_GATE = OFF_V + A_WIDTH
IN_COLS = OFF_GATE + N_BRANCH * D_MODEL
ALPHA = (2 * DEPTH) ** 0.25
BETA = (8 * DEPTH) ** -0.25
LN_EPS = 1e-5

kernel_name = 'hybrid_rwkv7_conformer_dilated_swa_step'


def layer_norm(x, g, b, eps=LN_EPS):
    xf = x.astype(jnp.float32)
    mu = xf.mean(-1, keepdims=True)
    var = jnp.square(xf - mu).mean(-1, keepdims=True)
    y = (xf - mu) * lax.rsqrt(var + eps) * g.astype(jnp.float32) + b.astype(jnp.float32)
    return y.astype(x.dtype)


def causal_dwconv(u, buf, w, b):
    full = jnp.concatenate([buf.astype(u.dtype), u], axis=1)
    y = lax.conv_general_dilated(full, w[:, None, :].astype(u.dtype), window_strides=(1,),
                                 padding='VALID', dimension_numbers=('NWC', 'WIO', 'NWC'),
                                 feature_group_count=u.shape[-1])
    return y + b, full[:, -(w.shape[0] - 1):]


def rwkv7_mix(zr, shift_prev, wkv0, mu, w0, w_up, a0, a_up, g_up, k_k, k_a, r_k, ln_g, ln_b, w_out):
    f32 = jnp.float32
    B, L, _ = zr.shape
    prev = jnp.concatenate([shift_prev.astype(zr.dtype), zr[:, :-1]], axis=1)
    zs = zr + (prev - zr) * mu
    cut = [R_WIDTH, 2 * R_WIDTH, 3 * R_WIDTH, 3 * R_WIDTH + DECAY_LORA, 3 * R_WIDTH + DECAY_LORA + ICLR_LORA]
    r, k, v, wd, ad, gd = jnp.split(zs, cut, axis=-1)
    w_log = -jax.nn.softplus(-(w0 + jnp.tanh(wd) @ w_up).astype(f32)) - 0.5
    decay = jnp.exp(-jnp.exp(w_log))
    a = jax.nn.sigmoid((a0 + ad @ a_up).astype(f32))
    g = jax.nn.sigmoid(gd) @ g_up
    heads = lambda t: t.astype(f32).reshape(B, L, R_HEADS, HEAD_DIM)
    r, k, v, decay, a = heads(r), heads(k), heads(v), heads(decay), heads(a)
    kk = k * k_k.astype(f32).reshape(R_HEADS, HEAD_DIM)
    kk = kk / jnp.maximum(jnp.sqrt(jnp.sum(kk * kk, axis=-1, keepdims=True)), 1e-12)
    k = k * (1.0 + (a - 1.0) * k_a.astype(f32).reshape(R_HEADS, HEAD_DIM))

    def step(S, inp):
        r_t, w_t, k_t, v_t, kk_t, b_t = inp
        sa = jnp.einsum('bhvk,bhk->bhv', S, -kk_t)
        S = S * w_t[:, :, None, :] + sa[..., None] * b_t[:, :, None, :] + v_t[..., None] * k_t[:, :, None, :]
        return S, jnp.einsum('bhvk,bhk->bhv', S, r_t)

    tm = lambda t: jnp.swapaxes(t, 0, 1)
    S_fin, o = lax.scan(step, wkv0.astype(f32), (tm(r), tm(decay), tm(k), tm(v), tm(kk), tm(kk * a)))
    o = tm(o)
    m = o.mean(-1, keepdims=True)
    var = jnp.square(o - m).mean(-1, keepdims=True)
    o = ((o - m) * lax.rsqrt(var + LNX_EPS)).reshape(B, L, R_WIDTH) * ln_g.astype(f32) + ln_b.astype(f32)
    bonus = jnp.sum(r * k * r_k.astype(f32), axis=-1, keepdims=True) * v
    o = (o + bonus.reshape(B, L, R_WIDTH)).astype(zr.dtype) * g
    return o @ w_out, zr[:, -1:], S_fin


def conformer_conv(zc, buf, dw, dw_b, ln_g, ln_b, w_out):
    u = zc[..., :CONV_CH] * jax.nn.sigmoid(zc[..., CONV_CH:])
    c, new_buf = causal_dwconv(u, buf, dw, dw_b)
    c = layer_norm(c, ln_g, ln_b)
    c = c * jax.nn.sigmoid(c)
    return c @ w_out, new_buf


def dilated_attn_prompt(q, k, v, dil, nk):
    B, S, H, E = q.shape
    L = S // dil
    nb = -(-L // Q_BLOCK)
    Lp = nb * Q_BLOCK

    def split(t):
        t = t.reshape(B, L, dil, H, E).transpose(0, 2, 1, 3, 4)
        t = jnp.pad(t, ((0, 0), (0, 0), (0, Lp - L), (0, 0), (0, 0)))
        return t.reshape(B, dil, nb, Q_BLOCK, H, E)

    def with_prev(t):
        prev = jnp.pad(t, ((0, 0), (0, 0), (1, 0), (0, 0), (0, 0), (0, 0)))[:, :, :-1]
        return jnp.concatenate([prev, t], axis=3)

    qb = split(q)
    kb, vb = with_prev(split(k)), with_prev(split(v))
    s = jnp.einsum('bdnqhe,bdnkhe->bdnhqk', qb, kb).astype(jnp.float32)
    qi = jnp.arange(Q_BLOCK)[:, None]
    ki = jnp.arange(2 * Q_BLOCK)[None, :]
    dist = qi + Q_BLOCK - ki
    keypos = jnp.arange(nb)[:, None, None] * Q_BLOCK + ki[None] - Q_BLOCK
    mask = (dist >= 0)[None] & (dist <= nk)[None] & (keypos >= 0)
    s = jnp.where(mask[None, None, :, None], s, -jnp.inf)
    lse = jax.nn.logsumexp(s, axis=-1)
    p = jnp.exp(s - lse[..., None])
    o = jnp.einsum('bdnhqk,bdnkhe->bdnqhe', p.astype(v.dtype), vb)
    o = o.reshape(B, dil, Lp, H, E)[:, :, :L].transpose(0, 2, 1, 3, 4).reshape(B, S, H, E)
    lse = lse.transpose(0, 1, 2, 4, 3).reshape(B, dil, Lp, H)[:, :, :L].transpose(0, 2, 1, 3).reshape(B, S, H)
    return o, lse


def dilated_attn_sample(q, k_all, v_all, n_buf, dil, nk):
    T = q.shape[1]
    idx = n_buf + jnp.arange(T)[:, None] - dil * jnp.arange(nk + 1)[None, :]
    valid = idx >= 0
    idx = jnp.maximum(idx, 0)
    kg, vg = k_all[:, idx], v_all[:, idx]
    s = jnp.einsum('bthe,btjhe->bthj', q, kg).astype(jnp.float32)
    s = jnp.where(valid[None, :, None, :], s, -jnp.inf)
    lse = jax.nn.logsumexp(s, axis=-1)
    p = jnp.exp(s - lse[..., None])
    o = jnp.einsum('bthj,btjhe->bthe', p.astype(v_all.dtype), vg)
    return o, lse


def decoder_layer(x, prm, shift, wkv, conv_buf, ffn_buf, kv_bufs):
    B, L, _ = x.shape
    z = x @ prm['w_in']
    zr = z[..., OFF_R:OFF_C]
    zc = z[..., OFF_C:OFF_Q]
    q = z[..., OFF_Q:OFF_K].reshape(B, L, N_GROUPS, G_HEADS, HEAD_DIM) * ATTN_SCALE
    k = z[..., OFF_K:OFF_V].reshape(B, L, N_GROUPS, G_HEADS, HEAD_DIM)
    v = z[..., OFF_V:OFF_GATE].reshape(B, L, N_GROUPS, G_HEADS, HEAD_DIM)
    zg = z[..., OFF_GATE:].reshape(B, L, N_BRANCH, D_MODEL)

    y_r, new_shift, new_wkv = rwkv7_mix(zr, shift, wkv, prm['rwkv_mu'], prm['rwkv_w0'], prm['rwkv_w_up'],
                                        prm['rwkv_a0'], prm['rwkv_a_up'], prm['rwkv_g_up'], prm['rwkv_k_k'],
                                        prm['rwkv_k_a'], prm['rwkv_r_k'], prm['rwkv_ln_g'], prm['rwkv_ln_b'],
                                        prm['w_rwkv_out'])
    y_c, new_conv = conformer_conv(zc, conv_buf, prm['conv_dw'], prm['conv_dw_b'], prm['conv_ln_g'],
                                   prm['conv_ln_b'], prm['w_conv_out'])

    outs, lses, new_kv = [], [], []
    for gi, (win, dil) in enumerate(SWA_GROUPS):
        qg, kg, vg = q[:, :, gi], k[:, :, gi], v[:, :, gi]
        if kv_bufs is None:
            o, lse = dilated_attn_prompt(qg, kg, vg, dil, win // dil)
            keep = min(win, L)
            new_kv.append(jnp.stack([kg[:, L - keep:], vg[:, L - keep:]], axis=2))
        else:
            buf = kv_bufs[gi]
            n_buf = buf.shape[1]
            k_all = jnp.concatenate([buf[:, :, 0].astype(kg.dtype), kg], axis=1)
            v_all = jnp.concatenate([buf[:, :, 1].astype(vg.dtype), vg], axis=1)
            o, lse = dilated_attn_sample(qg, k_all, v_all, n_buf, dil, win // dil)
            keep = min(win, n_buf + L)
            new_kv.append(jnp.stack([k_all[:, -keep:], v_all[:, -keep:]], axis=2))
        outs.append(o)
        lses.append(lse)
    wts = jax.nn.softmax(jnp.stack(lses, axis=2), axis=2)
    o = jnp.sum(wts[..., None] * jnp.stack(outs, axis=2).astype(jnp.float32), axis=2)
    y_a = o.astype(x.dtype).reshape(B, L, G_HEADS * HEAD_DIM) @ prm['w_attn_out']

    gates = jax.nn.sigmoid(zg + prm['gate_b'])
    merged = gates[:, :, 0] * y_r + gates[:, :, 1] * y_c + gates[:, :, 2] * y_a
    h = layer_norm(ALPHA * x + merged @ prm['w_o'], prm['ln1_g'], prm['ln1_b'])

    u, new_ffn = causal_dwconv(h @ prm['w_ffn_in'], ffn_buf, prm['ffn_dw'], prm['ffn_dw_b'])
    f = (jax.nn.silu(u[..., :D_FF]) * u[..., D_FF:]) @ prm['w_ffn_out']
    out = layer_norm(ALPHA * h + f, prm['ln2_g'], prm['ln2_b'])
    return out, new_shift, new_wkv, new_conv, new_kv, new_ffn


def setup_inputs(seed: int = 0) -> dict:
    key = jax.random.key(seed)
    ks = iter(jax.random.split(key, 40))
    f32 = jnp.float32

    def nrm(shape, scale=1.0):
        return scale * jax.random.normal(next(ks), shape, f32)

    def unif(shape, lo, hi):
        return jax.random.uniform(next(ks), shape, f32, lo, hi)

    kv_len = [min(win, PAST_LEN) for win, _ in SWA_GROUPS]
    return {
        'x_prompt': nrm((BATCH, SEQ, D_MODEL)),
        'x_sample': nrm((DEC_BATCH, DEC_SEQ, D_MODEL)),
        'state_shift': nrm((DEPTH, DEC_BATCH, 1, R_COLS)),
        'state_wkv': nrm((DEPTH, DEC_BATCH, R_HEADS, HEAD_DIM, HEAD_DIM), 0.5),
        'state_conv': nrm((DEPTH, DEC_BATCH, CONV_K - 1, CONV_CH), 0.5),
        'cache_swa_a': nrm((DEPTH, DEC_BATCH, kv_len[0], 2, G_HEADS, HEAD_DIM)),
        'cache_swa_b': nrm((DEPTH, DEC_BATCH, kv_len[1], 2, G_HEADS, HEAD_DIM)),
        'cache_swa_c': nrm((DEPTH, DEC_BATCH, kv_len[2], 2, G_HEADS, HEAD_DIM)),
        'state_ffn': nrm((DEPTH, DEC_BATCH, FFN_K - 1, 2 * D_FF)),
        'w_in': nrm((DEPTH, D_MODEL, IN_COLS), D_MODEL ** -0.5),
        'rwkv_mu': unif((DEPTH, R_COLS), 0.0, 1.0),
        'rwkv_w0': unif((DEPTH, R_WIDTH), -3.0, 1.0),
        'rwkv_w_up': nrm((DEPTH, DECAY_LORA, R_WIDTH), 0.1),
        'rwkv_a0': nrm((DEPTH, R_WIDTH), 0.1),
        'rwkv_a_up': nrm((DEPTH, ICLR_LORA, R_WIDTH), 0.1),
        'rwkv_g_up': nrm((DEPTH, GATE_LORA, R_WIDTH), GATE_LORA ** -0.5),
        'rwkv_k_k': 0.85 + nrm((DEPTH, R_WIDTH), 0.05),
        'rwkv_k_a': 1.0 + nrm((DEPTH, R_WIDTH), 0.05),
        'rwkv_r_k': nrm((DEPTH, R_HEADS, HEAD_DIM), 0.1),
        'rwkv_ln_g': 1.0 + nrm((DEPTH, R_WIDTH), 0.05),
        'rwkv_ln_b': nrm((DEPTH, R_WIDTH), 0.02),
        'w_rwkv_out': nrm((DEPTH, R_WIDTH, D_MODEL), R_WIDTH ** -0.5),
        'conv_dw': nrm((DEPTH, CONV_K, CONV_CH), CONV_K ** -0.5),
        'conv_dw_b': nrm((DEPTH, CONV_CH), 0.02),
        'conv_ln_g': 1.0 + nrm((DEPTH, CONV_CH), 0.05),
        'conv_ln_b': nrm((DEPTH, CONV_CH), 0.02),
        'w_conv_out': nrm((DEPTH, CONV_CH, D_MODEL), CONV_CH ** -0.5),
        'w_attn_out': nrm((DEPTH, G_HEADS * HEAD_DIM, D_MODEL), (G_HEADS * HEAD_DIM) ** -0.5),
        'gate_b': nrm((DEPTH, N_BRANCH, D_MODEL), 0.02),
        'w_o': nrm((DEPTH, D_MODEL, D_MODEL), BETA * D_MODEL ** -0.5),
        'ln1_g': 1.0 + nrm((DEPTH, D_MODEL), 0.05),
        'ln1_b': nrm((DEPTH, D_MODEL), 0.02),
        'w_ffn_in': nrm((DEPTH, D_MODEL, 2 * D_FF), D_MODEL ** -0.5),
        'ffn_dw': nrm((DEPTH, FFN_K, 2 * D_FF), FFN_K ** -0.5),
        'ffn_dw_b': nrm((DEPTH, 2 * D_FF), 0.02),
        'w_ffn_out': nrm((DEPTH, D_FF, D_MODEL), BETA * D_FF ** -0.5),
        'ln2_g': 1.0 + nrm((DEPTH, D_MODEL), 0.05),
        'ln2_b': nrm((DEPTH, D_MODEL), 0.02),
    }


def reference(x_prompt, x_sample, state_shift, state_wkv, state_conv, cache_swa_a, cache_swa_b,
              cache_swa_c, state_ffn, w_in, rwkv_mu, rwkv_w0, rwkv_w_up, rwkv_a0, rwkv_a_up, rwkv_g_up,
              rwkv_k_k, rwkv_k_a, rwkv_r_k, rwkv_ln_g, rwkv_ln_b, w_rwkv_out, conv_dw, conv_dw_b,
              conv_ln_g, conv_ln_b, w_conv_out, w_attn_out, gate_b, w_o, ln1_g, ln1_b, w_ffn_in,
              ffn_dw, ffn_dw_b, w_ffn_out, ln2_g, ln2_b):
    hp, hs = x_prompt, x_sample
    bp = x_prompt.shape[0]
    sh_p, sh_s, wk_p, wk_s, cv_p, cv_s, ff_p, ff_s = [], [], [], [], [], [], [], []
    sa_p, sa_s, sb_p, sb_s, sc_p, sc_s = [], [], [], [], [], []
    for l in range(DEPTH):
        prm = {
            'w_in': w_in[l], 'rwkv_mu': rwkv_mu[l], 'rwkv_w0': rwkv_w0[l], 'rwkv_w_up': rwkv_w_up[l],
            'rwkv_a0': rwkv_a0[l], 'rwkv_a_up': rwkv_a_up[l], 'rwkv_g_up': rwkv_g_up[l],
            'rwkv_k_k': rwkv_k_k[l], 'rwkv_k_a': rwkv_k_a[l], 'rwkv_r_k': rwkv_r_k[l],
            'rwkv_ln_g': rwkv_ln_g[l], 'rwkv_ln_b': rwkv_ln_b[l], 'w_rwkv_out': w_rwkv_out[l],
            'conv_dw': conv_dw[l], 'conv_dw_b': conv_dw_b[l], 'conv_ln_g': conv_ln_g[l],
            'conv_ln_b': conv_ln_b[l], 'w_conv_out': w_conv_out[l], 'w_attn_out': w_attn_out[l],
            'gate_b': gate_b[l], 'w_o': w_o[l], 'ln1_g': ln1_g[l], 'ln1_b': ln1_b[l],
            'w_ffn_in': w_ffn_in[l], 'ffn_dw': ffn_dw[l], 'ffn_dw_b': ffn_dw_b[l],
            'w_ffn_out': w_ffn_out[l], 'ln2_g': ln2_g[l], 'ln2_b': ln2_b[l],
        }
        hp, n_sh, n_wk, n_cv, n_kv, n_ff = decoder_layer(
            hp, prm,
            jnp.zeros((bp, 1, R_COLS), hp.dtype),
            jnp.zeros((bp, R_HEADS, HEAD_DIM, HEAD_DIM), jnp.float32),
            jnp.zeros((bp, CONV_K - 1, CONV_CH), hp.dtype),
            jnp.zeros((bp, FFN_K - 1, 2 * D_FF), hp.dtype),
            None)
        sh_p.append(n_sh); wk_p.append(n_wk); cv_p.append(n_cv); ff_p.append(n_ff)
        sa_p.append(n_kv[0]); sb_p.append(n_kv[1]); sc_p.append(n_kv[2])
        hs, n_sh, n_wk, n_cv, n_kv, n_ff = decoder_layer(
            hs, prm, state_shift[l], state_wkv[l], state_conv[l], state_ffn[l],
            (cache_swa_a[l], cache_swa_b[l], cache_swa_c[l]))
        sh_s.append(n_sh); wk_s.append(n_wk); cv_s.append(n_cv); ff_s.append(n_ff)
        sa_s.append(n_kv[0]); sb_s.append(n_kv[1]); sc_s.append(n_kv[2])
    y_prompt, y_sample = hp, hs
    new_shift_p, new_shift_s = jnp.stack(sh_p), jnp.stack(sh_s)
    new_wkv_p, new_wkv_s = jnp.stack(wk_p), jnp.stack(wk_s)
    new_conv_p, new_conv_s = jnp.stack(cv_p), jnp.stack(cv_s)
    new_swa_a_p, new_swa_a_s = jnp.stack(sa_p), jnp.stack(sa_s)
    new_swa_b_p, new_swa_b_s = jnp.stack(sb_p), jnp.stack(sb_s)
    new_swa_c_p, new_swa_c_s = jnp.stack(sc_p), jnp.stack(sc_s)
    new_ffn_p, new_ffn_s = jnp.stack(ff_p), jnp.stack(ff_s)
    return (y_prompt, y_sample, new_shift_p, new_shift_s, new_wkv_p, new_wkv_s, new_conv_p, new_conv_s,
            new_swa_a_p, new_swa_a_s, new_swa_b_p, new_swa_b_s, new_swa_c_p, new_swa_c_s, new_ffn_p, new_ffn_s)
```

```python
import numpy as np
from contextlib import ExitStack, contextmanager
import concourse.bass as bass
import concourse.mybir as mybir
from concourse.bass_utils import run_bass_kernel_spmd

F32 = mybir.dt.float32
BF16 = mybir.dt.bfloat16
AF = mybir.ActivationFunctionType
ALU = mybir.AluOpType
AX = mybir.AxisListType

D_MODEL = 1024
DEPTH = 2
HEAD_DIM = 64
R_HEADS = 8
R_WIDTH = 512
R_COLS = 1792
LNX_EPS = 64e-5
CONV_CH = 512
CONV_K = 31
SWA_GROUPS = ((128, 1), (512, 4), (2048, 16))
G_HEADS = 4
A_WIDTH = 768
D_FF = 2816
OFF_R = 0
OFF_C = 1792
OFF_Q = OFF_C + 1024
OFF_K = OFF_Q + 768
OFF_V = OFF_K + 768
OFF_GATE = OFF_V + 768
IN_COLS = 8192
ALPHA = (2 * DEPTH) ** 0.25
LN_EPS = 1e-5
NEG = -30000.0
NBUF = (128, 512, 2048)

P = 128
T = 512
C = 64
NCH = T // C
NSLOT = 5
SLAB = 4096
NSLAB = 42
SL_LORA = 0
SL_RP = 1
SL_ROUT = 5
SL_G0 = 6
SL_CU = 8
SL_CG = 9
SL_COUT = 10
SL_G1 = 11
SL_ATT = 13
SL_AOUT = 19
SL_G2 = 20
SL_WO = 22
SL_FIN = 24
SL_FOUT = 35
NSLAB = 43

PC = {}
_pc = 0
for _n, _w in (("mu", 14), ("omm", 14), ("w0", 4), ("nw0", 4), ("a0", 4), ("kk", 4), ("ka", 4), ("rk", 4),
               ("lng", 4), ("lnb", 4), ("cdw", 4 * 31), ("cdb", 4), ("clg", 4), ("clb", 4), ("gb", 24),
               ("l1g", 8), ("l1b", 8), ("fdw", 44 * 3), ("fdb", 44), ("l2g", 8), ("l2b", 8)):
    PC[_n] = _pc
    _pc += _w
NPC = _pc


class Buf:
    __slots__ = ("name", "w", "r")

    def __init__(self, name):
        self.name = name
        self.w = None
        self.r = []


class _Rec:
    def __init__(self):
        self.calls = []

    def __getattr__(self, name):
        def f(*a, **k):
            self.calls.append((name, a, k))
            return self
        return f


def _freeze(fn):
    rec = _Rec()
    fn(rec)
    calls = rec.calls

    def replay(e):
        out = [getattr(e, name)(*a, **k) for (name, a, k) in calls]
        return out
    return replay, len(calls)


class Sched:
    ENGS = ("pe", "act", "dve", "pool", "sp")
    LIMIT = 60000

    def __init__(self, nc, es):
        self.nc = nc
        self.es = es
        self.sems = []
        self.ops = {e: [] for e in self.ENGS}
        self.cnt = {e: 0 for e in self.ENGS}
        self.semi = {e: self._newsem(e) for e in self.ENGS}
        self.waited = {e: {} for e in self.ENGS}
        self.dkeys = {}
        self.pending = []
        self.out_toks = []
        self.lastsig = {}

    def _newsem(self, name):
        s = self.es.enter_context(self.nc.semaphore(f"s{len(self.sems)}_{name}"))
        self.sems.append(s)
        return len(self.sems) - 1

    def _force_sig(self, te):
        ops = self.ops[te]
        k = len(ops) - 1
        while ops[k][0] is None:
            k -= 1
        assert ops[k][2] is None
        self.cnt[te] += 1
        ops[k][2] = (self.semi[te], 1)
        self.lastsig[te] = True

    def _resolve(self, eng, tok):
        si, val, te = tok
        if te != "dma" and te != eng and si == self.semi[te] and val > self.cnt[te]:
            assert val == self.cnt[te] + 1
            self._force_sig(te)

    def _waits(self, eng, reads, writes, is_dma):
        toks = set()
        for b in reads:
            if b.w is not None:
                toks.add(b.w)
        for b in writes:
            if b.w is not None:
                toks.add(b.w)
            toks.update(b.r)
        out = []
        for (si, val, te) in toks:
            if te == eng and not is_dma and eng == "pe":
                continue
            if te == eng and is_dma:
                if te != "dma" and si == self.semi[te] and val > self.cnt[te]:
                    self._force_sig(te)
            self._resolve(eng, (si, val, te))
            if self.waited[eng].get(si, 0) >= val:
                continue
            out.append((si, val))
        best = {}
        for si, val in out:
            best[si] = max(best.get(si, 0), val)
        for si, val in best.items():
            self.waited[eng][si] = val
        return list(best.items())

    def op(self, eng, fn, reads=(), writes=(), sig=True):
        fn, ncalls = _freeze(fn)
        assert ncalls == 1
        waits = self._waits(eng, reads, writes, False)
        if self.cnt[eng] >= self.LIMIT:
            self.semi[eng] = self._newsem(eng)
            self.cnt[eng] = 0
        if sig:
            self.cnt[eng] += 1
            tok = (self.semi[eng], self.cnt[eng], eng)
            inc = (self.semi[eng], 1)
        else:
            tok = (self.semi[eng], self.cnt[eng] + 1, eng)
            inc = None
        self.ops[eng].append([fn, waits, inc, 1])
        self.lastsig[eng] = sig
        for b in reads:
            b.r.append(tok)
        for b in writes:
            b.w = tok
            b.r = []
        return tok

    def dma(self, eng, fn, key, reads=(), writes=(), n=1, local=True, out=False):
        fn, ncalls = _freeze(fn)
        assert ncalls == n, (ncalls, n)
        waits = self._waits(eng, reads, writes, True)
        if key not in self.dkeys or self.dkeys[key][1] + 16 * n > self.LIMIT:
            self.dkeys[key] = [self._newsem("d" + key), 0]
        d = self.dkeys[key]
        d[1] += 16 * n
        tok = (d[0], d[1], "dma")
        self.ops[eng].append([fn, waits, (d[0], 16), n])
        for b in reads:
            b.r.append(tok)
        for b in writes:
            b.w = tok
            b.r = []
        if local:
            self.pending.append(tok)
        if out:
            self.out_toks.append(tok)
        return tok

    def wait_only(self, eng, toks):
        out = []
        best = {}
        for (si, val, te) in toks:
            self._resolve(eng, (si, val, te))
            if self.waited[eng].get(si, 0) >= val:
                continue
            best[si] = max(best.get(si, 0), val)
        for si, val in best.items():
            self.waited[eng][si] = val
            out.append((si, val))
        if out:
            self.ops[eng].append([None, out, None, 0])

    def last_tok(self, eng):
        return (self.semi[eng], self.cnt[eng], eng)

    def barrier(self, engs=("pe", "act", "dve")):
        for e in engs:
            if not self.lastsig.get(e, True):
                self._force_sig(e)
        toks = [self.last_tok(e) for e in engs if self.cnt[e] > 0] + list(self.pending)
        for e in engs:
            self.wait_only(e, toks)
        self.pending = []

    def emit(self, block):
        nc = self.nc
        table = {"pe": block.tensor, "act": block.scalar, "dve": block.vector, "pool": block.gpsimd,
                 "sp": block.sync}
        for eng in self.ENGS:
            ops = self.ops[eng]
            sems = self.sems

            def body(e, ops=ops):
                for fn, waits, inc, n in ops:
                    for si, val in waits:
                        e.wait_ge(sems[si], val)
                    if fn is None:
                        continue
                    r = fn(e)
                    if inc is not None:
                        assert len(r) == n
                        for ins in r:
                            ins.then_inc(sems[inc[0]], inc[1])
            table[eng](body)


def _slab_k(Wc):
    K, n = Wc.shape
    kc = K // 128
    a = np.ascontiguousarray(Wc.reshape(kc, 128, n).transpose(1, 0, 2)).reshape(128, kc * n)
    out = np.zeros((128, SLAB), np.float32)
    out[:, :kc * n] = a
    return out


def _cols(v):
    return np.ascontiguousarray(np.asarray(v, np.float32).reshape(-1, 128).T)


def _ffn_chunk_order():
    order = []
    for s in range(11):
        for j in range(4):
            if j < 2:
                order.append(2 * s + j)
            else:
                order.append(22 + 2 * s + (j - 2))
    return order


def pack_weights(inp):
    wp = np.zeros((DEPTH, NSLAB, 128, SLAB), np.float32)
    pp = np.zeros((DEPTH, 128, NPC), np.float32)
    lw = np.zeros((DEPTH, 128, 1024), np.float32)
    forder = _ffn_chunk_order()
    for l in range(DEPTH):
        w_in = inp["w_in"][l]
        wp[l, SL_LORA] = _slab_k(w_in[:, 1536:1792])
        for p in range(4):
            cols = np.concatenate([w_in[:, p * 128:(p + 1) * 128], w_in[:, 512 + p * 128:512 + (p + 1) * 128],
                                   w_in[:, 1024 + p * 128:1024 + (p + 1) * 128]], axis=1)
            wp[l, SL_RP + p] = _slab_k(cols)
        wp[l, SL_ROUT] = _slab_k(inp["w_rwkv_out"][l])
        for bi, sl in enumerate((SL_G0, SL_G1, SL_G2)):
            for hf in range(2):
                c0 = OFF_GATE + bi * 1024 + hf * 512
                wp[l, sl + hf] = _slab_k(w_in[:, c0:c0 + 512])
        wp[l, SL_CU] = _slab_k(w_in[:, OFF_C:OFF_C + 512])
        wp[l, SL_CG] = _slab_k(w_in[:, OFF_C + 512:OFF_C + 1024])
        wp[l, SL_COUT] = _slab_k(inp["w_conv_out"][l])
        for g in range(3):
            q = w_in[:, OFF_Q + g * 256:OFF_Q + (g + 1) * 256]
            k = w_in[:, OFF_K + g * 256:OFF_K + (g + 1) * 256]
            v = w_in[:, OFF_V + g * 256:OFF_V + (g + 1) * 256]
            wp[l, SL_ATT + 2 * g] = _slab_k(np.concatenate([q, k], axis=1))
            wp[l, SL_ATT + 2 * g + 1] = _slab_k(np.concatenate([k, v], axis=1))
        wp[l, SL_AOUT] = _slab_k(inp["w_attn_out"][l])
        wp[l, SL_WO] = _slab_k(inp["w_o"][l][:, 0:512])
        wp[l, SL_WO + 1] = _slab_k(inp["w_o"][l][:, 512:1024])
        wfi = inp["w_ffn_in"][l]
        for s in range(11):
            cols = np.concatenate([wfi[:, ch * 128:(ch + 1) * 128] for ch in forder[4 * s:4 * s + 4]], axis=1)
            wp[l, SL_FIN + s] = _slab_k(cols)
        wfo = inp["w_ffn_out"][l]
        for oc in range(8):
            wp[l, SL_FOUT + oc] = _slab_k(wfo[:, oc * 128:(oc + 1) * 128])
        pr = pp[l]
        pr[:, PC["mu"]:PC["mu"] + 14] = _cols(inp["rwkv_mu"][l])
        pr[:, PC["w0"]:PC["w0"] + 4] = _cols(inp["rwkv_w0"][l])
        pr[:, PC["a0"]:PC["a0"] + 4] = _cols(inp["rwkv_a0"][l])
        pr[:, PC["kk"]:PC["kk"] + 4] = _cols(inp["rwkv_k_k"][l])
        pr[:, PC["ka"]:PC["ka"] + 4] = _cols(inp["rwkv_k_a"][l])
        pr[:, PC["rk"]:PC["rk"] + 4] = _cols(inp["rwkv_r_k"][l].reshape(-1))
        pr[:, PC["lng"]:PC["lng"] + 4] = _cols(inp["rwkv_ln_g"][l])
        pr[:, PC["lnb"]:PC["lnb"] + 4] = _cols(inp["rwkv_ln_b"][l])
        cdw = inp["conv_dw"][l]
        for c in range(4):
            pr[:, PC["cdw"] + c * 31:PC["cdw"] + (c + 1) * 31] = cdw[:, c * 128:(c + 1) * 128].T
        pr[:, PC["cdb"]:PC["cdb"] + 4] = _cols(inp["conv_dw_b"][l])
        pr[:, PC["clg"]:PC["clg"] + 4] = _cols(inp["conv_ln_g"][l])
        pr[:, PC["clb"]:PC["clb"] + 4] = _cols(inp["conv_ln_b"][l])
        pr[:, PC["gb"]:PC["gb"] + 24] = _cols(inp["gate_b"][l].reshape(-1))
        pr[:, PC["l1g"]:PC["l1g"] + 8] = _cols(inp["ln1_g"][l])
        pr[:, PC["l1b"]:PC["l1b"] + 8] = _cols(inp["ln1_b"][l])
        fdw = inp["ffn_dw"][l]
        fdb = inp["ffn_dw_b"][l]
        for qi, ch in enumerate(forder):
            pr[:, PC["fdw"] + qi * 3:PC["fdw"] + qi * 3 + 3] = fdw[:, ch * 128:(ch + 1) * 128].T
            pr[:, PC["fdb"] + qi] = fdb[ch * 128:(ch + 1) * 128]
        pr[:, PC["l2g"]:PC["l2g"] + 8] = _cols(inp["ln2_g"][l])
        pr[:, PC["l2b"]:PC["l2b"] + 8] = _cols(inp["ln2_b"][l])
        lw[l, 0:64, 0:512] = inp["rwkv_w_up"][l]
        lw[l, 64:128, 0:512] = inp["rwkv_a_up"][l]
        lw[l, :, 512:1024] = inp["rwkv_g_up"][l]
    return wp, pp, lw


def pack_samples(inp, s0, ns):
    forder = _ffn_chunk_order()
    x = inp["x_sample"][s0:s0 + ns, 0, :]
    xs_fm = np.ascontiguousarray(x.reshape(ns, 8, 128).transpose(2, 1, 0))
    sh = inp["state_shift"][:, s0:s0 + ns, 0, :]
    sh_fm = np.ascontiguousarray(sh.reshape(DEPTH, ns, 14, 128).transpose(0, 3, 2, 1))
    wk = inp["state_wkv"][:, s0:s0 + ns]
    wk = wk.reshape(DEPTH, ns, 4, 2, 64, 64)
    wkv_fm = np.ascontiguousarray(wk.transpose(0, 3, 5, 1, 2, 4)).reshape(DEPTH, 128, ns, 4, 64)
    cv = inp["state_conv"][:, s0:s0 + ns]
    cv_fm = np.ascontiguousarray(cv.reshape(DEPTH, ns, 30, 4, 128).transpose(0, 4, 3, 1, 2))
    ff = inp["state_ffn"][:, s0:s0 + ns]
    ff4 = ff.reshape(DEPTH, ns, 2, 44, 128)[:, :, :, forder, :]
    ff_fm = np.ascontiguousarray(ff4.transpose(0, 4, 3, 1, 2))
    out = {"xs_fm": xs_fm, "sh_fm": sh_fm, "wkv_fm": wkv_fm, "cv_fm": cv_fm, "cv_nat": np.ascontiguousarray(cv),
           "ff_fm": ff_fm, "ff_nat": np.ascontiguousarray(ff)}
    for g, nm in enumerate(("cache_swa_a", "cache_swa_b", "cache_swa_c")):
        c = inp[nm][:, s0:s0 + ns]
        out[f"cache{g}"] = np.ascontiguousarray(c.reshape(DEPTH, ns, c.shape[2], 512))
    return out


def make_consts():
    cf = np.zeros((128, 2048), np.float32)
    cf[:, 0:128] = np.eye(128, dtype=np.float32)
    blk = np.zeros((128, 128), np.float32)
    blk[:64, :64] = 1.0
    blk[64:, 64:] = 1.0
    cf[:, 128:256] = blk
    cf[:, 256:384] = blk / 64.0
    cf[:, 384:512] = 1.0 / 1024.0
    cf[:, 512:640] = 1.0 / 512.0
    rm = np.ones((128, 512), np.float32)
    rm[:, ::64] = 0.0
    cf[:, 640:1152] = rm
    s = np.arange(64)[:, None]
    t = np.arange(64)[None, :]
    m1 = np.concatenate([(s < t), (s <= t)], axis=1).astype(np.float32)
    cf[0:64, 1152:1280] = m1
    cf[64:128, 1152:1280] = m1
    m2 = (t < s).astype(np.float32)
    cf[0:64, 1280:1344] = m2
    cf[64:128, 1280:1344] = m2
    cf[0:64, 1344:1408] = np.eye(64, dtype=np.float32)
    cf[64:128, 1344:1408] = np.eye(64, dtype=np.float32)
    for s_ in range(4):
        cf[s_, 1408 + s_ * 128:1408 + (s_ + 1) * 128] = 1.0
    cf[:, 1920:1984] = 1.0
    cb = np.zeros((128, 1536), np.float32)
    j = np.arange(128)[:, None]
    q = np.arange(128)[None, :]
    same = np.where(j <= q, 0.0, NEG).astype(np.float32)
    prev = np.where(j >= q, 0.0, NEG).astype(np.float32)
    cb[:, 0:128] = np.eye(128, dtype=np.float32)
    cb[:, 128:256] = same
    cb[:, 256:384] = prev
    cb[:, 384:448] = 1.0
    off = 448
    for o in range(4):
        for kb, m in enumerate((same, prev)):
            for rep in range(4):
                cb[:, off:off + 32] = m[:, o * 32:(o + 1) * 32]
                off += 32
    assert off == 448 + 1024
    return cf, cb


STOP = None
DEBUG = False
DBGSTATE = {"tile": -1}


class _Stop(Exception):
    pass


def build(SEQ, NSAMP=0):
    NT = SEQ // T
    nc = bass.Bass("TRN2", target_bir_lowering=False)
    es = ExitStack()
    dram = {}

    def din(name, shape, dt=F32):
        dram[name] = nc.dram_tensor(name, list(shape), dt, kind="ExternalInput").ap()
        return dram[name]

    def dout(name, shape):
        dram[name] = nc.dram_tensor(name, list(shape), F32, kind="ExternalOutput").ap()
        return dram[name]

    x_fm = din("x_fm", [128, 8, SEQ])
    wpk = din("wpack", [DEPTH, NSLAB, 128, SLAB])
    ppk = din("ppack", [DEPTH, 128, NPC])
    lwk = din("lorapack", [DEPTH, 128, 1024])
    cfk = din("constf", [128, 2048])
    cbk = din("constb", [128, 1536])
    xs_d = nc.dram_tensor("xs_scratch", [128, 8, SEQ], F32, kind="Internal").ap()
    y_p = dout("y_p", [SEQ, D_MODEL])
    o_shift = dout("o_shift_p", [DEPTH, R_COLS])
    o_wkv = dout("o_wkv_p", [DEPTH, R_HEADS, 64, 64])
    o_conv = dout("o_conv_p", [DEPTH, CONV_K - 1, CONV_CH])
    keep = [min(w, SEQ) for w, _ in SWA_GROUPS]
    o_swa = [dout(f"o_swa{g}_p", [DEPTH, keep[g], 512]) for g in range(3)]
    o_ffn = dout("o_ffn_p", [DEPTH, 2, 2 * D_FF])
    dbg_d = dout("dbg", [16, 128, T]) if DEBUG else None

    NS = NSAMP
    if NS:
        xs_fm = din("xs_fm", [128, 8, NS])
        sh_fm = din("sh_fm", [DEPTH, 128, 14, NS])
        wkv_fm = din("wkv_fm", [DEPTH, 128, NS, 4, 64])
        cv_fm = din("cv_fm", [DEPTH, 128, 4, NS, 30])
        cv_nat = din("cv_nat", [DEPTH, NS, 30, 512])
        ff_fm = din("ff_fm", [DEPTH, 128, 44, NS, 2])
        ff_nat = din("ff_nat", [DEPTH, NS, 2, 2 * D_FF])
        cache_d = [din(f"cache{g}", [DEPTH, NS, NBUF[g], 512]) for g in range(3)]
        y_s = dout("y_s", [NS, D_MODEL])
        o_shift_s = dout("o_shift_s", [DEPTH, NS, R_COLS])
        o_wkv_s = dout("o_wkv_s", [DEPTH, NS, R_HEADS, 64, 64])
        o_conv_s = dout("o_conv_s", [DEPTH, NS, CONV_K - 1, CONV_CH])
        o_swa_s = [dout(f"o_swa{g}_s", [DEPTH, NS, NBUF[g], 512]) for g in range(3)]
        o_ffn_s = dout("o_ffn_s", [DEPTH, NS, 2, 2 * D_FF])

    S = Sched(nc, es)

    def sb(name, shape, dt=F32):
        return es.enter_context(nc.sbuf_tensor(name, list(shape), dt))

    ring = sb("ring", [128, NSLOT, SLAB], BF16)
    ring_b = [Buf(f"ring{i}") for i in range(NSLOT)]
    cf = sb("cf", [128, 2048])
    cb = sb("cb", [128, 1536], BF16)
    cf_b, cb_b = Buf("cf"), Buf("cb")
    pp = sb("pp", [128, DEPTH, NPC])
    pp_b = Buf("pp")
    lw = sb("lw", [128, DEPTH, 1024])
    lw_b = Buf("lw")
    xT32 = sb("xT32", [128, 8, T])
    xTb = sb("xTb", [128, 8, T], BF16)
    xT32_b, xTb_b = Buf("xT32"), Buf("xTb")
    merged = sb("merged", [128, 8, T])
    merged_b = Buf("merged")
    kT_h = [sb("kTa", [128, 2, 2 * T], BF16), sb("kTb", [128, 2, 2 * T], BF16), sb("kTc", [128, 2, max(SEQ, 4096)], BF16)]
    kT_hb = [Buf("kTa"), Buf("kTb"), Buf("kTc")]
    Vtm = [sb("Va", [128, 8, 256], BF16), sb("Vb", [128, 8, 256], BF16), sb("Vc", [128, 32, 256], BF16)]
    Vtm_b = [Buf("Va"), Buf("Vb"), Buf("Vc")]
    Hst = sb("Hst", [128, 4, 64])
    Hst_b = Buf("Hst")
    zcar = sb("zcar", [128, 14])
    zcar_b = Buf("zcar")
    uhist = sb("uhist", [128, 4, 30])
    uhist_b = Buf("uhist")
    fcar = sb("fcar", [128, 44, 2])
    fcar_b = Buf("fcar")
    ps = [es.enter_context(nc.psum_tensor(f"ps{i}", [128, 512], F32)) for i in range(8)]
    ps_b = [Buf(f"ps{i}") for i in range(8)]

    ARENA_N = 12288
    arena_t = sb("arena", [128, ARENA_N])

    class Arena:
        def __init__(self):
            self.top = 0
            self.stack = []

        def push(self):
            self.stack.append(self.top)

        def pop(self):
            self.top = self.stack.pop()

        def alloc(self, shape, dt=F32):
            nfree = int(np.prod(shape[1:]))
            n32 = nfree if dt == F32 else (nfree + 1) // 2
            ap = arena_t[0:shape[0], self.top:self.top + n32]
            self.top += n32
            assert self.top <= ARENA_N, ("arena overflow", self.top)
            if dt != F32:
                ap = ap.bitcast(dt)
            if len(shape) == 3:
                ap = ap.rearrange("p (a b) -> p a b", a=shape[1])
            elif len(shape) == 4:
                ap = ap.rearrange("p (a b c) -> p a b c", a=shape[1], b=shape[2])
            return ap

    AR = Arena()

    @contextmanager
    def scope():
        AR.push()
        try:
            yield None
        finally:
            AR.pop()

    @contextmanager
    def scope_alloc(shape, dt=F32):
        AR.push()
        try:
            yield AR.alloc(list(shape), dt)
        finally:
            AR.pop()

    ident = cf[:, 0:128]
    ones_blk = cf[:, 128:256]
    mean_blk = cf[:, 256:384]
    mean_d = cf[:, 384:512]
    mean_c = cf[:, 512:640]
    rmask = cf[:, 640:1152]
    identb = cb[:, 0:128]
    mb_same = cb[:, 128:256]
    mb_prev = cb[:, 256:384]
    ones_b64 = cb[:, 384:448]

    def pcol(l, name, i=0, rows=slice(0, 128)):
        c = PC[name] + i
        return pp[rows, l, c:c + 1]

    S.dma("sp", lambda e: e.dma_start(out=cf[:], in_=cfk[:, :]), "cf", writes=[cf_b], local=False)
    S.dma("pool", lambda e: e.dma_start(out=cb[:], in_=cbk[:, :]), "cb", writes=[cb_b], local=False)
    S.dma("sp", lambda e: [e.dma_start(out=pp[:, l, :], in_=ppk[l, :, :]) for l in range(DEPTH)], "pp",
          writes=[pp_b], n=DEPTH, local=False)
    S.dma("sp", lambda e: [e.dma_start(out=lw[:, l, :], in_=lwk[l, :, :]) for l in range(DEPTH)], "lw",
          writes=[lw_b], n=DEPTH, local=False)
    for l in range(DEPTH):
        S.op("dve", lambda e, l=l: e.tensor_scalar(out=pp[:, l, PC["omm"]:PC["omm"] + 14],
                                                   in0=pp[:, l, PC["mu"]:PC["mu"] + 14], scalar1=-1.0, scalar2=1.0,
                                                   op0=ALU.mult, op1=ALU.add), reads=[pp_b], writes=[pp_b])
        S.op("dve", lambda e, l=l: e.tensor_scalar(out=pp[:, l, PC["nw0"]:PC["nw0"] + 4],
                                                   in0=pp[:, l, PC["w0"]:PC["w0"] + 4], scalar1=-1.0, scalar2=None,
                                                   op0=ALU.mult), reads=[pp_b], writes=[pp_b])
    for g in range(3):
        S.op("dve", lambda e, g=g: e.memset(kT_h[g][:], 0.0), writes=[kT_hb[g]])
        S.op("dve", lambda e, g=g: e.memset(Vtm[g][:], 0.0), writes=[Vtm_b[g]])

    wstate = {"n": 0}

    def wload(l, slab):
        i = wstate["n"] % NSLOT
        wstate["n"] += 1
        S.dma("pool", lambda e: e.dma_start(out=ring[:, i, :], in_=wpk[l, slab, :, :]), f"ring{i}",
              writes=[ring_b[i]], local=False)
        return i

    class WQ:
        def __init__(self):
            self.plan = []
            self.issued = 0
            self.slots = {}

        def extend(self, items):
            self.plan.extend(items)

        def get(self, idx, ahead=NSLOT - 1):
            while self.issued < len(self.plan) and self.issued <= idx + ahead:
                l, slab = self.plan[self.issued]
                self.slots[self.issued] = wload(l, slab)
                self.issued += 1
            return self.slots[idx]

    wq = WQ()
    for l in range(DEPTH):
        if NSAMP:
            for sl in range(NSLAB):
                wq.extend([(l, sl)])
        for i in range(NT):
            if STOP and ((l, i) != (0, 0) or STOP == "load"):
                continue
            for sl in range(NSLAB):
                wq.extend([(l, sl)])
    wctr = {"i": 0}

    cur_layer = {"l": 0}

    def next_slab(expect):
        idx = wctr["i"]
        assert wq.plan[idx] == (cur_layer["l"], expect), (wq.plan[idx], cur_layer["l"], expect)
        slot = wq.get(idx, ahead=NSLOT - 3)
        wctr["i"] += 1
        return slot

    def W(slot, kc_n, ncols):
        return ring[:, slot, 0:kc_n * ncols].rearrange("p (k n) -> p k n", n=ncols)

    def dbg(idx, ap, buf):
        if DEBUG:
            S.dma("sp", lambda e: e.dma_start(out=dbg_d[idx, 0:ap.shape[0], 0:ap.shape[1]], in_=ap), "dbg", reads=[buf], out=True)

    epsc = sb("epsc", [128, 4])
    epsc_b = Buf("epsc")
    for ci, ev in enumerate((LN_EPS, LNX_EPS, 1e-24)):
        S.op("dve", lambda e, ci=ci, ev=ev: e.memset(epsc[:, ci:ci + 1], float(ev)), writes=[epsc_b])
    eps_col = {float(LN_EPS): 0, float(LNX_EPS): 1, 1e-24: 2}

    def rsqrt(out, in_, eps, reads, wbuf):
        ci = eps_col[float(eps)]
        S.op("act", lambda e: e.activation(out=out, in_=in_, func=AF.Sqrt, bias=epsc[0:out.shape[0], ci:ci + 1]),
             reads=list(reads) + [epsc_b], writes=[wbuf])
        S.op("dve", lambda e: e.reciprocal(out=out, in_=out), reads=[wbuf], writes=[wbuf])

    def mm(out, lhsT, rhs, start, stop, reads, writes, sig):
        S.op("pe", lambda e: e.matmul(out, lhsT=lhsT, rhs=rhs, start=start, stop=stop), reads=reads,
             writes=writes, sig=sig)

    def proj_fm(pi, slot, col0, rhs_t, rhs_b, ncol=T, kc_n=8, wcols=512, rhs_cols=slice(0, T)):
        Wv = W(slot, kc_n, wcols)
        for kc in range(kc_n):
            mm(ps[pi][:, 0:ncol], Wv[:, kc, col0:col0 + 128], rhs_t[:, kc, rhs_cols], kc == 0, kc == kc_n - 1,
               [ring_b[slot], rhs_b], [ps_b[pi]], kc == kc_n - 1)

    def ln_fm(l, src, src_b, nch, mean_mat, gname, bname, eps, outs, tmp, tmp_b, pA=6, pB=7, N=T):
        for c in range(nch):
            mm(ps[pA][:, 0:N], mean_mat, src[:, c, :], c == 0, c == nch - 1, [cf_b, src_b], [ps_b[pA]], c == nch - 1)
        mean_sb = tmp[:, 0, 0:N]
        S.op("act", lambda e: e.copy(out=mean_sb, in_=ps[pA][:, 0:N]), reads=[ps_b[pA]], writes=[tmp_b])
        sq = es_local["sq"]
        sq_b = es_local["sq_b"]
        for c in range(nch):
            S.op("dve", lambda e, c=c: e.tensor_sub(out=src[:, c, :], in0=src[:, c, :], in1=mean_sb),
                 reads=[src_b, tmp_b], writes=[src_b])
            S.op("act", lambda e, c=c: e.activation(out=sq[:, c % 2, 0:N], in_=src[:, c, :], func=AF.Square),
                 reads=[src_b], writes=[sq_b[c % 2]])
            mm(ps[pB][:, 0:N], mean_mat, sq[:, c % 2, 0:N], c == 0, c == nch - 1, [cf_b, sq_b[c % 2]], [ps_b[pB]], True)
        rsqrt(mean_sb, ps[pB][:, 0:N], float(eps), [ps_b[pB]], tmp_b)
        for c in range(nch):
            S.op("dve", lambda e, c=c: e.tensor_mul(out=src[:, c, :], in0=src[:, c, :], in1=mean_sb),
                 reads=[src_b, tmp_b], writes=[src_b])
            for (ot, ob) in outs:
                S.op("dve", lambda e, c=c, ot=ot: e.tensor_scalar(out=ot[:, c, :], in0=src[:, c, :],
                                                                   scalar1=pcol(l, gname, c), scalar2=pcol(l, bname, c),
                                                                   op0=ALU.mult, op1=ALU.add),
                     reads=[src_b, pp_b], writes=[ob])

    es_local = {}

    def merge_branch(l, bidx, slot_out, kc_n, actT, act_b, gslots, N=T, xb=None, xb_b=None, mg=None, mg_b=None):
        xb = xTb if xb is None else xb
        xb_b = xTb_b if xb_b is None else xb_b
        mg = merged if mg is None else mg
        mg_b = merged_b if mg_b is None else mg_b
        Wo = W(slot_out, kc_n, 1024)
        with scope_alloc([128, 2, N], F32) as gsb:
            gsb_b = [Buf("gsb0"), Buf("gsb1")]
            for oc in range(8):
                py, pg = oc % 2, 2 + oc % 2
                for kc in range(kc_n):
                    mm(ps[py][:, 0:N], Wo[:, kc, oc * 128:(oc + 1) * 128], actT[:, kc, :], kc == 0, kc == kc_n - 1,
                       [ring_b[slot_out], act_b], [ps_b[py]], kc == kc_n - 1)
                proj_fm(pg, gslots[oc // 4], (oc % 4) * 128, xb, xb_b, ncol=N, rhs_cols=slice(0, N))
                g_ = gsb[:, oc % 2, :]
                S.op("act", lambda e, g_=g_, pg=pg, oc=oc: e.activation(out=g_, in_=ps[pg][:, 0:N], func=AF.Sigmoid,
                                                                         bias=pcol(l, "gb", bidx * 8 + oc)),
                     reads=[ps_b[pg], pp_b], writes=[gsb_b[oc % 2]])
                if bidx == 0:
                    S.op("dve", lambda e, g_=g_, py=py, oc=oc: e.tensor_mul(out=mg[:, oc, :], in0=ps[py][:, 0:N], in1=g_),
                         reads=[ps_b[py], gsb_b[oc % 2]], writes=[mg_b])
                else:
                    S.op("dve", lambda e, g_=g_, py=py: e.tensor_mul(out=g_, in0=ps[py][:, 0:N], in1=g_),
                         reads=[ps_b[py], gsb_b[oc % 2]], writes=[gsb_b[oc % 2]])
                    S.op("dve", lambda e, g_=g_, oc=oc: e.tensor_add(out=mg[:, oc, :], in0=mg[:, oc, :], in1=g_),
                         reads=[mg_b, gsb_b[oc % 2]], writes=[mg_b])
            if DEBUG and N == T and l == 0 and DBGSTATE["tile"] == 0:
                dbg(10 + bidx, mg[:, 0, :], mg_b)
            S.barrier()

    def fm2tm_store(src_cols_fn, nchunk, nrows, dst_fn, key):
        with scope_alloc([128, 512], F32) as stg:
            stg_b = Buf("stg")
            for c in range(nchunk):
                ap, bf = src_cols_fn(c)
                S.op("pe", lambda e, ap=ap, c=c: e.matmul(ps[4][0:nrows, c * 128:(c + 1) * 128], lhsT=ap, rhs=ident,
                                                          start=True, stop=True),
                     reads=[bf, cf_b], writes=[ps_b[4]])
            S.op("act", lambda e: e.copy(out=stg[0:nrows, 0:nchunk * 128], in_=ps[4][0:nrows, 0:nchunk * 128]),
                 reads=[ps_b[4]], writes=[stg_b])
            S.dma("sp", dst_fn(stg), key, reads=[stg_b], out=True)
            S.barrier()

    def tile_prog(l, i):
        DBGSTATE["tile"] = i
        par = i % 2
        last = (i == NT - 1)
        src = x_fm if l == 0 else xs_d
        S.dma("sp", lambda e: e.dma_start(out=xT32[:], in_=src[:, :, i * T:(i + 1) * T]), "xload",
              writes=[xT32_b], local=False)
        S.op("act", lambda e: e.copy(out=xTb[:], in_=xT32[:]), reads=[xT32_b], writes=[xTb_b])

        if STOP == "load":
            return
        rwkv_phase(l, i, last)
        if STOP == "rwkv":
            return
        conv_phase(l, i, last)
        if STOP == "conv":
            return
        attn_phase(l, i, par, last)
        if STOP == "attn":
            return
        ffn_phase(l, i, last)

    def rwkv_phase(l, i, last):
        with scope():
            def lt(name, shape, dt=F32):
                return AR.alloc(list(shape), dt)
            lora = merged[:, 0:2, :]
            lora_b = Buf("lora")
            zraw = merged[:, 2:5, :].rearrange("p a t -> p (a t)")[:, 0:2 * (T + 1)].rearrange("p (a t) -> p a t", a=2)
            zraw_b = [Buf("zraw0"), Buf("zraw1")]
            ozT = lt("ozT", [128, 4, T], BF16)
            ozT_b = Buf("ozT")
            zctr = {"n": 0}

            def shift_chunk(pi, och, dst, dst_b):
                zi = zctr["n"] % 2
                zctr["n"] += 1
                zr = zraw[:, zi, :]
                S.op("act", lambda e: e.copy(out=zr[:, 1:T + 1], in_=ps[pi][:, :]), reads=[ps_b[pi]], writes=[zraw_b[zi]])
                S.op("dve", lambda e: e.tensor_copy(out=zr[:, 0:1], in_=zcar[:, och:och + 1]), reads=[zcar_b],
                     writes=[zraw_b[zi]])
                S.op("dve", lambda e: e.tensor_scalar(out=dst, in0=ps[pi][:, :], scalar1=pcol(l, "omm", och), scalar2=None,
                                                      op0=ALU.mult), reads=[ps_b[pi], pp_b], writes=[dst_b])
                S.op("dve", lambda e: e.scalar_tensor_tensor(out=dst, in0=zr[:, 0:T], scalar=pcol(l, "mu", och), in1=dst,
                                                             op0=ALU.mult, op1=ALU.add),
                     reads=[zraw_b[zi], pp_b, dst_b], writes=[dst_b])
                S.op("dve", lambda e: e.tensor_copy(out=zcar[:, och:och + 1], in_=zr[:, T:T + 1]), reads=[zraw_b[zi]],
                     writes=[zcar_b])

            slot = next_slab(SL_LORA)
            for c in range(2):
                proj_fm(c, slot, c * 128, xTb, xTb_b, wcols=256)
                shift_chunk(c, 12 + c, lora[:, c, :], lora_b)
            S.op("act", lambda e: e.activation(out=lora[0:64, 0, :], in_=lora[0:64, 0, :], func=AF.Tanh),
                 reads=[lora_b], writes=[lora_b])
            S.op("act", lambda e: e.activation(out=lora[:, 1, :], in_=lora[:, 1, :], func=AF.Sigmoid),
                 reads=[lora_b], writes=[lora_b])

            if STOP == "rw_lora":
                raise _Stop()
            for p in range(4):
                rwkv_pair(l, i, p, lora, lora_b, shift_chunk, ozT, ozT_b, None)

            s_out = next_slab(SL_ROUT)
            g0 = next_slab(SL_G0)
            g1 = next_slab(SL_G0 + 1)
            merge_branch(l, 0, s_out, 4, ozT, ozT_b, (g0, g1))
            if last:
                S.dma("sp", lambda e: e.dma_start(out=o_shift[l].rearrange("(c p) -> p c", p=128), in_=zcar[:, :],
                                                  allow_slow_non_contiguous=True), "oshift", reads=[zcar_b], out=True)
                with scope_alloc([64, 8, 64], F32) as stg:
                    stg_b = Buf("wkvst")
                    for h in range(8):
                        pr_, hl = h // 2, h % 2
                        S.op("pe", lambda e, pr_=pr_, hl=hl, h=h: e.matmul(
                            ps[4 + hl][0:64, pr_ * 64:(pr_ + 1) * 64], lhsT=Hst[hl * 64:(hl + 1) * 64, pr_, :],
                            rhs=cf[hl * 64:(hl + 1) * 64, 1344:1408], start=True, stop=True), reads=[Hst_b, cf_b],
                            writes=[ps_b[4 + hl]])
                    stg4 = stg[:].rearrange("p (pr hl) k -> p pr hl k", hl=2)
                    for hl in range(2):
                        S.op("act", lambda e, hl=hl: e.copy(out=stg4[:, :, hl, :],
                                                            in_=ps[4 + hl][0:64, 0:256].rearrange("p (pr k) -> p pr k", k=64)),
                             reads=[ps_b[4 + hl]], writes=[stg_b])
                    S.dma("sp", lambda e: e.dma_start(out=o_wkv[l].rearrange("h v k -> v h k"), in_=stg[:]), "owkv",
                          reads=[stg_b], out=True)
                    S.barrier()
            S.barrier()

    def rwkv_pair(l, i, p, lora, lora_b, shift_chunk, ozT, ozT_b, st_outer):
        with scope():
            cnt = {"n": 0}

            def lt(name, shape, dt=F32):
                cnt["n"] += 1
                return AR.alloc(list(shape), dt), Buf(name)
            rT, r_b = lt("rT", [128, T])
            kT, k_b = lt("kT", [128, T])
            vT, v_b = lt("vT", [128, T])
            lwT, lw_b2 = lt("lwT", [128, T])
            aT, a_b = lt("aT", [128, T])
            gT, g_b = lt("gT", [128, T])
            kkT, kk_b = lt("kkT", [128, T])
            bT, b_b = lt("bT", [128, T])
            csT, cs_b = lt("csT", [128, T])
            eT, e_b = lt("eT", [128, T])
            e2T, e2_b = lt("e2T", [128, T])
            KR, KR_b = lt("KR", [128, NCH, 2, C])
            bon, bon_b = lt("bon", [128, T])
            gam, gam_b = lt("gam", [128, NCH])
            Kt, Kt_b = kT, k_b
            Bt, Bt_b = bT, b_b
            Kh, Kh_b = rT, r_b
            Bh, Bh_b = kkT, kk_b
            OT, OT_b = aT, a_b
            slot = next_slab(SL_RP + p)
            for c, (dst, db, och) in enumerate(((rT, r_b, p), (kT, k_b, 4 + p), (vT, v_b, 8 + p))):
                proj_fm(c % 2, slot, c * 128, xTb, xTb_b, wcols=384)
                shift_chunk(c % 2, och, dst[:, :], db)
            cs128 = slice(p * 128, (p + 1) * 128)
            mm(ps[2][:, :], lw[0:64, l, cs128], lora[0:64, 0, :], True, True, [lw_b, lora_b], [ps_b[2]], True)
            mm(ps[3][:, :], lw[64:128, l, cs128], lora[64:128, 0, :], True, True, [lw_b, lora_b], [ps_b[3]], True)
            mm(ps[4][:, :], lw[:, l, 512 + p * 128:512 + (p + 1) * 128], lora[:, 1, :], True, True, [lw_b, lora_b],
               [ps_b[4]], True)
            S.op("act", lambda e: e.activation(out=eT[:, :], in_=ps[2][:, :], func=AF.Exp, scale=-1.0,
                                               bias=pcol(l, "nw0", p)), reads=[ps_b[2], pp_b], writes=[e_b])
            S.op("act", lambda e: e.activation(out=eT[:, :], in_=eT[:, :], func=AF.Ln, bias=1.0), reads=[e_b], writes=[e_b])
            S.op("act", lambda e: e.activation(out=eT[:, :], in_=eT[:, :], func=AF.Exp, scale=-1.0, bias=-0.5),
                 reads=[e_b], writes=[e_b])
            S.op("dve", lambda e: e.tensor_scalar(out=lwT[:, :], in0=eT[:, :], scalar1=-1.0, scalar2=None, op0=ALU.mult),
                 reads=[e_b], writes=[lw_b2])
            S.op("act", lambda e: e.activation(out=aT[:, :], in_=ps[3][:, :], func=AF.Sigmoid, bias=pcol(l, "a0", p)),
                 reads=[ps_b[3], pp_b], writes=[a_b])
            S.op("act", lambda e: e.copy(out=gT[:, :], in_=ps[4][:, :]), reads=[ps_b[4]], writes=[g_b])
            S.op("dve", lambda e: e.tensor_scalar(out=kkT[:, :], in0=kT[:, :], scalar1=pcol(l, "kk", p), scalar2=None,
                                                  op0=ALU.mult), reads=[k_b, pp_b], writes=[kk_b])
            S.op("dve", lambda e: e.tensor_mul(out=e2T[:, :], in0=kkT[:, :], in1=kkT[:, :]), reads=[kk_b], writes=[e2_b])
            mm(ps[2][:, :], ones_blk, e2T[:, :], True, True, [cf_b, e2_b], [ps_b[2]], True)
            rsqrt(e2T[:, :], ps[2][:, :], 1e-24, [ps_b[2]], e2_b)
            S.op("dve", lambda e: e.tensor_mul(out=kkT[:, :], in0=kkT[:, :], in1=e2T[:, :]), reads=[kk_b, e2_b],
                 writes=[kk_b])
            S.op("dve", lambda e: e.tensor_mul(out=bT[:, :], in0=kkT[:, :], in1=aT[:, :]), reads=[kk_b, a_b], writes=[b_b])
            S.op("dve", lambda e: e.tensor_scalar(out=aT[:, :], in0=aT[:, :], scalar1=-1.0, scalar2=pcol(l, "ka", p),
                                                  op0=ALU.add, op1=ALU.mult), reads=[a_b, pp_b], writes=[a_b])
            S.op("dve", lambda e: e.scalar_tensor_tensor(out=kT[:, :], in0=aT[:, :], scalar=1.0, in1=kT[:, :], op0=ALU.add,
                                                         op1=ALU.mult), reads=[a_b, k_b], writes=[k_b])
            S.op("dve", lambda e: e.scalar_tensor_tensor(out=e2T[:, :], in0=rT[:, :], scalar=pcol(l, "rk", p), in1=kT[:, :],
                                                         op0=ALU.mult, op1=ALU.mult), reads=[r_b, k_b, pp_b], writes=[e2_b])
            mm(ps[3][:, :], ones_blk, e2T[:, :], True, True, [cf_b, e2_b], [ps_b[3]], True)
            S.op("dve", lambda e: e.tensor_mul(out=bon[:, :], in0=ps[3][:, :], in1=vT[:, :]), reads=[ps_b[3], v_b],
                 writes=[bon_b])
            if (l, i, p) == (0, 0, 0):
                for di, (ap_, bf_) in enumerate(((rT, r_b), (kT, k_b), (vT, v_b), (lwT, lw_b2), (kkT, kk_b), (bT, b_b),
                                                 (gT, g_b), (bon, bon_b))):
                    dbg(di, ap_[:, :], bf_)
            S.op("dve", lambda e: e.tensor_tensor_scan(out=csT[:, :], data0=rmask, data1=lwT[:, :], initial=0.0,
                                                       op0=ALU.mult, op1=ALU.add), reads=[cf_b, lw_b2], writes=[cs_b])
            cs3 = csT[:, :].rearrange("p (j t) -> p j t", t=C)
            S.op("act", lambda e: e.activation(out=eT[:, :], in_=csT[:, :], func=AF.Exp), reads=[cs_b], writes=[e_b])
            S.op("dve", lambda e: e.tensor_mul(out=KR[:, :, 1, :], in0=rT[:, :].rearrange("p (j t) -> p j t", t=C),
                                               in1=eT[:, :].rearrange("p (j t) -> p j t", t=C)),
                 reads=[r_b, e_b], writes=[KR_b])
            S.op("dve", lambda e: e.tensor_copy(out=gam[:, :], in_=eT[:, :].rearrange("p (j t) -> p j t", t=C)[:, :, C - 1]),
                 reads=[e_b], writes=[gam_b])
            S.op("dve", lambda e: e.tensor_sub(out=OT[:, :], in0=csT[:, :], in1=lwT[:, :]), reads=[cs_b, lw_b2], writes=[OT_b])
            S.op("act", lambda e: e.activation(out=OT[:, :], in_=OT[:, :], func=AF.Exp), reads=[OT_b], writes=[OT_b])
            S.op("dve", lambda e: e.tensor_mul(out=KR[:, :, 0, :], in0=kkT[:, :].rearrange("p (j t) -> p j t", t=C),
                                               in1=OT[:, :].rearrange("p (j t) -> p j t", t=C)),
                 reads=[kk_b, OT_b], writes=[KR_b])
            S.op("dve", lambda e: e.tensor_sub(out=OT[:, :].rearrange("p (j t) -> p j t", t=C),
                                               in0=cs3[:, :, C - 1:C].broadcast_to([128, NCH, C]), in1=cs3),
                 reads=[cs_b], writes=[OT_b])
            S.op("act", lambda e: e.activation(out=OT[:, :], in_=OT[:, :], func=AF.Exp), reads=[OT_b], writes=[OT_b])
            S.op("dve", lambda e: e.tensor_mul(out=Kh[:, :], in0=kT[:, :], in1=OT[:, :]), reads=[k_b, OT_b], writes=[Kh_b])
            S.op("dve", lambda e: e.tensor_mul(out=Bh[:, :], in0=bT[:, :], in1=OT[:, :]), reads=[b_b, OT_b], writes=[Bh_b])

            S.op("act", lambda e: e.activation(out=eT[:, :], in_=csT[:, :], func=AF.Exp, scale=-1.0), reads=[cs_b],
                 writes=[e_b])
            S.op("dve", lambda e: e.tensor_mul(out=Kt[:, :], in0=kT[:, :], in1=eT[:, :]), reads=[k_b, e_b], writes=[Kt_b])
            S.op("dve", lambda e: e.tensor_mul(out=Bt[:, :], in0=bT[:, :], in1=eT[:, :]), reads=[b_b, e_b], writes=[Bt_b])
            if STOP == "rw_prep":
                raise _Stop()
            A1, A1_b = lt("A1", [128, 2, 128])
            Ab, Ab_b = lt("Ab", [128, 64])
            Y, Y_b = lt("Y", [128, 2, 64])
            Pm, Pm_b = lt("Pm", [128, 2, 64])
            TM, TM_b = lt("TM", [128, 3, 128])
            W1, W1_b = lt("W1", [128, 64])
            U, U_b = lt("U", [128, 64])
            dup, dup_b = lt("dup", [128, 3, 128])
            mask1 = cf[:, 1152:1280]
            mask2 = cf[:, 1280:1344]
            id64 = cf[:, 1344:1408]
            HP = [slice(0, 64), slice(64, 128)]
            BK_A, BK_B, BK_Y, BK_P, BK_W, BK_O, BK_H = (0, 2), (1, 3), (4, 5), (6, 7), (1, 3), (4, 5), (6, 7)
            for j in range(NCH):
                tc0 = j * C
                tcs = slice(tc0, tc0 + C)
                for hl in range(2):
                    hp = HP[hl]
                    krj = KR[hp, j, :, :].rearrange("p a t -> p (a t)")
                    bA, bB = BK_A[hl], BK_B[hl]
                    mm(ps[bA][hp, 0:128], Kt[hp, tcs], krj, True, True, [Kt_b, KR_b], [ps_b[bA]], False)
                    mm(ps[bA][hp, 128:256], Bt[hp, tcs], krj, True, True, [Bt_b, KR_b], [ps_b[bA]], False)
                    mm(ps[bB][hp, 0:64], KR[hp, j, 0, :], Bt[hp, tcs], True, True, [KR_b, Bt_b], [ps_b[bB]], True)
                for hl in range(2):
                    hp = HP[hl]
                    bA, bB = BK_A[hl], BK_B[hl]
                    S.op("dve", lambda e, hp=hp, bA=bA: e.tensor_mul(
                        out=A1[hp, :, :], in0=ps[bA][hp, 0:256].rearrange("p (x t) -> p x t", t=128),
                        in1=mask1[hp, :].unsqueeze(1).broadcast_to([64, 2, 128])), reads=[ps_b[bA], cf_b], writes=[A1_b])
                    S.op("dve", lambda e, hp=hp, bB=bB: e.tensor_mul(out=Ab[hp, :], in0=ps[bB][hp, 0:64], in1=mask2[hp, :]),
                         reads=[ps_b[bB], cf_b], writes=[Ab_b])
                if STOP == "rw_A":
                    raise _Stop()
                S.op("dve", lambda e: e.tensor_scalar(out=Y[:, 0, :], in0=A1[:, 1, 0:64], scalar1=-1.0, scalar2=None,
                                                      op0=ALU.mult), reads=[A1_b], writes=[Y_b])
                S.op("dve", lambda e: e.tensor_scalar(out=Y[:, 1, :], in0=Ab[:, :], scalar1=-1.0, scalar2=None,
                                                      op0=ALU.mult), reads=[Ab_b], writes=[Y_b])
                S.op("dve", lambda e: e.tensor_add(out=Pm[:], in0=Y[:], in1=id64.unsqueeze(1).broadcast_to([128, 2, 64])),
                     reads=[Y_b, cf_b], writes=[Pm_b])
                for lev in range(5):
                    for hl in range(2):
                        hp = HP[hl]
                        b_ = BK_Y[hl]
                        mm(ps[b_][hp, 0:64], Y[hp, 1, :], Y[hp, 0, :], True, True, [Y_b], [ps_b[b_]], False)
                        mm(ps[b_][hp, 64:128], Y[hp, 0, :], Y[hp, 1, :], True, True, [Y_b], [ps_b[b_]], True)
                    for hl in range(2):
                        hp = HP[hl]
                        b_ = BK_Y[hl]
                        S.op("act", lambda e, hp=hp, b_=b_: e.copy(out=Y[hp, :, :].rearrange("p a t -> p (a t)"),
                                                                   in_=ps[b_][hp, 0:128]), reads=[ps_b[b_]], writes=[Y_b])
                    for hl in range(2):
                        hp = HP[hl]
                        b_ = BK_P[hl]
                        mm(ps[b_][hp, 0:64], Pm[hp, 1, :], Y[hp, 0, :], True, True, [Pm_b, Y_b], [ps_b[b_]], False)
                        mm(ps[b_][hp, 64:128], Y[hp, 0, :], Pm[hp, 1, :], True, True, [Pm_b, Y_b], [ps_b[b_]], True)
                    for hl in range(2):
                        hp = HP[hl]
                        b_ = BK_P[hl]
                        S.op("dve", lambda e, hp=hp, b_=b_: e.tensor_add(out=Pm[hp, :, :].rearrange("p a t -> p (a t)"),
                                                                         in0=Pm[hp, :, :].rearrange("p a t -> p (a t)"),
                                                                         in1=ps[b_][hp, 0:128]),
                             reads=[ps_b[b_], Pm_b], writes=[Pm_b])
                if STOP == "rw_inv":
                    raise _Stop()
                import os as _os
                _dbg = _os.environ.get("DBG", "")
                for ti, (srcT, sbf) in enumerate(((vT, v_b), (Kh, Kh_b), (Bh, Bh_b))):
                    if "nobc" in _dbg:
                        for a_ in range(2):
                            S.op("dve", lambda e, ti=ti, srcT=srcT, a_=a_: e.tensor_copy(
                                out=dup[:, ti, a_ * C:(a_ + 1) * C], in_=srcT[:, tcs]), reads=[sbf], writes=[dup_b])
                    else:
                        S.op("dve", lambda e, ti=ti, srcT=srcT: e.tensor_copy(
                            out=dup[:, ti, :].rearrange("p (a t) -> p a t", a=2),
                            in_=srcT[:, tcs].unsqueeze(1).broadcast_to([128, 2, C])), reads=[sbf], writes=[dup_b])
                    if "nope" in _dbg:
                        continue
                    S.op("pe", lambda e, ti=ti: e.matmul(ps[0][:, ti * 128:(ti + 1) * 128], lhsT=dup[:, ti, :], rhs=ident,
                                                         start=True, stop=True),
                         reads=[dup_b, cf_b], writes=[ps_b[0]], sig=(ti == 2))
                if "nope" not in _dbg and "noact" not in _dbg:
                    S.op("act", lambda e: e.copy(out=TM[:].rearrange("p a f -> p (a f)"), in_=ps[0][:, 0:384]),
                         reads=[ps_b[0]], writes=[TM_b])
                if STOP == "rw_tm":
                    raise _Stop()
                for hl in range(2):
                    hp = HP[hl]
                    b_ = BK_W[hl]
                    o_ = ps[b_][hp, 0:64]
                    mm(o_, KR[hp, j, 0, :], Hst[hp, p, :], True, False, [KR_b, Hst_b], [ps_b[b_]], False)
                    mm(o_, A1[hp, 0, 0:64], TM[hp, 0, hp], False, True, [A1_b, TM_b], [ps_b[b_]], True)
                for hl in range(2):
                    hp = HP[hl]
                    b_ = BK_W[hl]
                    S.op("act", lambda e, hp=hp, b_=b_: e.copy(out=W1[hp, :], in_=ps[b_][hp, 0:64]), reads=[ps_b[b_]],
                         writes=[W1_b])
                if STOP == "rw_w1":
                    raise _Stop()
                for hl in range(2):
                    hp = HP[hl]
                    b_ = BK_W[hl]
                    mm(ps[b_][hp, 128:192], Pm[hp, 0, :], W1[hp, :], True, True, [Pm_b, W1_b], [ps_b[b_]], True)
                for hl in range(2):
                    hp = HP[hl]
                    b_ = BK_W[hl]
                    S.op("act", lambda e, hp=hp, b_=b_: e.mul(out=U[hp, :], in_=ps[b_][hp, 128:192], mul=-1.0),
                         reads=[ps_b[b_]], writes=[U_b])
                if STOP == "rw_u":
                    raise _Stop()
                for hl in range(2):
                    hp = HP[hl]
                    b_ = BK_O[hl]
                    o_ = ps[b_][hp, 0:64]
                    mm(o_, Hst[hp, p, :], KR[hp, j, 1, :], True, False, [Hst_b, KR_b], [ps_b[b_]], False)
                    mm(o_, TM[hp, 0, hp], A1[hp, 0, 64:128], False, False, [TM_b, A1_b], [ps_b[b_]], False)
                    mm(o_, U[hp, :], A1[hp, 1, 64:128], False, True, [U_b, A1_b], [ps_b[b_]], True)
                for hl in range(2):
                    hp = HP[hl]
                    b_ = BK_O[hl]
                    S.op("act", lambda e, hp=hp, b_=b_: e.copy(out=OT[hp, tcs], in_=ps[b_][hp, 0:64]), reads=[ps_b[b_]],
                         writes=[OT_b])
                if STOP == "rw_o":
                    raise _Stop()
                for hl in range(2):
                    hp = HP[hl]
                    b_ = BK_H[hl]
                    o_ = ps[b_][hp, 0:64]
                    mm(o_, TM[hp, 1, hp], TM[hp, 0, hp], True, False, [TM_b], [ps_b[b_]], False)
                    mm(o_, TM[hp, 2, hp], U[hp, :], False, True, [TM_b, U_b], [ps_b[b_]], True)
                for hl in range(2):
                    hp = HP[hl]
                    b_ = BK_H[hl]
                    S.op("dve", lambda e, hp=hp, b_=b_: e.scalar_tensor_tensor(
                        out=Hst[hp, p, :], in0=Hst[hp, p, :], scalar=gam[hp, j:j + 1], in1=ps[b_][hp, 0:64],
                        op0=ALU.mult, op1=ALU.add), reads=[Hst_b, gam_b, ps_b[b_]], writes=[Hst_b])
            if STOP == "rw_chunk":
                raise _Stop()
            if (l, i, p) == (0, 0, 0):
                dbg(8, OT[:, :], OT_b)
            mm(ps[0][:, :], mean_blk, OT[:, :], True, True, [cf_b, OT_b], [ps_b[0]], True)
            S.op("dve", lambda e: e.tensor_sub(out=OT[:, :], in0=OT[:, :], in1=ps[0][:, :]), reads=[OT_b, ps_b[0]],
                 writes=[OT_b])
            S.op("act", lambda e: e.activation(out=eT[:, :], in_=OT[:, :], func=AF.Square), reads=[OT_b], writes=[e_b])
            mm(ps[1][:, :], mean_blk, eT[:, :], True, True, [cf_b, e_b], [ps_b[1]], True)
            rsqrt(eT[:, :], ps[1][:, :], float(LNX_EPS), [ps_b[1]], e_b)
            S.op("dve", lambda e: e.tensor_mul(out=OT[:, :], in0=OT[:, :], in1=eT[:, :]), reads=[OT_b, e_b], writes=[OT_b])
            S.op("dve", lambda e: e.tensor_scalar(out=OT[:, :], in0=OT[:, :], scalar1=pcol(l, "lng", p),
                                                  scalar2=pcol(l, "lnb", p), op0=ALU.mult, op1=ALU.add),
                 reads=[OT_b, pp_b], writes=[OT_b])
            S.op("dve", lambda e: e.tensor_add(out=OT[:, :], in0=OT[:, :], in1=bon[:, :]), reads=[OT_b, bon_b], writes=[OT_b])
            if (l, i, p) == (0, 0, 0):
                dbg(9, OT[:, :], OT_b)
            S.op("dve", lambda e: e.tensor_mul(out=ozT[:, p, :], in0=OT[:, :], in1=gT[:, :]), reads=[OT_b, g_b],
                 writes=[ozT_b])
            S.barrier()

    def conv_phase(l, i, last):
        with scope():
            def lt(name, shape, dt=F32):
                return AR.alloc(list(shape), dt), Buf(name)
            uext, ue_b = lt("uext", [128, 4, T + 30])
            acc, acc_b = lt("cacc", [128, 4, T])
            sq, _ = lt("csq", [128, 2, T])
            sq_b = [Buf("csq0"), Buf("csq1")]
            sg, sg_b = lt("csg", [128, 2, T])
            cTb, cT_b = lt("cTb", [128, 4, T], BF16)
            tmp, tmp_b = lt("ctmp", [128, 1, T])
            es_local["sq"], es_local["sq_b"] = sq, sq_b
            su = next_slab(SL_CU)
            sgt = next_slab(SL_CG)
            for c in range(4):
                proj_fm(0, su, c * 128, xTb, xTb_b)
                proj_fm(1, sgt, c * 128, xTb, xTb_b)
                S.op("act", lambda e, c=c: e.activation(out=sg[:, c % 2, :], in_=ps[1][:, :], func=AF.Sigmoid),
                     reads=[ps_b[1]], writes=[sg_b])
                S.op("dve", lambda e, c=c: e.tensor_copy(out=uext[:, c, 0:30], in_=uhist[:, c, :]), reads=[uhist_b],
                     writes=[ue_b])
                S.op("dve", lambda e, c=c: e.tensor_mul(out=uext[:, c, 30:30 + T], in0=ps[0][:, :], in1=sg[:, c % 2, :]),
                     reads=[ps_b[0], sg_b], writes=[ue_b])
                S.op("dve", lambda e, c=c: e.tensor_copy(out=uhist[:, c, :], in_=uext[:, c, T:T + 30]), reads=[ue_b],
                     writes=[uhist_b])
                S.op("dve", lambda e, c=c: e.tensor_scalar(out=acc[:, c, :], in0=uext[:, c, 0:T],
                                                           scalar1=pcol(l, "cdw", c * 31), scalar2=pcol(l, "cdb", c),
                                                           op0=ALU.mult, op1=ALU.add), reads=[ue_b, pp_b], writes=[acc_b])
                for jj in range(1, 31):
                    S.op("dve", lambda e, c=c, jj=jj: e.scalar_tensor_tensor(out=acc[:, c, :], in0=uext[:, c, jj:jj + T],
                                                                             scalar=pcol(l, "cdw", c * 31 + jj),
                                                                             in1=acc[:, c, :], op0=ALU.mult, op1=ALU.add),
                         reads=[ue_b, pp_b, acc_b], writes=[acc_b])
            ln_fm(l, acc, acc_b, 4, mean_c, "clg", "clb", LN_EPS, [(acc, acc_b)], tmp, tmp_b)
            for c in range(4):
                S.op("act", lambda e, c=c: e.activation(out=cTb[:, c, :], in_=acc[:, c, :], func=AF.Silu), reads=[acc_b],
                     writes=[cT_b])
            s_out = next_slab(SL_COUT)
            g0 = next_slab(SL_G1)
            g1 = next_slab(SL_G1 + 1)
            merge_branch(l, 1, s_out, 4, cTb, cT_b, (g0, g1))
            if last:
                n30 = CONV_K - 1
                fm2tm_store(lambda c: (uhist[:, c, :], uhist_b), 4, n30,
                            lambda stg: (lambda e: e.dma_start(out=o_conv[l, :, :], in_=stg[0:n30, 0:512])), "oconv")
            S.barrier()

    def attn_phase(l, i, par, last):
        with scope():
            def lt(name, shape, dt=F32):
                return AR.alloc(list(shape), dt), Buf(name)
            qT, q_b = lt("qT", [128, 3, 2, T], BF16)
            pt, _ = lt("pt", [128, 2, 256], BF16)
            pt_b = [Buf("pt0"), Buf("pt1")]
            vstg, vstg_b = lt("vstg", [32, 16, 256], BF16)
            kvo, _ = lt("kvo", [128, 2, 512])
            kvo_b = [Buf("kvo0"), Buf("kvo1")]
            oTb, oT_b = lt("oTb", [128, 2, T], BF16)
            rec, rec_b = lt("rec", [128, T])
            B_c = i // 4
            o_c = (i % 4) * 32
            for g in range(3):
                win, dil = SWA_GROUPS[g]
                sA = next_slab(SL_ATT + 2 * g)
                sB = next_slab(SL_ATT + 2 * g + 1)
                kcol0 = (par * T) if g < 2 else i * T
                for c in range(4):
                    proj_fm(c % 2, sA, c * 128, xTb, xTb_b)
                    if c < 2:
                        S.op("act", lambda e, c=c, g=g: e.copy(out=qT[:, g, c, :], in_=ps[c % 2][:, :]),
                             reads=[ps_b[c % 2]], writes=[q_b])
                    else:
                        S.op("act", lambda e, c=c, g=g: e.copy(out=kT_h[g][:, c - 2, kcol0:kcol0 + T], in_=ps[c % 2][:, :]),
                             reads=[ps_b[c % 2]], writes=[kT_hb[g]])
                WB = W(sB, 8, 512)
                if g < 2:
                    for blk in range(4):
                        cols = slice(blk * 128, (blk + 1) * 128) if g == 0 else slice(blk, T, 4)
                        vb = (par * 4 + blk) if g == 0 else (blk * 2 + par)
                        pi = 2 + blk % 2
                        for kc in range(8):
                            mm(ps[pi][:, 0:256], xTb[:, kc, cols], WB[:, kc, 256:512], kc == 0, kc == 7,
                               [xTb_b, ring_b[sB]], [ps_b[pi]], kc == 7)
                        S.op("dve", lambda e, pi=pi, vb=vb, g=g: e.tensor_copy(out=Vtm[g][:, vb, :], in_=ps[pi][:, 0:256]),
                             reads=[ps_b[pi]], writes=[Vtm_b[g]])
                else:
                    for rho in range(16):
                        pi = 2 + rho % 2
                        for kc in range(8):
                            mm(ps[pi][0:32, 0:256], xTb[:, kc, slice(rho, T, 16)], WB[:, kc, 256:512], kc == 0, kc == 7,
                               [xTb_b, ring_b[sB]], [ps_b[pi]], kc == 7)
                        S.op("dve", lambda e, pi=pi, rho=rho: e.tensor_copy(out=vstg[:, rho, :], in_=ps[pi][0:32, 0:256]),
                             reads=[ps_b[pi]], writes=[vstg_b])
                    vdst = Vtm[2][o_c:o_c + 32, :, :].rearrange("p (r b) f -> p r b f", b=2)[:, :, B_c, :]
                    S.dma("sp", lambda e: e.dma_start(out=vdst, in_=vstg[:, :, :]), "vcdma", reads=[vstg_b],
                          writes=[Vtm_b[2]])
                kp = keep[g]
                for blk in range(4):
                    t0 = i * T + blk * 128
                    if t0 + 128 <= SEQ - kp:
                        continue
                    r0 = t0 - (SEQ - kp)
                    pi = 2 + blk % 2
                    for kc in range(8):
                        mm(ps[pi][:, :], xTb[:, kc, blk * 128:(blk + 1) * 128], WB[:, kc, :], kc == 0, kc == 7,
                           [xTb_b, ring_b[sB]], [ps_b[pi]], kc == 7)
                    S.op("act", lambda e, pi=pi, blk=blk: e.copy(out=kvo[:, blk % 2, :], in_=ps[pi][:, :]),
                         reads=[ps_b[pi]], writes=[kvo_b[blk % 2]])
                    S.dma("sp", lambda e, g=g, r0=r0, blk=blk: e.dma_start(out=o_swa[g][l, r0:r0 + 128, :],
                                                                           in_=kvo[:, blk % 2, :]), f"okv{blk % 2}",
                          reads=[kvo_b[blk % 2]], out=True)
            started = set()
            ptc = {"n": 0}

            def pv(pair, hl, vblk_ap, pt_ap, ptb, out_cols, vbuf):
                hp = slice(hl * 64, (hl + 1) * 64)
                first = (pair, hl) not in started
                started.add((pair, hl))
                mm(ps[4 + pair][hp, out_cols], vblk_ap, pt_ap, first, False, [vbuf, ptb], [ps_b[4 + pair]], False)
                mm(ps[6 + pair][hp, out_cols], ones_b64, pt_ap, first, False, [cb_b, ptb], [ps_b[6 + pair]], True)

            for h in range(4):
                pair, hl = h // 2, h % 2
                hp = slice(hl * 64, (hl + 1) * 64)
                for g in range(2):
                    for qb in range(4):
                        if g == 0:
                            qcols = slice(qb * 128, (qb + 1) * 128)
                            kbs = [(par * T + qb * 128, 1, par * 4 + qb, mb_same)]
                            if qb > 0:
                                kbs.append((par * T + (qb - 1) * 128, 1, par * 4 + qb - 1, mb_prev))
                            elif i > 0:
                                kbs.append(((1 - par) * T + 384, 1, (1 - par) * 4 + 3, mb_prev))
                        else:
                            qcols = slice(qb, T, 4)
                            kbs = [(par * T + qb, 4, qb * 2 + par, mb_same)]
                            if i > 0:
                                kbs.append(((1 - par) * T + qb, 4, qb * 2 + 1 - par, mb_prev))
                        k_ = ptc["n"] % 2
                        ptc["n"] += 1
                        pi = k_
                        for bi, (kc0, kst, vb, mbias) in enumerate(kbs):
                            kcols = slice(kc0, kc0 + 127 * kst + 1, kst)
                            mm(ps[pi][:, bi * 128:(bi + 1) * 128], kT_h[g][hp, pair, kcols], qT[hp, g, pair, qcols], True,
                               False, [kT_hb[g], q_b], [ps_b[pi]], False)
                            mm(ps[pi][:, bi * 128:(bi + 1) * 128], identb, mbias, False, True, [cb_b], [ps_b[pi]],
                               bi == len(kbs) - 1)
                        nb = len(kbs)
                        S.op("act", lambda e, k_=k_, pi=pi, nb=nb: e.activation(out=pt[:, k_, 0:nb * 128],
                                                                                in_=ps[pi][:, 0:nb * 128], func=AF.Exp,
                                                                                scale=0.125),
                             reads=[ps_b[pi]], writes=[pt_b[k_]])
                        for bi, (kc0, kst, vb, mbias) in enumerate(kbs):
                            pv(pair, hl, Vtm[g][:, vb, h * 64:(h + 1) * 64], pt[:, k_, bi * 128:(bi + 1) * 128], pt_b[k_],
                               qcols, Vtm_b[g])
                g = 2
                nkb = 2 if B_c >= 1 else 1
                for rg in range(4):
                    k_ = ptc["n"] % 2
                    ptc["n"] += 1
                    pi = k_
                    for kb in range(nkb):
                        Bk = B_c - kb
                        moff = 448 + (o_c // 32) * 256 + kb * 128
                        mm(ps[pi][:, kb * 128:(kb + 1) * 128], identb, cb[:, moff:moff + 128], True, False, [cb_b],
                           [ps_b[pi]], False)
                        for r4 in range(4):
                            rho = rg * 4 + r4
                            kc0 = 2048 * Bk + rho
                            kcols = slice(kc0, kc0 + 127 * 16 + 1, 16)
                            cc = kb * 128 + r4 * 32
                            mm(ps[pi][:, cc:cc + 32], kT_h[2][hp, pair, kcols], qT[hp, 2, pair, slice(rho, T, 16)], False,
                               r4 == 3, [kT_hb[2], q_b], [ps_b[pi]], r4 == 3 and kb == nkb - 1)
                    S.op("act", lambda e, k_=k_, pi=pi: e.activation(out=pt[:, k_, 0:nkb * 128], in_=ps[pi][:, 0:nkb * 128],
                                                                     func=AF.Exp, scale=0.125),
                         reads=[ps_b[pi]], writes=[pt_b[k_]])
                    for kb in range(nkb):
                        Bk = B_c - kb
                        for r4 in range(4):
                            rho = rg * 4 + r4
                            cc = kb * 128 + r4 * 32
                            pv(pair, hl, Vtm[2][:, rho * 2 + Bk, h * 64:(h + 1) * 64], pt[:, k_, cc:cc + 32], pt_b[k_],
                               slice(rho, T, 16), Vtm_b[2])
            for pair in range(2):
                S.op("dve", lambda e, pair=pair: e.reciprocal(out=rec[:, :], in_=ps[6 + pair][:, :]), reads=[ps_b[6 + pair]],
                     writes=[rec_b])
                S.op("dve", lambda e, pair=pair: e.tensor_mul(out=oTb[:, pair, :], in0=ps[4 + pair][:, :], in1=rec[:, :]),
                     reads=[ps_b[4 + pair], rec_b], writes=[oT_b])
            s_out = next_slab(SL_AOUT)
            g0 = next_slab(SL_G2)
            g1 = next_slab(SL_G2 + 1)
            merge_branch(l, 2, s_out, 2, oTb, oT_b, (g0, g1))
            S.barrier()

    def ffn_phase(l, i, last):
        with scope():
            def lt(name, shape, dt=F32):
                return AR.alloc(list(shape), dt), Buf(name)
            hb, hb_b = lt("hb", [128, 8, T], BF16)
            pre, pre_b = merged, merged_b
            with scope():
                mbf, mbf_b = lt("mbf", [128, 8, T], BF16)
                sq, _ = lt("fsq", [128, 2, T])
                sq_b = [Buf("fsq0"), Buf("fsq1")]
                tmp, tmp_b = lt("ftmp", [128, 1, T])
                es_local["sq"], es_local["sq_b"] = sq, sq_b
                S.op("act", lambda e: e.copy(out=mbf[:], in_=merged[:]), reads=[merged_b], writes=[mbf_b])
                wo = [next_slab(SL_WO), next_slab(SL_WO + 1)]
                for oc in range(8):
                    proj_fm(oc % 2, wo[oc // 4], (oc % 4) * 128, mbf, mbf_b)
                    S.op("dve", lambda e, oc=oc: e.scalar_tensor_tensor(out=pre[:, oc, :], in0=xT32[:, oc, :],
                                                                        scalar=float(ALPHA), in1=ps[oc % 2][:, :],
                                                                        op0=ALU.mult, op1=ALU.add),
                         reads=[xT32_b, ps_b[oc % 2]], writes=[pre_b])
                ln_fm(l, pre, pre_b, 8, mean_d, "l1g", "l1b", LN_EPS, [(xT32, xT32_b), (hb, hb_b)], tmp, tmp_b)
                S.barrier()
            actT, actT_b = lt("actT", [128, 22, T], BF16)
            with scope():
                uext, _ = lt("fuext", [128, 2, T + 2])
                ue_b = [Buf("fue0"), Buf("fue1")]
                cv, _ = lt("fcv", [128, 2, T])
                cv_b = [Buf("fcv0"), Buf("fcv1")]
                sl, _ = lt("fsl", [128, 2, T])
                sl_b = [Buf("fsl0"), Buf("fsl1")]
                for s in range(11):
                    slot = next_slab(SL_FIN + s)
                    for jj in range(4):
                        qi = s * 4 + jj
                        k_ = qi % 2
                        proj_fm(k_, slot, jj * 128, hb, hb_b)
                        S.op("act", lambda e, k_=k_: e.copy(out=uext[:, k_, 2:T + 2], in_=ps[k_][:, :]), reads=[ps_b[k_]],
                             writes=[ue_b[k_]])
                        S.op("dve", lambda e, k_=k_, qi=qi: e.tensor_copy(out=uext[:, k_, 0:2], in_=fcar[:, qi, :]),
                             reads=[fcar_b], writes=[ue_b[k_]])
                        S.op("dve", lambda e, k_=k_, qi=qi: e.tensor_copy(out=fcar[:, qi, :], in_=uext[:, k_, T:T + 2]),
                             reads=[ue_b[k_]], writes=[fcar_b])
                        S.op("dve", lambda e, k_=k_, qi=qi: e.tensor_scalar(out=cv[:, k_, :], in0=uext[:, k_, 2:T + 2],
                                                                            scalar1=pcol(l, "fdw", qi * 3 + 2),
                                                                            scalar2=pcol(l, "fdb", qi), op0=ALU.mult,
                                                                            op1=ALU.add),
                             reads=[ue_b[k_], pp_b], writes=[cv_b[k_]])
                        for tap in (1, 0):
                            S.op("dve", lambda e, k_=k_, qi=qi, tap=tap: e.scalar_tensor_tensor(
                                out=cv[:, k_, :], in0=uext[:, k_, tap:tap + T], scalar=pcol(l, "fdw", qi * 3 + tap),
                                in1=cv[:, k_, :], op0=ALU.mult, op1=ALU.add), reads=[ue_b[k_], pp_b, cv_b[k_]],
                                writes=[cv_b[k_]])
                        if jj < 2:
                            S.op("act", lambda e, k_=k_, jj=jj: e.activation(out=sl[:, jj, :], in_=cv[:, k_, :], func=AF.Silu),
                                 reads=[cv_b[k_]], writes=[sl_b[jj]])
                        else:
                            S.op("dve", lambda e, k_=k_, jj=jj, s=s: e.tensor_mul(out=actT[:, 2 * s + jj - 2, :],
                                                                                   in0=sl[:, jj - 2, :], in1=cv[:, k_, :]),
                                 reads=[sl_b[jj - 2], cv_b[k_]], writes=[actT_b])
                if last:
                    fc5 = fcar[:, :, :].rearrange("p (s hf jj) r -> p s hf jj r", s=11, hf=2, jj=2)
                    S.dma("sp", lambda e: [e.dma_start(
                        out=o_ffn[l, r_, hf * 2816:(hf + 1) * 2816].rearrange("(s jj p) -> p s jj", s=11, jj=2, p=128)[:, :, jj],
                        in_=fc5[:, :, hf, jj, r_], allow_slow_non_contiguous=True)
                        for hf in range(2) for r_ in range(2) for jj in range(2)],
                        "offn", reads=[fcar_b], n=8, out=True)
                S.barrier()
            with scope():
                sq, _ = lt("fsq2", [128, 2, T])
                sq_b = [Buf("fsq20"), Buf("fsq21")]
                tmp, tmp_b = lt("ftmp2", [128, 1, T])
                es_local["sq"], es_local["sq_b"] = sq, sq_b
                for oc in range(8):
                    slot = next_slab(SL_FOUT + oc)
                    Wf = W(slot, 22, 128)
                    for kc in range(22):
                        mm(ps[oc % 2][:, :], Wf[:, kc, :], actT[:, kc, :], kc == 0, kc == 21, [ring_b[slot], actT_b],
                           [ps_b[oc % 2]], kc == 21)
                    S.op("dve", lambda e, oc=oc: e.scalar_tensor_tensor(out=pre[:, oc, :], in0=xT32[:, oc, :],
                                                                        scalar=float(ALPHA), in1=ps[oc % 2][:, :],
                                                                        op0=ALU.mult, op1=ALU.add),
                         reads=[xT32_b, ps_b[oc % 2]], writes=[pre_b])
                ln_fm(l, pre, pre_b, 8, mean_d, "l2g", "l2b", LN_EPS, [(pre, pre_b)], tmp, tmp_b)
                if l < DEPTH - 1:
                    S.dma("sp", lambda e: e.dma_start(out=xs_d[:, :, i * T:(i + 1) * T], in_=pre[:]), "xsst",
                          reads=[pre_b])
                else:
                    for blk in range(4):
                        for hf in range(2):
                            fm2tm_store(lambda c, blk=blk, hf=hf: (pre[:, hf * 4 + c, blk * 128:(blk + 1) * 128], pre_b), 4,
                                        128, lambda stg, blk=blk, hf=hf: (lambda e: e.dma_start(
                                            out=y_p[i * T + blk * 128:i * T + (blk + 1) * 128, hf * 512:(hf + 1) * 512],
                                            in_=stg[:, :])), "yout")
                S.barrier()


    if NS:
        xs32 = sb("xs32", [128, 8, NS])
        xs32_b = Buf("xs32")
        sel = cf[0:NS, 1408:1408 + NS * 128]
        ones_f64 = cf[:, 1920:1984]
        id64r = cf[:, 1344:1408]
        S.dma("sp", lambda e: e.dma_start(out=xs32[:], in_=xs_fm[:, :, :]), "xsload", writes=[xs32_b], local=False)
        for l in range(DEPTH):
            S.dma("sp", lambda e, l=l: e.dma_start(out=o_conv_s[l, :, 0:CONV_K - 2, :], in_=cv_nat[l, :, 1:CONV_K - 1, :]),
                  "d2d", out=True, local=False)
            S.dma("sp", lambda e, l=l: e.dma_start(out=o_ffn_s[l, :, 0, :], in_=ff_nat[l, :, 1, :]), "d2d", out=True,
                  local=False)
            for g in range(3):
                nb = NBUF[g]
                for s_ in range(NS):
                    S.dma("sp", lambda e, l=l, g=g, s_=s_, nb=nb: e.dma_start(out=o_swa_s[g][l, s_, 0:nb - 1, :],
                                                                           in_=cache_d[g][l, s_, 1:nb, :]),
                          "d2d", out=True, local=False)

    def sample_layer(l):
        N = NS
        with scope():
            def lt(name, shape, dt=F32):
                return AR.alloc(list(shape), dt), Buf(name)
            xsb, xsb_b = lt("xsb", [128, 8, N], BF16)
            mgs, mgs_b = lt("mgs", [128, 8, N])
            shs, shs_b = lt("shs", [128, 14, N])
            zrs, zrs_b = lt("zrs", [128, 14, N])
            S.op("act", lambda e: e.copy(out=xsb[:], in_=xs32[:]), reads=[xs32_b], writes=[xsb_b])
            S.dma("sp", lambda e: e.dma_start(out=shs[:], in_=sh_fm[l, :, :, :]), "sld0", writes=[shs_b])

            def sshift(pi, och, dst, dst_b):
                S.op("act", lambda e: e.copy(out=zrs[:, och, :], in_=ps[pi][:, 0:N]), reads=[ps_b[pi]], writes=[zrs_b])
                S.op("dve", lambda e: e.tensor_scalar(out=dst, in0=ps[pi][:, 0:N], scalar1=pcol(l, "omm", och), scalar2=None,
                                                      op0=ALU.mult), reads=[ps_b[pi], pp_b], writes=[dst_b])
                S.op("dve", lambda e: e.scalar_tensor_tensor(out=dst, in0=shs[:, och, :], scalar=pcol(l, "mu", och), in1=dst,
                                                             op0=ALU.mult, op1=ALU.add), reads=[shs_b, pp_b, dst_b],
                     writes=[dst_b])

            def sproj(pi, slot, col0, wcols=512):
                proj_fm(pi, slot, col0, xsb, xsb_b, ncol=N, wcols=wcols, rhs_cols=slice(0, N))

            with scope():
                lora, lora_b = lt("slora", [128, 2, N])
                ozs, ozs_b = lt("ozs", [128, 4, N], BF16)
                Hs, Hs_b = lt("Hs", [128, N, 4, 64])
                S.dma("sp", lambda e: e.dma_start(out=Hs[:], in_=wkv_fm[l, :, :, :, :]), "sld1", writes=[Hs_b])
                slot = next_slab(SL_LORA)
                for c in range(2):
                    sproj(c, slot, c * 128, wcols=256)
                    sshift(c, 12 + c, lora[:, c, :], lora_b)
                S.op("act", lambda e: e.activation(out=lora[0:64, 0, :], in_=lora[0:64, 0, :], func=AF.Tanh),
                     reads=[lora_b], writes=[lora_b])
                S.op("act", lambda e: e.activation(out=lora[:, 1, :], in_=lora[:, 1, :], func=AF.Sigmoid),
                     reads=[lora_b], writes=[lora_b])
                for p in range(4):
                    with scope():
                        rT, r_b = lt("srT", [128, N])
                        kT, k_b = lt("skT", [128, N])
                        vT, v_b = lt("svT", [128, N])
                        wT, w_b = lt("swT", [128, N])
                        aT, a_b = lt("saT", [128, N])
                        gT, g_b = lt("sgT", [128, N])
                        kkT, kk_b = lt("skkT", [128, N])
                        bT, b_b = lt("sbT", [128, N])
                        eT, e_b = lt("seT", [128, N])
                        bon, bon_b = lt("sbon", [128, N])
                        OT, OT_b = lt("sOT", [128, N])
                        t1, t1_b = lt("st1", [128, 64])
                        t2, t2_b = lt("st2", [128, 64])
                        slot = next_slab(SL_RP + p)
                        for c, (dst, db, och) in enumerate(((rT, r_b, p), (kT, k_b, 4 + p), (vT, v_b, 8 + p))):
                            sproj(c % 2, slot, c * 128, wcols=384)
                            sshift(c % 2, och, dst[:, :], db)
                        cs128 = slice(p * 128, (p + 1) * 128)
                        mm(ps[2][:, 0:N], lw[0:64, l, cs128], lora[0:64, 0, :], True, True, [lw_b, lora_b], [ps_b[2]], True)
                        mm(ps[3][:, 0:N], lw[64:128, l, cs128], lora[64:128, 0, :], True, True, [lw_b, lora_b], [ps_b[3]], True)
                        mm(ps[4][:, 0:N], lw[:, l, 512 + p * 128:512 + (p + 1) * 128], lora[:, 1, :], True, True,
                           [lw_b, lora_b], [ps_b[4]], True)
                        S.op("act", lambda e: e.activation(out=eT[:, :], in_=ps[2][:, 0:N], func=AF.Exp, scale=-1.0,
                                                           bias=pcol(l, "nw0", p)), reads=[ps_b[2], pp_b], writes=[e_b])
                        S.op("act", lambda e: e.activation(out=eT[:, :], in_=eT[:, :], func=AF.Ln, bias=1.0), reads=[e_b],
                             writes=[e_b])
                        S.op("act", lambda e: e.activation(out=eT[:, :], in_=eT[:, :], func=AF.Exp, scale=-1.0, bias=-0.5),
                             reads=[e_b], writes=[e_b])
                        S.op("act", lambda e: e.activation(out=wT[:, :], in_=eT[:, :], func=AF.Exp, scale=-1.0),
                             reads=[e_b], writes=[w_b])
                        S.op("act", lambda e: e.activation(out=aT[:, :], in_=ps[3][:, 0:N], func=AF.Sigmoid,
                                                           bias=pcol(l, "a0", p)), reads=[ps_b[3], pp_b], writes=[a_b])
                        S.op("act", lambda e: e.copy(out=gT[:, :], in_=ps[4][:, 0:N]), reads=[ps_b[4]], writes=[g_b])
                        S.op("dve", lambda e: e.tensor_scalar(out=kkT[:, :], in0=kT[:, :], scalar1=pcol(l, "kk", p),
                                                              scalar2=None, op0=ALU.mult), reads=[k_b, pp_b], writes=[kk_b])
                        S.op("dve", lambda e: e.tensor_mul(out=eT[:, :], in0=kkT[:, :], in1=kkT[:, :]), reads=[kk_b],
                             writes=[e_b])
                        mm(ps[2][:, 0:N], ones_blk, eT[:, :], True, True, [cf_b, e_b], [ps_b[2]], True)
                        rsqrt(eT[:, :], ps[2][:, 0:N], 1e-24, [ps_b[2]], e_b)
                        S.op("dve", lambda e: e.tensor_mul(out=kkT[:, :], in0=kkT[:, :], in1=eT[:, :]), reads=[kk_b, e_b],
                             writes=[kk_b])
                        S.op("dve", lambda e: e.tensor_mul(out=bT[:, :], in0=kkT[:, :], in1=aT[:, :]), reads=[kk_b, a_b],
                             writes=[b_b])
                        S.op("dve", lambda e: e.tensor_scalar(out=aT[:, :], in0=aT[:, :], scalar1=-1.0,
                                                              scalar2=pcol(l, "ka", p), op0=ALU.add, op1=ALU.mult),
                             reads=[a_b, pp_b], writes=[a_b])
                        S.op("dve", lambda e: e.scalar_tensor_tensor(out=kT[:, :], in0=aT[:, :], scalar=1.0, in1=kT[:, :],
                                                                     op0=ALU.add, op1=ALU.mult), reads=[a_b, k_b],
                             writes=[k_b])
                        S.op("dve", lambda e: e.scalar_tensor_tensor(out=eT[:, :], in0=rT[:, :], scalar=pcol(l, "rk", p),
                                                                     in1=kT[:, :], op0=ALU.mult, op1=ALU.mult),
                             reads=[r_b, k_b, pp_b], writes=[e_b])
                        mm(ps[3][:, 0:N], ones_blk, eT[:, :], True, True, [cf_b, e_b], [ps_b[3]], True)
                        S.op("dve", lambda e: e.tensor_mul(out=bon[:, :], in0=ps[3][:, 0:N], in1=vT[:, :]),
                             reads=[ps_b[3], v_b], writes=[bon_b])
                        S.op("dve", lambda e: e.tensor_scalar(out=bT[:, :], in0=bT[:, :], scalar1=-1.0, scalar2=None,
                                                              op0=ALU.mult), reads=[b_b], writes=[b_b])
                        for s_ in range(N):
                            H0 = Hs[:, s_, p, :]
                            sc = slice(s_, s_ + 1)
                            S.op("dve", lambda e, H0=H0, sc=sc: e.tensor_scalar(out=t1[:, :], in0=H0, scalar1=kkT[:, sc],
                                                                               scalar2=None, op0=ALU.mult),
                                 reads=[Hs_b, kk_b], writes=[t1_b])
                            S.op("dve", lambda e, sc=sc: e.tensor_scalar(out=t2[:, :], in0=id64r, scalar1=vT[:, sc],
                                                                        scalar2=None, op0=ALU.mult),
                                 reads=[cf_b, v_b], writes=[t2_b])
                            mm(ps[0][:, 0:64], ones_blk, t1[:, :], True, True, [cf_b, t1_b], [ps_b[0]], True)
                            mm(ps[1][:, 0:64], ones_blk, t2[:, :], True, True, [cf_b, t2_b], [ps_b[1]], True)
                            S.op("dve", lambda e, H0=H0, sc=sc: e.tensor_scalar(out=H0, in0=H0, scalar1=wT[:, sc], scalar2=None,
                                                                               op0=ALU.mult), reads=[Hs_b, w_b],
                                 writes=[Hs_b])
                            S.op("dve", lambda e, H0=H0, sc=sc: e.scalar_tensor_tensor(out=H0, in0=ps[0][:, 0:64],
                                                                                      scalar=bT[:, sc], in1=H0,
                                                                                      op0=ALU.mult, op1=ALU.add),
                                 reads=[ps_b[0], b_b, Hs_b], writes=[Hs_b])
                            S.op("dve", lambda e, H0=H0, sc=sc: e.scalar_tensor_tensor(out=H0, in0=ps[1][:, 0:64],
                                                                                      scalar=kT[:, sc], in1=H0,
                                                                                      op0=ALU.mult, op1=ALU.add),
                                 reads=[ps_b[1], k_b, Hs_b], writes=[Hs_b])
                            for hl in range(2):
                                hp = slice(hl * 64, (hl + 1) * 64)
                                mm(ps[4 + hl][hp, s_:s_ + 1], Hs[hp, s_, p, :], rT[hp, sc], True, True, [Hs_b, r_b],
                                   [ps_b[4 + hl]], True)
                        for hl in range(2):
                            hp = slice(hl * 64, (hl + 1) * 64)
                            S.op("act", lambda e, hp=hp, hl=hl: e.copy(out=OT[hp, :], in_=ps[4 + hl][hp, 0:N]),
                                 reads=[ps_b[4 + hl]], writes=[OT_b])
                        mm(ps[0][:, 0:N], mean_blk, OT[:, :], True, True, [cf_b, OT_b], [ps_b[0]], True)
                        S.op("dve", lambda e: e.tensor_sub(out=OT[:, :], in0=OT[:, :], in1=ps[0][:, 0:N]),
                             reads=[OT_b, ps_b[0]], writes=[OT_b])
                        S.op("act", lambda e: e.activation(out=eT[:, :], in_=OT[:, :], func=AF.Square), reads=[OT_b],
                             writes=[e_b])
                        mm(ps[1][:, 0:N], mean_blk, eT[:, :], True, True, [cf_b, e_b], [ps_b[1]], True)
                        rsqrt(eT[:, :], ps[1][:, 0:N], float(LNX_EPS), [ps_b[1]], e_b)
                        S.op("dve", lambda e: e.tensor_mul(out=OT[:, :], in0=OT[:, :], in1=eT[:, :]), reads=[OT_b, e_b],
                             writes=[OT_b])
                        S.op("dve", lambda e: e.tensor_scalar(out=OT[:, :], in0=OT[:, :], scalar1=pcol(l, "lng", p),
                                                              scalar2=pcol(l, "lnb", p), op0=ALU.mult, op1=ALU.add),
                             reads=[OT_b, pp_b], writes=[OT_b])
                        S.op("dve", lambda e: e.tensor_add(out=OT[:, :], in0=OT[:, :], in1=bon[:, :]), reads=[OT_b, bon_b],
                             writes=[OT_b])
                        S.op("dve", lambda e: e.tensor_mul(out=ozs[:, p, :], in0=OT[:, :], in1=gT[:, :]), reads=[OT_b, g_b],
                             writes=[ozs_b])
                        S.barrier()
                s_out = next_slab(SL_ROUT)
                g0 = next_slab(SL_G0)
                g1 = next_slab(SL_G0 + 1)
                merge_branch(l, 0, s_out, 4, ozs, ozs_b, (g0, g1), N=N, xb=xsb, xb_b=xsb_b, mg=mgs, mg_b=mgs_b)
                S.dma("sp", lambda e: [e.dma_start(out=o_shift_s[l, s_, :].rearrange("(c p) -> p c", p=128), in_=zrs[:, :, s_],
                                                   allow_slow_non_contiguous=True) for s_ in range(N)], "sosh",
                      reads=[zrs_b], n=N, out=True)
                for s_ in range(N):
                    with scope_alloc([64, 8, 64], F32) as stg:
                        stg_b = Buf("swkvst")
                        for h in range(8):
                            pr_, hl = h // 2, h % 2
                            S.op("pe", lambda e, pr_=pr_, hl=hl, s_=s_: e.matmul(
                                ps[4 + hl][0:64, pr_ * 64:(pr_ + 1) * 64], lhsT=Hs[hl * 64:(hl + 1) * 64, s_, pr_, :],
                                rhs=cf[hl * 64:(hl + 1) * 64, 1344:1408], start=True, stop=True), reads=[Hs_b, cf_b],
                                writes=[ps_b[4 + hl]])
                        stg4 = stg[:].rearrange("p (pr hl) k -> p pr hl k", hl=2)
                        for hl in range(2):
                            S.op("act", lambda e, hl=hl: e.copy(out=stg4[:, :, hl, :],
                                                                in_=ps[4 + hl][0:64, 0:256].rearrange("p (pr k) -> p pr k", k=64)),
                                 reads=[ps_b[4 + hl]], writes=[stg_b])
                        S.dma("sp", lambda e, s_=s_: e.dma_start(out=o_wkv_s[l, s_].rearrange("h v k -> v h k"), in_=stg[:]),
                              "sowkv", reads=[stg_b], out=True)
                        S.barrier()
                S.barrier()

            with scope():
                stc, stc_b = lt("stc", [128, 4, N, 30])
                uS, uS_b = lt("uS", [128, 4, N])
                acc, acc_b = lt("sacc", [128, 4, N])
                prod, prod_b = lt("sprod", [128, N, 30])
                sg, sg_b = lt("ssg", [128, N])
                cTb, cT_b = lt("scTb", [128, 4, N], BF16)
                sq, _ = lt("ssq", [128, 2, N])
                sq_b = [Buf("ssq0"), Buf("ssq1")]
                tmp, tmp_b = lt("stmp", [128, 1, N])
                es_local["sq"], es_local["sq_b"] = sq, sq_b
                S.dma("sp", lambda e: e.dma_start(out=stc[:], in_=cv_fm[l, :, :, :, :]), "sld2", writes=[stc_b])
                su = next_slab(SL_CU)
                sgt = next_slab(SL_CG)
                for c in range(4):
                    sproj(0, su, c * 128)
                    sproj(1, sgt, c * 128)
                    S.op("act", lambda e: e.activation(out=sg[:, :], in_=ps[1][:, 0:N], func=AF.Sigmoid), reads=[ps_b[1]],
                         writes=[sg_b])
                    S.op("dve", lambda e, c=c: e.tensor_mul(out=uS[:, c, :], in0=ps[0][:, 0:N], in1=sg[:, :]),
                         reads=[ps_b[0], sg_b], writes=[uS_b])
                    wv = pp[:, l, PC["cdw"] + c * 31:PC["cdw"] + c * 31 + 30]
                    S.op("dve", lambda e, c=c, wv=wv: e.tensor_mul(out=prod[:], in0=stc[:, c, :, :],
                                                                   in1=wv.unsqueeze(1).broadcast_to([128, N, 30])),
                         reads=[stc_b, pp_b], writes=[prod_b])
                    S.op("dve", lambda e, c=c: e.tensor_reduce(out=acc[:, c, :], in_=prod[:], axis=AX.X, op=ALU.add),
                         reads=[prod_b], writes=[acc_b])
                    S.op("dve", lambda e, c=c: e.scalar_tensor_tensor(out=acc[:, c, :], in0=uS[:, c, :],
                                                                      scalar=pcol(l, "cdw", c * 31 + 30), in1=acc[:, c, :],
                                                                      op0=ALU.mult, op1=ALU.add),
                         reads=[uS_b, pp_b, acc_b], writes=[acc_b])
                    S.op("dve", lambda e, c=c: e.tensor_scalar(out=acc[:, c, :], in0=acc[:, c, :], scalar1=pcol(l, "cdb", c),
                                                               scalar2=None, op0=ALU.add), reads=[acc_b, pp_b],
                         writes=[acc_b])
                ln_fm(l, acc, acc_b, 4, mean_c, "clg", "clb", LN_EPS, [(acc, acc_b)], tmp, tmp_b, N=N)
                for c in range(4):
                    S.op("act", lambda e, c=c: e.activation(out=cTb[:, c, :], in_=acc[:, c, :], func=AF.Silu),
                         reads=[acc_b], writes=[cT_b])
                s_out = next_slab(SL_COUT)
                g0 = next_slab(SL_G1)
                g1 = next_slab(SL_G1 + 1)
                merge_branch(l, 1, s_out, 4, cTb, cT_b, (g0, g1), N=N, xb=xsb, xb_b=xsb_b, mg=mgs, mg_b=mgs_b)
                fm2tm_store(lambda c: (uS[:, c, :], uS_b), 4, N,
                            lambda stg: (lambda e: e.dma_start(out=o_conv_s[l, :, CONV_K - 2, :], in_=stg[0:N, 0:512])),
                            "soconv")
                S.barrier()

            with scope():
                qT, q_b = lt("sqT", [128, 2, N])
                kT, k_b = lt("skT2", [128, 2, N])
                vT, v_b = lt("svT2", [128, 2, N])
                Qtm, Qtm_b = lt("sQtm", [N, 256])
                Kt, _ = lt("sKt", [128, 2, 256])
                Kt_b = [Buf("sKt0"), Buf("sKt1")]
                Vt, _ = lt("sVt", [128, 2, 256])
                Vt_b = [Buf("sVt0"), Buf("sVt1")]
                prod, prod_b = lt("saprod", [128, 256])
                sc4, sc4_b = lt("ssc4", [128, 4])
                pself, pself_b = lt("spself", [128, 2, N])
                numS, numS_b = lt("snumS", [128, 2, N])
                denS, denS_b = lt("sdenS", [128, 2, N])
                oTs, oTs_b = lt("soTs", [128, 2, N], BF16)
                S.op("dve", lambda e: e.memset(numS[:], 0.0), writes=[numS_b])
                S.op("dve", lambda e: e.memset(denS[:], 0.0), writes=[denS_b])
                started = set()
                kvn = 0
                for g in range(3):
                    win, dil = SWA_GROUPS[g]
                    nb = NBUF[g]
                    sA = next_slab(SL_ATT + 2 * g)
                    sB = next_slab(SL_ATT + 2 * g + 1)
                    for c in range(4):
                        sproj(c % 2, sA, c * 128)
                        dst, db = (qT, q_b) if c < 2 else (kT, k_b)
                        S.op("act", lambda e, c=c, dst=dst: e.copy(out=dst[:, c % 2, :], in_=ps[c % 2][:, 0:N]),
                             reads=[ps_b[c % 2]], writes=[db])
                    for c in range(2):
                        sproj(2 + c, sB, 256 + c * 128)
                        S.op("act", lambda e, c=c: e.copy(out=vT[:, c, :], in_=ps[2 + c][:, 0:N]), reads=[ps_b[2 + c]],
                             writes=[v_b])
                    fm2tm_store(lambda c: ((kT[:, c, :], k_b) if c < 2 else (vT[:, c - 2, :], v_b)), 4, N,
                                lambda stg, g=g, nb=nb: (lambda e: e.dma_start(out=o_swa_s[g][l, :, nb - 1, :],
                                                                                in_=stg[0:N, 0:512])), "soswa")
                    S.op("dve", lambda e: e.tensor_mul(out=pself[:], in0=qT[:], in1=kT[:]), reads=[q_b, k_b],
                         writes=[pself_b])
                    mm(ps[0][:, 0:2 * N], ones_blk, pself[:].rearrange("p a n -> p (a n)"), True, True, [cf_b, pself_b],
                       [ps_b[0]], True)
                    S.op("act", lambda e: e.activation(out=pself[:].rearrange("p a n -> p (a n)"), in_=ps[0][:, 0:2 * N],
                                                       func=AF.Exp, scale=0.125), reads=[ps_b[0]], writes=[pself_b])
                    S.op("dve", lambda e: e.tensor_add(out=denS[:], in0=denS[:], in1=pself[:]), reads=[denS_b, pself_b],
                         writes=[denS_b])
                    S.op("dve", lambda e: e.tensor_mul(out=pself[:], in0=pself[:], in1=vT[:]), reads=[pself_b, v_b],
                         writes=[pself_b])
                    S.op("dve", lambda e: e.tensor_add(out=numS[:], in0=numS[:], in1=pself[:]), reads=[numS_b, pself_b],
                         writes=[numS_b])
                    for c in range(2):
                        S.op("pe", lambda e, c=c: e.matmul(ps[1][0:N, c * 128:(c + 1) * 128], lhsT=qT[:, c, :], rhs=ident,
                                                           start=True, stop=True), reads=[q_b, cf_b], writes=[ps_b[1]])
                    S.op("act", lambda e: e.copy(out=Qtm[:, :], in_=ps[1][0:N, 0:256]), reads=[ps_b[1]], writes=[Qtm_b])
                    for s_ in range(N):
                        k_ = kvn % 2
                        kvn += 1
                        S.dma("sp", lambda e, k_=k_, s_=s_, g=g, dil=dil, nb=nb: e.dma_start(
                            out=Kt[:, k_, :], in_=cache_d[g][l, s_, slice(0, nb, dil), 0:256]), f"sk{k_}",
                            writes=[Kt_b[k_]])
                        S.dma("sp", lambda e, k_=k_, s_=s_, g=g, dil=dil, nb=nb: e.dma_start(
                            out=Vt[:, k_, :], in_=cache_d[g][l, s_, slice(0, nb, dil), 256:512]), f"sv{k_}",
                            writes=[Vt_b[k_]])
                        mm(ps[2][:, 0:256], sel[:, s_ * 128:(s_ + 1) * 128], Qtm[:, :], True, True, [cf_b, Qtm_b],
                           [ps_b[2]], True)
                        S.op("dve", lambda e, k_=k_: e.tensor_mul(out=prod[:, :], in0=Kt[:, k_, :], in1=ps[2][:, 0:256]),
                             reads=[Kt_b[k_], ps_b[2]], writes=[prod_b])
                        S.op("dve", lambda e: e.tensor_reduce(out=sc4[:, :], in_=prod[:, :].rearrange("p (h e) -> p h e", e=64),
                                                              axis=AX.X, op=ALU.add), reads=[prod_b], writes=[sc4_b])
                        S.op("act", lambda e: e.activation(out=sc4[:, :], in_=sc4[:, :], func=AF.Exp, scale=0.125),
                             reads=[sc4_b], writes=[sc4_b])
                        for h in range(4):
                            pair, hl = h // 2, h % 2
                            hp = slice(hl * 64, (hl + 1) * 64)
                            col = pair * N + s_
                            first = (hl,) not in started
                            started.add((hl,))
                            mm(ps[4][hp, col:col + 1], Vt[:, k_, h * 64:(h + 1) * 64], sc4[:, h:h + 1], first, False,
                               [Vt_b[k_], sc4_b], [ps_b[4]], False)
                            mm(ps[5][hp, col:col + 1], ones_f64, sc4[:, h:h + 1], first, False, [cf_b, sc4_b], [ps_b[5]],
                               True)
                S.op("dve", lambda e: e.tensor_add(out=denS[:].rearrange("p a n -> p (a n)"),
                                                   in0=denS[:].rearrange("p a n -> p (a n)"), in1=ps[5][:, 0:2 * N]),
                     reads=[denS_b, ps_b[5]], writes=[denS_b])
                S.op("dve", lambda e: e.tensor_add(out=numS[:].rearrange("p a n -> p (a n)"),
                                                   in0=numS[:].rearrange("p a n -> p (a n)"), in1=ps[4][:, 0:2 * N]),
                     reads=[numS_b, ps_b[4]], writes=[numS_b])
                S.op("dve", lambda e: e.reciprocal(out=denS[:], in_=denS[:]), reads=[denS_b], writes=[denS_b])
                S.op("dve", lambda e: e.tensor_mul(out=oTs[:], in0=numS[:], in1=denS[:]), reads=[numS_b, denS_b],
                     writes=[oTs_b])
                s_out = next_slab(SL_AOUT)
                g0 = next_slab(SL_G2)
                g1 = next_slab(SL_G2 + 1)
                merge_branch(l, 2, s_out, 2, oTs, oTs_b, (g0, g1), N=N, xb=xsb, xb_b=xsb_b, mg=mgs, mg_b=mgs_b)
                S.barrier()

            with scope():
                hb, hb_b = lt("shb", [128, 8, N], BF16)
                mbf, mbf_b = lt("smbf", [128, 8, N], BF16)
                sq, _ = lt("sfsq", [128, 2, N])
                sq_b = [Buf("sfsq0"), Buf("sfsq1")]
                tmp, tmp_b = lt("sftmp", [128, 1, N])
                stf, stf_b = lt("stf", [128, 44, N, 2])
                urw, urw_b = lt("surw", [128, 44, N])
                cv, _ = lt("sfcv", [128, 2, N])
                cv_b = [Buf("sfcv0"), Buf("sfcv1")]
                sl, _ = lt("sfsl", [128, 2, N])
                sl_b = [Buf("sfsl0"), Buf("sfsl1")]
                actT, actT_b = lt("sactT", [128, 22, N], BF16)
                es_local["sq"], es_local["sq_b"] = sq, sq_b
                S.dma("sp", lambda e: e.dma_start(out=stf[:], in_=ff_fm[l, :, :, :, :]), "sld3", writes=[stf_b])
                S.op("act", lambda e: e.copy(out=mbf[:], in_=mgs[:]), reads=[mgs_b], writes=[mbf_b])
                wo = [next_slab(SL_WO), next_slab(SL_WO + 1)]
                for oc in range(8):
                    proj_fm(oc % 2, wo[oc // 4], (oc % 4) * 128, mbf, mbf_b, ncol=N, rhs_cols=slice(0, N))
                    S.op("dve", lambda e, oc=oc: e.scalar_tensor_tensor(out=mgs[:, oc, :], in0=xs32[:, oc, :],
                                                                        scalar=float(ALPHA), in1=ps[oc % 2][:, 0:N],
                                                                        op0=ALU.mult, op1=ALU.add),
                         reads=[xs32_b, ps_b[oc % 2], mgs_b], writes=[mgs_b])
                ln_fm(l, mgs, mgs_b, 8, mean_d, "l1g", "l1b", LN_EPS, [(xs32, xs32_b), (hb, hb_b)], tmp, tmp_b, N=N)
                for s in range(11):
                    slot = next_slab(SL_FIN + s)
                    for jj in range(4):
                        qi = s * 4 + jj
                        k_ = qi % 2
                        proj_fm(k_, slot, jj * 128, hb, hb_b, ncol=N, rhs_cols=slice(0, N))
                        S.op("act", lambda e, k_=k_, qi=qi: e.copy(out=urw[:, qi, :], in_=ps[k_][:, 0:N]), reads=[ps_b[k_]],
                             writes=[urw_b])
                        S.op("dve", lambda e, k_=k_, qi=qi: e.tensor_scalar(out=cv[:, k_, :], in0=ps[k_][:, 0:N],
                                                                            scalar1=pcol(l, "fdw", qi * 3 + 2),
                                                                            scalar2=pcol(l, "fdb", qi), op0=ALU.mult,
                                                                            op1=ALU.add),
                             reads=[ps_b[k_], pp_b], writes=[cv_b[k_]])
                        for tap in (1, 0):
                            S.op("dve", lambda e, k_=k_, qi=qi, tap=tap: e.scalar_tensor_tensor(
                                out=cv[:, k_, :], in0=stf[:, qi, :, tap], scalar=pcol(l, "fdw", qi * 3 + tap),
                                in1=cv[:, k_, :], op0=ALU.mult, op1=ALU.add), reads=[stf_b, pp_b, cv_b[k_]],
                                writes=[cv_b[k_]])
                        if jj < 2:
                            S.op("act", lambda e, k_=k_, jj=jj: e.activation(out=sl[:, jj, :], in_=cv[:, k_, :], func=AF.Silu),
                                 reads=[cv_b[k_]], writes=[sl_b[jj]])
                        else:
                            S.op("dve", lambda e, k_=k_, jj=jj, s=s: e.tensor_mul(out=actT[:, 2 * s + jj - 2, :],
                                                                                   in0=sl[:, jj - 2, :], in1=cv[:, k_, :]),
                                 reads=[sl_b[jj - 2], cv_b[k_]], writes=[actT_b])
                ur5 = urw[:, :, :].rearrange("p (s hf jj) n -> p s hf jj n", s=11, hf=2, jj=2)
                S.dma("sp", lambda e: [e.dma_start(
                    out=o_ffn_s[l, s_, 1, hf * 2816:(hf + 1) * 2816].rearrange("(s jj p) -> p s jj", s=11, jj=2, p=128)[:, :, jj],
                    in_=ur5[:, :, hf, jj, s_], allow_slow_non_contiguous=True)
                    for s_ in range(N) for hf in range(2) for jj in range(2)], "soffn", reads=[urw_b], n=4 * N, out=True)
                for oc in range(8):
                    slot = next_slab(SL_FOUT + oc)
                    Wf = W(slot, 22, 128)
                    for kc in range(22):
                        mm(ps[oc % 2][:, 0:N], Wf[:, kc, :], actT[:, kc, :], kc == 0, kc == 21, [ring_b[slot], actT_b],
                           [ps_b[oc % 2]], kc == 21)
                    S.op("dve", lambda e, oc=oc: e.scalar_tensor_tensor(out=mgs[:, oc, :], in0=xs32[:, oc, :],
                                                                        scalar=float(ALPHA), in1=ps[oc % 2][:, 0:N],
                                                                        op0=ALU.mult, op1=ALU.add),
                         reads=[xs32_b, ps_b[oc % 2], mgs_b], writes=[mgs_b])
                ln_fm(l, mgs, mgs_b, 8, mean_d, "l2g", "l2b", LN_EPS, [(xs32, xs32_b)], tmp, tmp_b, N=N)
                if l == DEPTH - 1:
                    for hf in range(2):
                        fm2tm_store(lambda c, hf=hf: (xs32[:, hf * 4 + c, :], xs32_b), 4, N,
                                    lambda stg, hf=hf: (lambda e: e.dma_start(out=y_s[:, hf * 512:(hf + 1) * 512],
                                                                              in_=stg[0:N, 0:512])), "sy")
                S.barrier()
            S.barrier()

    for l in range(DEPTH):
        S.op("dve", lambda e: e.memset(Hst[:], 0.0), writes=[Hst_b])
        S.op("dve", lambda e: e.memset(zcar[:], 0.0), writes=[zcar_b])
        S.op("dve", lambda e: e.memset(uhist[:], 0.0), writes=[uhist_b])
        S.op("dve", lambda e: e.memset(fcar[:], 0.0), writes=[fcar_b])
        if l > 0 and not STOP:
            S.wait_only("sp", [(S.dkeys["xsst"][0], S.dkeys["xsst"][1], "dma")])
        cur_layer["l"] = l
        if NS:
            sample_layer(l)
        for i in range(NT):
            if STOP and (l, i) != (0, 0):
                continue
            try:
                tile_prog(l, i)
            except _Stop:
                pass

    S.wait_only("sp", list(S.out_toks))
    with nc.Block() as block:
        S.emit(block)
    es.close()
    return nc


_NC_CACHE = {}


def kernel(**inputs):
    inp = {k: np.asarray(v) for k, v in inputs.items()}
    BATCH, SEQ, _ = inp["x_prompt"].shape
    NSAMP_ALL = inp["x_sample"].shape[0]
    NCORE = 8
    NS = NSAMP_ALL // NCORE
    key = (SEQ, NS)
    if key not in _NC_CACHE:
        _NC_CACHE[key] = build(SEQ, NSAMP=NS)
    nc = _NC_CACHE[key]
    wp, pp, lw = pack_weights(inp)
    cf, cb = make_consts()
    in_maps = []
    for c in range(NCORE):
        b = c % BATCH
        x = inp["x_prompt"][b]
        x_fm = np.ascontiguousarray(x.T.reshape(8, 128, SEQ).transpose(1, 0, 2))
        im = {"x_fm": x_fm, "wpack": wp, "ppack": pp, "lorapack": lw, "constf": cf, "constb": cb}
        im.update(pack_samples(inp, c * NS, NS))
        in_maps.append(im)
    res = run_bass_kernel_spmd(nc, in_maps, core_ids=list(range(NCORE)))
    R_ = res.results
    f32 = np.float32

    def pstack(name, shape_tail):
        return np.stack([np.asarray(R_[b][name], f32).reshape((DEPTH,) + shape_tail) for b in range(BATCH)], axis=1)

    def sstack(name, shape_tail):
        return np.concatenate([np.asarray(R_[c][name], f32).reshape((DEPTH, NS) + shape_tail) for c in range(NCORE)],
                              axis=1)
    y_prompt = np.stack([np.asarray(R_[b]["y_p"], f32) for b in range(BATCH)], axis=0)
    y_sample = np.concatenate([np.asarray(R_[c]["y_s"], f32) for c in range(NCORE)], axis=0)[:, None, :]
    keep = [min(w, SEQ) for w, _ in SWA_GROUPS]
    outs = [y_prompt, y_sample,
            pstack("o_shift_p", (1, R_COLS)), sstack("o_shift_s", (1, R_COLS)),
            pstack("o_wkv_p", (R_HEADS, 64, 64)), sstack("o_wkv_s", (R_HEADS, 64, 64)),
            pstack("o_conv_p", (CONV_K - 1, CONV_CH)), sstack("o_conv_s", (CONV_K - 1, CONV_CH))]
    for g in range(3):
        outs.append(pstack(f"o_swa{g}_p", (keep[g], 2, G_HEADS, HEAD_DIM)))
        outs.append(sstack(f"o_swa{g}_s", (NBUF[g], 2, G_HEADS, HEAD_DIM)))
    outs.append(pstack("o_ffn_p", (2, 2 * D_FF)))
    outs.append(sstack("o_ffn_s", (2, 2 * D_FF)))
    return tuple(np.ascontiguousarray(o, dtype=f32) for o in outs)
```

```python
import numpy as np
from contextlib import ExitStack, contextmanager
import concourse.bass as bass
import concourse.mybir as mybir
from concourse.bass_utils import run_bass_kernel_spmd

F32 = mybir.dt.float32
BF16 = mybir.dt.bfloat16
AF = mybir.ActivationFunctionType
ALU = mybir.AluOpType
AX = mybir.AxisListType

D_MODEL = 1024
DEPTH = 2
HEAD_DIM = 64
R_HEADS = 8
R_WIDTH = 512
R_COLS = 1792
LNX_EPS = 64e-5
CONV_CH = 512
CONV_K = 31
SWA_GROUPS = ((128, 1), (512, 4), (2048, 16))
G_HEADS = 4
A_WIDTH = 768
D_FF = 2816
OFF_R = 0
OFF_C = 1792
OFF_Q = OFF_C + 1024
OFF_K = OFF_Q + 768
OFF_V = OFF_K + 768
OFF_GATE = OFF_V + 768
IN_COLS = 8192
ALPHA = (2 * DEPTH) ** 0.25
LN_EPS = 1e-5
NEG = -30000.0
NBUF = (128, 512, 2048)

P = 128
T = 512
C = 64
NCH = T // C
NSLOT = 5
SLAB = 4096
NSLAB = 42
SL_LORA = 0
SL_RP = 1
SL_ROUT = 5
SL_G0 = 6
SL_CU = 8
SL_CG = 9
SL_COUT = 10
SL_G1 = 11
SL_ATT = 13
SL_AOUT = 19
SL_G2 = 20
SL_WO = 22
SL_FIN = 24
SL_FOUT = 35
NSLAB = 43

PC = {}
_pc = 0
for _n, _w in (("mu", 14), ("omm", 14), ("w0", 4), ("nw0", 4), ("a0", 4), ("kk", 4), ("ka", 4), ("rk", 4),
               ("lng", 4), ("lnb", 4), ("cdw", 4 * 31), ("cdb", 4), ("clg", 4), ("clb", 4), ("gb", 24),
               ("l1g", 8), ("l1b", 8), ("fdw", 44 * 3), ("fdb", 44), ("l2g", 8), ("l2b", 8)):
    PC[_n] = _pc
    _pc += _w
NPC = _pc


class Buf:
    __slots__ = ("name", "w", "r")

    def __init__(self, name):
        self.name = name
        self.w = None
        self.r = []


class _Rec:
    def __init__(self):
        self.calls = []

    def __getattr__(self, name):
        def f(*a, **k):
            self.calls.append((name, a, k))
            return self
        return f


def _freeze(fn):
    rec = _Rec()
    fn(rec)
    calls = rec.calls

    def replay(e):
        out = [getattr(e, name)(*a, **k) for (name, a, k) in calls]
        return out
    return replay, len(calls)


class Sched:
    ENGS = ("pe", "act", "dve", "pool", "sp")
    LIMIT = 60000

    def __init__(self, nc, es):
        self.nc = nc
        self.es = es
        self.sems = []
        self.ops = {e: [] for e in self.ENGS}
        self.cnt = {e: 0 for e in self.ENGS}
        self.semi = {e: self._newsem(e) for e in self.ENGS}
        self.waited = {e: {} for e in self.ENGS}
        self.dkeys = {}
        self.pending = []
        self.out_toks = []
        self.lastsig = {}

    def _newsem(self, name):
        s = self.es.enter_context(self.nc.semaphore(f"s{len(self.sems)}_{name}"))
        self.sems.append(s)
        return len(self.sems) - 1

    def _force_sig(self, te):
        ops = self.ops[te]
        k = len(ops) - 1
        while ops[k][0] is None:
            k -= 1
        assert ops[k][2] is None
        self.cnt[te] += 1
        ops[k][2] = (self.semi[te], 1)
        self.lastsig[te] = True

    def _resolve(self, eng, tok):
        si, val, te = tok
        if te != "dma" and te != eng and si == self.semi[te] and val > self.cnt[te]:
            assert val == self.cnt[te] + 1
            self._force_sig(te)

    def _waits(self, eng, reads, writes, is_dma):
        toks = set()
        for b in reads:
            if b.w is not None:
                toks.add(b.w)
        for b in writes:
            if b.w is not None:
                toks.add(b.w)
            toks.update(b.r)
        out = []
        for (si, val, te) in toks:
            if te == eng and not is_dma and eng == "pe":
                continue
            if te == eng and is_dma:
                if te != "dma" and si == self.semi[te] and val > self.cnt[te]:
                    self._force_sig(te)
            self._resolve(eng, (si, val, te))
            if self.waited[eng].get(si, 0) >= val:
                continue
            out.append((si, val))
        best = {}
        for si, val in out:
            best[si] = max(best.get(si, 0), val)
        for si, val in best.items():
            self.waited[eng][si] = val
        return list(best.items())

    def op(self, eng, fn, reads=(), writes=(), sig=True):
        fn, ncalls = _freeze(fn)
        assert ncalls == 1
        waits = self._waits(eng, reads, writes, False)
        if self.cnt[eng] >= self.LIMIT:
            self.semi[eng] = self._newsem(eng)
            self.cnt[eng] = 0
        if sig:
            self.cnt[eng] += 1
            tok = (self.semi[eng], self.cnt[eng], eng)
            inc = (self.semi[eng], 1)
        else:
            tok = (self.semi[eng], self.cnt[eng] + 1, eng)
            inc = None
        self.ops[eng].append([fn, waits, inc, 1])
        self.lastsig[eng] = sig
        for b in reads:
            b.r.append(tok)
        for b in writes:
            b.w = tok
            b.r = []
        return tok

    def dma(self, eng, fn, key, reads=(), writes=(), n=1, local=True, out=False):
        fn, ncalls = _freeze(fn)
        assert ncalls == n, (ncalls, n)
        waits = self._waits(eng, reads, writes, True)
        if key not in self.dkeys or self.dkeys[key][1] + 16 * n > self.LIMIT:
            self.dkeys[key] = [self._newsem("d" + key), 0]
        d = self.dkeys[key]
        d[1] += 16 * n
        tok = (d[0], d[1], "dma")
        self.ops[eng].append([fn, waits, (d[0], 16), n])
        for b in reads:
            b.r.append(tok)
        for b in writes:
            b.w = tok
            b.r = []
        if local:
            self.pending.append(tok)
        if out:
            self.out_toks.append(tok)
        return tok

    def wait_only(self, eng, toks):
        out = []
        best = {}
        for (si, val, te) in toks:
            self._resolve(eng, (si, val, te))
            if self.waited[eng].get(si, 0) >= val:
                continue
            best[si] = max(best.get(si, 0), val)
        for si, val in best.items():
            self.waited[eng][si] = val
            out.append((si, val))
        if out:
            self.ops[eng].append([None, out, None, 0])

    def last_tok(self, eng):
        return (self.semi[eng], self.cnt[eng], eng)

    def barrier(self, engs=("pe", "act", "dve")):
        for e in engs:
            if not self.lastsig.get(e, True):
                self._force_sig(e)
        toks = [self.last_tok(e) for e in engs if self.cnt[e] > 0] + list(self.pending)
        for e in engs:
            self.wait_only(e, toks)
        self.pending = []

    def emit(self, block):
        nc = self.nc
        table = {"pe": block.tensor, "act": block.scalar, "dve": block.vector, "pool": block.gpsimd,
                 "sp": block.sync}
        for eng in self.ENGS:
            ops = self.ops[eng]
            sems = self.sems

            def body(e, ops=ops):
                for fn, waits, inc, n in ops:
                    for si, val in waits:
                        e.wait_ge(sems[si], val)
                    if fn is None:
                        continue
                    r = fn(e)
                    if inc is not None:
                        assert len(r) == n
                        for ins in r:
                            ins.then_inc(sems[inc[0]], inc[1])
            table[eng](body)


def _slab_k(Wc):
    K, n = Wc.shape
    kc = K // 128
    a = np.ascontiguousarray(Wc.reshape(kc, 128, n).transpose(1, 0, 2)).reshape(128, kc * n)
    out = np.zeros((128, SLAB), np.float32)
    out[:, :kc * n] = a
    return out


def _cols(v):
    return np.ascontiguousarray(np.asarray(v, np.float32).reshape(-1, 128).T)


def _ffn_chunk_order():
    order = []
    for s in range(11):
        for j in range(4):
            if j < 2:
                order.append(2 * s + j)
            else:
                order.append(22 + 2 * s + (j - 2))
    return order


def pack_weights(inp):
    wp = np.zeros((DEPTH, NSLAB, 128, SLAB), np.float32)
    pp = np.zeros((DEPTH, 128, NPC), np.float32)
    lw = np.zeros((DEPTH, 128, 1024), np.float32)
    forder = _ffn_chunk_order()
    for l in range(DEPTH):
        w_in = inp["w_in"][l]
        wp[l, SL_LORA] = _slab_k(w_in[:, 1536:1792])
        for p in range(4):
            cols = np.concatenate([w_in[:, p * 128:(p + 1) * 128], w_in[:, 512 + p * 128:512 + (p + 1) * 128],
                                   w_in[:, 1024 + p * 128:1024 + (p + 1) * 128]], axis=1)
            wp[l, SL_RP + p] = _slab_k(cols)
        wp[l, SL_ROUT] = _slab_k(inp["w_rwkv_out"][l])
        for bi, sl in enumerate((SL_G0, SL_G1, SL_G2)):
            for hf in range(2):
                c0 = OFF_GATE + bi * 1024 + hf * 512
                wp[l, sl + hf] = _slab_k(w_in[:, c0:c0 + 512])
        wp[l, SL_CU] = _slab_k(w_in[:, OFF_C:OFF_C + 512])
        wp[l, SL_CG] = _slab_k(w_in[:, OFF_C + 512:OFF_C + 1024])
        wp[l, SL_COUT] = _slab_k(inp["w_conv_out"][l])
        for g in range(3):
            q = w_in[:, OFF_Q + g * 256:OFF_Q + (g + 1) * 256]
            k = w_in[:, OFF_K + g * 256:OFF_K + (g + 1) * 256]
            v = w_in[:, OFF_V + g * 256:OFF_V + (g + 1) * 256]
            wp[l, SL_ATT + 2 * g] = _slab_k(np.concatenate([q, k], axis=1))
            wp[l, SL_ATT + 2 * g + 1] = _slab_k(np.concatenate([k, v], axis=1))
        wp[l, SL_AOUT] = _slab_k(inp["w_attn_out"][l])
        wp[l, SL_WO] = _slab_k(inp["w_o"][l][:, 0:512])
        wp[l, SL_WO + 1] = _slab_k(inp["w_o"][l][:, 512:1024])
        wfi = inp["w_ffn_in"][l]
        for s in range(11):
            cols = np.concatenate([wfi[:, ch * 128:(ch + 1) * 128] for ch in forder[4 * s:4 * s + 4]], axis=1)
            wp[l, SL_FIN + s] = _slab_k(cols)
        wfo = inp["w_ffn_out"][l]
        for oc in range(8):
            wp[l, SL_FOUT + oc] = _slab_k(wfo[:, oc * 128:(oc + 1) * 128])
        pr = pp[l]
        pr[:, PC["mu"]:PC["mu"] + 14] = _cols(inp["rwkv_mu"][l])
        pr[:, PC["w0"]:PC["w0"] + 4] = _cols(inp["rwkv_w0"][l])
        pr[:, PC["a0"]:PC["a0"] + 4] = _cols(inp["rwkv_a0"][l])
        pr[:, PC["kk"]:PC["kk"] + 4] = _cols(inp["rwkv_k_k"][l])
        pr[:, PC["ka"]:PC["ka"] + 4] = _cols(inp["rwkv_k_a"][l])
        pr[:, PC["rk"]:PC["rk"] + 4] = _cols(inp["rwkv_r_k"][l].reshape(-1))
        pr[:, PC["lng"]:PC["lng"] + 4] = _cols(inp["rwkv_ln_g"][l])
        pr[:, PC["lnb"]:PC["lnb"] + 4] = _cols(inp["rwkv_ln_b"][l])
        cdw = inp["conv_dw"][l]
        for c in range(4):
            pr[:, PC["cdw"] + c * 31:PC["cdw"] + (c + 1) * 31] = cdw[:, c * 128:(c + 1) * 128].T
        pr[:, PC["cdb"]:PC["cdb"] + 4] = _cols(inp["conv_dw_b"][l])
        pr[:, PC["clg"]:PC["clg"] + 4] = _cols(inp["conv_ln_g"][l])
        pr[:, PC["clb"]:PC["clb"] + 4] = _cols(inp["conv_ln_b"][l])
        pr[:, PC["gb"]:PC["gb"] + 24] = _cols(inp["gate_b"][l].reshape(-1))
        pr[:, PC["l1g"]:PC["l1g"] + 8] = _cols(inp["ln1_g"][l])
        pr[:, PC["l1b"]:PC["l1b"] + 8] = _cols(inp["ln1_b"][l])
        fdw = inp["ffn_dw"][l]
        fdb = inp["ffn_dw_b"][l]
        for qi, ch in enumerate(forder):
            pr[:, PC["fdw"] + qi * 3:PC["fdw"] + qi * 3 + 3] = fdw[:, ch * 128:(ch + 1) * 128].T
            pr[:, PC["fdb"] + qi] = fdb[ch * 128:(ch + 1) * 128]
        pr[:, PC["l2g"]:PC["l2g"] + 8] = _cols(inp["ln2_g"][l])
        pr[:, PC["l2b"]:PC["l2b"] + 8] = _cols(inp["ln2_b"][l])
        lw[l, 0:64, 0:512] = inp["rwkv_w_up"][l]
        lw[l, 64:128, 0:512] = inp["rwkv_a_up"][l]
        lw[l, :, 512:1024] = inp["rwkv_g_up"][l]
    return wp, pp, lw


def pack_samples(inp, s0, ns):
    forder = _ffn_chunk_order()
    x = inp["x_sample"][s0:s0 + ns, 0, :]
    xs_fm = np.ascontiguousarray(x.reshape(ns, 8, 128).transpose(2, 1, 0))
    sh = inp["state_shift"][:, s0:s0 + ns, 0, :]
    sh_fm = np.ascontiguousarray(sh.reshape(DEPTH, ns, 14, 128).transpose(0, 3, 2, 1))
    wk = inp["state_wkv"][:, s0:s0 + ns]
    wk = wk.reshape(DEPTH, ns, 4, 2, 64, 64)
    wkv_fm = np.ascontiguousarray(wk.transpose(0, 3, 5, 1, 2, 4)).reshape(DEPTH, 128, ns, 4, 64)
    cv = inp["state_conv"][:, s0:s0 + ns]
    cv_fm = np.ascontiguousarray(cv.reshape(DEPTH, ns, 30, 4, 128).transpose(0, 4, 3, 1, 2))
    ff = inp["state_ffn"][:, s0:s0 + ns]
    ff4 = ff.reshape(DEPTH, ns, 2, 44, 128)[:, :, :, forder, :]
    ff_fm = np.ascontiguousarray(ff4.transpose(0, 4, 3, 1, 2))
    out = {"xs_fm": xs_fm, "sh_fm": sh_fm, "wkv_fm": wkv_fm, "cv_fm": cv_fm, "cv_nat": np.ascontiguousarray(cv),
           "ff_fm": ff_fm, "ff_nat": np.ascontiguousarray(ff)}
    for g, nm in enumerate(("cache_swa_a", "cache_swa_b", "cache_swa_c")):
        c = inp[nm][:, s0:s0 + ns]
        out[f"cache{g}"] = np.ascontiguousarray(c.reshape(DEPTH, ns, c.shape[2], 512))
    return out


def make_consts():
    cf = np.zeros((128, 2048), np.float32)
    cf[:, 0:128] = np.eye(128, dtype=np.float32)
    blk = np.zeros((128, 128), np.float32)
    blk[:64, :64] = 1.0
    blk[64:, 64:] = 1.0
    cf[:, 128:256] = blk
    cf[:, 256:384] = blk / 64.0
    cf[:, 384:512] = 1.0 / 1024.0
    cf[:, 512:640] = 1.0 / 512.0
    rm = np.ones((128, 512), np.float32)
    rm[:, ::64] = 0.0
    cf[:, 640:1152] = rm
    s = np.arange(64)[:, None]
    t = np.arange(64)[None, :]
    m1 = np.concatenate([(s < t), (s <= t)], axis=1).astype(np.float32)
    cf[0:64, 1152:1280] = m1
    cf[64:128, 1152:1280] = m1
    m2 = (t < s).astype(np.float32)
    cf[0:64, 1280:1344] = m2
    cf[64:128, 1280:1344] = m2
    cf[0:64, 1344:1408] = np.eye(64, dtype=np.float32)
    cf[64:128, 1344:1408] = np.eye(64, dtype=np.float32)
    for s_ in range(4):
        cf[s_, 1408 + s_ * 128:1408 + (s_ + 1) * 128] = 1.0
    cf[:, 1920:1984] = 1.0
    cb = np.zeros((128, 1536), np.float32)
    j = np.arange(128)[:, None]
    q = np.arange(128)[None, :]
    same = np.where(j <= q, 0.0, NEG).astype(np.float32)
    prev = np.where(j >= q, 0.0, NEG).astype(np.float32)
    cb[:, 0:128] = np.eye(128, dtype=np.float32)
    cb[:, 128:256] = same
    cb[:, 256:384] = prev
    cb[:, 384:448] = 1.0
    off = 448
    for o in range(4):
        for kb, m in enumerate((same, prev)):
            for rep in range(4):
                cb[:, off:off + 32] = m[:, o * 32:(o + 1) * 32]
                off += 32
    assert off == 448 + 1024
    return cf, cb


STOP = None
DEBUG = False
DBGSTATE = {"tile": -1}


class _Stop(Exception):
    pass


def build(SEQ, NSAMP=0):
    NT = SEQ // T
    nc = bass.Bass("TRN2", target_bir_lowering=False)
    es = ExitStack()
    dram = {}

    def din(name, shape, dt=F32):
        dram[name] = nc.dram_tensor(name, list(shape), dt, kind="ExternalInput").ap()
        return dram[name]

    def dout(name, shape):
        dram[name] = nc.dram_tensor(name, list(shape), F32, kind="ExternalOutput").ap()
        return dram[name]

    x_fm = din("x_fm", [128, 8, SEQ])
    wpk = din("wpack", [DEPTH, NSLAB, 128, SLAB])
    ppk = din("ppack", [DEPTH, 128, NPC])
    lwk = din("lorapack", [DEPTH, 128, 1024])
    cfk = din("constf", [128, 2048])
    cbk = din("constb", [128, 1536])
    xs_d = nc.dram_tensor("xs_scratch", [128, 8, SEQ], F32, kind="Internal").ap()
    y_p = dout("y_p", [SEQ, D_MODEL])
    o_shift = dout("o_shift_p", [DEPTH, R_COLS])
    o_wkv = dout("o_wkv_p", [DEPTH, R_HEADS, 64, 64])
    o_conv = dout("o_conv_p", [DEPTH, CONV_K - 1, CONV_CH])
    keep = [min(w, SEQ) for w, _ in SWA_GROUPS]
    o_swa = [dout(f"o_swa{g}_p", [DEPTH, keep[g], 512]) for g in range(3)]
    o_ffn = dout("o_ffn_p", [DEPTH, 2, 2 * D_FF])
    dbg_d = dout("dbg", [16, 128, T]) if DEBUG else None

    NS = NSAMP
    if NS:
        xs_fm = din("xs_fm", [128, 8, NS])
        sh_fm = din("sh_fm", [DEPTH, 128, 14, NS])
        wkv_fm = din("wkv_fm", [DEPTH, 128, NS, 4, 64])
        cv_fm = din("cv_fm", [DEPTH, 128, 4, NS, 30])
        cv_nat = din("cv_nat", [DEPTH, NS, 30, 512])
        ff_fm = din("ff_fm", [DEPTH, 128, 44, NS, 2])
        ff_nat = din("ff_nat", [DEPTH, NS, 2, 2 * D_FF])
        cache_d = [din(f"cache{g}", [DEPTH, NS, NBUF[g], 512]) for g in range(3)]
        y_s = dout("y_s", [NS, D_MODEL])
        o_shift_s = dout("o_shift_s", [DEPTH, NS, R_COLS])
        o_wkv_s = dout("o_wkv_s", [DEPTH, NS, R_HEADS, 64, 64])
        o_conv_s = dout("o_conv_s", [DEPTH, NS, CONV_K - 1, CONV_CH])
        o_swa_s = [dout(f"o_swa{g}_s", [DEPTH, NS, NBUF[g], 512]) for g in range(3)]
        o_ffn_s = dout("o_ffn_s", [DEPTH, NS, 2, 2 * D_FF])

    S = Sched(nc, es)

    def sb(name, shape, dt=F32):
        return es.enter_context(nc.sbuf_tensor(name, list(shape), dt))

    ring = sb("ring", [128, NSLOT, SLAB], BF16)
    ring_b = [Buf(f"ring{i}") for i in range(NSLOT)]
    cf = sb("cf", [128, 2048])
    cb = sb("cb", [128, 1536], BF16)
    cf_b, cb_b = Buf("cf"), Buf("cb")
    pp = sb("pp", [128, DEPTH, NPC])
    pp_b = Buf("pp")
    lw = sb("lw", [128, DEPTH, 1024])
    lw_b = Buf("lw")
    xT32 = sb("xT32", [128, 8, T])
    xTb = sb("xTb", [128, 8, T], BF16)
    xT32_b, xTb_b = Buf("xT32"), Buf("xTb")
    merged = sb("merged", [128, 8, T])
    merged_b = Buf("merged")
    kT_h = [sb("kTa", [128, 2, 2 * T], BF16), sb("kTb", [128, 2, 2 * T], BF16), sb("kTc", [128, 2, max(SEQ, 4096)], BF16)]
    kT_hb = [Buf("kTa"), Buf("kTb"), Buf("kTc")]
    Vtm = [sb("Va", [128, 8, 256], BF16), sb("Vb", [128, 8, 256], BF16), sb("Vc", [128, 32, 256], BF16)]
    Vtm_b = [Buf("Va"), Buf("Vb"), Buf("Vc")]
    Hst = sb("Hst", [128, 4, 64])
    Hst_b = Buf("Hst")
    zcar = sb("zcar", [128, 14])
    zcar_b = Buf("zcar")
    uhist = sb("uhist", [128, 4, 30])
    uhist_b = Buf("uhist")
    fcar = sb("fcar", [128, 44, 2])
    fcar_b = Buf("fcar")
    ps = [es.enter_context(nc.psum_tensor(f"ps{i}", [128, 512], F32)) for i in range(8)]
    ps_b = [Buf(f"ps{i}") for i in range(8)]

    ARENA_N = 12288
    arena_t = sb("arena", [128, ARENA_N])

    class Arena:
        def __init__(self):
            self.top = 0
            self.stack = []

        def push(self):
            self.stack.append(self.top)

        def pop(self):
            self.top = self.stack.pop()

        def alloc(self, shape, dt=F32):
            nfree = int(np.prod(shape[1:]))
            n32 = nfree if dt == F32 else (nfree + 1) // 2
            ap = arena_t[0:shape[0], self.top:self.top + n32]
            self.top += n32
            assert self.top <= ARENA_N, ("arena overflow", self.top)
            if dt != F32:
                ap = ap.bitcast(dt)
            if len(shape) == 3:
                ap = ap.rearrange("p (a b) -> p a b", a=shape[1])
            elif len(shape) == 4:
                ap = ap.rearrange("p (a b c) -> p a b c", a=shape[1], b=shape[2])
            return ap

    AR = Arena()

    @contextmanager
    def scope():
        AR.push()
        try:
            yield None
        finally:
            AR.pop()

    @contextmanager
    def scope_alloc(shape, dt=F32):
        AR.push()
        try:
            yield AR.alloc(list(shape), dt)
        finally:
            AR.pop()

    ident = cf[:, 0:128]
    ones_blk = cf[:, 128:256]
    mean_blk = cf[:, 256:384]
    mean_d = cf[:, 384:512]
    mean_c = cf[:, 512:640]
    rmask = cf[:, 640:1152]
    identb = cb[:, 0:128]
    mb_same = cb[:, 128:256]
    mb_prev = cb[:, 256:384]
    ones_b64 = cb[:, 384:448]

    def pcol(l, name, i=0, rows=slice(0, 128)):
        c = PC[name] + i
        return pp[rows, l, c:c + 1]

    S.dma("sp", lambda e: e.dma_start(out=cf[:], in_=cfk[:, :]), "cf", writes=[cf_b], local=False)
    S.dma("pool", lambda e: e.dma_start(out=cb[:], in_=cbk[:, :]), "cb", writes=[cb_b], local=False)
    S.dma("sp", lambda e: [e.dma_start(out=pp[:, l, :], in_=ppk[l, :, :]) for l in range(DEPTH)], "pp",
          writes=[pp_b], n=DEPTH, local=False)
    S.dma("sp", lambda e: [e.dma_start(out=lw[:, l, :], in_=lwk[l, :, :]) for l in range(DEPTH)], "lw",
          writes=[lw_b], n=DEPTH, local=False)
    for l in range(DEPTH):
        S.op("dve", lambda e, l=l: e.tensor_scalar(out=pp[:, l, PC["omm"]:PC["omm"] + 14],
                                                   in0=pp[:, l, PC["mu"]:PC["mu"] + 14], scalar1=-1.0, scalar2=1.0,
                                                   op0=ALU.mult, op1=ALU.add), reads=[pp_b], writes=[pp_b])
        S.op("dve", lambda e, l=l: e.tensor_scalar(out=pp[:, l, PC["nw0"]:PC["nw0"] + 4],
                                                   in0=pp[:, l, PC["w0"]:PC["w0"] + 4], scalar1=-1.0, scalar2=None,
                                                   op0=ALU.mult), reads=[pp_b], writes=[pp_b])
    for g in range(3):
        S.op("dve", lambda e, g=g: e.memset(kT_h[g][:], 0.0), writes=[kT_hb[g]])
        S.op("dve", lambda e, g=g: e.memset(Vtm[g][:], 0.0), writes=[Vtm_b[g]])

    wstate = {"n": 0}

    def wload(l, slab):
        i = wstate["n"] % NSLOT
        wstate["n"] += 1
        S.dma("pool", lambda e: e.dma_start(out=ring[:, i, :], in_=wpk[l, slab, :, :]), f"ring{i}",
              writes=[ring_b[i]], local=False)
        return i

    class WQ:
        def __init__(self):
            self.plan = []
            self.issued = 0
            self.slots = {}

        def extend(self, items):
            self.plan.extend(items)

        def get(self, idx, ahead=NSLOT - 1):
            while self.issued < len(self.plan) and self.issued <= idx + ahead:
                l, slab = self.plan[self.issued]
                self.slots[self.issued] = wload(l, slab)
                self.issued += 1
            return self.slots[idx]

    wq = WQ()
    for l in range(DEPTH):
        if NSAMP:
            for sl in range(NSLAB):
                wq.extend([(l, sl)])
        for i in range(NT):
            if STOP and ((l, i) != (0, 0) or STOP == "load"):
                continue
            for sl in range(NSLAB):
                wq.extend([(l, sl)])
    wctr = {"i": 0}

    cur_layer = {"l": 0}

    def next_slab(expect):
        idx = wctr["i"]
        assert wq.plan[idx] == (cur_layer["l"], expect), (wq.plan[idx], cur_layer["l"], expect)
        slot = wq.get(idx, ahead=NSLOT - 3)
        wctr["i"] += 1
        return slot

    def W(slot, kc_n, ncols):
        return ring[:, slot, 0:kc_n * ncols].rearrange("p (k n) -> p k n", n=ncols)

    def dbg(idx, ap, buf):
        if DEBUG:
            S.dma("sp", lambda e: e.dma_start(out=dbg_d[idx, 0:ap.shape[0], 0:ap.shape[1]], in_=ap), "dbg", reads=[buf], out=True)

    epsc = sb("epsc", [128, 4])
    epsc_b = Buf("epsc")
    for ci, ev in enumerate((LN_EPS, LNX_EPS, 1e-24)):
        S.op("dve", lambda e, ci=ci, ev=ev: e.memset(epsc[:, ci:ci + 1], float(ev)), writes=[epsc_b])
    eps_col = {float(LN_EPS): 0, float(LNX_EPS): 1, 1e-24: 2}

    def rsqrt(out, in_, eps, reads, wbuf):
        ci = eps_col[float(eps)]
        S.op("act", lambda e: e.activation(out=out, in_=in_, func=AF.Sqrt, bias=epsc[0:out.shape[0], ci:ci + 1]),
             reads=list(reads) + [epsc_b], writes=[wbuf])
        S.op("dve", lambda e: e.reciprocal(out=out, in_=out), reads=[wbuf], writes=[wbuf])

    def mm(out, lhsT, rhs, start, stop, reads, writes, sig):
        S.op("pe", lambda e: e.matmul(out, lhsT=lhsT, rhs=rhs, start=start, stop=stop), reads=reads,
             writes=writes, sig=sig)

    def proj_fm(pi, slot, col0, rhs_t, rhs_b, ncol=T, kc_n=8, wcols=512, rhs_cols=slice(0, T)):
        Wv = W(slot, kc_n, wcols)
        for kc in range(kc_n):
            mm(ps[pi][:, 0:ncol], Wv[:, kc, col0:col0 + 128], rhs_t[:, kc, rhs_cols], kc == 0, kc == kc_n - 1,
               [ring_b[slot], rhs_b], [ps_b[pi]], kc == kc_n - 1)

    def ln_fm(l, src, src_b, nch, mean_mat, gname, bname, eps, outs, tmp, tmp_b, pA=6, pB=7, N=T):
        for c in range(nch):
            mm(ps[pA][:, 0:N], mean_mat, src[:, c, :], c == 0, c == nch - 1, [cf_b, src_b], [ps_b[pA]], c == nch - 1)
        mean_sb = tmp[:, 0, 0:N]
        S.op("act", lambda e: e.copy(out=mean_sb, in_=ps[pA][:, 0:N]), reads=[ps_b[pA]], writes=[tmp_b])
        sq = es_local["sq"]
        sq_b = es_local["sq_b"]
        for c in range(nch):
            S.op("dve", lambda e, c=c: e.tensor_sub(out=src[:, c, :], in0=src[:, c, :], in1=mean_sb),
                 reads=[src_b, tmp_b], writes=[src_b])
            S.op("act", lambda e, c=c: e.activation(out=sq[:, c % 2, 0:N], in_=src[:, c, :], func=AF.Square),
                 reads=[src_b], writes=[sq_b[c % 2]])
            mm(ps[pB][:, 0:N], mean_mat, sq[:, c % 2, 0:N], c == 0, c == nch - 1, [cf_b, sq_b[c % 2]], [ps_b[pB]], True)
        rsqrt(mean_sb, ps[pB][:, 0:N], float(eps), [ps_b[pB]], tmp_b)
        for c in range(nch):
            S.op("dve", lambda e, c=c: e.tensor_mul(out=src[:, c, :], in0=src[:, c, :], in1=mean_sb),
                 reads=[src_b, tmp_b], writes=[src_b])
            for (ot, ob) in outs:
                S.op("dve", lambda e, c=c, ot=ot: e.tensor_scalar(out=ot[:, c, :], in0=src[:, c, :],
                                                                   scalar1=pcol(l, gname, c), scalar2=pcol(l, bname, c),
                                                                   op0=ALU.mult, op1=ALU.add),
                     reads=[src_b, pp_b], writes=[ob])

    es_local = {}

    def merge_branch(l, bidx, slot_out, kc_n, actT, act_b, gslots, N=T, xb=None, xb_b=None, mg=None, mg_b=None):
        xb = xTb if xb is None else xb
        xb_b = xTb_b if xb_b is None else xb_b
        mg = merged if mg is None else mg
        mg_b = merged_b if mg_b is None else mg_b
        Wo = W(slot_out, kc_n, 1024)
        with scope_alloc([128, 2, N], F32) as gsb:
            gsb_b = [Buf("gsb0"), Buf("gsb1")]
            for oc in range(8):
                py, pg = oc % 2, 2 + oc % 2
                for kc in range(kc_n):
                    mm(ps[py][:, 0:N], Wo[:, kc, oc * 128:(oc + 1) * 128], actT[:, kc, :], kc == 0, kc == kc_n - 1,
                       [ring_b[slot_out], act_b], [ps_b[py]], kc == kc_n - 1)
                proj_fm(pg, gslots[oc // 4], (oc % 4) * 128, xb, xb_b, ncol=N, rhs_cols=slice(0, N))
                g_ = gsb[:, oc % 2, :]
                S.op("act", lambda e, g_=g_, pg=pg, oc=oc: e.activation(out=g_, in_=ps[pg][:, 0:N], func=AF.Sigmoid,
                                                                         bias=pcol(l, "gb", bidx * 8 + oc)),
                     reads=[ps_b[pg], pp_b], writes=[gsb_b[oc % 2]])
                if bidx == 0:
                    S.op("dve", lambda e, g_=g_, py=py, oc=oc: e.tensor_mul(out=mg[:, oc, :], in0=ps[py][:, 0:N], in1=g_),
                         reads=[ps_b[py], gsb_b[oc % 2]], writes=[mg_b])
                else:
                    S.op("dve", lambda e, g_=g_, py=py: e.tensor_mul(out=g_, in0=ps[py][:, 0:N], in1=g_),
                         reads=[ps_b[py], gsb_b[oc % 2]], writes=[gsb_b[oc % 2]])
                    S.op("dve", lambda e, g_=g_, oc=oc: e.tensor_add(out=mg[:, oc, :], in0=mg[:, oc, :], in1=g_),
                         reads=[mg_b, gsb_b[oc % 2]], writes=[mg_b])
            if DEBUG and N == T and l == 0 and DBGSTATE["tile"] == 0:
                dbg(10 + bidx, mg[:, 0, :], mg_b)
            S.barrier()

    def fm2tm_store(src_cols_fn, nchunk, nrows, dst_fn, key):
        with scope_alloc([128, 512], F32) as stg:
            stg_b = Buf("stg")
            for c in range(nchunk):
                ap, bf = src_cols_fn(c)
                S.op("pe", lambda e, ap=ap, c=c: e.matmul(ps[4][0:nrows, c * 128:(c + 1) * 128], lhsT=ap, rhs=ident,
                                                          start=True, stop=True),
                     reads=[bf, cf_b], writes=[ps_b[4]])
            S.op("act", lambda e: e.copy(out=stg[0:nrows, 0:nchunk * 128], in_=ps[4][0:nrows, 0:nchunk * 128]),
                 reads=[ps_b[4]], writes=[stg_b])
            S.dma("sp", dst_fn(stg), key, reads=[stg_b], out=True)
            S.barrier()

    def tile_prog(l, i):
        DBGSTATE["tile"] = i
        par = i % 2
        last = (i == NT - 1)
        src = x_fm if l == 0 else xs_d
        S.dma("sp", lambda e: e.dma_start(out=xT32[:], in_=src[:, :, i * T:(i + 1) * T]), "xload",
              writes=[xT32_b], local=False)
        S.op("act", lambda e: e.copy(out=xTb[:], in_=xT32[:]), reads=[xT32_b], writes=[xTb_b])

        if STOP == "load":
            return
        rwkv_phase(l, i, last)
        if STOP == "rwkv":
            return
        conv_phase(l, i, last)
        if STOP == "conv":
            return
        attn_phase(l, i, par, last)
        if STOP == "attn":
            return
        ffn_phase(l, i, last)

    def rwkv_phase(l, i, last):
        with scope():
            def lt(name, shape, dt=F32):
                return AR.alloc(list(shape), dt)
            lora = merged[:, 0:2, :]
            lora_b = Buf("lora")
            zraw = merged[:, 2:5, :].rearrange("p a t -> p (a t)")[:, 0:2 * (T + 1)].rearrange("p (a t) -> p a t", a=2)
            zraw_b = [Buf("zraw0"), Buf("zraw1")]
            ozT = lt("ozT", [128, 4, T], BF16)
            ozT_b = Buf("ozT")
            zctr = {"n": 0}

            def shift_chunk(pi, och, dst, dst_b):
                zi = zctr["n"] % 2
                zctr["n"] += 1
                zr = zraw[:, zi, :]
                S.op("act", lambda e: e.copy(out=zr[:, 1:T + 1], in_=ps[pi][:, :]), reads=[ps_b[pi]], writes=[zraw_b[zi]])
                S.op("dve", lambda e: e.tensor_copy(out=zr[:, 0:1], in_=zcar[:, och:och + 1]), reads=[zcar_b],
                     writes=[zraw_b[zi]])
                S.op("dve", lambda e: e.tensor_scalar(out=dst, in0=ps[pi][:, :], scalar1=pcol(l, "omm", och), scalar2=None,
                                                      op0=ALU.mult), reads=[ps_b[pi], pp_b], writes=[dst_b])
                S.op("dve", lambda e: e.scalar_tensor_tensor(out=dst, in0=zr[:, 0:T], scalar=pcol(l, "mu", och), in1=dst,
                                                             op0=ALU.mult, op1=ALU.add),
                     reads=[zraw_b[zi], pp_b, dst_b], writes=[dst_b])
                S.op("dve", lambda e: e.tensor_copy(out=zcar[:, och:och + 1], in_=zr[:, T:T + 1]), reads=[zraw_b[zi]],
                     writes=[zcar_b])

            slot = next_slab(SL_LORA)
            for c in range(2):
                proj_fm(c, slot, c * 128, xTb, xTb_b, wcols=256)
                shift_chunk(c, 12 + c, lora[:, c, :], lora_b)
            S.op("act", lambda e: e.activation(out=lora[0:64, 0, :], in_=lora[0:64, 0, :], func=AF.Tanh),
                 reads=[lora_b], writes=[lora_b])
            S.op("act", lambda e: e.activation(out=lora[:, 1, :], in_=lora[:, 1, :], func=AF.Sigmoid),
                 reads=[lora_b], writes=[lora_b])

            if STOP == "rw_lora":
                raise _Stop()
            for p in range(4):
                rwkv_pair(l, i, p, lora, lora_b, shift_chunk, ozT, ozT_b, None)

            s_out = next_slab(SL_ROUT)
            g0 = next_slab(SL_G0)
            g1 = next_slab(SL_G0 + 1)
            merge_branch(l, 0, s_out, 4, ozT, ozT_b, (g0, g1))
            if last:
                S.dma("sp", lambda e: e.dma_start(out=o_shift[l].rearrange("(c p) -> p c", p=128), in_=zcar[:, :],
                                                  allow_slow_non_contiguous=True), "oshift", reads=[zcar_b], out=True)
                with scope_alloc([64, 8, 64], F32) as stg:
                    stg_b = Buf("wkvst")
                    for h in range(8):
                        pr_, hl = h // 2, h % 2
                        S.op("pe", lambda e, pr_=pr_, hl=hl, h=h: e.matmul(
                            ps[4 + hl][0:64, pr_ * 64:(pr_ + 1) * 64], lhsT=Hst[hl * 64:(hl + 1) * 64, pr_, :],
                            rhs=cf[hl * 64:(hl + 1) * 64, 1344:1408], start=True, stop=True), reads=[Hst_b, cf_b],
                            writes=[ps_b[4 + hl]])
                    stg4 = stg[:].rearrange("p (pr hl) k -> p pr hl k", hl=2)
                    for hl in range(2):
                        S.op("act", lambda e, hl=hl: e.copy(out=stg4[:, :, hl, :],
                                                            in_=ps[4 + hl][0:64, 0:256].rearrange("p (pr k) -> p pr k", k=64)),
                             reads=[ps_b[4 + hl]], writes=[stg_b])
                    S.dma("sp", lambda e: e.dma_start(out=o_wkv[l].rearrange("h v k -> v h k"), in_=stg[:]), "owkv",
                          reads=[stg_b], out=True)
                    S.barrier()
            S.barrier()

    def rwkv_pair(l, i, p, lora, lora_b, shift_chunk, ozT, ozT_b, st_outer):
        with scope():
            cnt = {"n": 0}

            def lt(name, shape, dt=F32):
                cnt["n"] += 1
                return AR.alloc(list(shape), dt), Buf(name)
            rT, r_b = lt("rT", [128, T])
            kT, k_b = lt("kT", [128, T])
            vT, v_b = lt("vT", [128, T])
            lwT, lw_b2 = lt("lwT", [128, T])
            aT, a_b = lt("aT", [128, T])
            gT, g_b = lt("gT", [128, T])
            kkT, kk_b = lt("kkT", [128, T])
            bT, b_b = lt("bT", [128, T])
            csT, cs_b = lt("csT", [128, T])
            eT, e_b = lt("eT", [128, T])
            e2T, e2_b = lt("e2T", [128, T])
            KR, KR_b = lt("KR", [128, NCH, 2, C])
            bon, bon_b = lt("bon", [128, T])
            gam, gam_b = lt("gam", [128, NCH])
            Kt, Kt_b = kT, k_b
            Bt, Bt_b = bT, b_b
            Kh, Kh_b = rT, r_b
            Bh, Bh_b = kkT, kk_b
            OT, OT_b = aT, a_b
            slot = next_slab(SL_RP + p)
            for c, (dst, db, och) in enumerate(((rT, r_b, p), (kT, k_b, 4 + p), (vT, v_b, 8 + p))):
                proj_fm(c % 2, slot, c * 128, xTb, xTb_b, wcols=384)
                shift_chunk(c % 2, och, dst[:, :], db)
            cs128 = slice(p * 128, (p + 1) * 128)
            mm(ps[2][:, :], lw[0:64, l, cs128], lora[0:64, 0, :], True, True, [lw_b, lora_b], [ps_b[2]], True)
            mm(ps[3][:, :], lw[64:128, l, cs128], lora[64:128, 0, :], True, True, [lw_b, lora_b], [ps_b[3]], True)
            mm(ps[4][:, :], lw[:, l, 512 + p * 128:512 + (p + 1) * 128], lora[:, 1, :], True, True, [lw_b, lora_b],
               [ps_b[4]], True)
            S.op("act", lambda e: e.activation(out=eT[:, :], in_=ps[2][:, :], func=AF.Exp, scale=-1.0,
                                               bias=pcol(l, "nw0", p)), reads=[ps_b[2], pp_b], writes=[e_b])
            S.op("act", lambda e: e.activation(out=eT[:, :], in_=eT[:, :], func=AF.Ln, bias=1.0), reads=[e_b], writes=[e_b])
            S.op("act", lambda e: e.activation(out=eT[:, :], in_=eT[:, :], func=AF.Exp, scale=-1.0, bias=-0.5),
                 reads=[e_b], writes=[e_b])
            S.op("dve", lambda e: e.tensor_scalar(out=lwT[:, :], in0=eT[:, :], scalar1=-1.0, scalar2=None, op0=ALU.mult),
                 reads=[e_b], writes=[lw_b2])
            S.op("act", lambda e: e.activation(out=aT[:, :], in_=ps[3][:, :], func=AF.Sigmoid, bias=pcol(l, "a0", p)),
                 reads=[ps_b[3], pp_b], writes=[a_b])
            S.op("act", lambda e: e.copy(out=gT[:, :], in_=ps[4][:, :]), reads=[ps_b[4]], writes=[g_b])
            S.op("dve", lambda e: e.tensor_scalar(out=kkT[:, :], in0=kT[:, :], scalar1=pcol(l, "kk", p), scalar2=None,
                                                  op0=ALU.mult), reads=[k_b, pp_b], writes=[kk_b])
            S.op("dve", lambda e: e.tensor_mul(out=e2T[:, :], in0=kkT[:, :], in1=kkT[:, :]), reads=[kk_b], writes=[e2_b])
            mm(ps[2][:, :], ones_blk, e2T[:, :], True, True, [cf_b, e2_b], [ps_b[2]], True)
            rsqrt(e2T[:, :], ps[2][:, :], 1e-24, [ps_b[2]], e2_b)
            S.op("dve", lambda e: e.tensor_mul(out=kkT[:, :], in0=kkT[:, :], in1=e2T[:, :]), reads=[kk_b, e2_b],
                 writes=[kk_b])
            S.op("dve", lambda e: e.tensor_mul(out=bT[:, :], in0=kkT[:, :], in1=aT[:, :]), reads=[kk_b, a_b], writes=[b_b])
            S.op("dve", lambda e: e.tensor_scalar(out=aT[:, :], in0=aT[:, :], scalar1=-1.0, scalar2=pcol(l, "ka", p),
                                                  op0=ALU.add, op1=ALU.mult), reads=[a_b, pp_b], writes=[a_b])
            S.op("dve", lambda e: e.scalar_tensor_tensor(out=kT[:, :], in0=aT[:, :], scalar=1.0, in1=kT[:, :], op0=ALU.add,
                                                         op1=ALU.mult), reads=[a_b, k_b], writes=[k_b])
            S.op("dve", lambda e: e.scalar_tensor_tensor(out=e2T[:, :], in0=rT[:, :], scalar=pcol(l, "rk", p), in1=kT[:, :],
                                                         op0=ALU.mult, op1=ALU.mult), reads=[r_b, k_b, pp_b], writes=[e2_b])
            mm(ps[3][:, :], ones_blk, e2T[:, :], True, True, [cf_b, e2_b], [ps_b[3]], True)
            S.op("dve", lambda e: e.tensor_mul(out=bon[:, :], in0=ps[3][:, :], in1=vT[:, :]), reads=[ps_b[3], v_b],
                 writes=[bon_b])
            if (l, i, p) == (0, 0, 0):
                for di, (ap_, bf_) in enumerate(((rT, r_b), (kT, k_b), (vT, v_b), (lwT, lw_b2), (kkT, kk_b), (bT, b_b),
                                                 (gT, g_b), (bon, bon_b))):
                    dbg(di, ap_[:, :], bf_)
            S.op("dve", lambda e: e.tensor_tensor_scan(out=csT[:, :], data0=rmask, data1=lwT[:, :], initial=0.0,
                                                       op0=ALU.mult, op1=ALU.add), reads=[cf_b, lw_b2], writes=[cs_b])
            cs3 = csT[:, :].rearrange("p (j t) -> p j t", t=C)
            S.op("act", lambda e: e.activation(out=eT[:, :], in_=csT[:, :], func=AF.Exp), reads=[cs_b], writes=[e_b])
            S.op("dve", lambda e: e.tensor_mul(out=KR[:, :, 1, :], in0=rT[:, :].rearrange("p (j t) -> p j t", t=C),
                                               in1=eT[:, :].rearrange("p (j t) -> p j t", t=C)),
                 reads=[r_b, e_b], writes=[KR_b])
            S.op("dve", lambda e: e.tensor_copy(out=gam[:, :], in_=eT[:, :].rearrange("p (j t) -> p j t", t=C)[:, :, C - 1]),
                 reads=[e_b], writes=[gam_b])
            S.op("dve", lambda e: e.tensor_sub(out=OT[:, :], in0=csT[:, :], in1=lwT[:, :]), reads=[cs_b, lw_b2], writes=[OT_b])
            S.op("act", lambda e: e.activation(out=OT[:, :], in_=OT[:, :], func=AF.Exp), reads=[OT_b], writes=[OT_b])
            S.op("dve", lambda e: e.tensor_mul(out=KR[:, :, 0, :], in0=kkT[:, :].rearrange("p (j t) -> p j t", t=C),
                                               in1=OT[:, :].rearrange("p (j t) -> p j t", t=C)),
                 reads=[kk_b, OT_b], writes=[KR_b])
            S.op("dve", lambda e: e.tensor_sub(out=OT[:, :].rearrange("p (j t) -> p j t", t=C),
                                               in0=cs3[:, :, C - 1:C].broadcast_to([128, NCH, C]), in1=cs3),
                 reads=[cs_b], writes=[OT_b])
            S.op("act", lambda e: e.activation(out=OT[:, :], in_=OT[:, :], func=AF.Exp), reads=[OT_b], writes=[OT_b])
            S.op("dve", lambda e: e.tensor_mul(out=Kh[:, :], in0=kT[:, :], in1=OT[:, :]), reads=[k_b, OT_b], writes=[Kh_b])
            S.op("dve", lambda e: e.tensor_mul(out=Bh[:, :], in0=bT[:, :], in1=OT[:, :]), reads=[b_b, OT_b], writes=[Bh_b])

            S.op("act", lambda e: e.activation(out=eT[:, :], in_=csT[:, :], func=AF.Exp, scale=-1.0), reads=[cs_b],
                 writes=[e_b])
            S.op("dve", lambda e: e.tensor_mul(out=Kt[:, :], in0=kT[:, :], in1=eT[:, :]), reads=[k_b, e_b], writes=[Kt_b])
            S.op("dve", lambda e: e.tensor_mul(out=Bt[:, :], in0=bT[:, :], in1=eT[:, :]), reads=[b_b, e_b], writes=[Bt_b])
            if STOP == "rw_prep":
                raise _Stop()
            GQ = 4
            A1, A1_b = lt("A1", [128, GQ, 2, 128])
            Ab, Ab_b = lt("Ab", [128, GQ, 64])
            Y, Y_b = lt("Y", [128, GQ, 2, 64])
            Pm, Pm_b = lt("Pm", [128, GQ, 2, 64])
            TM, TM_b = lt("TM", [128, 3, 128])
            W1, W1_b = lt("W1", [128, 64])
            U, U_b = lt("U", [128, 64])
            dup, dup_b = lt("dup", [128, 3, 128])
            mask1 = cf[:, 1152:1280]
            mask2 = cf[:, 1280:1344]
            id64 = cf[:, 1344:1408]
            HP = [slice(0, 64), slice(64, 128)]
            BK_W, BK_O, BK_H = (1, 3), (4, 5), (6, 7)

            def group_stage(jg):
                for jj in range(GQ):
                    j_ = jg * GQ + jj
                    tcs_ = slice(j_ * C, (j_ + 1) * C)
                    for hl in range(2):
                        hp = HP[hl]
                        krj = KR[hp, j_, :, :].rearrange("p a t -> p (a t)")
                        bA = 2 * hl + jj // 2
                        c0 = (jj % 2) * 256
                        bB = 4 + hl
                        mm(ps[bA][hp, c0:c0 + 128], Kt[hp, tcs_], krj, True, True, [Kt_b, KR_b], [ps_b[bA]], False)
                        mm(ps[bA][hp, c0 + 128:c0 + 256], Bt[hp, tcs_], krj, True, True, [Bt_b, KR_b], [ps_b[bA]], False)
                        mm(ps[bB][hp, jj * 64:(jj + 1) * 64], KR[hp, j_, 0, :], Bt[hp, tcs_], True, True, [KR_b, Bt_b],
                           [ps_b[bB]], True)
                for hl in range(2):
                    hp = HP[hl]
                    for hb_ in range(2):
                        bA = 2 * hl + hb_
                        S.op("dve", lambda e, hp=hp, bA=bA, hb_=hb_: e.tensor_mul(
                            out=A1[hp, 2 * hb_:2 * hb_ + 2, :, :].rearrange("p c x t -> p (c x) t"),
                            in0=ps[bA][hp, 0:512].rearrange("p (x t) -> p x t", t=128),
                            in1=mask1[hp, :].unsqueeze(1).broadcast_to([64, 4, 128])), reads=[ps_b[bA], cf_b],
                            writes=[A1_b])
                    S.op("dve", lambda e, hp=hp, hl=hl: e.tensor_mul(
                        out=Ab[hp, :, :], in0=ps[4 + hl][hp, 0:GQ * 64].rearrange("p (c t) -> p c t", t=64),
                        in1=mask2[hp, :].unsqueeze(1).broadcast_to([64, GQ, 64])), reads=[ps_b[4 + hl], cf_b],
                        writes=[Ab_b])
                S.op("dve", lambda e: e.tensor_scalar(out=Y[:, :, 0, :], in0=A1[:, :, 1, 0:64], scalar1=-1.0, scalar2=None,
                                                      op0=ALU.mult), reads=[A1_b], writes=[Y_b])
                S.op("dve", lambda e: e.tensor_scalar(out=Y[:, :, 1, :], in0=Ab[:, :, :], scalar1=-1.0, scalar2=None,
                                                      op0=ALU.mult), reads=[Ab_b], writes=[Y_b])
                S.op("dve", lambda e: e.tensor_add(out=Pm[:].rearrange("p c a t -> p (c a) t"),
                                                   in0=Y[:].rearrange("p c a t -> p (c a) t"),
                                                   in1=id64.unsqueeze(1).broadcast_to([128, 2 * GQ, 64])),
                     reads=[Y_b, cf_b], writes=[Pm_b])
                for lev in range(5):
                    for jj in range(GQ):
                        for hl in range(2):
                            hp = HP[hl]
                            b_ = 6 + hl
                            mm(ps[b_][hp, jj * 128:jj * 128 + 64], Y[hp, jj, 1, :], Y[hp, jj, 0, :], True, True, [Y_b],
                               [ps_b[b_]], False)
                            mm(ps[b_][hp, jj * 128 + 64:jj * 128 + 128], Y[hp, jj, 0, :], Y[hp, jj, 1, :], True, True,
                               [Y_b], [ps_b[b_]], True)
                    for hl in range(2):
                        hp = HP[hl]
                        S.op("act", lambda e, hp=hp, hl=hl: e.copy(out=Y[hp, :, :, :].rearrange("p c a t -> p (c a t)"),
                                                                   in_=ps[6 + hl][hp, 0:GQ * 128]), reads=[ps_b[6 + hl]],
                             writes=[Y_b])
                    for jj in range(GQ):
                        for hl in range(2):
                            hp = HP[hl]
                            b_ = 4 + hl
                            mm(ps[b_][hp, jj * 128:jj * 128 + 64], Pm[hp, jj, 1, :], Y[hp, jj, 0, :], True, True,
                               [Pm_b, Y_b], [ps_b[b_]], False)
                            mm(ps[b_][hp, jj * 128 + 64:jj * 128 + 128], Y[hp, jj, 0, :], Pm[hp, jj, 1, :], True, True,
                               [Pm_b, Y_b], [ps_b[b_]], True)
                    for hl in range(2):
                        hp = HP[hl]
                        S.op("dve", lambda e, hp=hp, hl=hl: e.tensor_add(
                            out=Pm[hp, :, :, :].rearrange("p c a t -> p (c a t)"),
                            in0=Pm[hp, :, :, :].rearrange("p c a t -> p (c a t)"), in1=ps[4 + hl][hp, 0:GQ * 128]),
                            reads=[ps_b[4 + hl], Pm_b], writes=[Pm_b])

            for j in range(NCH):
                tc0 = j * C
                tcs = slice(tc0, tc0 + C)
                jj = j % GQ
                if jj == 0:
                    group_stage(j // GQ)
                if STOP == "rw_inv":
                    raise _Stop()
                for ti, (srcT, sbf) in enumerate(((vT, v_b), (Kh, Kh_b), (Bh, Bh_b))):
                    for a_ in range(2):
                        S.op("pe", lambda e, ti=ti, srcT=srcT, a_=a_: e.matmul(
                            ps[0][a_ * 64:(a_ + 1) * 64, ti * 128:(ti + 1) * 128], lhsT=srcT[:, tcs], rhs=ident,
                            start=True, stop=True), reads=[sbf, cf_b], writes=[ps_b[0]], sig=(ti == 2 and a_ == 1))
                S.op("act", lambda e: e.copy(out=TM[:].rearrange("p a f -> p (a f)"), in_=ps[0][:, 0:384]),
                     reads=[ps_b[0]], writes=[TM_b])
                if STOP == "rw_tm":
                    raise _Stop()
                for hl in range(2):
                    hp = HP[hl]
                    b_ = BK_W[hl]
                    o_ = ps[b_][hp, 0:64]
                    mm(o_, KR[hp, j, 0, :], Hst[hp, p, :], True, False, [KR_b, Hst_b], [ps_b[b_]], False)
                    mm(o_, A1[hp, jj, 0, 0:64], TM[hp, 0, hp], False, True, [A1_b, TM_b], [ps_b[b_]], True)
                for hl in range(2):
                    hp = HP[hl]
                    b_ = BK_W[hl]
                    S.op("act", lambda e, hp=hp, b_=b_: e.copy(out=W1[hp, :], in_=ps[b_][hp, 0:64]), reads=[ps_b[b_]],
                         writes=[W1_b])
                if STOP == "rw_w1":
                    raise _Stop()
                for hl in range(2):
                    hp = HP[hl]
                    b_ = BK_W[hl]
                    mm(ps[b_][hp, 128:192], Pm[hp, jj, 0, :], W1[hp, :], True, True, [Pm_b, W1_b], [ps_b[b_]], True)
                for hl in range(2):
                    hp = HP[hl]
                    b_ = BK_W[hl]
                    S.op("act", lambda e, hp=hp, b_=b_: e.mul(out=U[hp, :], in_=ps[b_][hp, 128:192], mul=-1.0),
                         reads=[ps_b[b_]], writes=[U_b])
                if STOP == "rw_u":
                    raise _Stop()
                for hl in range(2):
                    hp = HP[hl]
                    b_ = BK_H[hl]
                    o_ = ps[b_][hp, 0:64]
                    mm(o_, TM[hp, 1, hp], TM[hp, 0, hp], True, False, [TM_b], [ps_b[b_]], False)
                    mm(o_, TM[hp, 2, hp], U[hp, :], False, True, [TM_b, U_b], [ps_b[b_]], True)
                for hl in range(2):
                    hp = HP[hl]
                    b_ = BK_O[hl]
                    o_ = ps[b_][hp, 0:64]
                    mm(o_, Hst[hp, p, :], KR[hp, j, 1, :], True, False, [Hst_b, KR_b], [ps_b[b_]], False)
                    mm(o_, TM[hp, 0, hp], A1[hp, jj, 0, 64:128], False, False, [TM_b, A1_b], [ps_b[b_]], False)
                    mm(o_, U[hp, :], A1[hp, jj, 1, 64:128], False, True, [U_b, A1_b], [ps_b[b_]], True)
                for hl in range(2):
                    hp = HP[hl]
                    b_ = BK_O[hl]
                    S.op("act", lambda e, hp=hp, b_=b_: e.copy(out=OT[hp, tcs], in_=ps[b_][hp, 0:64]), reads=[ps_b[b_]],
                         writes=[OT_b])
                if STOP == "rw_o":
                    raise _Stop()
                for hl in range(2):
                    hp = HP[hl]
                    b_ = BK_H[hl]
                    S.op("dve", lambda e, hp=hp, b_=b_: e.scalar_tensor_tensor(
                        out=Hst[hp, p, :], in0=Hst[hp, p, :], scalar=gam[hp, j:j + 1], in1=ps[b_][hp, 0:64],
                        op0=ALU.mult, op1=ALU.add), reads=[Hst_b, gam_b, ps_b[b_]], writes=[Hst_b])
            if STOP == "rw_chunk":
                raise _Stop()
            if (l, i, p) == (0, 0, 0):
                dbg(8, OT[:, :], OT_b)
            mm(ps[0][:, :], mean_blk, OT[:, :], True, True, [cf_b, OT_b], [ps_b[0]], True)
            S.op("dve", lambda e: e.tensor_sub(out=OT[:, :], in0=OT[:, :], in1=ps[0][:, :]), reads=[OT_b, ps_b[0]],
                 writes=[OT_b])
            S.op("act", lambda e: e.activation(out=eT[:, :], in_=OT[:, :], func=AF.Square), reads=[OT_b], writes=[e_b])
            mm(ps[1][:, :], mean_blk, eT[:, :], True, True, [cf_b, e_b], [ps_b[1]], True)
            rsqrt(eT[:, :], ps[1][:, :], float(LNX_EPS), [ps_b[1]], e_b)
            S.op("dve", lambda e: e.tensor_mul(out=OT[:, :], in0=OT[:, :], in1=eT[:, :]), reads=[OT_b, e_b], writes=[OT_b])
            S.op("dve", lambda e: e.tensor_scalar(out=OT[:, :], in0=OT[:, :], scalar1=pcol(l, "lng", p),
                                                  scalar2=pcol(l, "lnb", p), op0=ALU.mult, op1=ALU.add),
                 reads=[OT_b, pp_b], writes=[OT_b])
            S.op("dve", lambda e: e.tensor_add(out=OT[:, :], in0=OT[:, :], in1=bon[:, :]), reads=[OT_b, bon_b], writes=[OT_b])
            if (l, i, p) == (0, 0, 0):
                dbg(9, OT[:, :], OT_b)
            S.op("dve", lambda e: e.tensor_mul(out=ozT[:, p, :], in0=OT[:, :], in1=gT[:, :]), reads=[OT_b, g_b],
                 writes=[ozT_b])
            S.barrier()

    def conv_phase(l, i, last):
        with scope():
            def lt(name, shape, dt=F32):
                return AR.alloc(list(shape), dt), Buf(name)
            uext, ue_b = lt("uext", [128, 4, T + 30])
            acc, acc_b = lt("cacc", [128, 4, T])
            acc2, _ = lt("cacc2", [128, 2, T])
            acc2_b = [Buf("cacc20"), Buf("cacc21")]
            sq, _ = lt("csq", [128, 2, T])
            sq_b = [Buf("csq0"), Buf("csq1")]
            sg, sg_b = lt("csg", [128, 2, T])
            cTb, cT_b = lt("cTb", [128, 4, T], BF16)
            tmp, tmp_b = lt("ctmp", [128, 1, T])
            es_local["sq"], es_local["sq_b"] = sq, sq_b
            su = next_slab(SL_CU)
            sgt = next_slab(SL_CG)
            for c in range(4):
                proj_fm(0, su, c * 128, xTb, xTb_b)
                proj_fm(1, sgt, c * 128, xTb, xTb_b)
                S.op("act", lambda e, c=c: e.activation(out=sg[:, c % 2, :], in_=ps[1][:, :], func=AF.Sigmoid),
                     reads=[ps_b[1]], writes=[sg_b])
                S.op("dve", lambda e, c=c: e.tensor_copy(out=uext[:, c, 0:30], in_=uhist[:, c, :]), reads=[uhist_b],
                     writes=[ue_b])
                S.op("dve", lambda e, c=c: e.tensor_mul(out=uext[:, c, 30:30 + T], in0=ps[0][:, :], in1=sg[:, c % 2, :]),
                     reads=[ps_b[0], sg_b], writes=[ue_b])
                S.op("dve", lambda e, c=c: e.tensor_copy(out=uhist[:, c, :], in_=uext[:, c, T:T + 30]), reads=[ue_b],
                     writes=[uhist_b])
                S.op("dve", lambda e, c=c: e.tensor_scalar(out=acc[:, c, :], in0=uext[:, c, 0:T],
                                                           scalar1=pcol(l, "cdw", c * 31), scalar2=pcol(l, "cdb", c),
                                                           op0=ALU.mult, op1=ALU.add), reads=[ue_b, pp_b], writes=[acc_b])
                a2 = acc2[:, c % 2, :]
                a2_b = acc2_b[c % 2]
                S.op("dve", lambda e, c=c, a2=a2: e.tensor_scalar(out=a2, in0=uext[:, c, 1:1 + T],
                                                                  scalar1=pcol(l, "cdw", c * 31 + 1), scalar2=None,
                                                                  op0=ALU.mult), reads=[ue_b, pp_b], writes=[a2_b])
                for jj in range(2, 31):
                    if jj % 2 == 0:
                        S.op("dve", lambda e, c=c, jj=jj: e.scalar_tensor_tensor(out=acc[:, c, :], in0=uext[:, c, jj:jj + T],
                                                                                 scalar=pcol(l, "cdw", c * 31 + jj),
                                                                                 in1=acc[:, c, :], op0=ALU.mult, op1=ALU.add),
                             reads=[ue_b, pp_b, acc_b], writes=[acc_b])
                    else:
                        S.op("dve", lambda e, c=c, jj=jj, a2=a2: e.scalar_tensor_tensor(out=a2, in0=uext[:, c, jj:jj + T],
                                                                                         scalar=pcol(l, "cdw", c * 31 + jj),
                                                                                         in1=a2, op0=ALU.mult, op1=ALU.add),
                             reads=[ue_b, pp_b, a2_b], writes=[a2_b])
                S.op("dve", lambda e, c=c, a2=a2: e.tensor_add(out=acc[:, c, :], in0=acc[:, c, :], in1=a2),
                     reads=[acc_b, a2_b], writes=[acc_b])
            ln_fm(l, acc, acc_b, 4, mean_c, "clg", "clb", LN_EPS, [(acc, acc_b)], tmp, tmp_b)
            for c in range(4):
                S.op("act", lambda e, c=c: e.activation(out=cTb[:, c, :], in_=acc[:, c, :], func=AF.Silu), reads=[acc_b],
                     writes=[cT_b])
            s_out = next_slab(SL_COUT)
            g0 = next_slab(SL_G1)
            g1 = next_slab(SL_G1 + 1)
            merge_branch(l, 1, s_out, 4, cTb, cT_b, (g0, g1))
            if last:
                n30 = CONV_K - 1
                fm2tm_store(lambda c: (uhist[:, c, :], uhist_b), 4, n30,
                            lambda stg: (lambda e: e.dma_start(out=o_conv[l, :, :], in_=stg[0:n30, 0:512])), "oconv")
            S.barrier()

    def attn_phase(l, i, par, last):
        with scope():
            def lt(name, shape, dt=F32):
                return AR.alloc(list(shape), dt), Buf(name)
            qT, q_b = lt("qT", [128, 3, 2, T], BF16)
            pt, _ = lt("pt", [128, 2, 256], BF16)
            pt_b = [Buf("pt0"), Buf("pt1")]
            vstg, vstg_b = lt("vstg", [32, 16, 256], BF16)
            kvo, _ = lt("kvo", [128, 2, 512])
            kvo_b = [Buf("kvo0"), Buf("kvo1")]
            oTb, oT_b = lt("oTb", [128, 2, T], BF16)
            rec, rec_b = lt("rec", [128, T])
            B_c = i // 4
            o_c = (i % 4) * 32
            for g in range(3):
                win, dil = SWA_GROUPS[g]
                sA = next_slab(SL_ATT + 2 * g)
                sB = next_slab(SL_ATT + 2 * g + 1)
                kcol0 = (par * T) if g < 2 else i * T
                for c in range(4):
                    proj_fm(c % 2, sA, c * 128, xTb, xTb_b)
                    if c < 2:
                        S.op("act", lambda e, c=c, g=g: e.copy(out=qT[:, g, c, :], in_=ps[c % 2][:, :]),
                             reads=[ps_b[c % 2]], writes=[q_b])
                    else:
                        S.op("act", lambda e, c=c, g=g: e.copy(out=kT_h[g][:, c - 2, kcol0:kcol0 + T], in_=ps[c % 2][:, :]),
                             reads=[ps_b[c % 2]], writes=[kT_hb[g]])
                WB = W(sB, 8, 512)
                if g < 2:
                    for blk in range(4):
                        cols = slice(blk * 128, (blk + 1) * 128) if g == 0 else slice(blk, T, 4)
                        vb = (par * 4 + blk) if g == 0 else (blk * 2 + par)
                        pi = 2 + blk % 2
                        for kc in range(8):
                            mm(ps[pi][:, 0:256], xTb[:, kc, cols], WB[:, kc, 256:512], kc == 0, kc == 7,
                               [xTb_b, ring_b[sB]], [ps_b[pi]], kc == 7)
                        S.op("dve", lambda e, pi=pi, vb=vb, g=g: e.tensor_copy(out=Vtm[g][:, vb, :], in_=ps[pi][:, 0:256]),
                             reads=[ps_b[pi]], writes=[Vtm_b[g]])
                else:
                    for rho in range(16):
                        pi = 2 + rho % 2
                        for kc in range(8):
                            mm(ps[pi][0:32, 0:256], xTb[:, kc, slice(rho, T, 16)], WB[:, kc, 256:512], kc == 0, kc == 7,
                               [xTb_b, ring_b[sB]], [ps_b[pi]], kc == 7)
                        S.op("dve", lambda e, pi=pi, rho=rho: e.tensor_copy(out=vstg[:, rho, :], in_=ps[pi][0:32, 0:256]),
                             reads=[ps_b[pi]], writes=[vstg_b])
                    vdst = Vtm[2][o_c:o_c + 32, :, :].rearrange("p (r b) f -> p r b f", b=2)[:, :, B_c, :]
                    S.dma("sp", lambda e: e.dma_start(out=vdst, in_=vstg[:, :, :]), "vcdma", reads=[vstg_b],
                          writes=[Vtm_b[2]])
                kp = keep[g]
                for blk in range(4):
                    t0 = i * T + blk * 128
                    if t0 + 128 <= SEQ - kp:
                        continue
                    r0 = t0 - (SEQ - kp)
                    pi = 2 + blk % 2
                    for kc in range(8):
                        mm(ps[pi][:, :], xTb[:, kc, blk * 128:(blk + 1) * 128], WB[:, kc, :], kc == 0, kc == 7,
                           [xTb_b, ring_b[sB]], [ps_b[pi]], kc == 7)
                    S.op("act", lambda e, pi=pi, blk=blk: e.copy(out=kvo[:, blk % 2, :], in_=ps[pi][:, :]),
                         reads=[ps_b[pi]], writes=[kvo_b[blk % 2]])
                    S.dma("sp", lambda e, g=g, r0=r0, blk=blk: e.dma_start(out=o_swa[g][l, r0:r0 + 128, :],
                                                                           in_=kvo[:, blk % 2, :]), f"okv{blk % 2}",
                          reads=[kvo_b[blk % 2]], out=True)
            started = set()
            ptc = {"n": 0}

            def pv(pair, hl, vblk_ap, pt_ap, ptb, out_cols, vbuf):
                hp = slice(hl * 64, (hl + 1) * 64)
                first = (pair, hl) not in started
                started.add((pair, hl))
                mm(ps[4 + pair][hp, out_cols], vblk_ap, pt_ap, first, False, [vbuf, ptb], [ps_b[4 + pair]], False)
                mm(ps[6 + pair][hp, out_cols], ones_b64, pt_ap, first, False, [cb_b, ptb], [ps_b[6 + pair]], True)

            for h in range(4):
                pair, hl = h // 2, h % 2
                hp = slice(hl * 64, (hl + 1) * 64)
                for g in range(2):
                    for qb in range(4):
                        if g == 0:
                            qcols = slice(qb * 128, (qb + 1) * 128)
                            kbs = [(par * T + qb * 128, 1, par * 4 + qb, mb_same)]
                            if qb > 0:
                                kbs.append((par * T + (qb - 1) * 128, 1, par * 4 + qb - 1, mb_prev))
                            elif i > 0:
                                kbs.append(((1 - par) * T + 384, 1, (1 - par) * 4 + 3, mb_prev))
                        else:
                            qcols = slice(qb, T, 4)
                            kbs = [(par * T + qb, 4, qb * 2 + par, mb_same)]
                            if i > 0:
                                kbs.append(((1 - par) * T + qb, 4, qb * 2 + 1 - par, mb_prev))
                        k_ = ptc["n"] % 2
                        ptc["n"] += 1
                        pi = k_
                        for bi, (kc0, kst, vb, mbias) in enumerate(kbs):
                            kcols = slice(kc0, kc0 + 127 * kst + 1, kst)
                            mm(ps[pi][:, bi * 128:(bi + 1) * 128], kT_h[g][hp, pair, kcols], qT[hp, g, pair, qcols], True,
                               False, [kT_hb[g], q_b], [ps_b[pi]], False)
                            mm(ps[pi][:, bi * 128:(bi + 1) * 128], identb, mbias, False, True, [cb_b], [ps_b[pi]],
                               bi == len(kbs) - 1)
                        nb = len(kbs)
                        S.op("act", lambda e, k_=k_, pi=pi, nb=nb: e.activation(out=pt[:, k_, 0:nb * 128],
                                                                                in_=ps[pi][:, 0:nb * 128], func=AF.Exp,
                                                                                scale=0.125),
                             reads=[ps_b[pi]], writes=[pt_b[k_]])
                        for bi, (kc0, kst, vb, mbias) in enumerate(kbs):
                            pv(pair, hl, Vtm[g][:, vb, h * 64:(h + 1) * 64], pt[:, k_, bi * 128:(bi + 1) * 128], pt_b[k_],
                               qcols, Vtm_b[g])
                g = 2
                nkb = 2 if B_c >= 1 else 1
                for rg in range(4):
                    k_ = ptc["n"] % 2
                    ptc["n"] += 1
                    pi = k_
                    for kb in range(nkb):
                        Bk = B_c - kb
                        moff = 448 + (o_c // 32) * 256 + kb * 128
                        mm(ps[pi][:, kb * 128:(kb + 1) * 128], identb, cb[:, moff:moff + 128], True, False, [cb_b],
                           [ps_b[pi]], False)
                        for r4 in range(4):
                            rho = rg * 4 + r4
                            kc0 = 2048 * Bk + rho
                            kcols = slice(kc0, kc0 + 127 * 16 + 1, 16)
                            cc = kb * 128 + r4 * 32
                            mm(ps[pi][:, cc:cc + 32], kT_h[2][hp, pair, kcols], qT[hp, 2, pair, slice(rho, T, 16)], False,
                               r4 == 3, [kT_hb[2], q_b], [ps_b[pi]], r4 == 3 and kb == nkb - 1)
                    S.op("act", lambda e, k_=k_, pi=pi: e.activation(out=pt[:, k_, 0:nkb * 128], in_=ps[pi][:, 0:nkb * 128],
                                                                     func=AF.Exp, scale=0.125),
                         reads=[ps_b[pi]], writes=[pt_b[k_]])
                    for kb in range(nkb):
                        Bk = B_c - kb
                        for r4 in range(4):
                            rho = rg * 4 + r4
                            cc = kb * 128 + r4 * 32
                            pv(pair, hl, Vtm[2][:, rho * 2 + Bk, h * 64:(h + 1) * 64], pt[:, k_, cc:cc + 32], pt_b[k_],
                               slice(rho, T, 16), Vtm_b[2])
            for pair in range(2):
                S.op("dve", lambda e, pair=pair: e.reciprocal(out=rec[:, :], in_=ps[6 + pair][:, :]), reads=[ps_b[6 + pair]],
                     writes=[rec_b])
                S.op("dve", lambda e, pair=pair: e.tensor_mul(out=oTb[:, pair, :], in0=ps[4 + pair][:, :], in1=rec[:, :]),
                     reads=[ps_b[4 + pair], rec_b], writes=[oT_b])
            s_out = next_slab(SL_AOUT)
            g0 = next_slab(SL_G2)
            g1 = next_slab(SL_G2 + 1)
            merge_branch(l, 2, s_out, 2, oTb, oT_b, (g0, g1))
            S.barrier()

    def ffn_phase(l, i, last):
        with scope():
            def lt(name, shape, dt=F32):
                return AR.alloc(list(shape), dt), Buf(name)
            hb, hb_b = lt("hb", [128, 8, T], BF16)
            pre, pre_b = merged, merged_b
            with scope():
                mbf, mbf_b = lt("mbf", [128, 8, T], BF16)
                sq, _ = lt("fsq", [128, 2, T])
                sq_b = [Buf("fsq0"), Buf("fsq1")]
                tmp, tmp_b = lt("ftmp", [128, 1, T])
                es_local["sq"], es_local["sq_b"] = sq, sq_b
                S.op("act", lambda e: e.copy(out=mbf[:], in_=merged[:]), reads=[merged_b], writes=[mbf_b])
                wo = [next_slab(SL_WO), next_slab(SL_WO + 1)]
                for oc in range(8):
                    proj_fm(oc % 2, wo[oc // 4], (oc % 4) * 128, mbf, mbf_b)
                    S.op("dve", lambda e, oc=oc: e.scalar_tensor_tensor(out=pre[:, oc, :], in0=xT32[:, oc, :],
                                                                        scalar=float(ALPHA), in1=ps[oc % 2][:, :],
                                                                        op0=ALU.mult, op1=ALU.add),
                         reads=[xT32_b, ps_b[oc % 2]], writes=[pre_b])
                ln_fm(l, pre, pre_b, 8, mean_d, "l1g", "l1b", LN_EPS, [(xT32, xT32_b), (hb, hb_b)], tmp, tmp_b)
                S.barrier()
            actT, actT_b = lt("actT", [128, 22, T], BF16)
            with scope():
                uext, _ = lt("fuext", [128, 2, T + 2])
                ue_b = [Buf("fue0"), Buf("fue1")]
                cv, _ = lt("fcv", [128, 2, T])
                cv_b = [Buf("fcv0"), Buf("fcv1")]
                sl, _ = lt("fsl", [128, 2, T])
                sl_b = [Buf("fsl0"), Buf("fsl1")]
                for s in range(11):
                    slot = next_slab(SL_FIN + s)
                    for jj in range(4):
                        qi = s * 4 + jj
                        k_ = qi % 2
                        proj_fm(k_, slot, jj * 128, hb, hb_b)
                        S.op("act", lambda e, k_=k_: e.copy(out=uext[:, k_, 2:T + 2], in_=ps[k_][:, :]), reads=[ps_b[k_]],
                             writes=[ue_b[k_]])
                        S.op("dve", lambda e, k_=k_, qi=qi: e.tensor_copy(out=uext[:, k_, 0:2], in_=fcar[:, qi, :]),
                             reads=[fcar_b], writes=[ue_b[k_]])
                        S.op("dve", lambda e, k_=k_, qi=qi: e.tensor_copy(out=fcar[:, qi, :], in_=uext[:, k_, T:T + 2]),
                             reads=[ue_b[k_]], writes=[fcar_b])
                        S.op("dve", lambda e, k_=k_, qi=qi: e.tensor_scalar(out=cv[:, k_, :], in0=uext[:, k_, 2:T + 2],
                                                                            scalar1=pcol(l, "fdw", qi * 3 + 2),
                                                                            scalar2=pcol(l, "fdb", qi), op0=ALU.mult,
                                                                            op1=ALU.add),
                             reads=[ue_b[k_], pp_b], writes=[cv_b[k_]])
                        for tap in (1, 0):
                            S.op("dve", lambda e, k_=k_, qi=qi, tap=tap: e.scalar_tensor_tensor(
                                out=cv[:, k_, :], in0=uext[:, k_, tap:tap + T], scalar=pcol(l, "fdw", qi * 3 + tap),
                                in1=cv[:, k_, :], op0=ALU.mult, op1=ALU.add), reads=[ue_b[k_], pp_b, cv_b[k_]],
                                writes=[cv_b[k_]])
                        if jj < 2:
                            S.op("act", lambda e, k_=k_, jj=jj: e.activation(out=sl[:, jj, :], in_=cv[:, k_, :], func=AF.Silu),
                                 reads=[cv_b[k_]], writes=[sl_b[jj]])
                        else:
                            S.op("dve", lambda e, k_=k_, jj=jj, s=s: e.tensor_mul(out=actT[:, 2 * s + jj - 2, :],
                                                                                   in0=sl[:, jj - 2, :], in1=cv[:, k_, :]),
                                 reads=[sl_b[jj - 2], cv_b[k_]], writes=[actT_b])
                if last:
                    fc5 = fcar[:, :, :].rearrange("p (s hf jj) r -> p s hf jj r", s=11, hf=2, jj=2)
                    S.dma("sp", lambda e: [e.dma_start(
                        out=o_ffn[l, r_, hf * 2816:(hf + 1) * 2816].rearrange("(s jj p) -> p s jj", s=11, jj=2, p=128)[:, :, jj],
                        in_=fc5[:, :, hf, jj, r_], allow_slow_non_contiguous=True)
                        for hf in range(2) for r_ in range(2) for jj in range(2)],
                        "offn", reads=[fcar_b], n=8, out=True)
                S.barrier()
            with scope():
                sq, _ = lt("fsq2", [128, 2, T])
                sq_b = [Buf("fsq20"), Buf("fsq21")]
                tmp, tmp_b = lt("ftmp2", [128, 1, T])
                es_local["sq"], es_local["sq_b"] = sq, sq_b
                for oc in range(8):
                    slot = next_slab(SL_FOUT + oc)
                    Wf = W(slot, 22, 128)
                    for kc in range(22):
                        mm(ps[oc % 2][:, :], Wf[:, kc, :], actT[:, kc, :], kc == 0, kc == 21, [ring_b[slot], actT_b],
                           [ps_b[oc % 2]], kc == 21)
                    S.op("dve", lambda e, oc=oc: e.scalar_tensor_tensor(out=pre[:, oc, :], in0=xT32[:, oc, :],
                                                                        scalar=float(ALPHA), in1=ps[oc % 2][:, :],
                                                                        op0=ALU.mult, op1=ALU.add),
                         reads=[xT32_b, ps_b[oc % 2]], writes=[pre_b])
                ln_fm(l, pre, pre_b, 8, mean_d, "l2g", "l2b", LN_EPS, [(pre, pre_b)], tmp, tmp_b)
                if l < DEPTH - 1:
                    S.dma("sp", lambda e: e.dma_start(out=xs_d[:, :, i * T:(i + 1) * T], in_=pre[:]), "xsst",
                          reads=[pre_b])
                else:
                    for blk in range(4):
                        for hf in range(2):
                            fm2tm_store(lambda c, blk=blk, hf=hf: (pre[:, hf * 4 + c, blk * 128:(blk + 1) * 128], pre_b), 4,
                                        128, lambda stg, blk=blk, hf=hf: (lambda e: e.dma_start(
                                            out=y_p[i * T + blk * 128:i * T + (blk + 1) * 128, hf * 512:(hf + 1) * 512],
                                            in_=stg[:, :])), "yout")
                S.barrier()


    if NS:
        xs32 = sb("xs32", [128, 8, NS])
        xs32_b = Buf("xs32")
        sel = cf[0:NS, 1408:1408 + NS * 128]
        ones_f64 = cf[:, 1920:1984]
        id64r = cf[:, 1344:1408]
        S.dma("sp", lambda e: e.dma_start(out=xs32[:], in_=xs_fm[:, :, :]), "xsload", writes=[xs32_b], local=False)
        for l in range(DEPTH):
            S.dma("sp", lambda e, l=l: e.dma_start(out=o_conv_s[l, :, 0:CONV_K - 2, :], in_=cv_nat[l, :, 1:CONV_K - 1, :]),
                  "d2d", out=True, local=False)
            S.dma("sp", lambda e, l=l: e.dma_start(out=o_ffn_s[l, :, 0, :], in_=ff_nat[l, :, 1, :]), "d2d", out=True,
                  local=False)
            for g in range(3):
                nb = NBUF[g]
                for s_ in range(NS):
                    S.dma("sp", lambda e, l=l, g=g, s_=s_, nb=nb: e.dma_start(out=o_swa_s[g][l, s_, 0:nb - 1, :],
                                                                           in_=cache_d[g][l, s_, 1:nb, :]),
                          "d2d", out=True, local=False)

    def sample_layer(l):
        N = NS
        with scope():
            def lt(name, shape, dt=F32):
                return AR.alloc(list(shape), dt), Buf(name)
            xsb, xsb_b = lt("xsb", [128, 8, N], BF16)
            mgs, mgs_b = lt("mgs", [128, 8, N])
            shs, shs_b = lt("shs", [128, 14, N])
            zrs, zrs_b = lt("zrs", [128, 14, N])
            S.op("act", lambda e: e.copy(out=xsb[:], in_=xs32[:]), reads=[xs32_b], writes=[xsb_b])
            S.dma("sp", lambda e: e.dma_start(out=shs[:], in_=sh_fm[l, :, :, :]), "sld0", writes=[shs_b])

            def sshift(pi, och, dst, dst_b):
                S.op("act", lambda e: e.copy(out=zrs[:, och, :], in_=ps[pi][:, 0:N]), reads=[ps_b[pi]], writes=[zrs_b])
                S.op("dve", lambda e: e.tensor_scalar(out=dst, in0=ps[pi][:, 0:N], scalar1=pcol(l, "omm", och), scalar2=None,
                                                      op0=ALU.mult), reads=[ps_b[pi], pp_b], writes=[dst_b])
                S.op("dve", lambda e: e.scalar_tensor_tensor(out=dst, in0=shs[:, och, :], scalar=pcol(l, "mu", och), in1=dst,
                                                             op0=ALU.mult, op1=ALU.add), reads=[shs_b, pp_b, dst_b],
                     writes=[dst_b])

            def sproj(pi, slot, col0, wcols=512):
                proj_fm(pi, slot, col0, xsb, xsb_b, ncol=N, wcols=wcols, rhs_cols=slice(0, N))

            with scope():
                lora, lora_b = lt("slora", [128, 2, N])
                ozs, ozs_b = lt("ozs", [128, 4, N], BF16)
                Hs, Hs_b = lt("Hs", [128, N, 4, 64])
                S.dma("sp", lambda e: e.dma_start(out=Hs[:], in_=wkv_fm[l, :, :, :, :]), "sld1", writes=[Hs_b])
                slot = next_slab(SL_LORA)
                for c in range(2):
                    sproj(c, slot, c * 128, wcols=256)
                    sshift(c, 12 + c, lora[:, c, :], lora_b)
                S.op("act", lambda e: e.activation(out=lora[0:64, 0, :], in_=lora[0:64, 0, :], func=AF.Tanh),
                     reads=[lora_b], writes=[lora_b])
                S.op("act", lambda e: e.activation(out=lora[:, 1, :], in_=lora[:, 1, :], func=AF.Sigmoid),
                     reads=[lora_b], writes=[lora_b])
                for p in range(4):
                    with scope():
                        rT, r_b = lt("srT", [128, N])
                        kT, k_b = lt("skT", [128, N])
                        vT, v_b = lt("svT", [128, N])
                        wT, w_b = lt("swT", [128, N])
                        aT, a_b = lt("saT", [128, N])
                        gT, g_b = lt("sgT", [128, N])
                        kkT, kk_b = lt("skkT", [128, N])
                        bT, b_b = lt("sbT", [128, N])
                        eT, e_b = lt("seT", [128, N])
                        bon, bon_b = lt("sbon", [128, N])
                        OT, OT_b = lt("sOT", [128, N])
                        t1, t1_b = lt("st1", [128, 64])
                        t2, t2_b = lt("st2", [128, 64])
                        slot = next_slab(SL_RP + p)
                        for c, (dst, db, och) in enumerate(((rT, r_b, p), (kT, k_b, 4 + p), (vT, v_b, 8 + p))):
                            sproj(c % 2, slot, c * 128, wcols=384)
                            sshift(c % 2, och, dst[:, :], db)
                        cs128 = slice(p * 128, (p + 1) * 128)
                        mm(ps[2][:, 0:N], lw[0:64, l, cs128], lora[0:64, 0, :], True, True, [lw_b, lora_b], [ps_b[2]], True)
                        mm(ps[3][:, 0:N], lw[64:128, l, cs128], lora[64:128, 0, :], True, True, [lw_b, lora_b], [ps_b[3]], True)
                        mm(ps[4][:, 0:N], lw[:, l, 512 + p * 128:512 + (p + 1) * 128], lora[:, 1, :], True, True,
                           [lw_b, lora_b], [ps_b[4]], True)
                        S.op("act", lambda e: e.activation(out=eT[:, :], in_=ps[2][:, 0:N], func=AF.Exp, scale=-1.0,
                                                           bias=pcol(l, "nw0", p)), reads=[ps_b[2], pp_b], writes=[e_b])
                        S.op("act", lambda e: e.activation(out=eT[:, :], in_=eT[:, :], func=AF.Ln, bias=1.0), reads=[e_b],
                             writes=[e_b])
                        S.op("act", lambda e: e.activation(out=eT[:, :], in_=eT[:, :], func=AF.Exp, scale=-1.0, bias=-0.5),
                             reads=[e_b], writes=[e_b])
                        S.op("act", lambda e: e.activation(out=wT[:, :], in_=eT[:, :], func=AF.Exp, scale=-1.0),
                             reads=[e_b], writes=[w_b])
                        S.op("act", lambda e: e.activation(out=aT[:, :], in_=ps[3][:, 0:N], func=AF.Sigmoid,
                                                           bias=pcol(l, "a0", p)), reads=[ps_b[3], pp_b], writes=[a_b])
                        S.op("act", lambda e: e.copy(out=gT[:, :], in_=ps[4][:, 0:N]), reads=[ps_b[4]], writes=[g_b])
                        S.op("dve", lambda e: e.tensor_scalar(out=kkT[:, :], in0=kT[:, :], scalar1=pcol(l, "kk", p),
                                                              scalar2=None, op0=ALU.mult), reads=[k_b, pp_b], writes=[kk_b])
                        S.op("dve", lambda e: e.tensor_mul(out=eT[:, :], in0=kkT[:, :], in1=kkT[:, :]), reads=[kk_b],
                             writes=[e_b])
                        mm(ps[2][:, 0:N], ones_blk, eT[:, :], True, True, [cf_b, e_b], [ps_b[2]], True)
                        rsqrt(eT[:, :], ps[2][:, 0:N], 1e-24, [ps_b[2]], e_b)
                        S.op("dve", lambda e: e.tensor_mul(out=kkT[:, :], in0=kkT[:, :], in1=eT[:, :]), reads=[kk_b, e_b],
                             writes=[kk_b])
                        S.op("dve", lambda e: e.tensor_mul(out=bT[:, :], in0=kkT[:, :], in1=aT[:, :]), reads=[kk_b, a_b],
                             writes=[b_b])
                        S.op("dve", lambda e: e.tensor_scalar(out=aT[:, :], in0=aT[:, :], scalar1=-1.0,
                                                              scalar2=pcol(l, "ka", p), op0=ALU.add, op1=ALU.mult),
                             reads=[a_b, pp_b], writes=[a_b])
                        S.op("dve", lambda e: e.scalar_tensor_tensor(out=kT[:, :], in0=aT[:, :], scalar=1.0, in1=kT[:, :],
                                                                     op0=ALU.add, op1=ALU.mult), reads=[a_b, k_b],
                             writes=[k_b])
                        S.op("dve", lambda e: e.scalar_tensor_tensor(out=eT[:, :], in0=rT[:, :], scalar=pcol(l, "rk", p),
                                                                     in1=kT[:, :], op0=ALU.mult, op1=ALU.mult),
                             reads=[r_b, k_b, pp_b], writes=[e_b])
                        mm(ps[3][:, 0:N], ones_blk, eT[:, :], True, True, [cf_b, e_b], [ps_b[3]], True)
                        S.op("dve", lambda e: e.tensor_mul(out=bon[:, :], in0=ps[3][:, 0:N], in1=vT[:, :]),
                             reads=[ps_b[3], v_b], writes=[bon_b])
                        S.op("dve", lambda e: e.tensor_scalar(out=bT[:, :], in0=bT[:, :], scalar1=-1.0, scalar2=None,
                                                              op0=ALU.mult), reads=[b_b], writes=[b_b])
                        for s_ in range(N):
                            H0 = Hs[:, s_, p, :]
                            sc = slice(s_, s_ + 1)
                            S.op("dve", lambda e, H0=H0, sc=sc: e.tensor_scalar(out=t1[:, :], in0=H0, scalar1=kkT[:, sc],
                                                                               scalar2=None, op0=ALU.mult),
                                 reads=[Hs_b, kk_b], writes=[t1_b])
                            S.op("dve", lambda e, sc=sc: e.tensor_scalar(out=t2[:, :], in0=id64r, scalar1=vT[:, sc],
                                                                        scalar2=None, op0=ALU.mult),
                                 reads=[cf_b, v_b], writes=[t2_b])
                            mm(ps[0][:, 0:64], ones_blk, t1[:, :], True, True, [cf_b, t1_b], [ps_b[0]], True)
                            mm(ps[1][:, 0:64], ones_blk, t2[:, :], True, True, [cf_b, t2_b], [ps_b[1]], True)
                            S.op("dve", lambda e, H0=H0, sc=sc: e.tensor_scalar(out=H0, in0=H0, scalar1=wT[:, sc], scalar2=None,
                                                                               op0=ALU.mult), reads=[Hs_b, w_b],
                                 writes=[Hs_b])
                            S.op("dve", lambda e, H0=H0, sc=sc: e.scalar_tensor_tensor(out=H0, in0=ps[0][:, 0:64],
                                                                                      scalar=bT[:, sc], in1=H0,
                                                                                      op0=ALU.mult, op1=ALU.add),
                                 reads=[ps_b[0], b_b, Hs_b], writes=[Hs_b])
                            S.op("dve", lambda e, H0=H0, sc=sc: e.scalar_tensor_tensor(out=H0, in0=ps[1][:, 0:64],
                                                                                      scalar=kT[:, sc], in1=H0,
                                                                                      op0=ALU.mult, op1=ALU.add),
                                 reads=[ps_b[1], k_b, Hs_b], writes=[Hs_b])
                            for hl in range(2):
                                hp = slice(hl * 64, (hl + 1) * 64)
                                mm(ps[4 + hl][hp, s_:s_ + 1], Hs[hp, s_, p, :], rT[hp, sc], True, True, [Hs_b, r_b],
                                   [ps_b[4 + hl]], True)
                        for hl in range(2):
                            hp = slice(hl * 64, (hl + 1) * 64)
                            S.op("act", lambda e, hp=hp, hl=hl: e.copy(out=OT[hp, :], in_=ps[4 + hl][hp, 0:N]),
                                 reads=[ps_b[4 + hl]], writes=[OT_b])
                        mm(ps[0][:, 0:N], mean_blk, OT[:, :], True, True, [cf_b, OT_b], [ps_b[0]], True)
                        S.op("dve", lambda e: e.tensor_sub(out=OT[:, :], in0=OT[:, :], in1=ps[0][:, 0:N]),
                             reads=[OT_b, ps_b[0]], writes=[OT_b])
                        S.op("act", lambda e: e.activation(out=eT[:, :], in_=OT[:, :], func=AF.Square), reads=[OT_b],
                             writes=[e_b])
                        mm(ps[1][:, 0:N], mean_blk, eT[:, :], True, True, [cf_b, e_b], [ps_b[1]], True)
                        rsqrt(eT[:, :], ps[1][:, 0:N], float(LNX_EPS), [ps_b[1]], e_b)
                        S.op("dve", lambda e: e.tensor_mul(out=OT[:, :], in0=OT[:, :], in1=eT[:, :]), reads=[OT_b, e_b],
                             writes=[OT_b])
                        S.op("dve", lambda e: e.tensor_scalar(out=OT[:, :], in0=OT[:, :], scalar1=pcol(l, "lng", p),
                                                              scalar2=pcol(l, "lnb", p), op0=ALU.mult, op1=ALU.add),
                             reads=[OT_b, pp_b], writes=[OT_b])
                        S.op("dve", lambda e: e.tensor_add(out=OT[:, :], in0=OT[:, :], in1=bon[:, :]), reads=[OT_b, bon_b],
                             writes=[OT_b])
                        S.op("dve", lambda e: e.tensor_mul(out=ozs[:, p, :], in0=OT[:, :], in1=gT[:, :]), reads=[OT_b, g_b],
                             writes=[ozs_b])
                        S.barrier()
                s_out = next_slab(SL_ROUT)
                g0 = next_slab(SL_G0)
                g1 = next_slab(SL_G0 + 1)
                merge_branch(l, 0, s_out, 4, ozs, ozs_b, (g0, g1), N=N, xb=xsb, xb_b=xsb_b, mg=mgs, mg_b=mgs_b)
                S.dma("sp", lambda e: [e.dma_start(out=o_shift_s[l, s_, :].rearrange("(c p) -> p c", p=128), in_=zrs[:, :, s_],
                                                   allow_slow_non_contiguous=True) for s_ in range(N)], "sosh",
                      reads=[zrs_b], n=N, out=True)
                for s_ in range(N):
                    with scope_alloc([64, 8, 64], F32) as stg:
                        stg_b = Buf("swkvst")
                        for h in range(8):
                            pr_, hl = h // 2, h % 2
                            S.op("pe", lambda e, pr_=pr_, hl=hl, s_=s_: e.matmul(
                                ps[4 + hl][0:64, pr_ * 64:(pr_ + 1) * 64], lhsT=Hs[hl * 64:(hl + 1) * 64, s_, pr_, :],
                                rhs=cf[hl * 64:(hl + 1) * 64, 1344:1408], start=True, stop=True), reads=[Hs_b, cf_b],
                                writes=[ps_b[4 + hl]])
                        stg4 = stg[:].rearrange("p (pr hl) k -> p pr hl k", hl=2)
                        for hl in range(2):
                            S.op("act", lambda e, hl=hl: e.copy(out=stg4[:, :, hl, :],
                                                                in_=ps[4 + hl][0:64, 0:256].rearrange("p (pr k) -> p pr k", k=64)),
                                 reads=[ps_b[4 + hl]], writes=[stg_b])
                        S.dma("sp", lambda e, s_=s_: e.dma_start(out=o_wkv_s[l, s_].rearrange("h v k -> v h k"), in_=stg[:]),
                              "sowkv", reads=[stg_b], out=True)
                        S.barrier()
                S.barrier()

            with scope():
                stc, stc_b = lt("stc", [128, 4, N, 30])
                uS, uS_b = lt("uS", [128, 4, N])
                acc, acc_b = lt("sacc", [128, 4, N])
                prod, prod_b = lt("sprod", [128, N, 30])
                sg, sg_b = lt("ssg", [128, N])
                cTb, cT_b = lt("scTb", [128, 4, N], BF16)
                sq, _ = lt("ssq", [128, 2, N])
                sq_b = [Buf("ssq0"), Buf("ssq1")]
                tmp, tmp_b = lt("stmp", [128, 1, N])
                es_local["sq"], es_local["sq_b"] = sq, sq_b
                S.dma("sp", lambda e: e.dma_start(out=stc[:], in_=cv_fm[l, :, :, :, :]), "sld2", writes=[stc_b])
                su = next_slab(SL_CU)
                sgt = next_slab(SL_CG)
                for c in range(4):
                    sproj(0, su, c * 128)
                    sproj(1, sgt, c * 128)
                    S.op("act", lambda e: e.activation(out=sg[:, :], in_=ps[1][:, 0:N], func=AF.Sigmoid), reads=[ps_b[1]],
                         writes=[sg_b])
                    S.op("dve", lambda e, c=c: e.tensor_mul(out=uS[:, c, :], in0=ps[0][:, 0:N], in1=sg[:, :]),
                         reads=[ps_b[0], sg_b], writes=[uS_b])
                    wv = pp[:, l, PC["cdw"] + c * 31:PC["cdw"] + c * 31 + 30]
                    S.op("dve", lambda e, c=c, wv=wv: e.tensor_mul(out=prod[:], in0=stc[:, c, :, :],
                                                                   in1=wv.unsqueeze(1).broadcast_to([128, N, 30])),
                         reads=[stc_b, pp_b], writes=[prod_b])
                    S.op("dve", lambda e, c=c: e.tensor_reduce(out=acc[:, c, :], in_=prod[:], axis=AX.X, op=ALU.add),
                         reads=[prod_b], writes=[acc_b])
                    S.op("dve", lambda e, c=c: e.scalar_tensor_tensor(out=acc[:, c, :], in0=uS[:, c, :],
                                                                      scalar=pcol(l, "cdw", c * 31 + 30), in1=acc[:, c, :],
                                                                      op0=ALU.mult, op1=ALU.add),
                         reads=[uS_b, pp_b, acc_b], writes=[acc_b])
                    S.op("dve", lambda e, c=c: e.tensor_scalar(out=acc[:, c, :], in0=acc[:, c, :], scalar1=pcol(l, "cdb", c),
                                                               scalar2=None, op0=ALU.add), reads=[acc_b, pp_b],
                         writes=[acc_b])
                ln_fm(l, acc, acc_b, 4, mean_c, "clg", "clb", LN_EPS, [(acc, acc_b)], tmp, tmp_b, N=N)
                for c in range(4):
                    S.op("act", lambda e, c=c: e.activation(out=cTb[:, c, :], in_=acc[:, c, :], func=AF.Silu),
                         reads=[acc_b], writes=[cT_b])
                s_out = next_slab(SL_COUT)
                g0 = next_slab(SL_G1)
                g1 = next_slab(SL_G1 + 1)
                merge_branch(l, 1, s_out, 4, cTb, cT_b, (g0, g1), N=N, xb=xsb, xb_b=xsb_b, mg=mgs, mg_b=mgs_b)
                fm2tm_store(lambda c: (uS[:, c, :], uS_b), 4, N,
                            lambda stg: (lambda e: e.dma_start(out=o_conv_s[l, :, CONV_K - 2, :], in_=stg[0:N, 0:512])),
                            "soconv")
                S.barrier()

            with scope():
                qT, q_b = lt("sqT", [128, 2, N])
                kT, k_b = lt("skT2", [128, 2, N])
                vT, v_b = lt("svT2", [128, 2, N])
                Qtm, Qtm_b = lt("sQtm", [N, 256])
                Kt, _ = lt("sKt", [128, 2, 256])
                Kt_b = [Buf("sKt0"), Buf("sKt1")]
                Vt, _ = lt("sVt", [128, 2, 256])
                Vt_b = [Buf("sVt0"), Buf("sVt1")]
                prod, prod_b = lt("saprod", [128, 256])
                sc4, sc4_b = lt("ssc4", [128, 4])
                pself, pself_b = lt("spself", [128, 2, N])
                numS, numS_b = lt("snumS", [128, 2, N])
                denS, denS_b = lt("sdenS", [128, 2, N])
                oTs, oTs_b = lt("soTs", [128, 2, N], BF16)
                S.op("dve", lambda e: e.memset(numS[:], 0.0), writes=[numS_b])
                S.op("dve", lambda e: e.memset(denS[:], 0.0), writes=[denS_b])
                started = set()
                kvn = 0
                for g in range(3):
                    win, dil = SWA_GROUPS[g]
                    nb = NBUF[g]
                    sA = next_slab(SL_ATT + 2 * g)
                    sB = next_slab(SL_ATT + 2 * g + 1)
                    for c in range(4):
                        sproj(c % 2, sA, c * 128)
                        dst, db = (qT, q_b) if c < 2 else (kT, k_b)
                        S.op("act", lambda e, c=c, dst=dst: e.copy(out=dst[:, c % 2, :], in_=ps[c % 2][:, 0:N]),
                             reads=[ps_b[c % 2]], writes=[db])
                    for c in range(2):
                        sproj(2 + c, sB, 256 + c * 128)
                        S.op("act", lambda e, c=c: e.copy(out=vT[:, c, :], in_=ps[2 + c][:, 0:N]), reads=[ps_b[2 + c]],
                             writes=[v_b])
                    fm2tm_store(lambda c: ((kT[:, c, :], k_b) if c < 2 else (vT[:, c - 2, :], v_b)), 4, N,
                                lambda stg, g=g, nb=nb: (lambda e: e.dma_start(out=o_swa_s[g][l, :, nb - 1, :],
                                                                                in_=stg[0:N, 0:512])), "soswa")
                    S.op("dve", lambda e: e.tensor_mul(out=pself[:], in0=qT[:], in1=kT[:]), reads=[q_b, k_b],
                         writes=[pself_b])
                    mm(ps[0][:, 0:2 * N], ones_blk, pself[:].rearrange("p a n -> p (a n)"), True, True, [cf_b, pself_b],
                       [ps_b[0]], True)
                    S.op("act", lambda e: e.activation(out=pself[:].rearrange("p a n -> p (a n)"), in_=ps[0][:, 0:2 * N],
                                                       func=AF.Exp, scale=0.125), reads=[ps_b[0]], writes=[pself_b])
                    S.op("dve", lambda e: e.tensor_add(out=denS[:], in0=denS[:], in1=pself[:]), reads=[denS_b, pself_b],
                         writes=[denS_b])
                    S.op("dve", lambda e: e.tensor_mul(out=pself[:], in0=pself[:], in1=vT[:]), reads=[pself_b, v_b],
                         writes=[pself_b])
                    S.op("dve", lambda e: e.tensor_add(out=numS[:], in0=numS[:], in1=pself[:]), reads=[numS_b, pself_b],
                         writes=[numS_b])
                    for c in range(2):
                        S.op("pe", lambda e, c=c: e.matmul(ps[1][0:N, c * 128:(c + 1) * 128], lhsT=qT[:, c, :], rhs=ident,
                                                           start=True, stop=True), reads=[q_b, cf_b], writes=[ps_b[1]])
                    S.op("act", lambda e: e.copy(out=Qtm[:, :], in_=ps[1][0:N, 0:256]), reads=[ps_b[1]], writes=[Qtm_b])
                    for s_ in range(N):
                        k_ = kvn % 2
                        kvn += 1
                        S.dma("sp", lambda e, k_=k_, s_=s_, g=g, dil=dil, nb=nb: e.dma_start(
                            out=Kt[:, k_, :], in_=cache_d[g][l, s_, slice(0, nb, dil), 0:256]), f"sk{k_}",
                            writes=[Kt_b[k_]])
                        S.dma("sp", lambda e, k_=k_, s_=s_, g=g, dil=dil, nb=nb: e.dma_start(
                            out=Vt[:, k_, :], in_=cache_d[g][l, s_, slice(0, nb, dil), 256:512]), f"sv{k_}",
                            writes=[Vt_b[k_]])
                        mm(ps[2][:, 0:256], sel[:, s_ * 128:(s_ + 1) * 128], Qtm[:, :], True, True, [cf_b, Qtm_b],
                           [ps_b[2]], True)
                        S.op("dve", lambda e, k_=k_: e.tensor_mul(out=prod[:, :], in0=Kt[:, k_, :], in1=ps[2][:, 0:256]),
                             reads=[Kt_b[k_], ps_b[2]], writes=[prod_b])
                        S.op("dve", lambda e: e.tensor_reduce(out=sc4[:, :], in_=prod[:, :].rearrange("p (h e) -> p h e", e=64),
                                                              axis=AX.X, op=ALU.add), reads=[prod_b], writes=[sc4_b])
                        S.op("act", lambda e: e.activation(out=sc4[:, :], in_=sc4[:, :], func=AF.Exp, scale=0.125),
                             reads=[sc4_b], writes=[sc4_b])
                        for h in range(4):
                            pair, hl = h // 2, h % 2
                            hp = slice(hl * 64, (hl + 1) * 64)
                            col = pair * N + s_
                            first = (hl,) not in started
                            started.add((hl,))
                            mm(ps[4][hp, col:col + 1], Vt[:, k_, h * 64:(h + 1) * 64], sc4[:, h:h + 1], first, False,
                               [Vt_b[k_], sc4_b], [ps_b[4]], False)
                            mm(ps[5][hp, col:col + 1], ones_f64, sc4[:, h:h + 1], first, False, [cf_b, sc4_b], [ps_b[5]],
                               True)
                S.op("dve", lambda e: e.tensor_add(out=denS[:].rearrange("p a n -> p (a n)"),
                                                   in0=denS[:].rearrange("p a n -> p (a n)"), in1=ps[5][:, 0:2 * N]),
                     reads=[denS_b, ps_b[5]], writes=[denS_b])
                S.op("dve", lambda e: e.tensor_add(out=numS[:].rearrange("p a n -> p (a n)"),
                                                   in0=numS[:].rearrange("p a n -> p (a n)"), in1=ps[4][:, 0:2 * N]),
                     reads=[numS_b, ps_b[4]], writes=[numS_b])
                S.op("dve", lambda e: e.reciprocal(out=denS[:], in_=denS[:]), reads=[denS_b], writes=[denS_b])
                S.op("dve", lambda e: e.tensor_mul(out=oTs[:], in0=numS[:], in1=denS[:]), reads=[numS_b, denS_b],
                     writes=[oTs_b])
                s_out = next_slab(SL_AOUT)
                g0 = next_slab(SL_G2)
                g1 = next_slab(SL_G2 + 1)
                merge_branch(l, 2, s_out, 2, oTs, oTs_b, (g0, g1), N=N, xb=xsb, xb_b=xsb_b, mg=mgs, mg_b=mgs_b)
                S.barrier()

            with scope():
                hb, hb_b = lt("shb", [128, 8, N], BF16)
                mbf, mbf_b = lt("smbf", [128, 8, N], BF16)
                sq, _ = lt("sfsq", [128, 2, N])
                sq_b = [Buf("sfsq0"), Buf("sfsq1")]
                tmp, tmp_b = lt("sftmp", [128, 1, N])
                stf, stf_b = lt("stf", [128, 44, N, 2])
                urw, urw_b = lt("surw", [128, 44, N])
                cv, _ = lt("sfcv", [128, 2, N])
                cv_b = [Buf("sfcv0"), Buf("sfcv1")]
                sl, _ = lt("sfsl", [128, 2, N])
                sl_b = [Buf("sfsl0"), Buf("sfsl1")]
                actT, actT_b = lt("sactT", [128, 22, N], BF16)
                es_local["sq"], es_local["sq_b"] = sq, sq_b
                S.dma("sp", lambda e: e.dma_start(out=stf[:], in_=ff_fm[l, :, :, :, :]), "sld3", writes=[stf_b])
                S.op("act", lambda e: e.copy(out=mbf[:], in_=mgs[:]), reads=[mgs_b], writes=[mbf_b])
                wo = [next_slab(SL_WO), next_slab(SL_WO + 1)]
                for oc in range(8):
                    proj_fm(oc % 2, wo[oc // 4], (oc % 4) * 128, mbf, mbf_b, ncol=N, rhs_cols=slice(0, N))
                    S.op("dve", lambda e, oc=oc: e.scalar_tensor_tensor(out=mgs[:, oc, :], in0=xs32[:, oc, :],
                                                                        scalar=float(ALPHA), in1=ps[oc % 2][:, 0:N],
                                                                        op0=ALU.mult, op1=ALU.add),
                         reads=[xs32_b, ps_b[oc % 2], mgs_b], writes=[mgs_b])
                ln_fm(l, mgs, mgs_b, 8, mean_d, "l1g", "l1b", LN_EPS, [(xs32, xs32_b), (hb, hb_b)], tmp, tmp_b, N=N)
                for s in range(11):
                    slot = next_slab(SL_FIN + s)
                    for jj in range(4):
                        qi = s * 4 + jj
                        k_ = qi % 2
                        proj_fm(k_, slot, jj * 128, hb, hb_b, ncol=N, rhs_cols=slice(0, N))
                        S.op("act", lambda e, k_=k_, qi=qi: e.copy(out=urw[:, qi, :], in_=ps[k_][:, 0:N]), reads=[ps_b[k_]],
                             writes=[urw_b])
                        S.op("dve", lambda e, k_=k_, qi=qi: e.tensor_scalar(out=cv[:, k_, :], in0=ps[k_][:, 0:N],
                                                                            scalar1=pcol(l, "fdw", qi * 3 + 2),
                                                                            scalar2=pcol(l, "fdb", qi), op0=ALU.mult,
                                                                            op1=ALU.add),
                             reads=[ps_b[k_], pp_b], writes=[cv_b[k_]])
                        for tap in (1, 0):
                            S.op("dve", lambda e, k_=k_, qi=qi, tap=tap: e.scalar_tensor_tensor(
                                out=cv[:, k_, :], in0=stf[:, qi, :, tap], scalar=pcol(l, "fdw", qi * 3 + tap),
                                in1=cv[:, k_, :], op0=ALU.mult, op1=ALU.add), reads=[stf_b, pp_b, cv_b[k_]],
                                writes=[cv_b[k_]])
                        if jj < 2:
                            S.op("act", lambda e, k_=k_, jj=jj: e.activation(out=sl[:, jj, :], in_=cv[:, k_, :], func=AF.Silu),
                                 reads=[cv_b[k_]], writes=[sl_b[jj]])
                        else:
                            S.op("dve", lambda e, k_=k_, jj=jj, s=s: e.tensor_mul(out=actT[:, 2 * s + jj - 2, :],
                                                                                   in0=sl[:, jj - 2, :], in1=cv[:, k_, :]),
                                 reads=[sl_b[jj - 2], cv_b[k_]], writes=[actT_b])
                ur5 = urw[:, :, :].rearrange("p (s hf jj) n -> p s hf jj n", s=11, hf=2, jj=2)
                S.dma("sp", lambda e: [e.dma_start(
                    out=o_ffn_s[l, s_, 1, hf * 2816:(hf + 1) * 2816].rearrange("(s jj p) -> p s jj", s=11, jj=2, p=128)[:, :, jj],
                    in_=ur5[:, :, hf, jj, s_], allow_slow_non_contiguous=True)
                    for s_ in range(N) for hf in range(2) for jj in range(2)], "soffn", reads=[urw_b], n=4 * N, out=True)
                for oc in range(8):
                    slot = next_slab(SL_FOUT + oc)
                    Wf = W(slot, 22, 128)
                    for kc in range(22):
                        mm(ps[oc % 2][:, 0:N], Wf[:, kc, :], actT[:, kc, :], kc == 0, kc == 21, [ring_b[slot], actT_b],
                           [ps_b[oc % 2]], kc == 21)
                    S.op("dve", lambda e, oc=oc: e.scalar_tensor_tensor(out=mgs[:, oc, :], in0=xs32[:, oc, :],
                                                                        scalar=float(ALPHA), in1=ps[oc % 2][:, 0:N],
                                                                        op0=ALU.mult, op1=ALU.add),
                         reads=[xs32_b, ps_b[oc % 2], mgs_b], writes=[mgs_b])
                ln_fm(l, mgs, mgs_b, 8, mean_d, "l2g", "l2b", LN_EPS, [(xs32, xs32_b)], tmp, tmp_b, N=N)
                if l == DEPTH - 1:
                    for hf in range(2):
                        fm2tm_store(lambda c, hf=hf: (xs32[:, hf * 4 + c, :], xs32_b), 4, N,
                                    lambda stg, hf=hf: (lambda e: e.dma_start(out=y_s[:, hf * 512:(hf + 1) * 512],
                                                                              in_=stg[0:N, 0:512])), "sy")
                S.barrier()
            S.barrier()

    for l in range(DEPTH):
        S.op("dve", lambda e: e.memset(Hst[:], 0.0), writes=[Hst_b])
        S.op("dve", lambda e: e.memset(zcar[:], 0.0), writes=[zcar_b])
        S.op("dve", lambda e: e.memset(uhist[:], 0.0), writes=[uhist_b])
        S.op("dve", lambda e: e.memset(fcar[:], 0.0), writes=[fcar_b])
        if l > 0 and not STOP:
            S.wait_only("sp", [(S.dkeys["xsst"][0], S.dkeys["xsst"][1], "dma")])
        cur_layer["l"] = l
        if NS:
            sample_layer(l)
        for i in range(NT):
            if STOP and (l, i) != (0, 0):
                continue
            try:
                tile_prog(l, i)
            except _Stop:
                pass

    S.wait_only("sp", list(S.out_toks))
    with nc.Block() as block:
        S.emit(block)
    es.close()
    return nc


_NC_CACHE = {}


def kernel(**inputs):
    inp = {k: np.asarray(v) for k, v in inputs.items()}
    BATCH, SEQ, _ = inp["x_prompt"].shape
    NSAMP_ALL = inp["x_sample"].shape[0]
    NCORE = 8
    NS = NSAMP_ALL // NCORE
    key = (SEQ, NS)
    if key not in _NC_CACHE:
        _NC_CACHE[key] = build(SEQ, NSAMP=NS)
    nc = _NC_CACHE[key]
    wp, pp, lw = pack_weights(inp)
    cf, cb = make_consts()
    in_maps = []
    for c in range(NCORE):
        b = c % BATCH
        x = inp["x_prompt"][b]
        x_fm = np.ascontiguousarray(x.T.reshape(8, 128, SEQ).transpose(1, 0, 2))
        im = {"x_fm": x_fm, "wpack": wp, "ppack": pp, "lorapack": lw, "constf": cf, "constb": cb}
        im.update(pack_samples(inp, c * NS, NS))
        in_maps.append(im)
    res = run_bass_kernel_spmd(nc, in_maps, core_ids=list(range(NCORE)))
    R_ = res.results
    f32 = np.float32

    def pstack(name, shape_tail):
        return np.stack([np.asarray(R_[b][name], f32).reshape((DEPTH,) + shape_tail) for b in range(BATCH)], axis=1)

    def sstack(name, shape_tail):
        return np.concatenate([np.asarray(R_[c][name], f32).reshape((DEPTH, NS) + shape_tail) for c in range(NCORE)],
                              axis=1)
    y_prompt = np.stack([np.asarray(R_[b]["y_p"], f32) for b in range(BATCH)], axis=0)
    y_sample = np.concatenate([np.asarray(R_[c]["y_s"], f32) for c in range(NCORE)], axis=0)[:, None, :]
    keep = [min(w, SEQ) for w, _ in SWA_GROUPS]
    outs = [y_prompt, y_sample,
            pstack("o_shift_p", (1, R_COLS)), sstack("o_shift_s", (1, R_COLS)),
            pstack("o_wkv_p", (R_HEADS, 64, 64)), sstack("o_wkv_s", (R_HEADS, 64, 64)),
            pstack("o_conv_p", (CONV_K - 1, CONV_CH)), sstack("o_conv_s", (CONV_K - 1, CONV_CH))]
    for g in range(3):
        outs.append(pstack(f"o_swa{g}_p", (keep[g], 2, G_HEADS, HEAD_DIM)))
        outs.append(sstack(f"o_swa{g}_s", (NBUF[g], 2, G_HEADS, HEAD_DIM)))
    outs.append(pstack("o_ffn_p", (2, 2 * D_FF)))
    outs.append(sstack("o_ffn_s", (2, 2 * D_FF)))
    return tuple(np.ascontiguousarray(o, dtype=f32) for o in outs)
```

```python
import numpy as np
from contextlib import ExitStack, contextmanager
import concourse.bass as bass
import concourse.mybir as mybir
from concourse.bass_utils import run_bass_kernel_spmd

F32 = mybir.dt.float32
BF16 = mybir.dt.bfloat16
AF = mybir.ActivationFunctionType
ALU = mybir.AluOpType
AX = mybir.AxisListType

D_MODEL = 1024
DEPTH = 2
HEAD_DIM = 64
R_HEADS = 8
R_WIDTH = 512
R_COLS = 1792
LNX_EPS = 64e-5
CONV_CH = 512
CONV_K = 31
SWA_GROUPS = ((128, 1), (512, 4), (2048, 16))
G_HEADS = 4
A_WIDTH = 768
D_FF = 2816
OFF_R = 0
OFF_C = 1792
OFF_Q = OFF_C + 1024
OFF_K = OFF_Q + 768
OFF_V = OFF_K + 768
OFF_GATE = OFF_V + 768
IN_COLS = 8192
ALPHA = (2 * DEPTH) ** 0.25
LN_EPS = 1e-5
NEG = -30000.0
NBUF = (128, 512, 2048)

P = 128
T = 512
C = 64
NCH = T // C
NSLOT = 5
SLAB = 4096
NSLAB = 42
SL_LORA = 0
SL_RP = 1
SL_ROUT = 5
SL_G0 = 6
SL_CU = 8
SL_CG = 9
SL_COUT = 10
SL_G1 = 11
SL_ATT = 13
SL_AOUT = 19
SL_G2 = 20
SL_WO = 22
SL_FIN = 24
SL_FOUT = 35
NSLAB = 43

PC = {}
_pc = 0
for _n, _w in (("mu", 14), ("omm", 14), ("w0", 4), ("nw0", 4), ("a0", 4), ("kk", 4), ("ka", 4), ("rk", 4),
               ("lng", 4), ("lnb", 4), ("cdw", 4 * 31), ("cdb", 4), ("clg", 4), ("clb", 4), ("gb", 24),
               ("l1g", 8), ("l1b", 8), ("fdw", 44 * 3), ("fdb", 44), ("l2g", 8), ("l2b", 8)):
    PC[_n] = _pc
    _pc += _w
NPC = _pc


class Buf:
    __slots__ = ("name", "w", "r")

    def __init__(self, name):
        self.name = name
        self.w = None
        self.r = []


class _Rec:
    def __init__(self):
        self.calls = []

    def __getattr__(self, name):
        def f(*a, **k):
            self.calls.append((name, a, k))
            return self
        return f


def _freeze(fn):
    rec = _Rec()
    fn(rec)
    calls = rec.calls

    def replay(e):
        out = [getattr(e, name)(*a, **k) for (name, a, k) in calls]
        return out
    return replay, len(calls)


class Sched:
    ENGS = ("pe", "act", "dve", "pool", "sp")
    LIMIT = 60000

    def __init__(self, nc, es):
        self.nc = nc
        self.es = es
        self.sems = []
        self.ops = {e: [] for e in self.ENGS}
        self.cnt = {e: 0 for e in self.ENGS}
        self.semi = {e: self._newsem(e) for e in self.ENGS}
        self.waited = {e: {} for e in self.ENGS}
        self.dkeys = {}
        self.pending = []
        self.out_toks = []
        self.lastsig = {}

    def _newsem(self, name):
        s = self.es.enter_context(self.nc.semaphore(f"s{len(self.sems)}_{name}"))
        self.sems.append(s)
        return len(self.sems) - 1

    def _force_sig(self, te):
        ops = self.ops[te]
        k = len(ops) - 1
        while ops[k][0] is None:
            k -= 1
        assert ops[k][2] is None
        self.cnt[te] += 1
        ops[k][2] = (self.semi[te], 1)
        self.lastsig[te] = True

    def _resolve(self, eng, tok):
        si, val, te = tok
        if te != "dma" and te != eng and si == self.semi[te] and val > self.cnt[te]:
            assert val == self.cnt[te] + 1
            self._force_sig(te)

    def _waits(self, eng, reads, writes, is_dma):
        toks = set()
        for b in reads:
            if b.w is not None:
                toks.add(b.w)
        for b in writes:
            if b.w is not None:
                toks.add(b.w)
            toks.update(b.r)
        out = []
        for (si, val, te) in toks:
            if te == eng and not is_dma and eng == "pe":
                continue
            if te == eng and is_dma:
                if te != "dma" and si == self.semi[te] and val > self.cnt[te]:
                    self._force_sig(te)
            self._resolve(eng, (si, val, te))
            if self.waited[eng].get(si, 0) >= val:
                continue
            out.append((si, val))
        best = {}
        for si, val in out:
            best[si] = max(best.get(si, 0), val)
        for si, val in best.items():
            self.waited[eng][si] = val
        return list(best.items())

    def op(self, eng, fn, reads=(), writes=(), sig=True):
        fn, ncalls = _freeze(fn)
        assert ncalls == 1
        waits = self._waits(eng, reads, writes, False)
        if self.cnt[eng] >= self.LIMIT:
            self.semi[eng] = self._newsem(eng)
            self.cnt[eng] = 0
        if sig:
            self.cnt[eng] += 1
            tok = (self.semi[eng], self.cnt[eng], eng)
            inc = (self.semi[eng], 1)
        else:
            tok = (self.semi[eng], self.cnt[eng] + 1, eng)
            inc = None
        self.ops[eng].append([fn, waits, inc, 1])
        self.lastsig[eng] = sig
        for b in reads:
            b.r.append(tok)
        for b in writes:
            b.w = tok
            b.r = []
        return tok

    def dma(self, eng, fn, key, reads=(), writes=(), n=1, local=True, out=False):
        fn, ncalls = _freeze(fn)
        assert ncalls == n, (ncalls, n)
        waits = self._waits(eng, reads, writes, True)
        if key not in self.dkeys or self.dkeys[key][1] + 16 * n > self.LIMIT:
            self.dkeys[key] = [self._newsem("d" + key), 0]
        d = self.dkeys[key]
        d[1] += 16 * n
        tok = (d[0], d[1], "dma")
        self.ops[eng].append([fn, waits, (d[0], 16), n])
        for b in reads:
            b.r.append(tok)
        for b in writes:
            b.w = tok
            b.r = []
        if local:
            self.pending.append(tok)
        if out:
            self.out_toks.append(tok)
        return tok

    def wait_only(self, eng, toks):
        out = []
        best = {}
        for (si, val, te) in toks:
            self._resolve(eng, (si, val, te))
            if self.waited[eng].get(si, 0) >= val:
                continue
            best[si] = max(best.get(si, 0), val)
        for si, val in best.items():
            self.waited[eng][si] = val
            out.append((si, val))
        if out:
            self.ops[eng].append([None, out, None, 0])

    def last_tok(self, eng):
        return (self.semi[eng], self.cnt[eng], eng)

    def barrier(self, engs=("pe", "act", "dve")):
        for e in engs:
            if not self.lastsig.get(e, True):
                self._force_sig(e)
        toks = [self.last_tok(e) for e in engs if self.cnt[e] > 0] + list(self.pending)
        for e in engs:
            self.wait_only(e, toks)
        self.pending = []

    def emit(self, block):
        nc = self.nc
        table = {"pe": block.tensor, "act": block.scalar, "dve": block.vector, "pool": block.gpsimd,
                 "sp": block.sync}
        for eng in self.ENGS:
            ops = self.ops[eng]
            sems = self.sems

            def body(e, ops=ops):
                for fn, waits, inc, n in ops:
                    for si, val in waits:
                        e.wait_ge(sems[si], val)
                    if fn is None:
                        continue
                    r = fn(e)
                    if inc is not None:
                        assert len(r) == n
                        for ins in r:
                            ins.then_inc(sems[inc[0]], inc[1])
            table[eng](body)


def _slab_k(Wc):
    K, n = Wc.shape
    kc = K // 128
    a = np.ascontiguousarray(Wc.reshape(kc, 128, n).transpose(1, 0, 2)).reshape(128, kc * n)
    out = np.zeros((128, SLAB), np.float32)
    out[:, :kc * n] = a
    return out


def _cols(v):
    return np.ascontiguousarray(np.asarray(v, np.float32).reshape(-1, 128).T)


def _ffn_chunk_order():
    order = []
    for s in range(11):
        for j in range(4):
            if j < 2:
                order.append(2 * s + j)
            else:
                order.append(22 + 2 * s + (j - 2))
    return order


def pack_weights(inp):
    wp = np.zeros((DEPTH, NSLAB, 128, SLAB), np.float32)
    pp = np.zeros((DEPTH, 128, NPC), np.float32)
    lw = np.zeros((DEPTH, 128, 1024), np.float32)
    forder = _ffn_chunk_order()
    for l in range(DEPTH):
        w_in = inp["w_in"][l]
        wp[l, SL_LORA] = _slab_k(w_in[:, 1536:1792])
        for p in range(4):
            cols = np.concatenate([w_in[:, p * 128:(p + 1) * 128], w_in[:, 512 + p * 128:512 + (p + 1) * 128],
                                   w_in[:, 1024 + p * 128:1024 + (p + 1) * 128]], axis=1)
            wp[l, SL_RP + p] = _slab_k(cols)
        wp[l, SL_ROUT] = _slab_k(inp["w_rwkv_out"][l])
        for bi, sl in enumerate((SL_G0, SL_G1, SL_G2)):
            for hf in range(2):
                c0 = OFF_GATE + bi * 1024 + hf * 512
                wp[l, sl + hf] = _slab_k(w_in[:, c0:c0 + 512])
        wp[l, SL_CU] = _slab_k(w_in[:, OFF_C:OFF_C + 512])
        wp[l, SL_CG] = _slab_k(w_in[:, OFF_C + 512:OFF_C + 1024])
        wp[l, SL_COUT] = _slab_k(inp["w_conv_out"][l])
        for g in range(3):
            q = w_in[:, OFF_Q + g * 256:OFF_Q + (g + 1) * 256]
            k = w_in[:, OFF_K + g * 256:OFF_K + (g + 1) * 256]
            v = w_in[:, OFF_V + g * 256:OFF_V + (g + 1) * 256]
            wp[l, SL_ATT + 2 * g] = _slab_k(np.concatenate([q, k], axis=1))
            wp[l, SL_ATT + 2 * g + 1] = _slab_k(np.concatenate([k, v], axis=1))
        wp[l, SL_AOUT] = _slab_k(inp["w_attn_out"][l])
        wp[l, SL_WO] = _slab_k(inp["w_o"][l][:, 0:512])
        wp[l, SL_WO + 1] = _slab_k(inp["w_o"][l][:, 512:1024])
        wfi = inp["w_ffn_in"][l]
        for s in range(11):
            cols = np.concatenate([wfi[:, ch * 128:(ch + 1) * 128] for ch in forder[4 * s:4 * s + 4]], axis=1)
            wp[l, SL_FIN + s] = _slab_k(cols)
        wfo = inp["w_ffn_out"][l]
        for oc in range(8):
            wp[l, SL_FOUT + oc] = _slab_k(wfo[:, oc * 128:(oc + 1) * 128])
        pr = pp[l]
        pr[:, PC["mu"]:PC["mu"] + 14] = _cols(inp["rwkv_mu"][l])
        pr[:, PC["w0"]:PC["w0"] + 4] = _cols(inp["rwkv_w0"][l])
        pr[:, PC["a0"]:PC["a0"] + 4] = _cols(inp["rwkv_a0"][l])
        pr[:, PC["kk"]:PC["kk"] + 4] = _cols(inp["rwkv_k_k"][l])
        pr[:, PC["ka"]:PC["ka"] + 4] = _cols(inp["rwkv_k_a"][l])
        pr[:, PC["rk"]:PC["rk"] + 4] = _cols(inp["rwkv_r_k"][l].reshape(-1))
        pr[:, PC["lng"]:PC["lng"] + 4] = _cols(inp["rwkv_ln_g"][l])
        pr[:, PC["lnb"]:PC["lnb"] + 4] = _cols(inp["rwkv_ln_b"][l])
        cdw = inp["conv_dw"][l]
        for c in range(4):
            pr[:, PC["cdw"] + c * 31:PC["cdw"] + (c + 1) * 31] = cdw[:, c * 128:(c + 1) * 128].T
        pr[:, PC["cdb"]:PC["cdb"] + 4] = _cols(inp["conv_dw_b"][l])
        pr[:, PC["clg"]:PC["clg"] + 4] = _cols(inp["conv_ln_g"][l])
        pr[:, PC["clb"]:PC["clb"] + 4] = _cols(inp["conv_ln_b"][l])
        pr[:, PC["gb"]:PC["gb"] + 24] = _cols(inp["gate_b"][l].reshape(-1))
        pr[:, PC["l1g"]:PC["l1g"] + 8] = _cols(inp["ln1_g"][l])
        pr[:, PC["l1b"]:PC["l1b"] + 8] = _cols(inp["ln1_b"][l])
        fdw = inp["ffn_dw"][l]
        fdb = inp["ffn_dw_b"][l]
        for qi, ch in enumerate(forder):
            pr[:, PC["fdw"] + qi * 3:PC["fdw"] + qi * 3 + 3] = fdw[:, ch * 128:(ch + 1) * 128].T
            pr[:, PC["fdb"] + qi] = fdb[ch * 128:(ch + 1) * 128]
        pr[:, PC["l2g"]:PC["l2g"] + 8] = _cols(inp["ln2_g"][l])
        pr[:, PC["l2b"]:PC["l2b"] + 8] = _cols(inp["ln2_b"][l])
        lw[l, 0:64, 0:512] = inp["rwkv_w_up"][l]
        lw[l, 64:128, 0:512] = inp["rwkv_a_up"][l]
        lw[l, :, 512:1024] = inp["rwkv_g_up"][l]
    return wp, pp, lw


def pack_samples(inp, s0, ns):
    forder = _ffn_chunk_order()
    x = inp["x_sample"][s0:s0 + ns, 0, :]
    xs_fm = np.ascontiguousarray(x.reshape(ns, 8, 128).transpose(2, 1, 0))
    sh = inp["state_shift"][:, s0:s0 + ns, 0, :]
    sh_fm = np.ascontiguousarray(sh.reshape(DEPTH, ns, 14, 128).transpose(0, 3, 2, 1))
    wk = inp["state_wkv"][:, s0:s0 + ns]
    wk = wk.reshape(DEPTH, ns, 4, 2, 64, 64)
    wkv_fm = np.ascontiguousarray(wk.transpose(0, 3, 5, 1, 2, 4)).reshape(DEPTH, 128, ns, 4, 64)
    cv = inp["state_conv"][:, s0:s0 + ns]
    cv_fm = np.ascontiguousarray(cv.reshape(DEPTH, ns, 30, 4, 128).transpose(0, 4, 3, 1, 2))
    ff = inp["state_ffn"][:, s0:s0 + ns]
    ff4 = ff.reshape(DEPTH, ns, 2, 44, 128)[:, :, :, forder, :]
    ff_fm = np.ascontiguousarray(ff4.transpose(0, 4, 3, 1, 2))
    out = {"xs_fm": xs_fm, "sh_fm": sh_fm, "wkv_fm": wkv_fm, "cv_fm": cv_fm, "cv_nat": np.ascontiguousarray(cv),
           "ff_fm": ff_fm, "ff_nat": np.ascontiguousarray(ff)}
    for g, nm in enumerate(("cache_swa_a", "cache_swa_b", "cache_swa_c")):
        c = inp[nm][:, s0:s0 + ns]
        out[f"cache{g}"] = np.ascontiguousarray(c.reshape(DEPTH, ns, c.shape[2], 512))
    return out


def make_consts():
    cf = np.zeros((128, 2048), np.float32)
    cf[:, 0:128] = np.eye(128, dtype=np.float32)
    blk = np.zeros((128, 128), np.float32)
    blk[:64, :64] = 1.0
    blk[64:, 64:] = 1.0
    cf[:, 128:256] = blk
    cf[:, 256:384] = blk / 64.0
    cf[:, 384:512] = 1.0 / 1024.0
    cf[:, 512:640] = 1.0 / 512.0
    rm = np.ones((128, 512), np.float32)
    rm[:, ::64] = 0.0
    cf[:, 640:1152] = rm
    s = np.arange(64)[:, None]
    t = np.arange(64)[None, :]
    m1 = np.concatenate([(s < t), (s <= t)], axis=1).astype(np.float32)
    cf[0:64, 1152:1280] = m1
    cf[64:128, 1152:1280] = m1
    m2 = (t < s).astype(np.float32)
    cf[0:64, 1280:1344] = m2
    cf[64:128, 1280:1344] = m2
    cf[0:64, 1344:1408] = np.eye(64, dtype=np.float32)
    cf[64:128, 1344:1408] = np.eye(64, dtype=np.float32)
    for s_ in range(4):
        cf[s_, 1408 + s_ * 128:1408 + (s_ + 1) * 128] = 1.0
    cf[:, 1920:1984] = 1.0
    cb = np.zeros((128, 1536), np.float32)
    j = np.arange(128)[:, None]
    q = np.arange(128)[None, :]
    same = np.where(j <= q, 0.0, NEG).astype(np.float32)
    prev = np.where(j >= q, 0.0, NEG).astype(np.float32)
    cb[:, 0:128] = np.eye(128, dtype=np.float32)
    cb[:, 128:256] = same
    cb[:, 256:384] = prev
    cb[:, 384:448] = 1.0
    off = 448
    for o in range(4):
        for kb, m in enumerate((same, prev)):
            for rep in range(4):
                cb[:, off:off + 32] = m[:, o * 32:(o + 1) * 32]
                off += 32
    assert off == 448 + 1024
    return cf, cb


STOP = None
DEBUG = False
DBGSTATE = {"tile": -1}


class _Stop(Exception):
    pass


def build(SEQ, NSAMP=0):
    NT = SEQ // T
    nc = bass.Bass("TRN2", target_bir_lowering=False)
    es = ExitStack()
    dram = {}

    def din(name, shape, dt=F32):
        dram[name] = nc.dram_tensor(name, list(shape), dt, kind="ExternalInput").ap()
        return dram[name]

    def dout(name, shape):
        dram[name] = nc.dram_tensor(name, list(shape), F32, kind="ExternalOutput").ap()
        return dram[name]

    x_fm = din("x_fm", [128, 8, SEQ])
    wpk = din("wpack", [DEPTH, NSLAB, 128, SLAB])
    ppk = din("ppack", [DEPTH, 128, NPC])
    lwk = din("lorapack", [DEPTH, 128, 1024])
    cfk = din("constf", [128, 2048])
    cbk = din("constb", [128, 1536])
    xs_d = nc.dram_tensor("xs_scratch", [128, 8, SEQ], F32, kind="Internal").ap()
    y_p = dout("y_p", [SEQ, D_MODEL])
    o_shift = dout("o_shift_p", [DEPTH, R_COLS])
    o_wkv = dout("o_wkv_p", [DEPTH, R_HEADS, 64, 64])
    o_conv = dout("o_conv_p", [DEPTH, CONV_K - 1, CONV_CH])
    keep = [min(w, SEQ) for w, _ in SWA_GROUPS]
    o_swa = [dout(f"o_swa{g}_p", [DEPTH, keep[g], 512]) for g in range(3)]
    o_ffn = dout("o_ffn_p", [DEPTH, 2, 2 * D_FF])
    dbg_d = dout("dbg", [16, 128, T]) if DEBUG else None

    NS = NSAMP
    if NS:
        xs_fm = din("xs_fm", [128, 8, NS])
        sh_fm = din("sh_fm", [DEPTH, 128, 14, NS])
        wkv_fm = din("wkv_fm", [DEPTH, 128, NS, 4, 64])
        cv_fm = din("cv_fm", [DEPTH, 128, 4, NS, 30])
        cv_nat = din("cv_nat", [DEPTH, NS, 30, 512])
        ff_fm = din("ff_fm", [DEPTH, 128, 44, NS, 2])
        ff_nat = din("ff_nat", [DEPTH, NS, 2, 2 * D_FF])
        cache_d = [din(f"cache{g}", [DEPTH, NS, NBUF[g], 512]) for g in range(3)]
        y_s = dout("y_s", [NS, D_MODEL])
        o_shift_s = dout("o_shift_s", [DEPTH, NS, R_COLS])
        o_wkv_s = dout("o_wkv_s", [DEPTH, NS, R_HEADS, 64, 64])
        o_conv_s = dout("o_conv_s", [DEPTH, NS, CONV_K - 1, CONV_CH])
        o_swa_s = [dout(f"o_swa{g}_s", [DEPTH, NS, NBUF[g], 512]) for g in range(3)]
        o_ffn_s = dout("o_ffn_s", [DEPTH, NS, 2, 2 * D_FF])

    S = Sched(nc, es)

    def sb(name, shape, dt=F32):
        return es.enter_context(nc.sbuf_tensor(name, list(shape), dt))

    ring = sb("ring", [128, NSLOT, SLAB], BF16)
    ring_b = [Buf(f"ring{i}") for i in range(NSLOT)]
    cf = sb("cf", [128, 2048])
    cb = sb("cb", [128, 1536], BF16)
    cf_b, cb_b = Buf("cf"), Buf("cb")
    pp = sb("pp", [128, DEPTH, NPC])
    pp_b = Buf("pp")
    lw = sb("lw", [128, DEPTH, 1024])
    lw_b = Buf("lw")
    xT32 = sb("xT32", [128, 8, T])
    xTb = sb("xTb", [128, 8, T], BF16)
    xT32_b, xTb_b = Buf("xT32"), Buf("xTb")
    merged = sb("merged", [128, 8, T])
    merged_b = Buf("merged")
    kT_h = [sb("kTa", [128, 2, 2 * T], BF16), sb("kTb", [128, 2, 2 * T], BF16), sb("kTc", [128, 2, max(SEQ, 4096)], BF16)]
    kT_hb = [Buf("kTa"), Buf("kTb"), Buf("kTc")]
    Vtm = [sb("Va", [128, 8, 256], BF16), sb("Vb", [128, 8, 256], BF16), sb("Vc", [128, 32, 256], BF16)]
    Vtm_b = [Buf("Va"), Buf("Vb"), Buf("Vc")]
    Hst = sb("Hst", [128, 4, 64])
    Hst_b = Buf("Hst")
    zcar = sb("zcar", [128, 14])
    zcar_b = Buf("zcar")
    uhist = sb("uhist", [128, 4, 30])
    uhist_b = Buf("uhist")
    fcar = sb("fcar", [128, 44, 2])
    fcar_b = Buf("fcar")
    ps = [es.enter_context(nc.psum_tensor(f"ps{i}", [128, 512], F32)) for i in range(8)]
    ps_b = [Buf(f"ps{i}") for i in range(8)]

    ARENA_N = 12288
    arena_t = sb("arena", [128, ARENA_N])

    class Arena:
        def __init__(self):
            self.top = 0
            self.stack = []

        def push(self):
            self.stack.append(self.top)

        def pop(self):
            self.top = self.stack.pop()

        def alloc(self, shape, dt=F32):
            nfree = int(np.prod(shape[1:]))
            n32 = nfree if dt == F32 else (nfree + 1) // 2
            ap = arena_t[0:shape[0], self.top:self.top + n32]
            self.top += n32
            assert self.top <= ARENA_N, ("arena overflow", self.top)
            if dt != F32:
                ap = ap.bitcast(dt)
            if len(shape) == 3:
                ap = ap.rearrange("p (a b) -> p a b", a=shape[1])
            elif len(shape) == 4:
                ap = ap.rearrange("p (a b c) -> p a b c", a=shape[1], b=shape[2])
            return ap

    AR = Arena()

    @contextmanager
    def scope():
        AR.push()
        try:
            yield None
        finally:
            AR.pop()

    @contextmanager
    def scope_alloc(shape, dt=F32):
        AR.push()
        try:
            yield AR.alloc(list(shape), dt)
        finally:
            AR.pop()

    ident = cf[:, 0:128]
    ones_blk = cf[:, 128:256]
    mean_blk = cf[:, 256:384]
    mean_d = cf[:, 384:512]
    mean_c = cf[:, 512:640]
    rmask = cf[:, 640:1152]
    identb = cb[:, 0:128]
    mb_same = cb[:, 128:256]
    mb_prev = cb[:, 256:384]
    ones_b64 = cb[:, 384:448]

    def pcol(l, name, i=0, rows=slice(0, 128)):
        c = PC[name] + i
        return pp[rows, l, c:c + 1]

    S.dma("sp", lambda e: e.dma_start(out=cf[:], in_=cfk[:, :]), "cf", writes=[cf_b], local=False)
    S.dma("pool", lambda e: e.dma_start(out=cb[:], in_=cbk[:, :]), "cb", writes=[cb_b], local=False)
    S.dma("sp", lambda e: [e.dma_start(out=pp[:, l, :], in_=ppk[l, :, :]) for l in range(DEPTH)], "pp",
          writes=[pp_b], n=DEPTH, local=False)
    S.dma("sp", lambda e: [e.dma_start(out=lw[:, l, :], in_=lwk[l, :, :]) for l in range(DEPTH)], "lw",
          writes=[lw_b], n=DEPTH, local=False)
    for l in range(DEPTH):
        S.op("dve", lambda e, l=l: e.tensor_scalar(out=pp[:, l, PC["omm"]:PC["omm"] + 14],
                                                   in0=pp[:, l, PC["mu"]:PC["mu"] + 14], scalar1=-1.0, scalar2=1.0,
                                                   op0=ALU.mult, op1=ALU.add), reads=[pp_b], writes=[pp_b])
        S.op("dve", lambda e, l=l: e.tensor_scalar(out=pp[:, l, PC["nw0"]:PC["nw0"] + 4],
                                                   in0=pp[:, l, PC["w0"]:PC["w0"] + 4], scalar1=-1.0, scalar2=None,
                                                   op0=ALU.mult), reads=[pp_b], writes=[pp_b])
    for g in range(3):
        S.op("dve", lambda e, g=g: e.memset(kT_h[g][:], 0.0), writes=[kT_hb[g]])
        S.op("dve", lambda e, g=g: e.memset(Vtm[g][:], 0.0), writes=[Vtm_b[g]])

    wstate = {"n": 0}

    def wload(l, slab):
        i = wstate["n"] % NSLOT
        wstate["n"] += 1
        S.dma("pool", lambda e: e.dma_start(out=ring[:, i, :], in_=wpk[l, slab, :, :]), f"ring{i}",
              writes=[ring_b[i]], local=False)
        return i

    class WQ:
        def __init__(self):
            self.plan = []
            self.issued = 0
            self.slots = {}

        def extend(self, items):
            self.plan.extend(items)

        def get(self, idx, ahead=NSLOT - 1):
            while self.issued < len(self.plan) and self.issued <= idx + ahead:
                l, slab = self.plan[self.issued]
                self.slots[self.issued] = wload(l, slab)
                self.issued += 1
            return self.slots[idx]

    wq = WQ()
    for l in range(DEPTH):
        if NSAMP:
            for sl in range(NSLAB):
                wq.extend([(l, sl)])
        for i in range(NT):
            if STOP and ((l, i) != (0, 0) or STOP == "load"):
                continue
            for sl in range(NSLAB):
                wq.extend([(l, sl)])
    wctr = {"i": 0}

    cur_layer = {"l": 0}

    def next_slab(expect):
        idx = wctr["i"]
        assert wq.plan[idx] == (cur_layer["l"], expect), (wq.plan[idx], cur_layer["l"], expect)
        slot = wq.get(idx, ahead=NSLOT - 3)
        wctr["i"] += 1
        return slot

    def W(slot, kc_n, ncols):
        return ring[:, slot, 0:kc_n * ncols].rearrange("p (k n) -> p k n", n=ncols)

    def dbg(idx, ap, buf):
        if DEBUG:
            S.dma("sp", lambda e: e.dma_start(out=dbg_d[idx, 0:ap.shape[0], 0:ap.shape[1]], in_=ap), "dbg", reads=[buf], out=True)

    epsc = sb("epsc", [128, 4])
    epsc_b = Buf("epsc")
    for ci, ev in enumerate((LN_EPS, LNX_EPS, 1e-24)):
        S.op("dve", lambda e, ci=ci, ev=ev: e.memset(epsc[:, ci:ci + 1], float(ev)), writes=[epsc_b])
    eps_col = {float(LN_EPS): 0, float(LNX_EPS): 1, 1e-24: 2}

    def rsqrt(out, in_, eps, reads, wbuf):
        ci = eps_col[float(eps)]
        S.op("act", lambda e: e.activation(out=out, in_=in_, func=AF.Sqrt, bias=epsc[0:out.shape[0], ci:ci + 1]),
             reads=list(reads) + [epsc_b], writes=[wbuf])
        S.op("dve", lambda e: e.reciprocal(out=out, in_=out), reads=[wbuf], writes=[wbuf])

    def mm(out, lhsT, rhs, start, stop, reads, writes, sig):
        S.op("pe", lambda e: e.matmul(out, lhsT=lhsT, rhs=rhs, start=start, stop=stop), reads=reads,
             writes=writes, sig=sig)

    def proj_fm(pi, slot, col0, rhs_t, rhs_b, ncol=T, kc_n=8, wcols=512, rhs_cols=slice(0, T)):
        Wv = W(slot, kc_n, wcols)
        for kc in range(kc_n):
            mm(ps[pi][:, 0:ncol], Wv[:, kc, col0:col0 + 128], rhs_t[:, kc, rhs_cols], kc == 0, kc == kc_n - 1,
               [ring_b[slot], rhs_b], [ps_b[pi]], kc == kc_n - 1)

    def ln_fm(l, src, src_b, nch, mean_mat, gname, bname, eps, outs, tmp, tmp_b, pA=6, pB=7, N=T):
        for c in range(nch):
            mm(ps[pA][:, 0:N], mean_mat, src[:, c, :], c == 0, c == nch - 1, [cf_b, src_b], [ps_b[pA]], c == nch - 1)
        mean_sb = tmp[:, 0, 0:N]
        S.op("act", lambda e: e.copy(out=mean_sb, in_=ps[pA][:, 0:N]), reads=[ps_b[pA]], writes=[tmp_b])
        sq = es_local["sq"]
        sq_b = es_local["sq_b"]
        for c in range(nch):
            S.op("dve", lambda e, c=c: e.tensor_sub(out=src[:, c, :], in0=src[:, c, :], in1=mean_sb),
                 reads=[src_b, tmp_b], writes=[src_b])
            S.op("act", lambda e, c=c: e.activation(out=sq[:, c % 2, 0:N], in_=src[:, c, :], func=AF.Square),
                 reads=[src_b], writes=[sq_b[c % 2]])
            mm(ps[pB][:, 0:N], mean_mat, sq[:, c % 2, 0:N], c == 0, c == nch - 1, [cf_b, sq_b[c % 2]], [ps_b[pB]], True)
        rsqrt(mean_sb, ps[pB][:, 0:N], float(eps), [ps_b[pB]], tmp_b)
        for c in range(nch):
            S.op("dve", lambda e, c=c: e.tensor_mul(out=src[:, c, :], in0=src[:, c, :], in1=mean_sb),
                 reads=[src_b, tmp_b], writes=[src_b])
            for (ot, ob) in outs:
                S.op("dve", lambda e, c=c, ot=ot: e.tensor_scalar(out=ot[:, c, :], in0=src[:, c, :],
                                                                   scalar1=pcol(l, gname, c), scalar2=pcol(l, bname, c),
                                                                   op0=ALU.mult, op1=ALU.add),
                     reads=[src_b, pp_b], writes=[ob])

    es_local = {}

    def merge_branch(l, bidx, slot_out, kc_n, actT, act_b, gslots, N=T, xb=None, xb_b=None, mg=None, mg_b=None):
        xb = xTb if xb is None else xb
        xb_b = xTb_b if xb_b is None else xb_b
        mg = merged if mg is None else mg
        mg_b = merged_b if mg_b is None else mg_b
        Wo = W(slot_out, kc_n, 1024)
        with scope_alloc([128, 2, N], F32) as gsb:
            gsb_b = [Buf("gsb0"), Buf("gsb1")]
            for oc in range(8):
                py, pg = oc % 2, 2 + oc % 2
                for kc in range(kc_n):
                    mm(ps[py][:, 0:N], Wo[:, kc, oc * 128:(oc + 1) * 128], actT[:, kc, :], kc == 0, kc == kc_n - 1,
                       [ring_b[slot_out], act_b], [ps_b[py]], kc == kc_n - 1)
                proj_fm(pg, gslots[oc // 4], (oc % 4) * 128, xb, xb_b, ncol=N, rhs_cols=slice(0, N))
                g_ = gsb[:, oc % 2, :]
                S.op("act", lambda e, g_=g_, pg=pg, oc=oc: e.activation(out=g_, in_=ps[pg][:, 0:N], func=AF.Sigmoid,
                                                                         bias=pcol(l, "gb", bidx * 8 + oc)),
                     reads=[ps_b[pg], pp_b], writes=[gsb_b[oc % 2]])
                if bidx == 0:
                    S.op("dve", lambda e, g_=g_, py=py, oc=oc: e.tensor_mul(out=mg[:, oc, :], in0=ps[py][:, 0:N], in1=g_),
                         reads=[ps_b[py], gsb_b[oc % 2]], writes=[mg_b])
                else:
                    S.op("dve", lambda e, g_=g_, py=py: e.tensor_mul(out=g_, in0=ps[py][:, 0:N], in1=g_),
                         reads=[ps_b[py], gsb_b[oc % 2]], writes=[gsb_b[oc % 2]])
                    S.op("dve", lambda e, g_=g_, oc=oc: e.tensor_add(out=mg[:, oc, :], in0=mg[:, oc, :], in1=g_),
                         reads=[mg_b, gsb_b[oc % 2]], writes=[mg_b])
            if DEBUG and N == T and l == 0 and DBGSTATE["tile"] == 0:
                dbg(10 + bidx, mg[:, 0, :], mg_b)
            S.barrier()

    def fm2tm_store(src_cols_fn, nchunk, nrows, dst_fn, key):
        with scope_alloc([128, 512], F32) as stg:
            stg_b = Buf("stg")
            for c in range(nchunk):
                ap, bf = src_cols_fn(c)
                S.op("pe", lambda e, ap=ap, c=c: e.matmul(ps[4][0:nrows, c * 128:(c + 1) * 128], lhsT=ap, rhs=ident,
                                                          start=True, stop=True),
                     reads=[bf, cf_b], writes=[ps_b[4]])
            S.op("act", lambda e: e.copy(out=stg[0:nrows, 0:nchunk * 128], in_=ps[4][0:nrows, 0:nchunk * 128]),
                 reads=[ps_b[4]], writes=[stg_b])
            S.dma("sp", dst_fn(stg), key, reads=[stg_b], out=True)
            S.barrier()

    def tile_prog(l, i):
        DBGSTATE["tile"] = i
        par = i % 2
        last = (i == NT - 1)
        src = x_fm if l == 0 else xs_d
        S.dma("sp", lambda e: e.dma_start(out=xT32[:], in_=src[:, :, i * T:(i + 1) * T]), "xload",
              writes=[xT32_b], local=False)
        S.op("act", lambda e: e.copy(out=xTb[:], in_=xT32[:]), reads=[xT32_b], writes=[xTb_b])

        if STOP == "load":
            return
        rwkv_phase(l, i, last)
        if STOP == "rwkv":
            return
        conv_phase(l, i, last)
        if STOP == "conv":
            return
        attn_phase(l, i, par, last)
        if STOP == "attn":
            return
        ffn_phase(l, i, last)

    def rwkv_phase(l, i, last):
        with scope():
            def lt(name, shape, dt=F32):
                return AR.alloc(list(shape), dt)
            lora = merged[:, 0:2, :]
            lora_b = Buf("lora")
            zraw = merged[:, 2:5, :].rearrange("p a t -> p (a t)")[:, 0:2 * (T + 1)].rearrange("p (a t) -> p a t", a=2)
            zraw_b = [Buf("zraw0"), Buf("zraw1")]
            ozT = lt("ozT", [128, 4, T], BF16)
            ozT_b = Buf("ozT")
            zctr = {"n": 0}

            def shift_chunk(pi, och, dst, dst_b):
                zi = zctr["n"] % 2
                zctr["n"] += 1
                zr = zraw[:, zi, :]
                S.op("act", lambda e: e.copy(out=zr[:, 1:T + 1], in_=ps[pi][:, :]), reads=[ps_b[pi]], writes=[zraw_b[zi]])
                S.op("dve", lambda e: e.tensor_copy(out=zr[:, 0:1], in_=zcar[:, och:och + 1]), reads=[zcar_b],
                     writes=[zraw_b[zi]])
                S.op("dve", lambda e: e.tensor_scalar(out=dst, in0=ps[pi][:, :], scalar1=pcol(l, "omm", och), scalar2=None,
                                                      op0=ALU.mult), reads=[ps_b[pi], pp_b], writes=[dst_b])
                S.op("dve", lambda e: e.scalar_tensor_tensor(out=dst, in0=zr[:, 0:T], scalar=pcol(l, "mu", och), in1=dst,
                                                             op0=ALU.mult, op1=ALU.add),
                     reads=[zraw_b[zi], pp_b, dst_b], writes=[dst_b])
                S.op("dve", lambda e: e.tensor_copy(out=zcar[:, och:och + 1], in_=zr[:, T:T + 1]), reads=[zraw_b[zi]],
                     writes=[zcar_b])

            slot = next_slab(SL_LORA)
            for c in range(2):
                proj_fm(c, slot, c * 128, xTb, xTb_b, wcols=256)
                shift_chunk(c, 12 + c, lora[:, c, :], lora_b)
            S.op("act", lambda e: e.activation(out=lora[0:64, 0, :], in_=lora[0:64, 0, :], func=AF.Tanh),
                 reads=[lora_b], writes=[lora_b])
            S.op("act", lambda e: e.activation(out=lora[:, 1, :], in_=lora[:, 1, :], func=AF.Sigmoid),
                 reads=[lora_b], writes=[lora_b])

            if STOP == "rw_lora":
                raise _Stop()
            for p in range(4):
                rwkv_pair(l, i, p, lora, lora_b, shift_chunk, ozT, ozT_b, None)

            s_out = next_slab(SL_ROUT)
            g0 = next_slab(SL_G0)
            g1 = next_slab(SL_G0 + 1)
            merge_branch(l, 0, s_out, 4, ozT, ozT_b, (g0, g1))
            if last:
                S.dma("sp", lambda e: e.dma_start(out=o_shift[l].rearrange("(c p) -> p c", p=128), in_=zcar[:, :],
                                                  allow_slow_non_contiguous=True), "oshift", reads=[zcar_b], out=True)
                with scope_alloc([64, 8, 64], F32) as stg:
                    stg_b = Buf("wkvst")
                    for h in range(8):
                        pr_, hl = h // 2, h % 2
                        S.op("pe", lambda e, pr_=pr_, hl=hl, h=h: e.matmul(
                            ps[4 + hl][0:64, pr_ * 64:(pr_ + 1) * 64], lhsT=Hst[hl * 64:(hl + 1) * 64, pr_, :],
                            rhs=cf[hl * 64:(hl + 1) * 64, 1344:1408], start=True, stop=True), reads=[Hst_b, cf_b],
                            writes=[ps_b[4 + hl]])
                    stg4 = stg[:].rearrange("p (pr hl) k -> p pr hl k", hl=2)
                    for hl in range(2):
                        S.op("act", lambda e, hl=hl: e.copy(out=stg4[:, :, hl, :],
                                                            in_=ps[4 + hl][0:64, 0:256].rearrange("p (pr k) -> p pr k", k=64)),
                             reads=[ps_b[4 + hl]], writes=[stg_b])
                    S.dma("sp", lambda e: e.dma_start(out=o_wkv[l].rearrange("h v k -> v h k"), in_=stg[:]), "owkv",
                          reads=[stg_b], out=True)
                    S.barrier()
            S.barrier()

    def rwkv_pair(l, i, p, lora, lora_b, shift_chunk, ozT, ozT_b, st_outer):
        with scope():
            cnt = {"n": 0}

            def lt(name, shape, dt=F32):
                cnt["n"] += 1
                return AR.alloc(list(shape), dt), Buf(name)
            rT, r_b = lt("rT", [128, T])
            kT, k_b = lt("kT", [128, T])
            vT, v_b = lt("vT", [128, T])
            lwT, lw_b2 = lt("lwT", [128, T])
            aT, a_b = lt("aT", [128, T])
            gT, g_b = lt("gT", [128, T])
            kkT, kk_b = lt("kkT", [128, T])
            bT, b_b = lt("bT", [128, T])
            csT, cs_b = lt("csT", [128, T])
            eT, e_b = lt("eT", [128, T])
            e2T, e2_b = lt("e2T", [128, T])
            KR, KR_b = lt("KR", [128, NCH, 2, C])
            bon, bon_b = lt("bon", [128, T])
            gam, gam_b = lt("gam", [128, NCH])
            Kt, Kt_b = kT, k_b
            Bt, Bt_b = bT, b_b
            Kh, Kh_b = rT, r_b
            Bh, Bh_b = kkT, kk_b
            OT, OT_b = aT, a_b
            slot = next_slab(SL_RP + p)
            for c, (dst, db, och) in enumerate(((rT, r_b, p), (kT, k_b, 4 + p), (vT, v_b, 8 + p))):
                proj_fm(c % 2, slot, c * 128, xTb, xTb_b, wcols=384)
                shift_chunk(c % 2, och, dst[:, :], db)
            cs128 = slice(p * 128, (p + 1) * 128)
            mm(ps[2][:, :], lw[0:64, l, cs128], lora[0:64, 0, :], True, True, [lw_b, lora_b], [ps_b[2]], True)
            mm(ps[3][:, :], lw[64:128, l, cs128], lora[64:128, 0, :], True, True, [lw_b, lora_b], [ps_b[3]], True)
            mm(ps[4][:, :], lw[:, l, 512 + p * 128:512 + (p + 1) * 128], lora[:, 1, :], True, True, [lw_b, lora_b],
               [ps_b[4]], True)
            S.op("act", lambda e: e.activation(out=eT[:, :], in_=ps[2][:, :], func=AF.Exp, scale=-1.0,
                                               bias=pcol(l, "nw0", p)), reads=[ps_b[2], pp_b], writes=[e_b])
            S.op("act", lambda e: e.activation(out=eT[:, :], in_=eT[:, :], func=AF.Ln, bias=1.0), reads=[e_b], writes=[e_b])
            S.op("act", lambda e: e.activation(out=eT[:, :], in_=eT[:, :], func=AF.Exp, scale=-1.0, bias=-0.5),
                 reads=[e_b], writes=[e_b])
            S.op("dve", lambda e: e.tensor_scalar(out=lwT[:, :], in0=eT[:, :], scalar1=-1.0, scalar2=None, op0=ALU.mult),
                 reads=[e_b], writes=[lw_b2])
            S.op("act", lambda e: e.activation(out=aT[:, :], in_=ps[3][:, :], func=AF.Sigmoid, bias=pcol(l, "a0", p)),
                 reads=[ps_b[3], pp_b], writes=[a_b])
            S.op("act", lambda e: e.copy(out=gT[:, :], in_=ps[4][:, :]), reads=[ps_b[4]], writes=[g_b])
            S.op("dve", lambda e: e.tensor_scalar(out=kkT[:, :], in0=kT[:, :], scalar1=pcol(l, "kk", p), scalar2=None,
                                                  op0=ALU.mult), reads=[k_b, pp_b], writes=[kk_b])
            S.op("dve", lambda e: e.tensor_mul(out=e2T[:, :], in0=kkT[:, :], in1=kkT[:, :]), reads=[kk_b], writes=[e2_b])
            mm(ps[2][:, :], ones_blk, e2T[:, :], True, True, [cf_b, e2_b], [ps_b[2]], True)
            rsqrt(e2T[:, :], ps[2][:, :], 1e-24, [ps_b[2]], e2_b)
            S.op("dve", lambda e: e.tensor_mul(out=kkT[:, :], in0=kkT[:, :], in1=e2T[:, :]), reads=[kk_b, e2_b],
                 writes=[kk_b])
            S.op("dve", lambda e: e.tensor_mul(out=bT[:, :], in0=kkT[:, :], in1=aT[:, :]), reads=[kk_b, a_b], writes=[b_b])
            S.op("dve", lambda e: e.tensor_scalar(out=aT[:, :], in0=aT[:, :], scalar1=-1.0, scalar2=pcol(l, "ka", p),
                                                  op0=ALU.add, op1=ALU.mult), reads=[a_b, pp_b], writes=[a_b])
            S.op("dve", lambda e: e.scalar_tensor_tensor(out=kT[:, :], in0=aT[:, :], scalar=1.0, in1=kT[:, :], op0=ALU.add,
                                                         op1=ALU.mult), reads=[a_b, k_b], writes=[k_b])
            S.op("dve", lambda e: e.scalar_tensor_tensor(out=e2T[:, :], in0=rT[:, :], scalar=pcol(l, "rk", p), in1=kT[:, :],
                                                         op0=ALU.mult, op1=ALU.mult), reads=[r_b, k_b, pp_b], writes=[e2_b])
            mm(ps[3][:, :], ones_blk, e2T[:, :], True, True, [cf_b, e2_b], [ps_b[3]], True)
            S.op("dve", lambda e: e.tensor_mul(out=bon[:, :], in0=ps[3][:, :], in1=vT[:, :]), reads=[ps_b[3], v_b],
                 writes=[bon_b])
            if (l, i, p) == (0, 0, 0):
                for di, (ap_, bf_) in enumerate(((rT, r_b), (kT, k_b), (vT, v_b), (lwT, lw_b2), (kkT, kk_b), (bT, b_b),
                                                 (gT, g_b), (bon, bon_b))):
                    dbg(di, ap_[:, :], bf_)
            S.op("dve", lambda e: e.tensor_tensor_scan(out=csT[:, :], data0=rmask, data1=lwT[:, :], initial=0.0,
                                                       op0=ALU.mult, op1=ALU.add), reads=[cf_b, lw_b2], writes=[cs_b])
            cs3 = csT[:, :].rearrange("p (j t) -> p j t", t=C)
            S.op("act", lambda e: e.activation(out=eT[:, :], in_=csT[:, :], func=AF.Exp), reads=[cs_b], writes=[e_b])
            S.op("dve", lambda e: e.tensor_mul(out=KR[:, :, 1, :], in0=rT[:, :].rearrange("p (j t) -> p j t", t=C),
                                               in1=eT[:, :].rearrange("p (j t) -> p j t", t=C)),
                 reads=[r_b, e_b], writes=[KR_b])
            S.op("dve", lambda e: e.tensor_copy(out=gam[:, :], in_=eT[:, :].rearrange("p (j t) -> p j t", t=C)[:, :, C - 1]),
                 reads=[e_b], writes=[gam_b])
            S.op("dve", lambda e: e.tensor_sub(out=OT[:, :], in0=csT[:, :], in1=lwT[:, :]), reads=[cs_b, lw_b2], writes=[OT_b])
            S.op("act", lambda e: e.activation(out=OT[:, :], in_=OT[:, :], func=AF.Exp), reads=[OT_b], writes=[OT_b])
            S.op("dve", lambda e: e.tensor_mul(out=KR[:, :, 0, :], in0=kkT[:, :].rearrange("p (j t) -> p j t", t=C),
                                               in1=OT[:, :].rearrange("p (j t) -> p j t", t=C)),
                 reads=[kk_b, OT_b], writes=[KR_b])
            S.op("dve", lambda e: e.tensor_sub(out=OT[:, :].rearrange("p (j t) -> p j t", t=C),
                                               in0=cs3[:, :, C - 1:C].broadcast_to([128, NCH, C]), in1=cs3),
                 reads=[cs_b], writes=[OT_b])
            S.op("act", lambda e: e.activation(out=OT[:, :], in_=OT[:, :], func=AF.Exp), reads=[OT_b], writes=[OT_b])
            S.op("dve", lambda e: e.tensor_mul(out=Kh[:, :], in0=kT[:, :], in1=OT[:, :]), reads=[k_b, OT_b], writes=[Kh_b])
            S.op("dve", lambda e: e.tensor_mul(out=Bh[:, :], in0=bT[:, :], in1=OT[:, :]), reads=[b_b, OT_b], writes=[Bh_b])

            S.op("act", lambda e: e.activation(out=eT[:, :], in_=csT[:, :], func=AF.Exp, scale=-1.0), reads=[cs_b],
                 writes=[e_b])
            S.op("dve", lambda e: e.tensor_mul(out=Kt[:, :], in0=kT[:, :], in1=eT[:, :]), reads=[k_b, e_b], writes=[Kt_b])
            S.op("dve", lambda e: e.tensor_mul(out=Bt[:, :], in0=bT[:, :], in1=eT[:, :]), reads=[b_b, e_b], writes=[Bt_b])
            if STOP == "rw_prep":
                raise _Stop()
            GQ = 4
            A1, A1_b = lt("A1", [128, GQ, 2, 128])
            Ab, Ab_b = lt("Ab", [128, GQ, 64])
            Y, Y_b = lt("Y", [128, GQ, 2, 64])
            Pm, Pm_b = lt("Pm", [128, GQ, 2, 64])
            TMx, _ = lt("TM", [128, 2, 3, 128])
            TMx_b = [Buf("TM0"), Buf("TM1")]
            W1, W1_b = lt("W1", [128, 64])
            U, U_b = lt("U", [128, 64])
            dup, dup_b = lt("dup", [128, 3, 128])
            mask1 = cf[:, 1152:1280]
            mask2 = cf[:, 1280:1344]
            id64 = cf[:, 1344:1408]
            HP = [slice(0, 64), slice(64, 128)]
            BK_W, BK_O, BK_H = (1, 3), (4, 5), (6, 7)

            def group_stage(jg):
                for jj in range(GQ):
                    j_ = jg * GQ + jj
                    tcs_ = slice(j_ * C, (j_ + 1) * C)
                    for hl in range(2):
                        hp = HP[hl]
                        krj = KR[hp, j_, :, :].rearrange("p a t -> p (a t)")
                        bA = 2 * hl + jj // 2
                        c0 = (jj % 2) * 256
                        bB = 4 + hl
                        mm(ps[bA][hp, c0:c0 + 128], Kt[hp, tcs_], krj, True, True, [Kt_b, KR_b], [ps_b[bA]], False)
                        mm(ps[bA][hp, c0 + 128:c0 + 256], Bt[hp, tcs_], krj, True, True, [Bt_b, KR_b], [ps_b[bA]], False)
                        mm(ps[bB][hp, jj * 64:(jj + 1) * 64], KR[hp, j_, 0, :], Bt[hp, tcs_], True, True, [KR_b, Bt_b],
                           [ps_b[bB]], True)
                for hl in range(2):
                    hp = HP[hl]
                    for hb_ in range(2):
                        bA = 2 * hl + hb_
                        S.op("dve", lambda e, hp=hp, bA=bA, hb_=hb_: e.tensor_mul(
                            out=A1[hp, 2 * hb_:2 * hb_ + 2, :, :].rearrange("p c x t -> p (c x) t"),
                            in0=ps[bA][hp, 0:512].rearrange("p (x t) -> p x t", t=128),
                            in1=mask1[hp, :].unsqueeze(1).broadcast_to([64, 4, 128])), reads=[ps_b[bA], cf_b],
                            writes=[A1_b])
                    S.op("dve", lambda e, hp=hp, hl=hl: e.tensor_mul(
                        out=Ab[hp, :, :], in0=ps[4 + hl][hp, 0:GQ * 64].rearrange("p (c t) -> p c t", t=64),
                        in1=mask2[hp, :].unsqueeze(1).broadcast_to([64, GQ, 64])), reads=[ps_b[4 + hl], cf_b],
                        writes=[Ab_b])
                S.op("dve", lambda e: e.tensor_scalar(out=Y[:, :, 0, :], in0=A1[:, :, 1, 0:64], scalar1=-1.0, scalar2=None,
                                                      op0=ALU.mult), reads=[A1_b], writes=[Y_b])
                S.op("dve", lambda e: e.tensor_scalar(out=Y[:, :, 1, :], in0=Ab[:, :, :], scalar1=-1.0, scalar2=None,
                                                      op0=ALU.mult), reads=[Ab_b], writes=[Y_b])
                S.op("dve", lambda e: e.tensor_add(out=Pm[:].rearrange("p c a t -> p (c a) t"),
                                                   in0=Y[:].rearrange("p c a t -> p (c a) t"),
                                                   in1=id64.unsqueeze(1).broadcast_to([128, 2 * GQ, 64])),
                     reads=[Y_b, cf_b], writes=[Pm_b])
                for lev in range(5):
                    for jj in range(GQ):
                        for hl in range(2):
                            hp = HP[hl]
                            b_ = 6 + hl
                            mm(ps[b_][hp, jj * 128:jj * 128 + 64], Y[hp, jj, 1, :], Y[hp, jj, 0, :], True, True, [Y_b],
                               [ps_b[b_]], False)
                            mm(ps[b_][hp, jj * 128 + 64:jj * 128 + 128], Y[hp, jj, 0, :], Y[hp, jj, 1, :], True, True,
                               [Y_b], [ps_b[b_]], True)
                    S.op("act", lambda e: e.copy(out=Y[HP[0], :, :, :].rearrange("p c a t -> p (c a t)"),
                                                 in_=ps[6][HP[0], 0:GQ * 128]), reads=[ps_b[6]], writes=[Y_b])
                    S.op("act", lambda e: e.copy(out=Y[HP[1], :, :, :].rearrange("p c a t -> p (c a t)"),
                                                 in_=ps[7][HP[1], 0:GQ * 128]), reads=[ps_b[7]], writes=[Y_b])
                    for jj in range(GQ):
                        for hl in range(2):
                            hp = HP[hl]
                            b_ = 4 + hl
                            mm(ps[b_][hp, jj * 128:jj * 128 + 64], Pm[hp, jj, 1, :], Y[hp, jj, 0, :], True, True,
                               [Pm_b, Y_b], [ps_b[b_]], False)
                            mm(ps[b_][hp, jj * 128 + 64:jj * 128 + 128], Y[hp, jj, 0, :], Pm[hp, jj, 1, :], True, True,
                               [Pm_b, Y_b], [ps_b[b_]], True)
                    for hl in range(2):
                        hp = HP[hl]
                        S.op("dve", lambda e, hp=hp, hl=hl: e.tensor_add(
                            out=Pm[hp, :, :, :].rearrange("p c a t -> p (c a t)"),
                            in0=Pm[hp, :, :, :].rearrange("p c a t -> p (c a t)"), in1=ps[4 + hl][hp, 0:GQ * 128]),
                            reads=[ps_b[4 + hl], Pm_b], writes=[Pm_b])

            def emit_TM(j_):
                tcs_ = slice(j_ * C, (j_ + 1) * C)
                for ti, (srcT, sbf) in enumerate(((vT, v_b), (Kh, Kh_b), (Bh, Bh_b))):
                    for a_ in range(2):
                        S.op("pe", lambda e, ti=ti, srcT=srcT, a_=a_: e.matmul(
                            ps[0][a_ * 64:(a_ + 1) * 64, ti * 128:(ti + 1) * 128], lhsT=srcT[:, tcs_], rhs=ident,
                            start=True, stop=True), reads=[sbf, cf_b], writes=[ps_b[0]], sig=(ti == 2 and a_ == 1))
                S.op("act", lambda e: e.copy(out=TMx[:, j_ % 2, :, :].rearrange("p a f -> p (a f)"), in_=ps[0][:, 0:384]),
                     reads=[ps_b[0]], writes=[TMx_b[j_ % 2]])

            for j in range(NCH):
                tc0 = j * C
                tcs = slice(tc0, tc0 + C)
                jj = j % GQ
                if jj == 0:
                    group_stage(j // GQ)
                if STOP == "rw_inv":
                    raise _Stop()
                if jj == 0:
                    emit_TM(j)
                TM = TMx[:, j % 2, :, :]
                TM_b = TMx_b[j % 2]
                if STOP == "rw_tm":
                    raise _Stop()
                for hl in range(2):
                    hp = HP[hl]
                    b_ = BK_W[hl]
                    o_ = ps[b_][hp, 0:64]
                    mm(o_, KR[hp, j, 0, :], Hst[hp, p, :], True, False, [KR_b, Hst_b], [ps_b[b_]], False)
                    mm(o_, A1[hp, jj, 0, 0:64], TM[hp, 0, hp], False, True, [A1_b, TM_b], [ps_b[b_]], True)
                for hl in range(2):
                    hp = HP[hl]
                    b_ = BK_W[hl]
                    S.op("act", lambda e, hp=hp, b_=b_: e.copy(out=W1[hp, :], in_=ps[b_][hp, 0:64]), reads=[ps_b[b_]],
                         writes=[W1_b])
                if j + 1 < NCH and (j + 1) % GQ != 0:
                    emit_TM(j + 1)
                if STOP == "rw_w1":
                    raise _Stop()
                for hl in range(2):
                    hp = HP[hl]
                    b_ = BK_W[hl]
                    mm(ps[b_][hp, 128:192], Pm[hp, jj, 0, :], W1[hp, :], True, True, [Pm_b, W1_b], [ps_b[b_]], True)
                for hl in range(2):
                    hp = HP[hl]
                    b_ = BK_W[hl]
                    S.op("act", lambda e, hp=hp, b_=b_: e.mul(out=U[hp, :], in_=ps[b_][hp, 128:192], mul=-1.0),
                         reads=[ps_b[b_]], writes=[U_b])
                if STOP == "rw_u":
                    raise _Stop()
                for hl in range(2):
                    hp = HP[hl]
                    b_ = BK_H[hl]
                    o_ = ps[b_][hp, 0:64]
                    mm(o_, TM[hp, 1, hp], TM[hp, 0, hp], True, False, [TM_b], [ps_b[b_]], False)
                    mm(o_, TM[hp, 2, hp], U[hp, :], False, True, [TM_b, U_b], [ps_b[b_]], True)
                for hl in range(2):
                    hp = HP[hl]
                    b_ = BK_O[hl]
                    o_ = ps[b_][hp, 0:64]
                    mm(o_, Hst[hp, p, :], KR[hp, j, 1, :], True, False, [Hst_b, KR_b], [ps_b[b_]], False)
                    mm(o_, TM[hp, 0, hp], A1[hp, jj, 0, 64:128], False, False, [TM_b, A1_b], [ps_b[b_]], False)
                    mm(o_, U[hp, :], A1[hp, jj, 1, 64:128], False, True, [U_b, A1_b], [ps_b[b_]], True)
                for hl in range(2):
                    hp = HP[hl]
                    b_ = BK_O[hl]
                    S.op("act", lambda e, hp=hp, b_=b_: e.copy(out=OT[hp, tcs], in_=ps[b_][hp, 0:64]), reads=[ps_b[b_]],
                         writes=[OT_b])
                if STOP == "rw_o":
                    raise _Stop()
                for hl in range(2):
                    hp = HP[hl]
                    b_ = BK_H[hl]
                    S.op("dve", lambda e, hp=hp, b_=b_: e.scalar_tensor_tensor(
                        out=Hst[hp, p, :], in0=Hst[hp, p, :], scalar=gam[hp, j:j + 1], in1=ps[b_][hp, 0:64],
                        op0=ALU.mult, op1=ALU.add), reads=[Hst_b, gam_b, ps_b[b_]], writes=[Hst_b])
            if STOP == "rw_chunk":
                raise _Stop()
            if (l, i, p) == (0, 0, 0):
                dbg(8, OT[:, :], OT_b)
            mm(ps[0][:, :], mean_blk, OT[:, :], True, True, [cf_b, OT_b], [ps_b[0]], True)
            S.op("dve", lambda e: e.tensor_sub(out=OT[:, :], in0=OT[:, :], in1=ps[0][:, :]), reads=[OT_b, ps_b[0]],
                 writes=[OT_b])
            S.op("act", lambda e: e.activation(out=eT[:, :], in_=OT[:, :], func=AF.Square), reads=[OT_b], writes=[e_b])
            mm(ps[1][:, :], mean_blk, eT[:, :], True, True, [cf_b, e_b], [ps_b[1]], True)
            rsqrt(eT[:, :], ps[1][:, :], float(LNX_EPS), [ps_b[1]], e_b)
            S.op("dve", lambda e: e.tensor_mul(out=OT[:, :], in0=OT[:, :], in1=eT[:, :]), reads=[OT_b, e_b], writes=[OT_b])
            S.op("dve", lambda e: e.tensor_scalar(out=OT[:, :], in0=OT[:, :], scalar1=pcol(l, "lng", p),
                                                  scalar2=pcol(l, "lnb", p), op0=ALU.mult, op1=ALU.add),
                 reads=[OT_b, pp_b], writes=[OT_b])
            S.op("dve", lambda e: e.tensor_add(out=OT[:, :], in0=OT[:, :], in1=bon[:, :]), reads=[OT_b, bon_b], writes=[OT_b])
            if (l, i, p) == (0, 0, 0):
                dbg(9, OT[:, :], OT_b)
            S.op("dve", lambda e: e.tensor_mul(out=ozT[:, p, :], in0=OT[:, :], in1=gT[:, :]), reads=[OT_b, g_b],
                 writes=[ozT_b])
            S.barrier()

    def conv_phase(l, i, last):
        with scope():
            def lt(name, shape, dt=F32):
                return AR.alloc(list(shape), dt), Buf(name)
            uext, ue_b = lt("uext", [128, 4, T + 30])
            acc, acc_b = lt("cacc", [128, 4, T])
            acc2, _ = lt("cacc2", [128, 2, T])
            acc2_b = [Buf("cacc20"), Buf("cacc21")]
            sq, _ = lt("csq", [128, 2, T])
            sq_b = [Buf("csq0"), Buf("csq1")]
            sg, sg_b = lt("csg", [128, 2, T])
            cTb, cT_b = lt("cTb", [128, 4, T], BF16)
            tmp, tmp_b = lt("ctmp", [128, 1, T])
            es_local["sq"], es_local["sq_b"] = sq, sq_b
            su = next_slab(SL_CU)
            sgt = next_slab(SL_CG)
            for c in range(4):
                proj_fm(0, su, c * 128, xTb, xTb_b)
                proj_fm(1, sgt, c * 128, xTb, xTb_b)
                S.op("act", lambda e, c=c: e.activation(out=sg[:, c % 2, :], in_=ps[1][:, :], func=AF.Sigmoid),
                     reads=[ps_b[1]], writes=[sg_b])
                S.op("dve", lambda e, c=c: e.tensor_copy(out=uext[:, c, 0:30], in_=uhist[:, c, :]), reads=[uhist_b],
                     writes=[ue_b])
                S.op("dve", lambda e, c=c: e.tensor_mul(out=uext[:, c, 30:30 + T], in0=ps[0][:, :], in1=sg[:, c % 2, :]),
                     reads=[ps_b[0], sg_b], writes=[ue_b])
                S.op("dve", lambda e, c=c: e.tensor_copy(out=uhist[:, c, :], in_=uext[:, c, T:T + 30]), reads=[ue_b],
                     writes=[uhist_b])
                S.op("dve", lambda e, c=c: e.tensor_scalar(out=acc[:, c, :], in0=uext[:, c, 0:T],
                                                           scalar1=pcol(l, "cdw", c * 31), scalar2=pcol(l, "cdb", c),
                                                           op0=ALU.mult, op1=ALU.add), reads=[ue_b, pp_b], writes=[acc_b])
                a2 = acc2[:, c % 2, :]
                a2_b = acc2_b[c % 2]
                S.op("dve", lambda e, c=c, a2=a2: e.tensor_scalar(out=a2, in0=uext[:, c, 1:1 + T],
                                                                  scalar1=pcol(l, "cdw", c * 31 + 1), scalar2=None,
                                                                  op0=ALU.mult), reads=[ue_b, pp_b], writes=[a2_b])
                for jj in range(2, 31):
                    if jj % 2 == 0:
                        S.op("dve", lambda e, c=c, jj=jj: e.scalar_tensor_tensor(out=acc[:, c, :], in0=uext[:, c, jj:jj + T],
                                                                                 scalar=pcol(l, "cdw", c * 31 + jj),
                                                                                 in1=acc[:, c, :], op0=ALU.mult, op1=ALU.add),
                             reads=[ue_b, pp_b, acc_b], writes=[acc_b])
                    else:
                        S.op("dve", lambda e, c=c, jj=jj, a2=a2: e.scalar_tensor_tensor(out=a2, in0=uext[:, c, jj:jj + T],
                                                                                         scalar=pcol(l, "cdw", c * 31 + jj),
                                                                                         in1=a2, op0=ALU.mult, op1=ALU.add),
                             reads=[ue_b, pp_b, a2_b], writes=[a2_b])
                S.op("dve", lambda e, c=c, a2=a2: e.tensor_add(out=acc[:, c, :], in0=acc[:, c, :], in1=a2),
                     reads=[acc_b, a2_b], writes=[acc_b])
            ln_fm(l, acc, acc_b, 4, mean_c, "clg", "clb", LN_EPS, [(acc, acc_b)], tmp, tmp_b)
            for c in range(4):
                S.op("act", lambda e, c=c: e.activation(out=cTb[:, c, :], in_=acc[:, c, :], func=AF.Silu), reads=[acc_b],
                     writes=[cT_b])
            s_out = next_slab(SL_COUT)
            g0 = next_slab(SL_G1)
            g1 = next_slab(SL_G1 + 1)
            merge_branch(l, 1, s_out, 4, cTb, cT_b, (g0, g1))
            if last:
                n30 = CONV_K - 1
                fm2tm_store(lambda c: (uhist[:, c, :], uhist_b), 4, n30,
                            lambda stg: (lambda e: e.dma_start(out=o_conv[l, :, :], in_=stg[0:n30, 0:512])), "oconv")
            S.barrier()

    def attn_phase(l, i, par, last):
        with scope():
            def lt(name, shape, dt=F32):
                return AR.alloc(list(shape), dt), Buf(name)
            qT, q_b = lt("qT", [128, 3, 2, T], BF16)
            pt, _ = lt("pt", [128, 2, 256], BF16)
            pt_b = [Buf("pt0"), Buf("pt1")]
            vstg, vstg_b = lt("vstg", [32, 16, 256], BF16)
            kvo, _ = lt("kvo", [128, 2, 512])
            kvo_b = [Buf("kvo0"), Buf("kvo1")]
            oTb, oT_b = lt("oTb", [128, 2, T], BF16)
            rec, rec_b = lt("rec", [128, T])
            B_c = i // 4
            o_c = (i % 4) * 32
            for g in range(3):
                win, dil = SWA_GROUPS[g]
                sA = next_slab(SL_ATT + 2 * g)
                sB = next_slab(SL_ATT + 2 * g + 1)
                kcol0 = (par * T) if g < 2 else i * T
                for c in range(4):
                    proj_fm(c % 2, sA, c * 128, xTb, xTb_b)
                    if c < 2:
                        S.op("act", lambda e, c=c, g=g: e.copy(out=qT[:, g, c, :], in_=ps[c % 2][:, :]),
                             reads=[ps_b[c % 2]], writes=[q_b])
                    else:
                        S.op("act", lambda e, c=c, g=g: e.copy(out=kT_h[g][:, c - 2, kcol0:kcol0 + T], in_=ps[c % 2][:, :]),
                             reads=[ps_b[c % 2]], writes=[kT_hb[g]])
                WB = W(sB, 8, 512)
                if g < 2:
                    for blk in range(4):
                        cols = slice(blk * 128, (blk + 1) * 128) if g == 0 else slice(blk, T, 4)
                        vb = (par * 4 + blk) if g == 0 else (blk * 2 + par)
                        pi = 2 + blk % 2
                        for kc in range(8):
                            mm(ps[pi][:, 0:256], xTb[:, kc, cols], WB[:, kc, 256:512], kc == 0, kc == 7,
                               [xTb_b, ring_b[sB]], [ps_b[pi]], kc == 7)
                        S.op("dve", lambda e, pi=pi, vb=vb, g=g: e.tensor_copy(out=Vtm[g][:, vb, :], in_=ps[pi][:, 0:256]),
                             reads=[ps_b[pi]], writes=[Vtm_b[g]])
                else:
                    for rho in range(16):
                        pi = 2 + rho % 2
                        for kc in range(8):
                            mm(ps[pi][0:32, 0:256], xTb[:, kc, slice(rho, T, 16)], WB[:, kc, 256:512], kc == 0, kc == 7,
                               [xTb_b, ring_b[sB]], [ps_b[pi]], kc == 7)
                        S.op("dve", lambda e, pi=pi, rho=rho: e.tensor_copy(out=vstg[:, rho, :], in_=ps[pi][0:32, 0:256]),
                             reads=[ps_b[pi]], writes=[vstg_b])
                    vdst = Vtm[2][o_c:o_c + 32, :, :].rearrange("p (r b) f -> p r b f", b=2)[:, :, B_c, :]
                    S.dma("sp", lambda e: e.dma_start(out=vdst, in_=vstg[:, :, :]), "vcdma", reads=[vstg_b],
                          writes=[Vtm_b[2]])
                kp = keep[g]
                for blk in range(4):
                    t0 = i * T + blk * 128
                    if t0 + 128 <= SEQ - kp:
                        continue
                    r0 = t0 - (SEQ - kp)
                    pi = 2 + blk % 2
                    for kc in range(8):
                        mm(ps[pi][:, :], xTb[:, kc, blk * 128:(blk + 1) * 128], WB[:, kc, :], kc == 0, kc == 7,
                           [xTb_b, ring_b[sB]], [ps_b[pi]], kc == 7)
                    S.op("act", lambda e, pi=pi, blk=blk: e.copy(out=kvo[:, blk % 2, :], in_=ps[pi][:, :]),
                         reads=[ps_b[pi]], writes=[kvo_b[blk % 2]])
                    S.dma("sp", lambda e, g=g, r0=r0, blk=blk: e.dma_start(out=o_swa[g][l, r0:r0 + 128, :],
                                                                           in_=kvo[:, blk % 2, :]), f"okv{blk % 2}",
                          reads=[kvo_b[blk % 2]], out=True)
            started = set()
            ptc = {"n": 0}

            def pv(pair, hl, vblk_ap, pt_ap, ptb, out_cols, vbuf):
                hp = slice(hl * 64, (hl + 1) * 64)
                first = (pair, hl) not in started
                started.add((pair, hl))
                mm(ps[4 + pair][hp, out_cols], vblk_ap, pt_ap, first, False, [vbuf, ptb], [ps_b[4 + pair]], False)
                mm(ps[6 + pair][hp, out_cols], ones_b64, pt_ap, first, False, [cb_b, ptb], [ps_b[6 + pair]], True)

            for h in range(4):
                pair, hl = h // 2, h % 2
                hp = slice(hl * 64, (hl + 1) * 64)
                for g in range(2):
                    for qb in range(4):
                        if g == 0:
                            qcols = slice(qb * 128, (qb + 1) * 128)
                            kbs = [(par * T + qb * 128, 1, par * 4 + qb, mb_same)]
                            if qb > 0:
                                kbs.append((par * T + (qb - 1) * 128, 1, par * 4 + qb - 1, mb_prev))
                            elif i > 0:
                                kbs.append(((1 - par) * T + 384, 1, (1 - par) * 4 + 3, mb_prev))
                        else:
                            qcols = slice(qb, T, 4)
                            kbs = [(par * T + qb, 4, qb * 2 + par, mb_same)]
                            if i > 0:
                                kbs.append(((1 - par) * T + qb, 4, qb * 2 + 1 - par, mb_prev))
                        k_ = ptc["n"] % 2
                        ptc["n"] += 1
                        pi = k_
                        for bi, (kc0, kst, vb, mbias) in enumerate(kbs):
                            kcols = slice(kc0, kc0 + 127 * kst + 1, kst)
                            mm(ps[pi][:, bi * 128:(bi + 1) * 128], kT_h[g][hp, pair, kcols], qT[hp, g, pair, qcols], True,
                               False, [kT_hb[g], q_b], [ps_b[pi]], False)
                            mm(ps[pi][:, bi * 128:(bi + 1) * 128], identb, mbias, False, True, [cb_b], [ps_b[pi]],
                               bi == len(kbs) - 1)
                        nb = len(kbs)
                        S.op("act", lambda e, k_=k_, pi=pi, nb=nb: e.activation(out=pt[:, k_, 0:nb * 128],
                                                                                in_=ps[pi][:, 0:nb * 128], func=AF.Exp,
                                                                                scale=0.125),
                             reads=[ps_b[pi]], writes=[pt_b[k_]])
                        for bi, (kc0, kst, vb, mbias) in enumerate(kbs):
                            pv(pair, hl, Vtm[g][:, vb, h * 64:(h + 1) * 64], pt[:, k_, bi * 128:(bi + 1) * 128], pt_b[k_],
                               qcols, Vtm_b[g])
                g = 2
                nkb = 2 if B_c >= 1 else 1
                for rg in range(4):
                    k_ = ptc["n"] % 2
                    ptc["n"] += 1
                    pi = k_
                    for kb in range(nkb):
                        Bk = B_c - kb
                        moff = 448 + (o_c // 32) * 256 + kb * 128
                        mm(ps[pi][:, kb * 128:(kb + 1) * 128], identb, cb[:, moff:moff + 128], True, False, [cb_b],
                           [ps_b[pi]], False)
                        for r4 in range(4):
                            rho = rg * 4 + r4
                            kc0 = 2048 * Bk + rho
                            kcols = slice(kc0, kc0 + 127 * 16 + 1, 16)
                            cc = kb * 128 + r4 * 32
                            mm(ps[pi][:, cc:cc + 32], kT_h[2][hp, pair, kcols], qT[hp, 2, pair, slice(rho, T, 16)], False,
                               r4 == 3, [kT_hb[2], q_b], [ps_b[pi]], r4 == 3 and kb == nkb - 1)
                    S.op("act", lambda e, k_=k_, pi=pi: e.activation(out=pt[:, k_, 0:nkb * 128], in_=ps[pi][:, 0:nkb * 128],
                                                                     func=AF.Exp, scale=0.125),
                         reads=[ps_b[pi]], writes=[pt_b[k_]])
                    for kb in range(nkb):
                        Bk = B_c - kb
                        for r4 in range(4):
                            rho = rg * 4 + r4
                            cc = kb * 128 + r4 * 32
                            pv(pair, hl, Vtm[2][:, rho * 2 + Bk, h * 64:(h + 1) * 64], pt[:, k_, cc:cc + 32], pt_b[k_],
                               slice(rho, T, 16), Vtm_b[2])
            for pair in range(2):
                S.op("dve", lambda e, pair=pair: e.reciprocal(out=rec[:, :], in_=ps[6 + pair][:, :]), reads=[ps_b[6 + pair]],
                     writes=[rec_b])
                S.op("dve", lambda e, pair=pair: e.tensor_mul(out=oTb[:, pair, :], in0=ps[4 + pair][:, :], in1=rec[:, :]),
                     reads=[ps_b[4 + pair], rec_b], writes=[oT_b])
            s_out = next_slab(SL_AOUT)
            g0 = next_slab(SL_G2)
            g1 = next_slab(SL_G2 + 1)
            merge_branch(l, 2, s_out, 2, oTb, oT_b, (g0, g1))
            S.barrier()

    def ffn_phase(l, i, last):
        with scope():
            def lt(name, shape, dt=F32):
                return AR.alloc(list(shape), dt), Buf(name)
            hb, hb_b = lt("hb", [128, 8, T], BF16)
            pre, pre_b = merged, merged_b
            with scope():
                mbf, mbf_b = lt("mbf", [128, 8, T], BF16)
                sq, _ = lt("fsq", [128, 2, T])
                sq_b = [Buf("fsq0"), Buf("fsq1")]
                tmp, tmp_b = lt("ftmp", [128, 1, T])
                es_local["sq"], es_local["sq_b"] = sq, sq_b
                S.op("act", lambda e: e.copy(out=mbf[:], in_=merged[:]), reads=[merged_b], writes=[mbf_b])
                wo = [next_slab(SL_WO), next_slab(SL_WO + 1)]
                for oc in range(8):
                    proj_fm(oc % 2, wo[oc // 4], (oc % 4) * 128, mbf, mbf_b)
                    S.op("dve", lambda e, oc=oc: e.scalar_tensor_tensor(out=pre[:, oc, :], in0=xT32[:, oc, :],
                                                                        scalar=float(ALPHA), in1=ps[oc % 2][:, :],
                                                                        op0=ALU.mult, op1=ALU.add),
                         reads=[xT32_b, ps_b[oc % 2]], writes=[pre_b])
                if STOP == "ffn_wo":
                    raise _Stop()
                ln_fm(l, pre, pre_b, 8, mean_d, "l1g", "l1b", LN_EPS, [(xT32, xT32_b), (hb, hb_b)], tmp, tmp_b)
                S.barrier()
                if STOP == "ffn_ln1":
                    raise _Stop()
            actT, actT_b = lt("actT", [128, 22, T], BF16)
            with scope():
                uext, _ = lt("fuext", [128, 2, T + 2])
                ue_b = [Buf("fue0"), Buf("fue1")]
                cv, _ = lt("fcv", [128, 2, T])
                cv_b = [Buf("fcv0"), Buf("fcv1")]
                sl, _ = lt("fsl", [128, 2, T])
                sl_b = [Buf("fsl0"), Buf("fsl1")]
                for s in range(11):
                    slot = next_slab(SL_FIN + s)
                    for jj in range(4):
                        qi = s * 4 + jj
                        k_ = qi % 2
                        proj_fm(k_, slot, jj * 128, hb, hb_b)
                        S.op("act", lambda e, k_=k_: e.copy(out=uext[:, k_, 2:T + 2], in_=ps[k_][:, :]), reads=[ps_b[k_]],
                             writes=[ue_b[k_]])
                        S.op("act", lambda e, k_=k_, qi=qi: e.copy(out=uext[:, k_, 0:2], in_=fcar[:, qi, :]),
                             reads=[fcar_b], writes=[ue_b[k_]])
                        S.op("act", lambda e, k_=k_, qi=qi: e.activation(out=cv[:, k_, :], in_=ps[k_][:, :], func=AF.Identity,
                                                                         scale=pcol(l, "fdw", qi * 3 + 2),
                                                                         bias=pcol(l, "fdb", qi)),
                             reads=[ps_b[k_], pp_b], writes=[cv_b[k_]])
                        S.op("act", lambda e, k_=k_, qi=qi: e.copy(out=fcar[:, qi, :], in_=uext[:, k_, T:T + 2]),
                             reads=[ue_b[k_]], writes=[fcar_b])
                        for tap in (1, 0):
                            S.op("dve", lambda e, k_=k_, qi=qi, tap=tap: e.scalar_tensor_tensor(
                                out=cv[:, k_, :], in0=uext[:, k_, tap:tap + T], scalar=pcol(l, "fdw", qi * 3 + tap),
                                in1=cv[:, k_, :], op0=ALU.mult, op1=ALU.add), reads=[ue_b[k_], pp_b, cv_b[k_]],
                                writes=[cv_b[k_]])
                        if jj < 2:
                            S.op("act", lambda e, k_=k_, jj=jj: e.activation(out=sl[:, jj, :], in_=cv[:, k_, :], func=AF.Silu),
                                 reads=[cv_b[k_]], writes=[sl_b[jj]])
                        else:
                            S.op("dve", lambda e, k_=k_, jj=jj, s=s: e.tensor_mul(out=actT[:, 2 * s + jj - 2, :],
                                                                                   in0=sl[:, jj - 2, :], in1=cv[:, k_, :]),
                                 reads=[sl_b[jj - 2], cv_b[k_]], writes=[actT_b])
                if last:
                    fc5 = fcar[:, :, :].rearrange("p (s hf jj) r -> p s hf jj r", s=11, hf=2, jj=2)
                    S.dma("sp", lambda e: [e.dma_start(
                        out=o_ffn[l, r_, hf * 2816:(hf + 1) * 2816].rearrange("(s jj p) -> p s jj", s=11, jj=2, p=128)[:, :, jj],
                        in_=fc5[:, :, hf, jj, r_], allow_slow_non_contiguous=True)
                        for hf in range(2) for r_ in range(2) for jj in range(2)],
                        "offn", reads=[fcar_b], n=8, out=True)
                S.barrier()
            if STOP == "ffn_in":
                raise _Stop()
            with scope():
                sq, _ = lt("fsq2", [128, 2, T])
                sq_b = [Buf("fsq20"), Buf("fsq21")]
                tmp, tmp_b = lt("ftmp2", [128, 1, T])
                es_local["sq"], es_local["sq_b"] = sq, sq_b
                for oc in range(8):
                    slot = next_slab(SL_FOUT + oc)
                    Wf = W(slot, 22, 128)
                    for kc in range(22):
                        mm(ps[oc % 2][:, :], Wf[:, kc, :], actT[:, kc, :], kc == 0, kc == 21, [ring_b[slot], actT_b],
                           [ps_b[oc % 2]], kc == 21)
                    S.op("dve", lambda e, oc=oc: e.scalar_tensor_tensor(out=pre[:, oc, :], in0=xT32[:, oc, :],
                                                                        scalar=float(ALPHA), in1=ps[oc % 2][:, :],
                                                                        op0=ALU.mult, op1=ALU.add),
                         reads=[xT32_b, ps_b[oc % 2]], writes=[pre_b])
                if STOP == "ffn_out":
                    raise _Stop()
                ln_fm(l, pre, pre_b, 8, mean_d, "l2g", "l2b", LN_EPS, [(pre, pre_b)], tmp, tmp_b)
                if l < DEPTH - 1:
                    S.dma("sp", lambda e: e.dma_start(out=xs_d[:, :, i * T:(i + 1) * T], in_=pre[:]), "xsst",
                          reads=[pre_b])
                else:
                    for blk in range(4):
                        for hf in range(2):
                            fm2tm_store(lambda c, blk=blk, hf=hf: (pre[:, hf * 4 + c, blk * 128:(blk + 1) * 128], pre_b), 4,
                                        128, lambda stg, blk=blk, hf=hf: (lambda e: e.dma_start(
                                            out=y_p[i * T + blk * 128:i * T + (blk + 1) * 128, hf * 512:(hf + 1) * 512],
                                            in_=stg[:, :])), "yout")
                S.barrier()


    if NS:
        xs32 = sb("xs32", [128, 8, NS])
        xs32_b = Buf("xs32")
        sel = cf[0:NS, 1408:1408 + NS * 128]
        ones_f64 = cf[:, 1920:1984]
        id64r = cf[:, 1344:1408]
        S.dma("sp", lambda e: e.dma_start(out=xs32[:], in_=xs_fm[:, :, :]), "xsload", writes=[xs32_b], local=False)
        for l in range(DEPTH):
            S.dma("sp", lambda e, l=l: e.dma_start(out=o_conv_s[l, :, 0:CONV_K - 2, :], in_=cv_nat[l, :, 1:CONV_K - 1, :]),
                  "d2d", out=True, local=False)
            S.dma("sp", lambda e, l=l: e.dma_start(out=o_ffn_s[l, :, 0, :], in_=ff_nat[l, :, 1, :]), "d2d", out=True,
                  local=False)
            for g in range(3):
                nb = NBUF[g]
                for s_ in range(NS):
                    S.dma("sp", lambda e, l=l, g=g, s_=s_, nb=nb: e.dma_start(out=o_swa_s[g][l, s_, 0:nb - 1, :],
                                                                           in_=cache_d[g][l, s_, 1:nb, :]),
                          "d2d", out=True, local=False)

    def sample_layer(l):
        N = NS
        with scope():
            def lt(name, shape, dt=F32):
                return AR.alloc(list(shape), dt), Buf(name)
            xsb, xsb_b = lt("xsb", [128, 8, N], BF16)
            mgs, mgs_b = lt("mgs", [128, 8, N])
            shs, shs_b = lt("shs", [128, 14, N])
            zrs, zrs_b = lt("zrs", [128, 14, N])
            S.op("act", lambda e: e.copy(out=xsb[:], in_=xs32[:]), reads=[xs32_b], writes=[xsb_b])
            S.dma("sp", lambda e: e.dma_start(out=shs[:], in_=sh_fm[l, :, :, :]), "sld0", writes=[shs_b])

            def sshift(pi, och, dst, dst_b):
                S.op("act", lambda e: e.copy(out=zrs[:, och, :], in_=ps[pi][:, 0:N]), reads=[ps_b[pi]], writes=[zrs_b])
                S.op("dve", lambda e: e.tensor_scalar(out=dst, in0=ps[pi][:, 0:N], scalar1=pcol(l, "omm", och), scalar2=None,
                                                      op0=ALU.mult), reads=[ps_b[pi], pp_b], writes=[dst_b])
                S.op("dve", lambda e: e.scalar_tensor_tensor(out=dst, in0=shs[:, och, :], scalar=pcol(l, "mu", och), in1=dst,
                                                             op0=ALU.mult, op1=ALU.add), reads=[shs_b, pp_b, dst_b],
                     writes=[dst_b])

            def sproj(pi, slot, col0, wcols=512):
                proj_fm(pi, slot, col0, xsb, xsb_b, ncol=N, wcols=wcols, rhs_cols=slice(0, N))

            with scope():
                lora, lora_b = lt("slora", [128, 2, N])
                ozs, ozs_b = lt("ozs", [128, 4, N], BF16)
                Hs, Hs_b = lt("Hs", [128, N, 4, 64])
                S.dma("sp", lambda e: e.dma_start(out=Hs[:], in_=wkv_fm[l, :, :, :, :]), "sld1", writes=[Hs_b])
                slot = next_slab(SL_LORA)
                for c in range(2):
                    sproj(c, slot, c * 128, wcols=256)
                    sshift(c, 12 + c, lora[:, c, :], lora_b)
                S.op("act", lambda e: e.activation(out=lora[0:64, 0, :], in_=lora[0:64, 0, :], func=AF.Tanh),
                     reads=[lora_b], writes=[lora_b])
                S.op("act", lambda e: e.activation(out=lora[:, 1, :], in_=lora[:, 1, :], func=AF.Sigmoid),
                     reads=[lora_b], writes=[lora_b])
                for p in range(4):
                    with scope():
                        rT, r_b = lt("srT", [128, N])
                        kT, k_b = lt("skT", [128, N])
                        vT, v_b = lt("svT", [128, N])
                        wT, w_b = lt("swT", [128, N])
                        aT, a_b = lt("saT", [128, N])
                        gT, g_b = lt("sgT", [128, N])
                        kkT, kk_b = lt("skkT", [128, N])
                        bT, b_b = lt("sbT", [128, N])
                        eT, e_b = lt("seT", [128, N])
                        bon, bon_b = lt("sbon", [128, N])
                        OT, OT_b = lt("sOT", [128, N])
                        t1, t1_b = lt("st1", [128, 64])
                        t2, t2_b = lt("st2", [128, 64])
                        slot = next_slab(SL_RP + p)
                        for c, (dst, db, och) in enumerate(((rT, r_b, p), (kT, k_b, 4 + p), (vT, v_b, 8 + p))):
                            sproj(c % 2, slot, c * 128, wcols=384)
                            sshift(c % 2, och, dst[:, :], db)
                        cs128 = slice(p * 128, (p + 1) * 128)
                        mm(ps[2][:, 0:N], lw[0:64, l, cs128], lora[0:64, 0, :], True, True, [lw_b, lora_b], [ps_b[2]], True)
                        mm(ps[3][:, 0:N], lw[64:128, l, cs128], lora[64:128, 0, :], True, True, [lw_b, lora_b], [ps_b[3]], True)
                        mm(ps[4][:, 0:N], lw[:, l, 512 + p * 128:512 + (p + 1) * 128], lora[:, 1, :], True, True,
                           [lw_b, lora_b], [ps_b[4]], True)
                        S.op("act", lambda e: e.activation(out=eT[:, :], in_=ps[2][:, 0:N], func=AF.Exp, scale=-1.0,
                                                           bias=pcol(l, "nw0", p)), reads=[ps_b[2], pp_b], writes=[e_b])
                        S.op("act", lambda e: e.activation(out=eT[:, :], in_=eT[:, :], func=AF.Ln, bias=1.0), reads=[e_b],
                             writes=[e_b])
                        S.op("act", lambda e: e.activation(out=eT[:, :], in_=eT[:, :], func=AF.Exp, scale=-1.0, bias=-0.5),
                             reads=[e_b], writes=[e_b])
                        S.op("act", lambda e: e.activation(out=wT[:, :], in_=eT[:, :], func=AF.Exp, scale=-1.0),
                             reads=[e_b], writes=[w_b])
                        S.op("act", lambda e: e.activation(out=aT[:, :], in_=ps[3][:, 0:N], func=AF.Sigmoid,
                                                           bias=pcol(l, "a0", p)), reads=[ps_b[3], pp_b], writes=[a_b])
                        S.op("act", lambda e: e.copy(out=gT[:, :], in_=ps[4][:, 0:N]), reads=[ps_b[4]], writes=[g_b])
                        S.op("dve", lambda e: e.tensor_scalar(out=kkT[:, :], in0=kT[:, :], scalar1=pcol(l, "kk", p),
                                                              scalar2=None, op0=ALU.mult), reads=[k_b, pp_b], writes=[kk_b])
                        S.op("dve", lambda e: e.tensor_mul(out=eT[:, :], in0=kkT[:, :], in1=kkT[:, :]), reads=[kk_b],
                             writes=[e_b])
                        mm(ps[2][:, 0:N], ones_blk, eT[:, :], True, True, [cf_b, e_b], [ps_b[2]], True)
                        rsqrt(eT[:, :], ps[2][:, 0:N], 1e-24, [ps_b[2]], e_b)
                        S.op("dve", lambda e: e.tensor_mul(out=kkT[:, :], in0=kkT[:, :], in1=eT[:, :]), reads=[kk_b, e_b],
                             writes=[kk_b])
                        S.op("dve", lambda e: e.tensor_mul(out=bT[:, :], in0=kkT[:, :], in1=aT[:, :]), reads=[kk_b, a_b],
                             writes=[b_b])
                        S.op("dve", lambda e: e.tensor_scalar(out=aT[:, :], in0=aT[:, :], scalar1=-1.0,
                                                              scalar2=pcol(l, "ka", p), op0=ALU.add, op1=ALU.mult),
                             reads=[a_b, pp_b], writes=[a_b])
                        S.op("dve", lambda e: e.scalar_tensor_tensor(out=kT[:, :], in0=aT[:, :], scalar=1.0, in1=kT[:, :],
                                                                     op0=ALU.add, op1=ALU.mult), reads=[a_b, k_b],
                             writes=[k_b])
                        S.op("dve", lambda e: e.scalar_tensor_tensor(out=eT[:, :], in0=rT[:, :], scalar=pcol(l, "rk", p),
                                                                     in1=kT[:, :], op0=ALU.mult, op1=ALU.mult),
                             reads=[r_b, k_b, pp_b], writes=[e_b])
                        mm(ps[3][:, 0:N], ones_blk, eT[:, :], True, True, [cf_b, e_b], [ps_b[3]], True)
                        S.op("dve", lambda e: e.tensor_mul(out=bon[:, :], in0=ps[3][:, 0:N], in1=vT[:, :]),
                             reads=[ps_b[3], v_b], writes=[bon_b])
                        S.op("dve", lambda e: e.tensor_scalar(out=bT[:, :], in0=bT[:, :], scalar1=-1.0, scalar2=None,
                                                              op0=ALU.mult), reads=[b_b], writes=[b_b])
                        for s_ in range(N):
                            H0 = Hs[:, s_, p, :]
                            sc = slice(s_, s_ + 1)
                            S.op("dve", lambda e, H0=H0, sc=sc: e.tensor_scalar(out=t1[:, :], in0=H0, scalar1=kkT[:, sc],
                                                                               scalar2=None, op0=ALU.mult),
                                 reads=[Hs_b, kk_b], writes=[t1_b])
                            S.op("dve", lambda e, sc=sc: e.tensor_scalar(out=t2[:, :], in0=id64r, scalar1=vT[:, sc],
                                                                        scalar2=None, op0=ALU.mult),
                                 reads=[cf_b, v_b], writes=[t2_b])
                            mm(ps[0][:, 0:64], ones_blk, t1[:, :], True, True, [cf_b, t1_b], [ps_b[0]], True)
                            mm(ps[1][:, 0:64], ones_blk, t2[:, :], True, True, [cf_b, t2_b], [ps_b[1]], True)
                            S.op("dve", lambda e, H0=H0, sc=sc: e.tensor_scalar(out=H0, in0=H0, scalar1=wT[:, sc], scalar2=None,
                                                                               op0=ALU.mult), reads=[Hs_b, w_b],
                                 writes=[Hs_b])
                            S.op("dve", lambda e, H0=H0, sc=sc: e.scalar_tensor_tensor(out=H0, in0=ps[0][:, 0:64],
                                                                                      scalar=bT[:, sc], in1=H0,
                                                                                      op0=ALU.mult, op1=ALU.add),
                                 reads=[ps_b[0], b_b, Hs_b], writes=[Hs_b])
                            S.op("dve", lambda e, H0=H0, sc=sc: e.scalar_tensor_tensor(out=H0, in0=ps[1][:, 0:64],
                                                                                      scalar=kT[:, sc], in1=H0,
                                                                                      op0=ALU.mult, op1=ALU.add),
                                 reads=[ps_b[1], k_b, Hs_b], writes=[Hs_b])
                            for hl in range(2):
                                hp = slice(hl * 64, (hl + 1) * 64)
                                mm(ps[4 + hl][hp, s_:s_ + 1], Hs[hp, s_, p, :], rT[hp, sc], True, True, [Hs_b, r_b],
                                   [ps_b[4 + hl]], True)
                        for hl in range(2):
                            hp = slice(hl * 64, (hl + 1) * 64)
                            S.op("act", lambda e, hp=hp, hl=hl: e.copy(out=OT[hp, :], in_=ps[4 + hl][hp, 0:N]),
                                 reads=[ps_b[4 + hl]], writes=[OT_b])
                        mm(ps[0][:, 0:N], mean_blk, OT[:, :], True, True, [cf_b, OT_b], [ps_b[0]], True)
                        S.op("dve", lambda e: e.tensor_sub(out=OT[:, :], in0=OT[:, :], in1=ps[0][:, 0:N]),
                             reads=[OT_b, ps_b[0]], writes=[OT_b])
                        S.op("act", lambda e: e.activation(out=eT[:, :], in_=OT[:, :], func=AF.Square), reads=[OT_b],
                             writes=[e_b])
                        mm(ps[1][:, 0:N], mean_blk, eT[:, :], True, True, [cf_b, e_b], [ps_b[1]], True)
                        rsqrt(eT[:, :], ps[1][:, 0:N], float(LNX_EPS), [ps_b[1]], e_b)
                        S.op("dve", lambda e: e.tensor_mul(out=OT[:, :], in0=OT[:, :], in1=eT[:, :]), reads=[OT_b, e_b],
                             writes=[OT_b])
                        S.op("dve", lambda e: e.tensor_scalar(out=OT[:, :], in0=OT[:, :], scalar1=pcol(l, "lng", p),
                                                              scalar2=pcol(l, "lnb", p), op0=ALU.mult, op1=ALU.add),
                             reads=[OT_b, pp_b], writes=[OT_b])
                        S.op("dve", lambda e: e.tensor_add(out=OT[:, :], in0=OT[:, :], in1=bon[:, :]), reads=[OT_b, bon_b],
                             writes=[OT_b])
                        S.op("dve", lambda e: e.tensor_mul(out=ozs[:, p, :], in0=OT[:, :], in1=gT[:, :]), reads=[OT_b, g_b],
                             writes=[ozs_b])
                        S.barrier()
                s_out = next_slab(SL_ROUT)
                g0 = next_slab(SL_G0)
                g1 = next_slab(SL_G0 + 1)
                merge_branch(l, 0, s_out, 4, ozs, ozs_b, (g0, g1), N=N, xb=xsb, xb_b=xsb_b, mg=mgs, mg_b=mgs_b)
                S.dma("sp", lambda e: [e.dma_start(out=o_shift_s[l, s_, :].rearrange("(c p) -> p c", p=128), in_=zrs[:, :, s_],
                                                   allow_slow_non_contiguous=True) for s_ in range(N)], "sosh",
                      reads=[zrs_b], n=N, out=True)
                for s_ in range(N):
                    with scope_alloc([64, 8, 64], F32) as stg:
                        stg_b = Buf("swkvst")
                        for h in range(8):
                            pr_, hl = h // 2, h % 2
                            S.op("pe", lambda e, pr_=pr_, hl=hl, s_=s_: e.matmul(
                                ps[4 + hl][0:64, pr_ * 64:(pr_ + 1) * 64], lhsT=Hs[hl * 64:(hl + 1) * 64, s_, pr_, :],
                                rhs=cf[hl * 64:(hl + 1) * 64, 1344:1408], start=True, stop=True), reads=[Hs_b, cf_b],
                                writes=[ps_b[4 + hl]])
                        stg4 = stg[:].rearrange("p (pr hl) k -> p pr hl k", hl=2)
                        for hl in range(2):
                            S.op("act", lambda e, hl=hl: e.copy(out=stg4[:, :, hl, :],
                                                                in_=ps[4 + hl][0:64, 0:256].rearrange("p (pr k) -> p pr k", k=64)),
                                 reads=[ps_b[4 + hl]], writes=[stg_b])
                        S.dma("sp", lambda e, s_=s_: e.dma_start(out=o_wkv_s[l, s_].rearrange("h v k -> v h k"), in_=stg[:]),
                              "sowkv", reads=[stg_b], out=True)
                        S.barrier()
                S.barrier()

            with scope():
                stc, stc_b = lt("stc", [128, 4, N, 30])
                uS, uS_b = lt("uS", [128, 4, N])
                acc, acc_b = lt("sacc", [128, 4, N])
                prod, prod_b = lt("sprod", [128, N, 30])
                sg, sg_b = lt("ssg", [128, N])
                cTb, cT_b = lt("scTb", [128, 4, N], BF16)
                sq, _ = lt("ssq", [128, 2, N])
                sq_b = [Buf("ssq0"), Buf("ssq1")]
                tmp, tmp_b = lt("stmp", [128, 1, N])
                es_local["sq"], es_local["sq_b"] = sq, sq_b
                S.dma("sp", lambda e: e.dma_start(out=stc[:], in_=cv_fm[l, :, :, :, :]), "sld2", writes=[stc_b])
                su = next_slab(SL_CU)
                sgt = next_slab(SL_CG)
                for c in range(4):
                    sproj(0, su, c * 128)
                    sproj(1, sgt, c * 128)
                    S.op("act", lambda e: e.activation(out=sg[:, :], in_=ps[1][:, 0:N], func=AF.Sigmoid), reads=[ps_b[1]],
                         writes=[sg_b])
                    S.op("dve", lambda e, c=c: e.tensor_mul(out=uS[:, c, :], in0=ps[0][:, 0:N], in1=sg[:, :]),
                         reads=[ps_b[0], sg_b], writes=[uS_b])
                    wv = pp[:, l, PC["cdw"] + c * 31:PC["cdw"] + c * 31 + 30]
                    S.op("dve", lambda e, c=c, wv=wv: e.tensor_mul(out=prod[:], in0=stc[:, c, :, :],
                                                                   in1=wv.unsqueeze(1).broadcast_to([128, N, 30])),
                         reads=[stc_b, pp_b], writes=[prod_b])
                    S.op("dve", lambda e, c=c: e.tensor_reduce(out=acc[:, c, :], in_=prod[:], axis=AX.X, op=ALU.add),
                         reads=[prod_b], writes=[acc_b])
                    S.op("dve", lambda e, c=c: e.scalar_tensor_tensor(out=acc[:, c, :], in0=uS[:, c, :],
                                                                      scalar=pcol(l, "cdw", c * 31 + 30), in1=acc[:, c, :],
                                                                      op0=ALU.mult, op1=ALU.add),
                         reads=[uS_b, pp_b, acc_b], writes=[acc_b])
                    S.op("dve", lambda e, c=c: e.tensor_scalar(out=acc[:, c, :], in0=acc[:, c, :], scalar1=pcol(l, "cdb", c),
                                                               scalar2=None, op0=ALU.add), reads=[acc_b, pp_b],
                         writes=[acc_b])
                ln_fm(l, acc, acc_b, 4, mean_c, "clg", "clb", LN_EPS, [(acc, acc_b)], tmp, tmp_b, N=N)
                for c in range(4):
                    S.op("act", lambda e, c=c: e.activation(out=cTb[:, c, :], in_=acc[:, c, :], func=AF.Silu),
                         reads=[acc_b], writes=[cT_b])
                s_out = next_slab(SL_COUT)
                g0 = next_slab(SL_G1)
                g1 = next_slab(SL_G1 + 1)
                merge_branch(l, 1, s_out, 4, cTb, cT_b, (g0, g1), N=N, xb=xsb, xb_b=xsb_b, mg=mgs, mg_b=mgs_b)
                fm2tm_store(lambda c: (uS[:, c, :], uS_b), 4, N,
                            lambda stg: (lambda e: e.dma_start(out=o_conv_s[l, :, CONV_K - 2, :], in_=stg[0:N, 0:512])),
                            "soconv")
                S.barrier()

            with scope():
                qT, q_b = lt("sqT", [128, 2, N])
                kT, k_b = lt("skT2", [128, 2, N])
                vT, v_b = lt("svT2", [128, 2, N])
                Qtm, Qtm_b = lt("sQtm", [N, 256])
                Kt, _ = lt("sKt", [128, 2, 256])
                Kt_b = [Buf("sKt0"), Buf("sKt1")]
                Vt, _ = lt("sVt", [128, 2, 256])
                Vt_b = [Buf("sVt0"), Buf("sVt1")]
                prod, prod_b = lt("saprod", [128, 256])
                sc4, sc4_b = lt("ssc4", [128, 4])
                pself, pself_b = lt("spself", [128, 2, N])
                numS, numS_b = lt("snumS", [128, 2, N])
                denS, denS_b = lt("sdenS", [128, 2, N])
                oTs, oTs_b = lt("soTs", [128, 2, N], BF16)
                S.op("dve", lambda e: e.memset(numS[:], 0.0), writes=[numS_b])
                S.op("dve", lambda e: e.memset(denS[:], 0.0), writes=[denS_b])
                started = set()
                kvn = 0
                for g in range(3):
                    win, dil = SWA_GROUPS[g]
                    nb = NBUF[g]
                    sA = next_slab(SL_ATT + 2 * g)
                    sB = next_slab(SL_ATT + 2 * g + 1)
                    for c in range(4):
                        sproj(c % 2, sA, c * 128)
                        dst, db = (qT, q_b) if c < 2 else (kT, k_b)
                        S.op("act", lambda e, c=c, dst=dst: e.copy(out=dst[:, c % 2, :], in_=ps[c % 2][:, 0:N]),
                             reads=[ps_b[c % 2]], writes=[db])
                    for c in range(2):
                        sproj(2 + c, sB, 256 + c * 128)
                        S.op("act", lambda e, c=c: e.copy(out=vT[:, c, :], in_=ps[2 + c][:, 0:N]), reads=[ps_b[2 + c]],
                             writes=[v_b])
                    fm2tm_store(lambda c: ((kT[:, c, :], k_b) if c < 2 else (vT[:, c - 2, :], v_b)), 4, N,
                                lambda stg, g=g, nb=nb: (lambda e: e.dma_start(out=o_swa_s[g][l, :, nb - 1, :],
                                                                                in_=stg[0:N, 0:512])), "soswa")
                    S.op("dve", lambda e: e.tensor_mul(out=pself[:], in0=qT[:], in1=kT[:]), reads=[q_b, k_b],
                         writes=[pself_b])
                    mm(ps[0][:, 0:2 * N], ones_blk, pself[:].rearrange("p a n -> p (a n)"), True, True, [cf_b, pself_b],
                       [ps_b[0]], True)
                    S.op("act", lambda e: e.activation(out=pself[:].rearrange("p a n -> p (a n)"), in_=ps[0][:, 0:2 * N],
                                                       func=AF.Exp, scale=0.125), reads=[ps_b[0]], writes=[pself_b])
                    S.op("dve", lambda e: e.tensor_add(out=denS[:], in0=denS[:], in1=pself[:]), reads=[denS_b, pself_b],
                         writes=[denS_b])
                    S.op("dve", lambda e: e.tensor_mul(out=pself[:], in0=pself[:], in1=vT[:]), reads=[pself_b, v_b],
                         writes=[pself_b])
                    S.op("dve", lambda e: e.tensor_add(out=numS[:], in0=numS[:], in1=pself[:]), reads=[numS_b, pself_b],
                         writes=[numS_b])
                    for c in range(2):
                        S.op("pe", lambda e, c=c: e.matmul(ps[1][0:N, c * 128:(c + 1) * 128], lhsT=qT[:, c, :], rhs=ident,
                                                           start=True, stop=True), reads=[q_b, cf_b], writes=[ps_b[1]])
                    S.op("act", lambda e: e.copy(out=Qtm[:, :], in_=ps[1][0:N, 0:256]), reads=[ps_b[1]], writes=[Qtm_b])
                    for s_ in range(N):
                        k_ = kvn % 2
                        kvn += 1
                        S.dma("sp", lambda e, k_=k_, s_=s_, g=g, dil=dil, nb=nb: e.dma_start(
                            out=Kt[:, k_, :], in_=cache_d[g][l, s_, slice(0, nb, dil), 0:256]), f"sk{k_}",
                            writes=[Kt_b[k_]])
                        S.dma("sp", lambda e, k_=k_, s_=s_, g=g, dil=dil, nb=nb: e.dma_start(
                            out=Vt[:, k_, :], in_=cache_d[g][l, s_, slice(0, nb, dil), 256:512]), f"sv{k_}",
                            writes=[Vt_b[k_]])
                        mm(ps[2][:, 0:256], sel[:, s_ * 128:(s_ + 1) * 128], Qtm[:, :], True, True, [cf_b, Qtm_b],
                           [ps_b[2]], True)
                        S.op("dve", lambda e, k_=k_: e.tensor_mul(out=prod[:, :], in0=Kt[:, k_, :], in1=ps[2][:, 0:256]),
                             reads=[Kt_b[k_], ps_b[2]], writes=[prod_b])
                        S.op("dve", lambda e: e.tensor_reduce(out=sc4[:, :], in_=prod[:, :].rearrange("p (h e) -> p h e", e=64),
                                                              axis=AX.X, op=ALU.add), reads=[prod_b], writes=[sc4_b])
                        S.op("act", lambda e: e.activation(out=sc4[:, :], in_=sc4[:, :], func=AF.Exp, scale=0.125),
                             reads=[sc4_b], writes=[sc4_b])
                        for h in range(4):
                            pair, hl = h // 2, h % 2
                            hp = slice(hl * 64, (hl + 1) * 64)
                            col = pair * N + s_
                            first = (hl,) not in started
                            started.add((hl,))
                            mm(ps[4][hp, col:col + 1], Vt[:, k_, h * 64:(h + 1) * 64], sc4[:, h:h + 1], first, False,
                               [Vt_b[k_], sc4_b], [ps_b[4]], False)
                            mm(ps[5][hp, col:col + 1], ones_f64, sc4[:, h:h + 1], first, False, [cf_b, sc4_b], [ps_b[5]],
                               True)
                S.op("dve", lambda e: e.tensor_add(out=denS[:].rearrange("p a n -> p (a n)"),
                                                   in0=denS[:].rearrange("p a n -> p (a n)"), in1=ps[5][:, 0:2 * N]),
                     reads=[denS_b, ps_b[5]], writes=[denS_b])
                S.op("dve", lambda e: e.tensor_add(out=numS[:].rearrange("p a n -> p (a n)"),
                                                   in0=numS[:].rearrange("p a n -> p (a n)"), in1=ps[4][:, 0:2 * N]),
                     reads=[numS_b, ps_b[4]], writes=[numS_b])
                S.op("dve", lambda e: e.reciprocal(out=denS[:], in_=denS[:]), reads=[denS_b], writes=[denS_b])
                S.op("dve", lambda e: e.tensor_mul(out=oTs[:], in0=numS[:], in1=denS[:]), reads=[numS_b, denS_b],
                     writes=[oTs_b])
                s_out = next_slab(SL_AOUT)
                g0 = next_slab(SL_G2)
                g1 = next_slab(SL_G2 + 1)
                merge_branch(l, 2, s_out, 2, oTs, oTs_b, (g0, g1), N=N, xb=xsb, xb_b=xsb_b, mg=mgs, mg_b=mgs_b)
                S.barrier()

            with scope():
                hb, hb_b = lt("shb", [128, 8, N], BF16)
                mbf, mbf_b = lt("smbf", [128, 8, N], BF16)
                sq, _ = lt("sfsq", [128, 2, N])
                sq_b = [Buf("sfsq0"), Buf("sfsq1")]
                tmp, tmp_b = lt("sftmp", [128, 1, N])
                stf, stf_b = lt("stf", [128, 44, N, 2])
                urw, urw_b = lt("surw", [128, 44, N])
                cv, _ = lt("sfcv", [128, 2, N])
                cv_b = [Buf("sfcv0"), Buf("sfcv1")]
                sl, _ = lt("sfsl", [128, 2, N])
                sl_b = [Buf("sfsl0"), Buf("sfsl1")]
                actT, actT_b = lt("sactT", [128, 22, N], BF16)
                es_local["sq"], es_local["sq_b"] = sq, sq_b
                S.dma("sp", lambda e: e.dma_start(out=stf[:], in_=ff_fm[l, :, :, :, :]), "sld3", writes=[stf_b])
                S.op("act", lambda e: e.copy(out=mbf[:], in_=mgs[:]), reads=[mgs_b], writes=[mbf_b])
                wo = [next_slab(SL_WO), next_slab(SL_WO + 1)]
                for oc in range(8):
                    proj_fm(oc % 2, wo[oc // 4], (oc % 4) * 128, mbf, mbf_b, ncol=N, rhs_cols=slice(0, N))
                    S.op("dve", lambda e, oc=oc: e.scalar_tensor_tensor(out=mgs[:, oc, :], in0=xs32[:, oc, :],
                                                                        scalar=float(ALPHA), in1=ps[oc % 2][:, 0:N],
                                                                        op0=ALU.mult, op1=ALU.add),
                         reads=[xs32_b, ps_b[oc % 2], mgs_b], writes=[mgs_b])
                ln_fm(l, mgs, mgs_b, 8, mean_d, "l1g", "l1b", LN_EPS, [(xs32, xs32_b), (hb, hb_b)], tmp, tmp_b, N=N)
                for s in range(11):
                    slot = next_slab(SL_FIN + s)
                    for jj in range(4):
                        qi = s * 4 + jj
                        k_ = qi % 2
                        proj_fm(k_, slot, jj * 128, hb, hb_b, ncol=N, rhs_cols=slice(0, N))
                        S.op("act", lambda e, k_=k_, qi=qi: e.copy(out=urw[:, qi, :], in_=ps[k_][:, 0:N]), reads=[ps_b[k_]],
                             writes=[urw_b])
                        S.op("dve", lambda e, k_=k_, qi=qi: e.tensor_scalar(out=cv[:, k_, :], in0=ps[k_][:, 0:N],
                                                                            scalar1=pcol(l, "fdw", qi * 3 + 2),
                                                                            scalar2=pcol(l, "fdb", qi), op0=ALU.mult,
                                                                            op1=ALU.add),
                             reads=[ps_b[k_], pp_b], writes=[cv_b[k_]])
                        for tap in (1, 0):
                            S.op("dve", lambda e, k_=k_, qi=qi, tap=tap: e.scalar_tensor_tensor(
                                out=cv[:, k_, :], in0=stf[:, qi, :, tap], scalar=pcol(l, "fdw", qi * 3 + tap),
                                in1=cv[:, k_, :], op0=ALU.mult, op1=ALU.add), reads=[stf_b, pp_b, cv_b[k_]],
                                writes=[cv_b[k_]])
                        if jj < 2:
                            S.op("act", lambda e, k_=k_, jj=jj: e.activation(out=sl[:, jj, :], in_=cv[:, k_, :], func=AF.Silu),
                                 reads=[cv_b[k_]], writes=[sl_b[jj]])
                        else:
                            S.op("dve", lambda e, k_=k_, jj=jj, s=s: e.tensor_mul(out=actT[:, 2 * s + jj - 2, :],
                                                                                   in0=sl[:, jj - 2, :], in1=cv[:, k_, :]),
                                 reads=[sl_b[jj - 2], cv_b[k_]], writes=[actT_b])
                ur5 = urw[:, :, :].rearrange("p (s hf jj) n -> p s hf jj n", s=11, hf=2, jj=2)
                S.dma("sp", lambda e: [e.dma_start(
                    out=o_ffn_s[l, s_, 1, hf * 2816:(hf + 1) * 2816].rearrange("(s jj p) -> p s jj", s=11, jj=2, p=128)[:, :, jj],
                    in_=ur5[:, :, hf, jj, s_], allow_slow_non_contiguous=True)
                    for s_ in range(N) for hf in range(2) for jj in range(2)], "soffn", reads=[urw_b], n=4 * N, out=True)
                for oc in range(8):
                    slot = next_slab(SL_FOUT + oc)
                    Wf = W(slot, 22, 128)
                    for kc in range(22):
                        mm(ps[oc % 2][:, 0:N], Wf[:, kc, :], actT[:, kc, :], kc == 0, kc == 21, [ring_b[slot], actT_b],
                           [ps_b[oc % 2]], kc == 21)
                    S.op("dve", lambda e, oc=oc: e.scalar_tensor_tensor(out=mgs[:, oc, :], in0=xs32[:, oc, :],
                                                                        scalar=float(ALPHA), in1=ps[oc % 2][:, 0:N],
                                                                        op0=ALU.mult, op1=ALU.add),
                         reads=[xs32_b, ps_b[oc % 2], mgs_b], writes=[mgs_b])
                ln_fm(l, mgs, mgs_b, 8, mean_d, "l2g", "l2b", LN_EPS, [(xs32, xs32_b)], tmp, tmp_b, N=N)
                if l == DEPTH - 1:
                    for hf in range(2):
                        fm2tm_store(lambda c, hf=hf: (xs32[:, hf * 4 + c, :], xs32_b), 4, N,
                                    lambda stg, hf=hf: (lambda e: e.dma_start(out=y_s[:, hf * 512:(hf + 1) * 512],
                                                                              in_=stg[0:N, 0:512])), "sy")
                S.barrier()
            S.barrier()

    for l in range(DEPTH):
        S.op("dve", lambda e: e.memset(Hst[:], 0.0), writes=[Hst_b])
        S.op("dve", lambda e: e.memset(zcar[:], 0.0), writes=[zcar_b])
        S.op("dve", lambda e: e.memset(uhist[:], 0.0), writes=[uhist_b])
        S.op("dve", lambda e: e.memset(fcar[:], 0.0), writes=[fcar_b])
        if l > 0 and not STOP:
            S.wait_only("sp", [(S.dkeys["xsst"][0], S.dkeys["xsst"][1], "dma")])
        cur_layer["l"] = l
        if NS:
            sample_layer(l)
        for i in range(NT):
            if STOP and (l, i) != (0, 0):
                continue
            try:
                tile_prog(l, i)
            except _Stop:
                pass

    S.wait_only("sp", list(S.out_toks))
    with nc.Block() as block:
        S.emit(block)
    es.close()
    return nc


_NC_CACHE = {}


def kernel(**inputs):
    inp = {k: np.asarray(v) for k, v in inputs.items()}
    BATCH, SEQ, _ = inp["x_prompt"].shape
    NSAMP_ALL = inp["x_sample"].shape[0]
    NCORE = 8
    NS = NSAMP_ALL // NCORE
    key = (SEQ, NS)
    if key not in _NC_CACHE:
        _NC_CACHE[key] = build(SEQ, NSAMP=NS)
    nc = _NC_CACHE[key]
    wp, pp, lw = pack_weights(inp)
    cf, cb = make_consts()
    in_maps = []
    for c in range(NCORE):
        b = c % BATCH
        x = inp["x_prompt"][b]
        x_fm = np.ascontiguousarray(x.T.reshape(8, 128, SEQ).transpose(1, 0, 2))
        im = {"x_fm": x_fm, "wpack": wp, "ppack": pp, "lorapack": lw, "constf": cf, "constb": cb}
        im.update(pack_samples(inp, c * NS, NS))
        in_maps.append(im)
    res = run_bass_kernel_spmd(nc, in_maps, core_ids=list(range(NCORE)))
    R_ = res.results
    f32 = np.float32

    def pstack(name, shape_tail):
        return np.stack([np.asarray(R_[b][name], f32).reshape((DEPTH,) + shape_tail) for b in range(BATCH)], axis=1)

    def sstack(name, shape_tail):
        return np.concatenate([np.asarray(R_[c][name], f32).reshape((DEPTH, NS) + shape_tail) for c in range(NCORE)],
                              axis=1)
    y_prompt = np.stack([np.asarray(R_[b]["y_p"], f32) for b in range(BATCH)], axis=0)
    y_sample = np.concatenate([np.asarray(R_[c]["y_s"], f32) for c in range(NCORE)], axis=0)[:, None, :]
    keep = [min(w, SEQ) for w, _ in SWA_GROUPS]
    outs = [y_prompt, y_sample,
            pstack("o_shift_p", (1, R_COLS)), sstack("o_shift_s", (1, R_COLS)),
            pstack("o_wkv_p", (R_HEADS, 64, 64)), sstack("o_wkv_s", (R_HEADS, 64, 64)),
            pstack("o_conv_p", (CONV_K - 1, CONV_CH)), sstack("o_conv_s", (CONV_K - 1, CONV_CH))]
    for g in range(3):
        outs.append(pstack(f"o_swa{g}_p", (keep[g], 2, G_HEADS, HEAD_DIM)))
        outs.append(sstack(f"o_swa{g}_s", (NBUF[g], 2, G_HEADS, HEAD_DIM)))
    outs.append(pstack("o_ffn_p", (2, 2 * D_FF)))
    outs.append(sstack("o_ffn_s", (2, 2 * D_FF)))
    return tuple(np.ascontiguousarray(o, dtype=f32) for o in outs)
```

```python
import numpy as np
from contextlib import ExitStack, contextmanager
import concourse.bass as bass
import concourse.mybir as mybir
from concourse.bass_utils import run_bass_kernel_spmd

F32 = mybir.dt.float32
BF16 = mybir.dt.bfloat16
AF = mybir.ActivationFunctionType
ALU = mybir.AluOpType
AX = mybir.AxisListType

D_MODEL = 1024
DEPTH = 2
HEAD_DIM = 64
R_HEADS = 8
R_WIDTH = 512
R_COLS = 1792
LNX_EPS = 64e-5
CONV_CH = 512
CONV_K = 31
SWA_GROUPS = ((128, 1), (512, 4), (2048, 16))
G_HEADS = 4
A_WIDTH = 768
D_FF = 2816
OFF_R = 0
OFF_C = 1792
OFF_Q = OFF_C + 1024
OFF_K = OFF_Q + 768
OFF_V = OFF_K + 768
OFF_GATE = OFF_V + 768
IN_COLS = 8192
ALPHA = (2 * DEPTH) ** 0.25
LN_EPS = 1e-5
NEG = -30000.0
NBUF = (128, 512, 2048)

P = 128
T = 512
C = 64
NCH = T // C
NSLOT = 5
SLAB = 4096
NSLAB = 42
SL_LORA = 0
SL_RP = 1
SL_ROUT = 5
SL_G0 = 6
SL_CU = 8
SL_CG = 9
SL_COUT = 10
SL_G1 = 11
SL_ATT = 13
SL_AOUT = 19
SL_G2 = 20
SL_WO = 22
SL_FIN = 24
SL_FOUT = 35
NSLAB = 43

PC = {}
_pc = 0
for _n, _w in (("mu", 14), ("omm", 14), ("w0", 4), ("nw0", 4), ("a0", 4), ("kk", 4), ("ka", 4), ("rk", 4),
               ("lng", 4), ("lnb", 4), ("cdw", 4 * 31), ("cdb", 4), ("clg", 4), ("clb", 4), ("gb", 24),
               ("l1g", 8), ("l1b", 8), ("fdw", 44 * 3), ("fdb", 44), ("l2g", 8), ("l2b", 8)):
    PC[_n] = _pc
    _pc += _w
NPC = _pc


class Buf:
    __slots__ = ("name", "w", "r")

    def __init__(self, name):
        self.name = name
        self.w = None
        self.r = []


class _Rec:
    def __init__(self):
        self.calls = []

    def __getattr__(self, name):
        def f(*a, **k):
            self.calls.append((name, a, k))
            return self
        return f


def _freeze(fn):
    rec = _Rec()
    fn(rec)
    calls = rec.calls

    def replay(e):
        out = [getattr(e, name)(*a, **k) for (name, a, k) in calls]
        return out
    return replay, len(calls)


class Sched:
    ENGS = ("pe", "act", "dve", "pool", "sp")
    LIMIT = 60000

    def __init__(self, nc, es):
        self.nc = nc
        self.es = es
        self.sems = []
        self.ops = {e: [] for e in self.ENGS}
        self.cnt = {e: 0 for e in self.ENGS}
        self.semi = {e: self._newsem(e) for e in self.ENGS}
        self.waited = {e: {} for e in self.ENGS}
        self.dkeys = {}
        self.pending = []
        self.out_toks = []
        self.lastsig = {}

    def _newsem(self, name):
        s = self.es.enter_context(self.nc.semaphore(f"s{len(self.sems)}_{name}"))
        self.sems.append(s)
        return len(self.sems) - 1

    def _force_sig(self, te):
        ops = self.ops[te]
        k = len(ops) - 1
        while ops[k][0] is None:
            k -= 1
        assert ops[k][2] is None
        self.cnt[te] += 1
        ops[k][2] = (self.semi[te], 1)
        self.lastsig[te] = True

    def _resolve(self, eng, tok):
        si, val, te = tok
        if te != "dma" and te != eng and si == self.semi[te] and val > self.cnt[te]:
            assert val == self.cnt[te] + 1
            self._force_sig(te)

    def _waits(self, eng, reads, writes, is_dma):
        toks = set()
        for b in reads:
            if b.w is not None:
                toks.add(b.w)
        for b in writes:
            if b.w is not None:
                toks.add(b.w)
            toks.update(b.r)
        out = []
        for (si, val, te) in toks:
            if te == eng and not is_dma and eng == "pe":
                continue
            if te == eng and is_dma:
                if te != "dma" and si == self.semi[te] and val > self.cnt[te]:
                    self._force_sig(te)
            self._resolve(eng, (si, val, te))
            if self.waited[eng].get(si, 0) >= val:
                continue
            out.append((si, val))
        best = {}
        for si, val in out:
            best[si] = max(best.get(si, 0), val)
        for si, val in best.items():
            self.waited[eng][si] = val
        return list(best.items())

    def op(self, eng, fn, reads=(), writes=(), sig=True):
        fn, ncalls = _freeze(fn)
        assert ncalls == 1
        waits = self._waits(eng, reads, writes, False)
        if self.cnt[eng] >= self.LIMIT:
            self.semi[eng] = self._newsem(eng)
            self.cnt[eng] = 0
        if sig:
            self.cnt[eng] += 1
            tok = (self.semi[eng], self.cnt[eng], eng)
            inc = (self.semi[eng], 1)
        else:
            tok = (self.semi[eng], self.cnt[eng] + 1, eng)
            inc = None
        self.ops[eng].append([fn, waits, inc, 1])
        self.lastsig[eng] = sig
        for b in reads:
            b.r.append(tok)
        for b in writes:
            b.w = tok
            b.r = []
        return tok

    def dma(self, eng, fn, key, reads=(), writes=(), n=1, local=True, out=False):
        fn, ncalls = _freeze(fn)
        assert ncalls == n, (ncalls, n)
        waits = self._waits(eng, reads, writes, True)
        if key not in self.dkeys or self.dkeys[key][1] + 16 * n > self.LIMIT:
            self.dkeys[key] = [self._newsem("d" + key), 0]
        d = self.dkeys[key]
        d[1] += 16 * n
        tok = (d[0], d[1], "dma")
        self.ops[eng].append([fn, waits, (d[0], 16), n])
        for b in reads:
            b.r.append(tok)
        for b in writes:
            b.w = tok
            b.r = []
        if local:
            self.pending.append(tok)
        if out:
            self.out_toks.append(tok)
        return tok

    def wait_only(self, eng, toks):
        out = []
        best = {}
        for (si, val, te) in toks:
            self._resolve(eng, (si, val, te))
            if self.waited[eng].get(si, 0) >= val:
                continue
            best[si] = max(best.get(si, 0), val)
        for si, val in best.items():
            self.waited[eng][si] = val
            out.append((si, val))
        if out:
            self.ops[eng].append([None, out, None, 0])

    def last_tok(self, eng):
        return (self.semi[eng], self.cnt[eng], eng)

    def barrier(self, engs=("pe", "act", "dve")):
        for e in engs:
            if not self.lastsig.get(e, True):
                self._force_sig(e)
        toks = [self.last_tok(e) for e in engs if self.cnt[e] > 0] + list(self.pending)
        for e in engs:
            self.wait_only(e, toks)
        self.pending = []

    def emit(self, block):
        nc = self.nc
        table = {"pe": block.tensor, "act": block.scalar, "dve": block.vector, "pool": block.gpsimd,
                 "sp": block.sync}
        for eng in self.ENGS:
            ops = self.ops[eng]
            sems = self.sems

            def body(e, ops=ops):
                for fn, waits, inc, n in ops:
                    for si, val in waits:
                        e.wait_ge(sems[si], val)
                    if fn is None:
                        continue
                    r = fn(e)
                    if inc is not None:
                        assert len(r) == n
                        for ins in r:
                            ins.then_inc(sems[inc[0]], inc[1])
            table[eng](body)


def _slab_k(Wc):
    K, n = Wc.shape
    kc = K // 128
    a = np.ascontiguousarray(Wc.reshape(kc, 128, n).transpose(1, 0, 2)).reshape(128, kc * n)
    out = np.zeros((128, SLAB), np.float32)
    out[:, :kc * n] = a
    return out


def _cols(v):
    return np.ascontiguousarray(np.asarray(v, np.float32).reshape(-1, 128).T)


def _ffn_chunk_order():
    order = []
    for s in range(11):
        for j in range(4):
            if j < 2:
                order.append(2 * s + j)
            else:
                order.append(22 + 2 * s + (j - 2))
    return order


def pack_weights(inp):
    wp = np.zeros((DEPTH, NSLAB, 128, SLAB), np.float32)
    pp = np.zeros((DEPTH, 128, NPC), np.float32)
    lw = np.zeros((DEPTH, 128, 1024), np.float32)
    forder = _ffn_chunk_order()
    for l in range(DEPTH):
        w_in = inp["w_in"][l]
        wp[l, SL_LORA] = _slab_k(w_in[:, 1536:1792])
        for p in range(4):
            cols = np.concatenate([w_in[:, p * 128:(p + 1) * 128], w_in[:, 512 + p * 128:512 + (p + 1) * 128],
                                   w_in[:, 1024 + p * 128:1024 + (p + 1) * 128]], axis=1)
            wp[l, SL_RP + p] = _slab_k(cols)
        wp[l, SL_ROUT] = _slab_k(inp["w_rwkv_out"][l])
        for bi, sl in enumerate((SL_G0, SL_G1, SL_G2)):
            for hf in range(2):
                c0 = OFF_GATE + bi * 1024 + hf * 512
                wp[l, sl + hf] = _slab_k(w_in[:, c0:c0 + 512])
        wp[l, SL_CU] = _slab_k(w_in[:, OFF_C:OFF_C + 512])
        wp[l, SL_CG] = _slab_k(w_in[:, OFF_C + 512:OFF_C + 1024])
        wp[l, SL_COUT] = _slab_k(inp["w_conv_out"][l])
        for g in range(3):
            q = w_in[:, OFF_Q + g * 256:OFF_Q + (g + 1) * 256]
            k = w_in[:, OFF_K + g * 256:OFF_K + (g + 1) * 256]
            v = w_in[:, OFF_V + g * 256:OFF_V + (g + 1) * 256]
            wp[l, SL_ATT + 2 * g] = _slab_k(np.concatenate([q, k], axis=1))
            wp[l, SL_ATT + 2 * g + 1] = _slab_k(np.concatenate([k, v], axis=1))
        wp[l, SL_AOUT] = _slab_k(inp["w_attn_out"][l])
        wp[l, SL_WO] = _slab_k(inp["w_o"][l][:, 0:512])
        wp[l, SL_WO + 1] = _slab_k(inp["w_o"][l][:, 512:1024])
        wfi = inp["w_ffn_in"][l]
        for s in range(11):
            cols = np.concatenate([wfi[:, ch * 128:(ch + 1) * 128] for ch in forder[4 * s:4 * s + 4]], axis=1)
            wp[l, SL_FIN + s] = _slab_k(cols)
        wfo = inp["w_ffn_out"][l]
        for oc in range(8):
            wp[l, SL_FOUT + oc] = _slab_k(wfo[:, oc * 128:(oc + 1) * 128])
        pr = pp[l]
        pr[:, PC["mu"]:PC["mu"] + 14] = _cols(inp["rwkv_mu"][l])
        pr[:, PC["w0"]:PC["w0"] + 4] = _cols(inp["rwkv_w0"][l])
        pr[:, PC["a0"]:PC["a0"] + 4] = _cols(inp["rwkv_a0"][l])
        pr[:, PC["kk"]:PC["kk"] + 4] = _cols(inp["rwkv_k_k"][l])
        pr[:, PC["ka"]:PC["ka"] + 4] = _cols(inp["rwkv_k_a"][l])
        pr[:, PC["rk"]:PC["rk"] + 4] = _cols(inp["rwkv_r_k"][l].reshape(-1))
        pr[:, PC["lng"]:PC["lng"] + 4] = _cols(inp["rwkv_ln_g"][l])
        pr[:, PC["lnb"]:PC["lnb"] + 4] = _cols(inp["rwkv_ln_b"][l])
        cdw = inp["conv_dw"][l]
        for c in range(4):
            pr[:, PC["cdw"] + c * 31:PC["cdw"] + (c + 1) * 31] = cdw[:, c * 128:(c + 1) * 128].T
        pr[:, PC["cdb"]:PC["cdb"] + 4] = _cols(inp["conv_dw_b"][l])
        pr[:, PC["clg"]:PC["clg"] + 4] = _cols(inp["conv_ln_g"][l])
        pr[:, PC["clb"]:PC["clb"] + 4] = _cols(inp["conv_ln_b"][l])
        pr[:, PC["gb"]:PC["gb"] + 24] = _cols(inp["gate_b"][l].reshape(-1))
        pr[:, PC["l1g"]:PC["l1g"] + 8] = _cols(inp["ln1_g"][l])
        pr[:, PC["l1b"]:PC["l1b"] + 8] = _cols(inp["ln1_b"][l])
        fdw = inp["ffn_dw"][l]
        fdb = inp["ffn_dw_b"][l]
        for qi, ch in enumerate(forder):
            pr[:, PC["fdw"] + qi * 3:PC["fdw"] + qi * 3 + 3] = fdw[:, ch * 128:(ch + 1) * 128].T
            pr[:, PC["fdb"] + qi] = fdb[ch * 128:(ch + 1) * 128]
        pr[:, PC["l2g"]:PC["l2g"] + 8] = _cols(inp["ln2_g"][l])
        pr[:, PC["l2b"]:PC["l2b"] + 8] = _cols(inp["ln2_b"][l])
        lw[l, 0:64, 0:512] = inp["rwkv_w_up"][l]
        lw[l, 64:128, 0:512] = inp["rwkv_a_up"][l]
        lw[l, :, 512:1024] = inp["rwkv_g_up"][l]
    return wp, pp, lw


def pack_samples(inp, s0, ns):
    forder = _ffn_chunk_order()
    x = inp["x_sample"][s0:s0 + ns, 0, :]
    xs_fm = np.ascontiguousarray(x.reshape(ns, 8, 128).transpose(2, 1, 0))
    sh = inp["state_shift"][:, s0:s0 + ns, 0, :]
    sh_fm = np.ascontiguousarray(sh.reshape(DEPTH, ns, 14, 128).transpose(0, 3, 2, 1))
    wk = inp["state_wkv"][:, s0:s0 + ns]
    wk = wk.reshape(DEPTH, ns, 4, 2, 64, 64)
    wkv_fm = np.ascontiguousarray(wk.transpose(0, 3, 5, 1, 2, 4)).reshape(DEPTH, 128, ns, 4, 64)
    cv = inp["state_conv"][:, s0:s0 + ns]
    cv_fm = np.ascontiguousarray(cv.reshape(DEPTH, ns, 30, 4, 128).transpose(0, 4, 3, 1, 2))
    ff = inp["state_ffn"][:, s0:s0 + ns]
    ff4 = ff.reshape(DEPTH, ns, 2, 44, 128)[:, :, :, forder, :]
    ff_fm = np.ascontiguousarray(ff4.transpose(0, 4, 3, 1, 2))
    out = {"xs_fm": xs_fm, "sh_fm": sh_fm, "wkv_fm": wkv_fm, "cv_fm": cv_fm, "cv_nat": np.ascontiguousarray(cv),
           "ff_fm": ff_fm, "ff_nat": np.ascontiguousarray(ff)}
    for g, nm in enumerate(("cache_swa_a", "cache_swa_b", "cache_swa_c")):
        c = inp[nm][:, s0:s0 + ns]
        out[f"cache{g}"] = np.ascontiguousarray(c.reshape(DEPTH, ns, c.shape[2], 512))
    return out


def make_consts():
    cf = np.zeros((128, 2048), np.float32)
    cf[:, 0:128] = np.eye(128, dtype=np.float32)
    blk = np.zeros((128, 128), np.float32)
    blk[:64, :64] = 1.0
    blk[64:, 64:] = 1.0
    cf[:, 128:256] = blk
    cf[:, 256:384] = blk / 64.0
    cf[:, 384:512] = 1.0 / 1024.0
    cf[:, 512:640] = 1.0 / 512.0
    rm = np.ones((128, 512), np.float32)
    rm[:, ::64] = 0.0
    cf[:, 640:1152] = rm
    s = np.arange(64)[:, None]
    t = np.arange(64)[None, :]
    m1 = np.concatenate([(s < t), (s <= t)], axis=1).astype(np.float32)
    cf[0:64, 1152:1280] = m1
    cf[64:128, 1152:1280] = m1
    m2 = (t < s).astype(np.float32)
    cf[0:64, 1280:1344] = m2
    cf[64:128, 1280:1344] = m2
    cf[0:64, 1344:1408] = np.eye(64, dtype=np.float32)
    cf[64:128, 1344:1408] = np.eye(64, dtype=np.float32)
    for s_ in range(4):
        cf[s_, 1408 + s_ * 128:1408 + (s_ + 1) * 128] = 1.0
    cf[:, 1920:1984] = 1.0
    cb = np.zeros((128, 1536), np.float32)
    j = np.arange(128)[:, None]
    q = np.arange(128)[None, :]
    same = np.where(j <= q, 0.0, NEG).astype(np.float32)
    prev = np.where(j >= q, 0.0, NEG).astype(np.float32)
    cb[:, 0:128] = np.eye(128, dtype=np.float32)
    cb[:, 128:256] = same
    cb[:, 256:384] = prev
    cb[:, 384:448] = 1.0
    off = 448
    for o in range(4):
        for kb, m in enumerate((same, prev)):
            for rep in range(4):
                cb[:, off:off + 32] = m[:, o * 32:(o + 1) * 32]
                off += 32
    assert off == 448 + 1024
    return cf, cb


STOP = None
DEBUG = False
DBGSTATE = {"tile": -1}


class _Stop(Exception):
    pass


def build(SEQ, NSAMP=0):
    NT = SEQ // T
    nc = bass.Bass("TRN2", target_bir_lowering=False)
    es = ExitStack()
    dram = {}

    def din(name, shape, dt=F32):
        dram[name] = nc.dram_tensor(name, list(shape), dt, kind="ExternalInput").ap()
        return dram[name]

    def dout(name, shape):
        dram[name] = nc.dram_tensor(name, list(shape), F32, kind="ExternalOutput").ap()
        return dram[name]

    x_fm = din("x_fm", [128, 8, SEQ])
    wpk = din("wpack", [DEPTH, NSLAB, 128, SLAB])
    ppk = din("ppack", [DEPTH, 128, NPC])
    lwk = din("lorapack", [DEPTH, 128, 1024])
    cfk = din("constf", [128, 2048])
    cbk = din("constb", [128, 1536])
    xs_d = nc.dram_tensor("xs_scratch", [128, 8, SEQ], F32, kind="Internal").ap()
    y_p = dout("y_p", [SEQ, D_MODEL])
    o_shift = dout("o_shift_p", [DEPTH, R_COLS])
    o_wkv = dout("o_wkv_p", [DEPTH, R_HEADS, 64, 64])
    o_conv = dout("o_conv_p", [DEPTH, CONV_K - 1, CONV_CH])
    keep = [min(w, SEQ) for w, _ in SWA_GROUPS]
    o_swa = [dout(f"o_swa{g}_p", [DEPTH, keep[g], 512]) for g in range(3)]
    o_ffn = dout("o_ffn_p", [DEPTH, 2, 2 * D_FF])
    dbg_d = dout("dbg", [16, 128, T]) if DEBUG else None

    NS = NSAMP
    if NS:
        xs_fm = din("xs_fm", [128, 8, NS])
        sh_fm = din("sh_fm", [DEPTH, 128, 14, NS])
        wkv_fm = din("wkv_fm", [DEPTH, 128, NS, 4, 64])
        cv_fm = din("cv_fm", [DEPTH, 128, 4, NS, 30])
        cv_nat = din("cv_nat", [DEPTH, NS, 30, 512])
        ff_fm = din("ff_fm", [DEPTH, 128, 44, NS, 2])
        ff_nat = din("ff_nat", [DEPTH, NS, 2, 2 * D_FF])
        cache_d = [din(f"cache{g}", [DEPTH, NS, NBUF[g], 512]) for g in range(3)]
        y_s = dout("y_s", [NS, D_MODEL])
        o_shift_s = dout("o_shift_s", [DEPTH, NS, R_COLS])
        o_wkv_s = dout("o_wkv_s", [DEPTH, NS, R_HEADS, 64, 64])
        o_conv_s = dout("o_conv_s", [DEPTH, NS, CONV_K - 1, CONV_CH])
        o_swa_s = [dout(f"o_swa{g}_s", [DEPTH, NS, NBUF[g], 512]) for g in range(3)]
        o_ffn_s = dout("o_ffn_s", [DEPTH, NS, 2, 2 * D_FF])

    S = Sched(nc, es)

    def sb(name, shape, dt=F32):
        return es.enter_context(nc.sbuf_tensor(name, list(shape), dt))

    ring = sb("ring", [128, NSLOT, SLAB], BF16)
    ring_b = [Buf(f"ring{i}") for i in range(NSLOT)]
    cf = sb("cf", [128, 2048])
    cb = sb("cb", [128, 1536], BF16)
    cf_b, cb_b = Buf("cf"), Buf("cb")
    pp = sb("pp", [128, DEPTH, NPC])
    pp_b = Buf("pp")
    lw = sb("lw", [128, DEPTH, 1024])
    lw_b = Buf("lw")
    xT32 = sb("xT32", [128, 8, T])
    xTb = sb("xTb", [128, 8, T], BF16)
    xT32_b, xTb_b = Buf("xT32"), Buf("xTb")
    merged = sb("merged", [128, 8, T])
    merged_b = Buf("merged")
    kT_h = [sb("kTa", [128, 2, 2 * T], BF16), sb("kTb", [128, 2, 2 * T], BF16), sb("kTc", [128, 2, max(SEQ, 4096)], BF16)]
    kT_hb = [Buf("kTa"), Buf("kTb"), Buf("kTc")]
    Vtm = [sb("Va", [128, 8, 256], BF16), sb("Vb", [128, 8, 256], BF16), sb("Vc", [128, 32, 256], BF16)]
    Vtm_b = [Buf("Va"), Buf("Vb"), Buf("Vc")]
    Hst = sb("Hst", [128, 4, 64])
    Hst_b = Buf("Hst")
    zcar = sb("zcar", [128, 14])
    zcar_b = Buf("zcar")
    uhist = sb("uhist", [128, 4, 30])
    uhist_b = Buf("uhist")
    fcar = sb("fcar", [128, 44, 2])
    fcar_b = Buf("fcar")
    ps = [es.enter_context(nc.psum_tensor(f"ps{i}", [128, 512], F32)) for i in range(8)]
    ps_b = [Buf(f"ps{i}") for i in range(8)]

    ARENA_N = 12288
    arena_t = sb("arena", [128, ARENA_N])

    class Arena:
        def __init__(self):
            self.top = 0
            self.stack = []

        def push(self):
            self.stack.append(self.top)

        def pop(self):
            self.top = self.stack.pop()

        def alloc(self, shape, dt=F32):
            nfree = int(np.prod(shape[1:]))
            n32 = nfree if dt == F32 else (nfree + 1) // 2
            ap = arena_t[0:shape[0], self.top:self.top + n32]
            self.top += n32
            assert self.top <= ARENA_N, ("arena overflow", self.top)
            if dt != F32:
                ap = ap.bitcast(dt)
            if len(shape) == 3:
                ap = ap.rearrange("p (a b) -> p a b", a=shape[1])
            elif len(shape) == 4:
                ap = ap.rearrange("p (a b c) -> p a b c", a=shape[1], b=shape[2])
            return ap

    AR = Arena()

    @contextmanager
    def scope():
        AR.push()
        try:
            yield None
        finally:
            AR.pop()

    @contextmanager
    def scope_alloc(shape, dt=F32):
        AR.push()
        try:
            yield AR.alloc(list(shape), dt)
        finally:
            AR.pop()

    ident = cf[:, 0:128]
    ones_blk = cf[:, 128:256]
    mean_blk = cf[:, 256:384]
    mean_d = cf[:, 384:512]
    mean_c = cf[:, 512:640]
    rmask = cf[:, 640:1152]
    identb = cb[:, 0:128]
    mb_same = cb[:, 128:256]
    mb_prev = cb[:, 256:384]
    ones_b64 = cb[:, 384:448]

    def pcol(l, name, i=0, rows=slice(0, 128)):
        c = PC[name] + i
        return pp[rows, l, c:c + 1]

    S.dma("sp", lambda e: e.dma_start(out=cf[:], in_=cfk[:, :]), "cf", writes=[cf_b], local=False)
    S.dma("pool", lambda e: e.dma_start(out=cb[:], in_=cbk[:, :]), "cb", writes=[cb_b], local=False)
    S.dma("sp", lambda e: [e.dma_start(out=pp[:, l, :], in_=ppk[l, :, :]) for l in range(DEPTH)], "pp",
          writes=[pp_b], n=DEPTH, local=False)
    S.dma("sp", lambda e: [e.dma_start(out=lw[:, l, :], in_=lwk[l, :, :]) for l in range(DEPTH)], "lw",
          writes=[lw_b], n=DEPTH, local=False)
    for l in range(DEPTH):
        S.op("dve", lambda e, l=l: e.tensor_scalar(out=pp[:, l, PC["omm"]:PC["omm"] + 14],
                                                   in0=pp[:, l, PC["mu"]:PC["mu"] + 14], scalar1=-1.0, scalar2=1.0,
                                                   op0=ALU.mult, op1=ALU.add), reads=[pp_b], writes=[pp_b])
        S.op("dve", lambda e, l=l: e.tensor_scalar(out=pp[:, l, PC["nw0"]:PC["nw0"] + 4],
                                                   in0=pp[:, l, PC["w0"]:PC["w0"] + 4], scalar1=-1.0, scalar2=None,
                                                   op0=ALU.mult), reads=[pp_b], writes=[pp_b])
    for g in range(3):
        S.op("dve", lambda e, g=g: e.memset(kT_h[g][:], 0.0), writes=[kT_hb[g]])
        S.op("dve", lambda e, g=g: e.memset(Vtm[g][:], 0.0), writes=[Vtm_b[g]])

    wstate = {"n": 0}

    def wload(l, slab):
        i = wstate["n"] % NSLOT
        wstate["n"] += 1
        S.dma("pool", lambda e: e.dma_start(out=ring[:, i, :], in_=wpk[l, slab, :, :]), f"ring{i}",
              writes=[ring_b[i]], local=False)
        return i

    class WQ:
        def __init__(self):
            self.plan = []
            self.issued = 0
            self.slots = {}

        def extend(self, items):
            self.plan.extend(items)

        def get(self, idx, ahead=NSLOT - 1):
            while self.issued < len(self.plan) and self.issued <= idx + ahead:
                l, slab = self.plan[self.issued]
                self.slots[self.issued] = wload(l, slab)
                self.issued += 1
            return self.slots[idx]

    wq = WQ()
    for l in range(DEPTH):
        if NSAMP:
            for sl in range(NSLAB):
                wq.extend([(l, sl)])
        for i in range(NT):
            if STOP and ((l, i) != (0, 0) or STOP == "load"):
                continue
            for sl in range(NSLAB):
                wq.extend([(l, sl)])
    wctr = {"i": 0}

    cur_layer = {"l": 0}

    def next_slab(expect):
        idx = wctr["i"]
        assert wq.plan[idx] == (cur_layer["l"], expect), (wq.plan[idx], cur_layer["l"], expect)
        slot = wq.get(idx, ahead=NSLOT - 3)
        wctr["i"] += 1
        return slot

    def W(slot, kc_n, ncols):
        return ring[:, slot, 0:kc_n * ncols].rearrange("p (k n) -> p k n", n=ncols)

    def dbg(idx, ap, buf):
        if DEBUG:
            S.dma("sp", lambda e: e.dma_start(out=dbg_d[idx, 0:ap.shape[0], 0:ap.shape[1]], in_=ap), "dbg", reads=[buf], out=True)

    epsc = sb("epsc", [128, 4])
    epsc_b = Buf("epsc")
    for ci, ev in enumerate((LN_EPS, LNX_EPS, 1e-24)):
        S.op("dve", lambda e, ci=ci, ev=ev: e.memset(epsc[:, ci:ci + 1], float(ev)), writes=[epsc_b])
    eps_col = {float(LN_EPS): 0, float(LNX_EPS): 1, 1e-24: 2}

    def rsqrt(out, in_, eps, reads, wbuf):
        ci = eps_col[float(eps)]
        S.op("act", lambda e: e.activation(out=out, in_=in_, func=AF.Sqrt, bias=epsc[0:out.shape[0], ci:ci + 1]),
             reads=list(reads) + [epsc_b], writes=[wbuf])
        S.op("dve", lambda e: e.reciprocal(out=out, in_=out), reads=[wbuf], writes=[wbuf])

    def mm(out, lhsT, rhs, start, stop, reads, writes, sig):
        S.op("pe", lambda e: e.matmul(out, lhsT=lhsT, rhs=rhs, start=start, stop=stop), reads=reads,
             writes=writes, sig=sig)

    def proj_fm(pi, slot, col0, rhs_t, rhs_b, ncol=T, kc_n=8, wcols=512, rhs_cols=slice(0, T)):
        Wv = W(slot, kc_n, wcols)
        for kc in range(kc_n):
            mm(ps[pi][:, 0:ncol], Wv[:, kc, col0:col0 + 128], rhs_t[:, kc, rhs_cols], kc == 0, kc == kc_n - 1,
               [ring_b[slot], rhs_b], [ps_b[pi]], kc == kc_n - 1)

    def ln_fm(l, src, src_b, nch, mean_mat, gname, bname, eps, outs, tmp, tmp_b, pA=6, pB=7, N=T):
        for c in range(nch):
            mm(ps[pA][:, 0:N], mean_mat, src[:, c, :], c == 0, c == nch - 1, [cf_b, src_b], [ps_b[pA]], c == nch - 1)
        mean_sb = tmp[:, 0, 0:N]
        S.op("act", lambda e: e.copy(out=mean_sb, in_=ps[pA][:, 0:N]), reads=[ps_b[pA]], writes=[tmp_b])
        sq = es_local["sq"]
        sq_b = es_local["sq_b"]
        for c in range(nch):
            S.op("dve", lambda e, c=c: e.tensor_sub(out=src[:, c, :], in0=src[:, c, :], in1=mean_sb),
                 reads=[src_b, tmp_b], writes=[src_b])
            S.op("act", lambda e, c=c: e.activation(out=sq[:, c % 2, 0:N], in_=src[:, c, :], func=AF.Square),
                 reads=[src_b], writes=[sq_b[c % 2]])
            mm(ps[pB][:, 0:N], mean_mat, sq[:, c % 2, 0:N], c == 0, c == nch - 1, [cf_b, sq_b[c % 2]], [ps_b[pB]], True)
        rsqrt(mean_sb, ps[pB][:, 0:N], float(eps), [ps_b[pB]], tmp_b)
        for c in range(nch):
            S.op("dve", lambda e, c=c: e.tensor_mul(out=src[:, c, :], in0=src[:, c, :], in1=mean_sb),
                 reads=[src_b, tmp_b], writes=[src_b])
            for (ot, ob) in outs:
                S.op("dve", lambda e, c=c, ot=ot: e.tensor_scalar(out=ot[:, c, :], in0=src[:, c, :],
                                                                   scalar1=pcol(l, gname, c), scalar2=pcol(l, bname, c),
                                                                   op0=ALU.mult, op1=ALU.add),
                     reads=[src_b, pp_b], writes=[ob])

    es_local = {}

    def merge_branch(l, bidx, slot_out, kc_n, actT, act_b, gslots, N=T, xb=None, xb_b=None, mg=None, mg_b=None):
        xb = xTb if xb is None else xb
        xb_b = xTb_b if xb_b is None else xb_b
        mg = merged if mg is None else mg
        mg_b = merged_b if mg_b is None else mg_b
        Wo = W(slot_out, kc_n, 1024)
        with scope_alloc([128, 2, N], F32) as gsb:
            gsb_b = [Buf("gsb0"), Buf("gsb1")]
            for oc in range(8):
                py, pg = oc % 2, 2 + oc % 2
                for kc in range(kc_n):
                    mm(ps[py][:, 0:N], Wo[:, kc, oc * 128:(oc + 1) * 128], actT[:, kc, :], kc == 0, kc == kc_n - 1,
                       [ring_b[slot_out], act_b], [ps_b[py]], kc == kc_n - 1)
                proj_fm(pg, gslots[oc // 4], (oc % 4) * 128, xb, xb_b, ncol=N, rhs_cols=slice(0, N))
                g_ = gsb[:, oc % 2, :]
                S.op("act", lambda e, g_=g_, pg=pg, oc=oc: e.activation(out=g_, in_=ps[pg][:, 0:N], func=AF.Sigmoid,
                                                                         bias=pcol(l, "gb", bidx * 8 + oc)),
                     reads=[ps_b[pg], pp_b], writes=[gsb_b[oc % 2]])
                if bidx == 0:
                    S.op("dve", lambda e, g_=g_, py=py, oc=oc: e.tensor_mul(out=mg[:, oc, :], in0=ps[py][:, 0:N], in1=g_),
                         reads=[ps_b[py], gsb_b[oc % 2]], writes=[mg_b])
                else:
                    S.op("dve", lambda e, g_=g_, py=py: e.tensor_mul(out=g_, in0=ps[py][:, 0:N], in1=g_),
                         reads=[ps_b[py], gsb_b[oc % 2]], writes=[gsb_b[oc % 2]])
                    S.op("dve", lambda e, g_=g_, oc=oc: e.tensor_add(out=mg[:, oc, :], in0=mg[:, oc, :], in1=g_),
                         reads=[mg_b, gsb_b[oc % 2]], writes=[mg_b])
            if DEBUG and N == T and l == 0 and DBGSTATE["tile"] == 0:
                dbg(10 + bidx, mg[:, 0, :], mg_b)
            S.barrier()

    def fm2tm_store(src_cols_fn, nchunk, nrows, dst_fn, key):
        with scope_alloc([128, 512], F32) as stg:
            stg_b = Buf("stg")
            for c in range(nchunk):
                ap, bf = src_cols_fn(c)
                S.op("pe", lambda e, ap=ap, c=c: e.matmul(ps[4][0:nrows, c * 128:(c + 1) * 128], lhsT=ap, rhs=ident,
                                                          start=True, stop=True),
                     reads=[bf, cf_b], writes=[ps_b[4]])
            S.op("act", lambda e: e.copy(out=stg[0:nrows, 0:nchunk * 128], in_=ps[4][0:nrows, 0:nchunk * 128]),
                 reads=[ps_b[4]], writes=[stg_b])
            S.dma("sp", dst_fn(stg), key, reads=[stg_b], out=True)
            S.barrier()

    def tile_prog(l, i):
        DBGSTATE["tile"] = i
        par = i % 2
        last = (i == NT - 1)
        src = x_fm if l == 0 else xs_d
        S.dma("sp", lambda e: e.dma_start(out=xT32[:], in_=src[:, :, i * T:(i + 1) * T]), "xload",
              writes=[xT32_b], local=False)
        S.op("act", lambda e: e.copy(out=xTb[:], in_=xT32[:]), reads=[xT32_b], writes=[xTb_b])

        if STOP == "load":
            return
        rwkv_phase(l, i, last)
        if STOP == "rwkv":
            return
        conv_phase(l, i, last)
        if STOP == "conv":
            return
        attn_phase(l, i, par, last)
        if STOP == "attn":
            return
        ffn_phase(l, i, last)

    def rwkv_phase(l, i, last):
        with scope():
            def lt(name, shape, dt=F32):
                return AR.alloc(list(shape), dt)
            lora = merged[:, 0:2, :]
            lora_b = Buf("lora")
            zraw = merged[:, 2:5, :].rearrange("p a t -> p (a t)")[:, 0:2 * (T + 1)].rearrange("p (a t) -> p a t", a=2)
            zraw_b = [Buf("zraw0"), Buf("zraw1")]
            ozT = lt("ozT", [128, 4, T], BF16)
            ozT_b = Buf("ozT")
            zctr = {"n": 0}

            def shift_chunk(pi, och, dst, dst_b):
                zi = zctr["n"] % 2
                zctr["n"] += 1
                zr = zraw[:, zi, :]
                S.op("act", lambda e: e.copy(out=zr[:, 1:T + 1], in_=ps[pi][:, :]), reads=[ps_b[pi]], writes=[zraw_b[zi]])
                S.op("dve", lambda e: e.tensor_copy(out=zr[:, 0:1], in_=zcar[:, och:och + 1]), reads=[zcar_b],
                     writes=[zraw_b[zi]])
                S.op("dve", lambda e: e.tensor_scalar(out=dst, in0=ps[pi][:, :], scalar1=pcol(l, "omm", och), scalar2=None,
                                                      op0=ALU.mult), reads=[ps_b[pi], pp_b], writes=[dst_b])
                S.op("dve", lambda e: e.scalar_tensor_tensor(out=dst, in0=zr[:, 0:T], scalar=pcol(l, "mu", och), in1=dst,
                                                             op0=ALU.mult, op1=ALU.add),
                     reads=[zraw_b[zi], pp_b, dst_b], writes=[dst_b])
                S.op("dve", lambda e: e.tensor_copy(out=zcar[:, och:och + 1], in_=zr[:, T:T + 1]), reads=[zraw_b[zi]],
                     writes=[zcar_b])

            slot = next_slab(SL_LORA)
            for c in range(2):
                proj_fm(c, slot, c * 128, xTb, xTb_b, wcols=256)
                shift_chunk(c, 12 + c, lora[:, c, :], lora_b)
            S.op("act", lambda e: e.activation(out=lora[0:64, 0, :], in_=lora[0:64, 0, :], func=AF.Tanh),
                 reads=[lora_b], writes=[lora_b])
            S.op("act", lambda e: e.activation(out=lora[:, 1, :], in_=lora[:, 1, :], func=AF.Sigmoid),
                 reads=[lora_b], writes=[lora_b])

            if STOP == "rw_lora":
                raise _Stop()
            for p in range(4):
                rwkv_pair(l, i, p, lora, lora_b, shift_chunk, ozT, ozT_b, None)

            s_out = next_slab(SL_ROUT)
            g0 = next_slab(SL_G0)
            g1 = next_slab(SL_G0 + 1)
            merge_branch(l, 0, s_out, 4, ozT, ozT_b, (g0, g1))
            if last:
                S.dma("sp", lambda e: e.dma_start(out=o_shift[l].rearrange("(c p) -> p c", p=128), in_=zcar[:, :],
                                                  allow_slow_non_contiguous=True), "oshift", reads=[zcar_b], out=True)
                with scope_alloc([64, 8, 64], F32) as stg:
                    stg_b = Buf("wkvst")
                    for h in range(8):
                        pr_, hl = h // 2, h % 2
                        S.op("pe", lambda e, pr_=pr_, hl=hl, h=h: e.matmul(
                            ps[4 + hl][0:64, pr_ * 64:(pr_ + 1) * 64], lhsT=Hst[hl * 64:(hl + 1) * 64, pr_, :],
                            rhs=cf[hl * 64:(hl + 1) * 64, 1344:1408], start=True, stop=True), reads=[Hst_b, cf_b],
                            writes=[ps_b[4 + hl]])
                    stg4 = stg[:].rearrange("p (pr hl) k -> p pr hl k", hl=2)
                    for hl in range(2):
                        S.op("act", lambda e, hl=hl: e.copy(out=stg4[:, :, hl, :],
                                                            in_=ps[4 + hl][0:64, 0:256].rearrange("p (pr k) -> p pr k", k=64)),
                             reads=[ps_b[4 + hl]], writes=[stg_b])
                    S.dma("sp", lambda e: e.dma_start(out=o_wkv[l].rearrange("h v k -> v h k"), in_=stg[:]), "owkv",
                          reads=[stg_b], out=True)
                    S.barrier()
            S.barrier()

    def rwkv_pair(l, i, p, lora, lora_b, shift_chunk, ozT, ozT_b, st_outer):
        with scope():
            cnt = {"n": 0}

            def lt(name, shape, dt=F32):
                cnt["n"] += 1
                return AR.alloc(list(shape), dt), Buf(name)
            rT, r_b = lt("rT", [128, T])
            kT, k_b = lt("kT", [128, T])
            vT, v_b = lt("vT", [128, T])
            lwT, lw_b2 = lt("lwT", [128, T])
            aT, a_b = lt("aT", [128, T])
            gT, g_b = lt("gT", [128, T])
            kkT, kk_b = lt("kkT", [128, T])
            bT, b_b = lt("bT", [128, T])
            csT, cs_b = lt("csT", [128, T])
            eT, e_b = lt("eT", [128, T])
            e2T, e2_b = lt("e2T", [128, T])
            KR, KR_b = lt("KR", [128, NCH, 2, C])
            bon, bon_b = lt("bon", [128, T])
            gam, gam_b = lt("gam", [128, NCH])
            Kt, Kt_b = kT, k_b
            Bt, Bt_b = bT, b_b
            Kh, Kh_b = rT, r_b
            Bh, Bh_b = kkT, kk_b
            OT, OT_b = aT, a_b
            slot = next_slab(SL_RP + p)
            for c, (dst, db, och) in enumerate(((rT, r_b, p), (kT, k_b, 4 + p), (vT, v_b, 8 + p))):
                proj_fm(c % 2, slot, c * 128, xTb, xTb_b, wcols=384)
                shift_chunk(c % 2, och, dst[:, :], db)
            cs128 = slice(p * 128, (p + 1) * 128)
            mm(ps[2][:, :], lw[0:64, l, cs128], lora[0:64, 0, :], True, True, [lw_b, lora_b], [ps_b[2]], True)
            mm(ps[3][:, :], lw[64:128, l, cs128], lora[64:128, 0, :], True, True, [lw_b, lora_b], [ps_b[3]], True)
            mm(ps[4][:, :], lw[:, l, 512 + p * 128:512 + (p + 1) * 128], lora[:, 1, :], True, True, [lw_b, lora_b],
               [ps_b[4]], True)
            S.op("act", lambda e: e.activation(out=eT[:, :], in_=ps[2][:, :], func=AF.Exp, scale=-1.0,
                                               bias=pcol(l, "nw0", p)), reads=[ps_b[2], pp_b], writes=[e_b])
            S.op("act", lambda e: e.activation(out=eT[:, :], in_=eT[:, :], func=AF.Ln, bias=1.0), reads=[e_b], writes=[e_b])
            S.op("act", lambda e: e.activation(out=eT[:, :], in_=eT[:, :], func=AF.Exp, scale=-1.0, bias=-0.5),
                 reads=[e_b], writes=[e_b])
            S.op("dve", lambda e: e.tensor_scalar(out=lwT[:, :], in0=eT[:, :], scalar1=-1.0, scalar2=None, op0=ALU.mult),
                 reads=[e_b], writes=[lw_b2])
            S.op("act", lambda e: e.activation(out=aT[:, :], in_=ps[3][:, :], func=AF.Sigmoid, bias=pcol(l, "a0", p)),
                 reads=[ps_b[3], pp_b], writes=[a_b])
            S.op("act", lambda e: e.copy(out=gT[:, :], in_=ps[4][:, :]), reads=[ps_b[4]], writes=[g_b])
            S.op("dve", lambda e: e.tensor_scalar(out=kkT[:, :], in0=kT[:, :], scalar1=pcol(l, "kk", p), scalar2=None,
                                                  op0=ALU.mult), reads=[k_b, pp_b], writes=[kk_b])
            S.op("dve", lambda e: e.tensor_mul(out=e2T[:, :], in0=kkT[:, :], in1=kkT[:, :]), reads=[kk_b], writes=[e2_b])
            mm(ps[2][:, :], ones_blk, e2T[:, :], True, True, [cf_b, e2_b], [ps_b[2]], True)
            rsqrt(e2T[:, :], ps[2][:, :], 1e-24, [ps_b[2]], e2_b)
            S.op("dve", lambda e: e.tensor_mul(out=kkT[:, :], in0=kkT[:, :], in1=e2T[:, :]), reads=[kk_b, e2_b],
                 writes=[kk_b])
            S.op("dve", lambda e: e.tensor_mul(out=bT[:, :], in0=kkT[:, :], in1=aT[:, :]), reads=[kk_b, a_b], writes=[b_b])
            S.op("dve", lambda e: e.tensor_scalar(out=aT[:, :], in0=aT[:, :], scalar1=-1.0, scalar2=pcol(l, "ka", p),
                                                  op0=ALU.add, op1=ALU.mult), reads=[a_b, pp_b], writes=[a_b])
            S.op("dve", lambda e: e.scalar_tensor_tensor(out=kT[:, :], in0=aT[:, :], scalar=1.0, in1=kT[:, :], op0=ALU.add,
                                                         op1=ALU.mult), reads=[a_b, k_b], writes=[k_b])
            S.op("dve", lambda e: e.scalar_tensor_tensor(out=e2T[:, :], in0=rT[:, :], scalar=pcol(l, "rk", p), in1=kT[:, :],
                                                         op0=ALU.mult, op1=ALU.mult), reads=[r_b, k_b, pp_b], writes=[e2_b])
            mm(ps[3][:, :], ones_blk, e2T[:, :], True, True, [cf_b, e2_b], [ps_b[3]], True)
            S.op("dve", lambda e: e.tensor_mul(out=bon[:, :], in0=ps[3][:, :], in1=vT[:, :]), reads=[ps_b[3], v_b],
                 writes=[bon_b])
            if (l, i, p) == (0, 0, 0):
                for di, (ap_, bf_) in enumerate(((rT, r_b), (kT, k_b), (vT, v_b), (lwT, lw_b2), (kkT, kk_b), (bT, b_b),
                                                 (gT, g_b), (bon, bon_b))):
                    dbg(di, ap_[:, :], bf_)
            S.op("dve", lambda e: e.tensor_tensor_scan(out=csT[:, :], data0=rmask, data1=lwT[:, :], initial=0.0,
                                                       op0=ALU.mult, op1=ALU.add), reads=[cf_b, lw_b2], writes=[cs_b])
            cs3 = csT[:, :].rearrange("p (j t) -> p j t", t=C)
            S.op("act", lambda e: e.activation(out=eT[:, :], in_=csT[:, :], func=AF.Exp), reads=[cs_b], writes=[e_b])
            S.op("dve", lambda e: e.tensor_mul(out=KR[:, :, 1, :], in0=rT[:, :].rearrange("p (j t) -> p j t", t=C),
                                               in1=eT[:, :].rearrange("p (j t) -> p j t", t=C)),
                 reads=[r_b, e_b], writes=[KR_b])
            S.op("dve", lambda e: e.tensor_copy(out=gam[:, :], in_=eT[:, :].rearrange("p (j t) -> p j t", t=C)[:, :, C - 1]),
                 reads=[e_b], writes=[gam_b])
            S.op("dve", lambda e: e.tensor_sub(out=OT[:, :], in0=csT[:, :], in1=lwT[:, :]), reads=[cs_b, lw_b2], writes=[OT_b])
            S.op("act", lambda e: e.activation(out=OT[:, :], in_=OT[:, :], func=AF.Exp), reads=[OT_b], writes=[OT_b])
            S.op("dve", lambda e: e.tensor_mul(out=KR[:, :, 0, :], in0=kkT[:, :].rearrange("p (j t) -> p j t", t=C),
                                               in1=OT[:, :].rearrange("p (j t) -> p j t", t=C)),
                 reads=[kk_b, OT_b], writes=[KR_b])
            S.op("dve", lambda e: e.tensor_sub(out=OT[:, :].rearrange("p (j t) -> p j t", t=C),
                                               in0=cs3[:, :, C - 1:C].broadcast_to([128, NCH, C]), in1=cs3),
                 reads=[cs_b], writes=[OT_b])
            S.op("act", lambda e: e.activation(out=OT[:, :], in_=OT[:, :], func=AF.Exp), reads=[OT_b], writes=[OT_b])
            S.op("dve", lambda e: e.tensor_mul(out=Kh[:, :], in0=kT[:, :], in1=OT[:, :]), reads=[k_b, OT_b], writes=[Kh_b])
            S.op("dve", lambda e: e.tensor_mul(out=Bh[:, :], in0=bT[:, :], in1=OT[:, :]), reads=[b_b, OT_b], writes=[Bh_b])

            S.op("act", lambda e: e.activation(out=eT[:, :], in_=csT[:, :], func=AF.Exp, scale=-1.0), reads=[cs_b],
                 writes=[e_b])
            S.op("dve", lambda e: e.tensor_mul(out=Kt[:, :], in0=kT[:, :], in1=eT[:, :]), reads=[k_b, e_b], writes=[Kt_b])
            S.op("dve", lambda e: e.tensor_mul(out=Bt[:, :], in0=bT[:, :], in1=eT[:, :]), reads=[b_b, e_b], writes=[Bt_b])
            if STOP == "rw_prep":
                raise _Stop()
            GQ = 4
            A1, A1_b = lt("A1", [128, GQ, 2, 128])
            Ab, Ab_b = lt("Ab", [128, GQ, 64])
            Y, Y_b = lt("Y", [128, GQ, 2, 64])
            Pm, Pm_b = lt("Pm", [128, GQ, 2, 64])
            TMx, _ = lt("TM", [128, 2, 3, 128])
            TMx_b = [Buf("TM0"), Buf("TM1")]
            W1, W1_b = lt("W1", [128, 64])
            U, U_b = lt("U", [128, 64])
            dup, dup_b = lt("dup", [128, 3, 128])
            mask1 = cf[:, 1152:1280]
            mask2 = cf[:, 1280:1344]
            id64 = cf[:, 1344:1408]
            HP = [slice(0, 64), slice(64, 128)]
            BK_W, BK_O, BK_H = (1, 3), (4, 5), (6, 7)

            def group_stage(jg):
                for jj in range(GQ):
                    j_ = jg * GQ + jj
                    tcs_ = slice(j_ * C, (j_ + 1) * C)
                    for hl in range(2):
                        hp = HP[hl]
                        krj = KR[hp, j_, :, :].rearrange("p a t -> p (a t)")
                        bA = 2 * hl + jj // 2
                        c0 = (jj % 2) * 256
                        bB = 4 + hl
                        mm(ps[bA][hp, c0:c0 + 128], Kt[hp, tcs_], krj, True, True, [Kt_b, KR_b], [ps_b[bA]], False)
                        mm(ps[bA][hp, c0 + 128:c0 + 256], Bt[hp, tcs_], krj, True, True, [Bt_b, KR_b], [ps_b[bA]], False)
                        mm(ps[bB][hp, jj * 64:(jj + 1) * 64], KR[hp, j_, 0, :], Bt[hp, tcs_], True, True, [KR_b, Bt_b],
                           [ps_b[bB]], True)
                for hl in range(2):
                    hp = HP[hl]
                    for hb_ in range(2):
                        bA = 2 * hl + hb_
                        S.op("dve", lambda e, hp=hp, bA=bA, hb_=hb_: e.tensor_mul(
                            out=A1[hp, 2 * hb_:2 * hb_ + 2, :, :].rearrange("p c x t -> p (c x) t"),
                            in0=ps[bA][hp, 0:512].rearrange("p (x t) -> p x t", t=128),
                            in1=mask1[hp, :].unsqueeze(1).broadcast_to([64, 4, 128])), reads=[ps_b[bA], cf_b],
                            writes=[A1_b])
                    S.op("dve", lambda e, hp=hp, hl=hl: e.tensor_mul(
                        out=Ab[hp, :, :], in0=ps[4 + hl][hp, 0:GQ * 64].rearrange("p (c t) -> p c t", t=64),
                        in1=mask2[hp, :].unsqueeze(1).broadcast_to([64, GQ, 64])), reads=[ps_b[4 + hl], cf_b],
                        writes=[Ab_b])
                S.op("dve", lambda e: e.tensor_scalar(out=Y[:, :, 0, :], in0=A1[:, :, 1, 0:64], scalar1=-1.0, scalar2=None,
                                                      op0=ALU.mult), reads=[A1_b], writes=[Y_b])
                S.op("dve", lambda e: e.tensor_scalar(out=Y[:, :, 1, :], in0=Ab[:, :, :], scalar1=-1.0, scalar2=None,
                                                      op0=ALU.mult), reads=[Ab_b], writes=[Y_b])
                S.op("dve", lambda e: e.tensor_add(out=Pm[:].rearrange("p c a t -> p (c a) t"),
                                                   in0=Y[:].rearrange("p c a t -> p (c a) t"),
                                                   in1=id64.unsqueeze(1).broadcast_to([128, 2 * GQ, 64])),
                     reads=[Y_b, cf_b], writes=[Pm_b])
                for lev in range(5):
                    for jj in range(GQ):
                        for hl in range(2):
                            hp = HP[hl]
                            b_ = 6 + hl
                            mm(ps[b_][hp, jj * 128:jj * 128 + 64], Y[hp, jj, 1, :], Y[hp, jj, 0, :], True, True, [Y_b],
                               [ps_b[b_]], False)
                            mm(ps[b_][hp, jj * 128 + 64:jj * 128 + 128], Y[hp, jj, 0, :], Y[hp, jj, 1, :], True, True,
                               [Y_b], [ps_b[b_]], True)
                    S.op("act", lambda e: e.copy(out=Y[HP[0], :, :, :].rearrange("p c a t -> p (c a t)"),
                                                 in_=ps[6][HP[0], 0:GQ * 128]), reads=[ps_b[6]], writes=[Y_b])
                    S.op("act", lambda e: e.copy(out=Y[HP[1], :, :, :].rearrange("p c a t -> p (c a t)"),
                                                 in_=ps[7][HP[1], 0:GQ * 128]), reads=[ps_b[7]], writes=[Y_b])
                    for jj in range(GQ):
                        for hl in range(2):
                            hp = HP[hl]
                            b_ = 4 + hl
                            mm(ps[b_][hp, jj * 128:jj * 128 + 64], Pm[hp, jj, 1, :], Y[hp, jj, 0, :], True, True,
                               [Pm_b, Y_b], [ps_b[b_]], False)
                            mm(ps[b_][hp, jj * 128 + 64:jj * 128 + 128], Y[hp, jj, 0, :], Pm[hp, jj, 1, :], True, True,
                               [Pm_b, Y_b], [ps_b[b_]], True)
                    for hl in range(2):
                        hp = HP[hl]
                        S.op("dve", lambda e, hp=hp, hl=hl: e.tensor_add(
                            out=Pm[hp, :, :, :].rearrange("p c a t -> p (c a t)"),
                            in0=Pm[hp, :, :, :].rearrange("p c a t -> p (c a t)"), in1=ps[4 + hl][hp, 0:GQ * 128]),
                            reads=[ps_b[4 + hl], Pm_b], writes=[Pm_b])

            def emit_TM(j_):
                tcs_ = slice(j_ * C, (j_ + 1) * C)
                for ti, (srcT, sbf) in enumerate(((vT, v_b), (Kh, Kh_b), (Bh, Bh_b))):
                    for a_ in range(2):
                        S.op("pe", lambda e, ti=ti, srcT=srcT, a_=a_: e.matmul(
                            ps[0][a_ * 64:(a_ + 1) * 64, ti * 128:(ti + 1) * 128], lhsT=srcT[:, tcs_], rhs=ident,
                            start=True, stop=True), reads=[sbf, cf_b], writes=[ps_b[0]], sig=(ti == 2 and a_ == 1))
                S.op("act", lambda e: e.copy(out=TMx[:, j_ % 2, :, :].rearrange("p a f -> p (a f)"), in_=ps[0][:, 0:384]),
                     reads=[ps_b[0]], writes=[TMx_b[j_ % 2]])

            for j in range(NCH):
                tc0 = j * C
                tcs = slice(tc0, tc0 + C)
                jj = j % GQ
                if jj == 0:
                    group_stage(j // GQ)
                if STOP == "rw_inv":
                    raise _Stop()
                if jj == 0:
                    emit_TM(j)
                TM = TMx[:, j % 2, :, :]
                TM_b = TMx_b[j % 2]
                if STOP == "rw_tm":
                    raise _Stop()
                for hl in range(2):
                    hp = HP[hl]
                    b_ = BK_W[hl]
                    o_ = ps[b_][hp, 0:64]
                    mm(o_, KR[hp, j, 0, :], Hst[hp, p, :], True, False, [KR_b, Hst_b], [ps_b[b_]], False)
                    mm(o_, A1[hp, jj, 0, 0:64], TM[hp, 0, hp], False, True, [A1_b, TM_b], [ps_b[b_]], True)
                for hl in range(2):
                    hp = HP[hl]
                    b_ = BK_W[hl]
                    S.op("act", lambda e, hp=hp, b_=b_: e.copy(out=W1[hp, :], in_=ps[b_][hp, 0:64]), reads=[ps_b[b_]],
                         writes=[W1_b])
                if j + 1 < NCH and (j + 1) % GQ != 0:
                    emit_TM(j + 1)
                if STOP == "rw_w1":
                    raise _Stop()
                for hl in range(2):
                    hp = HP[hl]
                    b_ = BK_W[hl]
                    mm(ps[b_][hp, 128:192], Pm[hp, jj, 0, :], W1[hp, :], True, True, [Pm_b, W1_b], [ps_b[b_]], True)
                for hl in range(2):
                    hp = HP[hl]
                    b_ = BK_W[hl]
                    S.op("act", lambda e, hp=hp, b_=b_: e.mul(out=U[hp, :], in_=ps[b_][hp, 128:192], mul=-1.0),
                         reads=[ps_b[b_]], writes=[U_b])
                if STOP == "rw_u":
                    raise _Stop()
                for hl in range(2):
                    hp = HP[hl]
                    b_ = BK_H[hl]
                    o_ = ps[b_][hp, 0:64]
                    mm(o_, TM[hp, 1, hp], TM[hp, 0, hp], True, False, [TM_b], [ps_b[b_]], False)
                    mm(o_, TM[hp, 2, hp], U[hp, :], False, True, [TM_b, U_b], [ps_b[b_]], True)
                for hl in range(2):
                    hp = HP[hl]
                    b_ = BK_O[hl]
                    o_ = ps[b_][hp, 0:64]
                    mm(o_, Hst[hp, p, :], KR[hp, j, 1, :], True, False, [Hst_b, KR_b], [ps_b[b_]], False)
                    mm(o_, TM[hp, 0, hp], A1[hp, jj, 0, 64:128], False, False, [TM_b, A1_b], [ps_b[b_]], False)
                    mm(o_, U[hp, :], A1[hp, jj, 1, 64:128], False, True, [U_b, A1_b], [ps_b[b_]], True)
                for hl in range(2):
                    hp = HP[hl]
                    b_ = BK_O[hl]
                    S.op("act", lambda e, hp=hp, b_=b_: e.copy(out=OT[hp, tcs], in_=ps[b_][hp, 0:64]), reads=[ps_b[b_]],
                         writes=[OT_b])
                if STOP == "rw_o":
                    raise _Stop()
                for hl in range(2):
                    hp = HP[hl]
                    b_ = BK_H[hl]
                    S.op("dve", lambda e, hp=hp, b_=b_: e.scalar_tensor_tensor(
                        out=Hst[hp, p, :], in0=Hst[hp, p, :], scalar=gam[hp, j:j + 1], in1=ps[b_][hp, 0:64],
                        op0=ALU.mult, op1=ALU.add), reads=[Hst_b, gam_b, ps_b[b_]], writes=[Hst_b])
            if STOP == "rw_chunk":
                raise _Stop()
            if (l, i, p) == (0, 0, 0):
                dbg(8, OT[:, :], OT_b)
            mm(ps[0][:, :], mean_blk, OT[:, :], True, True, [cf_b, OT_b], [ps_b[0]], True)
            S.op("dve", lambda e: e.tensor_sub(out=OT[:, :], in0=OT[:, :], in1=ps[0][:, :]), reads=[OT_b, ps_b[0]],
                 writes=[OT_b])
            S.op("act", lambda e: e.activation(out=eT[:, :], in_=OT[:, :], func=AF.Square), reads=[OT_b], writes=[e_b])
            mm(ps[1][:, :], mean_blk, eT[:, :], True, True, [cf_b, e_b], [ps_b[1]], True)
            rsqrt(eT[:, :], ps[1][:, :], float(LNX_EPS), [ps_b[1]], e_b)
            S.op("dve", lambda e: e.tensor_mul(out=OT[:, :], in0=OT[:, :], in1=eT[:, :]), reads=[OT_b, e_b], writes=[OT_b])
            S.op("dve", lambda e: e.tensor_scalar(out=OT[:, :], in0=OT[:, :], scalar1=pcol(l, "lng", p),
                                                  scalar2=pcol(l, "lnb", p), op0=ALU.mult, op1=ALU.add),
                 reads=[OT_b, pp_b], writes=[OT_b])
            S.op("dve", lambda e: e.tensor_add(out=OT[:, :], in0=OT[:, :], in1=bon[:, :]), reads=[OT_b, bon_b], writes=[OT_b])
            if (l, i, p) == (0, 0, 0):
                dbg(9, OT[:, :], OT_b)
            S.op("dve", lambda e: e.tensor_mul(out=ozT[:, p, :], in0=OT[:, :], in1=gT[:, :]), reads=[OT_b, g_b],
                 writes=[ozT_b])
            S.barrier()

    def conv_phase(l, i, last):
        with scope():
            def lt(name, shape, dt=F32):
                return AR.alloc(list(shape), dt), Buf(name)
            uext, ue_b = lt("uext", [128, 4, T + 30])
            acc, acc_b = lt("cacc", [128, 4, T])
            acc2, _ = lt("cacc2", [128, 2, T])
            acc2_b = [Buf("cacc20"), Buf("cacc21")]
            sq, _ = lt("csq", [128, 2, T])
            sq_b = [Buf("csq0"), Buf("csq1")]
            sg, sg_b = lt("csg", [128, 2, T])
            cTb, cT_b = lt("cTb", [128, 4, T], BF16)
            tmp, tmp_b = lt("ctmp", [128, 1, T])
            es_local["sq"], es_local["sq_b"] = sq, sq_b
            su = next_slab(SL_CU)
            sgt = next_slab(SL_CG)
            for c in range(4):
                proj_fm(0, su, c * 128, xTb, xTb_b)
                proj_fm(1, sgt, c * 128, xTb, xTb_b)
                S.op("act", lambda e, c=c: e.activation(out=sg[:, c % 2, :], in_=ps[1][:, :], func=AF.Sigmoid),
                     reads=[ps_b[1]], writes=[sg_b])
                S.op("dve", lambda e, c=c: e.tensor_copy(out=uext[:, c, 0:30], in_=uhist[:, c, :]), reads=[uhist_b],
                     writes=[ue_b])
                S.op("dve", lambda e, c=c: e.tensor_mul(out=uext[:, c, 30:30 + T], in0=ps[0][:, :], in1=sg[:, c % 2, :]),
                     reads=[ps_b[0], sg_b], writes=[ue_b])
                S.op("dve", lambda e, c=c: e.tensor_copy(out=uhist[:, c, :], in_=uext[:, c, T:T + 30]), reads=[ue_b],
                     writes=[uhist_b])
                S.op("dve", lambda e, c=c: e.tensor_scalar(out=acc[:, c, :], in0=uext[:, c, 0:T],
                                                           scalar1=pcol(l, "cdw", c * 31), scalar2=pcol(l, "cdb", c),
                                                           op0=ALU.mult, op1=ALU.add), reads=[ue_b, pp_b], writes=[acc_b])
                a2 = acc2[:, c % 2, :]
                a2_b = acc2_b[c % 2]
                S.op("dve", lambda e, c=c, a2=a2: e.tensor_scalar(out=a2, in0=uext[:, c, 1:1 + T],
                                                                  scalar1=pcol(l, "cdw", c * 31 + 1), scalar2=None,
                                                                  op0=ALU.mult), reads=[ue_b, pp_b], writes=[a2_b])
                for jj in range(2, 31):
                    if jj % 2 == 0:
                        S.op("dve", lambda e, c=c, jj=jj: e.scalar_tensor_tensor(out=acc[:, c, :], in0=uext[:, c, jj:jj + T],
                                                                                 scalar=pcol(l, "cdw", c * 31 + jj),
                                                                                 in1=acc[:, c, :], op0=ALU.mult, op1=ALU.add),
                             reads=[ue_b, pp_b, acc_b], writes=[acc_b])
                    else:
                        S.op("dve", lambda e, c=c, jj=jj, a2=a2: e.scalar_tensor_tensor(out=a2, in0=uext[:, c, jj:jj + T],
                                                                                         scalar=pcol(l, "cdw", c * 31 + jj),
                                                                                         in1=a2, op0=ALU.mult, op1=ALU.add),
                             reads=[ue_b, pp_b, a2_b], writes=[a2_b])
                S.op("dve", lambda e, c=c, a2=a2: e.tensor_add(out=acc[:, c, :], in0=acc[:, c, :], in1=a2),
                     reads=[acc_b, a2_b], writes=[acc_b])
            ln_fm(l, acc, acc_b, 4, mean_c, "clg", "clb", LN_EPS, [(acc, acc_b)], tmp, tmp_b)
            for c in range(4):
                S.op("act", lambda e, c=c: e.activation(out=cTb[:, c, :], in_=acc[:, c, :], func=AF.Silu), reads=[acc_b],
                     writes=[cT_b])
            s_out = next_slab(SL_COUT)
            g0 = next_slab(SL_G1)
            g1 = next_slab(SL_G1 + 1)
            merge_branch(l, 1, s_out, 4, cTb, cT_b, (g0, g1))
            if last:
                n30 = CONV_K - 1
                fm2tm_store(lambda c: (uhist[:, c, :], uhist_b), 4, n30,
                            lambda stg: (lambda e: e.dma_start(out=o_conv[l, :, :], in_=stg[0:n30, 0:512])), "oconv")
            S.barrier()

    def attn_phase(l, i, par, last):
        with scope():
            def lt(name, shape, dt=F32):
                return AR.alloc(list(shape), dt), Buf(name)
            qT, q_b = lt("qT", [128, 3, 2, T], BF16)
            pt, _ = lt("pt", [128, 2, 256], BF16)
            pt_b = [Buf("pt0"), Buf("pt1")]
            vstg, vstg_b = lt("vstg", [32, 16, 256], BF16)
            kvo, _ = lt("kvo", [128, 2, 512])
            kvo_b = [Buf("kvo0"), Buf("kvo1")]
            oTb, oT_b = lt("oTb", [128, 2, T], BF16)
            rec, rec_b = lt("rec", [128, T])
            B_c = i // 4
            o_c = (i % 4) * 32
            for g in range(3):
                win, dil = SWA_GROUPS[g]
                sA = next_slab(SL_ATT + 2 * g)
                sB = next_slab(SL_ATT + 2 * g + 1)
                kcol0 = (par * T) if g < 2 else i * T
                for c in range(4):
                    proj_fm(c % 2, sA, c * 128, xTb, xTb_b)
                    if c < 2:
                        S.op("act", lambda e, c=c, g=g: e.copy(out=qT[:, g, c, :], in_=ps[c % 2][:, :]),
                             reads=[ps_b[c % 2]], writes=[q_b])
                    else:
                        S.op("act", lambda e, c=c, g=g: e.copy(out=kT_h[g][:, c - 2, kcol0:kcol0 + T], in_=ps[c % 2][:, :]),
                             reads=[ps_b[c % 2]], writes=[kT_hb[g]])
                WB = W(sB, 8, 512)
                if g < 2:
                    for blk in range(4):
                        cols = slice(blk * 128, (blk + 1) * 128) if g == 0 else slice(blk, T, 4)
                        vb = (par * 4 + blk) if g == 0 else (blk * 2 + par)
                        pi = 2 + blk % 2
                        for kc in range(8):
                            mm(ps[pi][:, 0:256], xTb[:, kc, cols], WB[:, kc, 256:512], kc == 0, kc == 7,
                               [xTb_b, ring_b[sB]], [ps_b[pi]], kc == 7)
                        S.op("dve", lambda e, pi=pi, vb=vb, g=g: e.tensor_copy(out=Vtm[g][:, vb, :], in_=ps[pi][:, 0:256]),
                             reads=[ps_b[pi]], writes=[Vtm_b[g]])
                else:
                    for rho in range(16):
                        pi = 2 + rho % 2
                        for kc in range(8):
                            mm(ps[pi][0:32, 0:256], xTb[:, kc, slice(rho, T, 16)], WB[:, kc, 256:512], kc == 0, kc == 7,
                               [xTb_b, ring_b[sB]], [ps_b[pi]], kc == 7)
                        S.op("dve", lambda e, pi=pi, rho=rho: e.tensor_copy(out=vstg[:, rho, :], in_=ps[pi][0:32, 0:256]),
                             reads=[ps_b[pi]], writes=[vstg_b])
                    vdst = Vtm[2][o_c:o_c + 32, :, :].rearrange("p (r b) f -> p r b f", b=2)[:, :, B_c, :]
                    S.dma("sp", lambda e: e.dma_start(out=vdst, in_=vstg[:, :, :]), "vcdma", reads=[vstg_b],
                          writes=[Vtm_b[2]])
                kp = keep[g]
                for blk in range(4):
                    t0 = i * T + blk * 128
                    if t0 + 128 <= SEQ - kp:
                        continue
                    r0 = t0 - (SEQ - kp)
                    pi = 2 + blk % 2
                    for kc in range(8):
                        mm(ps[pi][:, :], xTb[:, kc, blk * 128:(blk + 1) * 128], WB[:, kc, :], kc == 0, kc == 7,
                           [xTb_b, ring_b[sB]], [ps_b[pi]], kc == 7)
                    S.op("act", lambda e, pi=pi, blk=blk: e.copy(out=kvo[:, blk % 2, :], in_=ps[pi][:, :]),
                         reads=[ps_b[pi]], writes=[kvo_b[blk % 2]])
                    S.dma("sp", lambda e, g=g, r0=r0, blk=blk: e.dma_start(out=o_swa[g][l, r0:r0 + 128, :],
                                                                           in_=kvo[:, blk % 2, :]), f"okv{blk % 2}",
                          reads=[kvo_b[blk % 2]], out=True)
            started = set()
            ptc = {"n": 0}

            def pv(pair, hl, vblk_ap, pt_ap, ptb, out_cols, vbuf):
                hp = slice(hl * 64, (hl + 1) * 64)
                first = (pair, hl) not in started
                started.add((pair, hl))
                mm(ps[4 + pair][hp, out_cols], vblk_ap, pt_ap, first, False, [vbuf, ptb], [ps_b[4 + pair]], False)
                mm(ps[6 + pair][hp, out_cols], ones_b64, pt_ap, first, False, [cb_b, ptb], [ps_b[6 + pair]], True)

            for h in range(4):
                pair, hl = h // 2, h % 2
                hp = slice(hl * 64, (hl + 1) * 64)
                for g in range(2):
                    for qb in range(4):
                        if g == 0:
                            qcols = slice(qb * 128, (qb + 1) * 128)
                            kbs = [(par * T + qb * 128, 1, par * 4 + qb, mb_same)]
                            if qb > 0:
                                kbs.append((par * T + (qb - 1) * 128, 1, par * 4 + qb - 1, mb_prev))
                            elif i > 0:
                                kbs.append(((1 - par) * T + 384, 1, (1 - par) * 4 + 3, mb_prev))
                        else:
                            qcols = slice(qb, T, 4)
                            kbs = [(par * T + qb, 4, qb * 2 + par, mb_same)]
                            if i > 0:
                                kbs.append(((1 - par) * T + qb, 4, qb * 2 + 1 - par, mb_prev))
                        k_ = ptc["n"] % 2
                        ptc["n"] += 1
                        pi = k_
                        for bi, (kc0, kst, vb, mbias) in enumerate(kbs):
                            kcols = slice(kc0, kc0 + 127 * kst + 1, kst)
                            mm(ps[pi][:, bi * 128:(bi + 1) * 128], kT_h[g][hp, pair, kcols], qT[hp, g, pair, qcols], True,
                               False, [kT_hb[g], q_b], [ps_b[pi]], False)
                            mm(ps[pi][:, bi * 128:(bi + 1) * 128], identb, mbias, False, True, [cb_b], [ps_b[pi]],
                               bi == len(kbs) - 1)
                        nb = len(kbs)
                        S.op("act", lambda e, k_=k_, pi=pi, nb=nb: e.activation(out=pt[:, k_, 0:nb * 128],
                                                                                in_=ps[pi][:, 0:nb * 128], func=AF.Exp,
                                                                                scale=0.125),
                             reads=[ps_b[pi]], writes=[pt_b[k_]])
                        for bi, (kc0, kst, vb, mbias) in enumerate(kbs):
                            pv(pair, hl, Vtm[g][:, vb, h * 64:(h + 1) * 64], pt[:, k_, bi * 128:(bi + 1) * 128], pt_b[k_],
                               qcols, Vtm_b[g])
                g = 2
                nkb = 2 if B_c >= 1 else 1
                for rg in range(4):
                    k_ = ptc["n"] % 2
                    ptc["n"] += 1
                    pi = k_
                    for kb in range(nkb):
                        Bk = B_c - kb
                        moff = 448 + (o_c // 32) * 256 + kb * 128
                        mm(ps[pi][:, kb * 128:(kb + 1) * 128], identb, cb[:, moff:moff + 128], True, False, [cb_b],
                           [ps_b[pi]], False)
                        for r4 in range(4):
                            rho = rg * 4 + r4
                            kc0 = 2048 * Bk + rho
                            kcols = slice(kc0, kc0 + 127 * 16 + 1, 16)
                            cc = kb * 128 + r4 * 32
                            mm(ps[pi][:, cc:cc + 32], kT_h[2][hp, pair, kcols], qT[hp, 2, pair, slice(rho, T, 16)], False,
                               r4 == 3, [kT_hb[2], q_b], [ps_b[pi]], r4 == 3 and kb == nkb - 1)
                    S.op("act", lambda e, k_=k_, pi=pi: e.activation(out=pt[:, k_, 0:nkb * 128], in_=ps[pi][:, 0:nkb * 128],
                                                                     func=AF.Exp, scale=0.125),
                         reads=[ps_b[pi]], writes=[pt_b[k_]])
                    for kb in range(nkb):
                        Bk = B_c - kb
                        for r4 in range(4):
                            rho = rg * 4 + r4
                            cc = kb * 128 + r4 * 32
                            pv(pair, hl, Vtm[2][:, rho * 2 + Bk, h * 64:(h + 1) * 64], pt[:, k_, cc:cc + 32], pt_b[k_],
                               slice(rho, T, 16), Vtm_b[2])
            for pair in range(2):
                S.op("dve", lambda e, pair=pair: e.reciprocal(out=rec[:, :], in_=ps[6 + pair][:, :]), reads=[ps_b[6 + pair]],
                     writes=[rec_b])
                S.op("dve", lambda e, pair=pair: e.tensor_mul(out=oTb[:, pair, :], in0=ps[4 + pair][:, :], in1=rec[:, :]),
                     reads=[ps_b[4 + pair], rec_b], writes=[oT_b])
            s_out = next_slab(SL_AOUT)
            g0 = next_slab(SL_G2)
            g1 = next_slab(SL_G2 + 1)
            merge_branch(l, 2, s_out, 2, oTb, oT_b, (g0, g1))
            S.barrier()

    def ffn_phase(l, i, last):
        with scope():
            def lt(name, shape, dt=F32):
                return AR.alloc(list(shape), dt), Buf(name)
            hb, hb_b = lt("hb", [128, 8, T], BF16)
            pre, pre_b = merged, merged_b
            with scope():
                mbf, mbf_b = lt("mbf", [128, 8, T], BF16)
                sq, _ = lt("fsq", [128, 2, T])
                sq_b = [Buf("fsq0"), Buf("fsq1")]
                tmp, tmp_b = lt("ftmp", [128, 1, T])
                es_local["sq"], es_local["sq_b"] = sq, sq_b
                S.op("act", lambda e: e.copy(out=mbf[:], in_=merged[:]), reads=[merged_b], writes=[mbf_b])
                wo = [next_slab(SL_WO), next_slab(SL_WO + 1)]
                for oc in range(8):
                    proj_fm(oc % 2, wo[oc // 4], (oc % 4) * 128, mbf, mbf_b)
                    S.op("dve", lambda e, oc=oc: e.scalar_tensor_tensor(out=pre[:, oc, :], in0=xT32[:, oc, :],
                                                                        scalar=float(ALPHA), in1=ps[oc % 2][:, :],
                                                                        op0=ALU.mult, op1=ALU.add),
                         reads=[xT32_b, ps_b[oc % 2]], writes=[pre_b])
                if STOP == "ffn_wo":
                    raise _Stop()
                ln_fm(l, pre, pre_b, 8, mean_d, "l1g", "l1b", LN_EPS, [(xT32, xT32_b), (hb, hb_b)], tmp, tmp_b)
                S.barrier()
                if STOP == "ffn_ln1":
                    raise _Stop()
            actT, actT_b = lt("actT", [128, 22, T], BF16)
            with scope():
                NB3 = 3
                uext, _ = lt("fuext", [128, NB3, T + 2])
                ue_b = [Buf(f"fue{k}") for k in range(NB3)]
                cv, _ = lt("fcv", [128, NB3, T])
                cv_b = [Buf(f"fcv{k}") for k in range(NB3)]
                sl, _ = lt("fsl", [128, 2, T])
                sl_b = [Buf("fsl0"), Buf("fsl1")]

                def ffn_tail(qi):
                    s_, jj = divmod(qi, 4)
                    k_ = qi % NB3
                    if jj < 2:
                        S.op("act", lambda e: e.activation(out=sl[:, jj, :], in_=cv[:, k_, :], func=AF.Silu),
                             reads=[cv_b[k_]], writes=[sl_b[jj]])
                    else:
                        S.op("dve", lambda e: e.tensor_mul(out=actT[:, 2 * s_ + jj - 2, :], in0=sl[:, jj - 2, :],
                                                           in1=cv[:, k_, :]),
                             reads=[sl_b[jj - 2], cv_b[k_]], writes=[actT_b])

                slot = None
                for qi in range(44):
                    s_, jj = divmod(qi, 4)
                    if jj == 0:
                        slot = next_slab(SL_FIN + s_)
                    k_ = qi % NB3
                    pb = qi % 4
                    proj_fm(pb, slot, jj * 128, hb, hb_b)
                    S.op("act", lambda e: e.copy(out=uext[:, k_, 2:T + 2], in_=ps[pb][:, :]), reads=[ps_b[pb]],
                         writes=[ue_b[k_]])
                    S.op("act", lambda e: e.copy(out=uext[:, k_, 0:2], in_=fcar[:, qi, :]), reads=[fcar_b],
                         writes=[ue_b[k_]])
                    S.op("act", lambda e: e.activation(out=cv[:, k_, :], in_=ps[pb][:, :], func=AF.Identity,
                                                       scale=pcol(l, "fdw", qi * 3 + 2), bias=pcol(l, "fdb", qi)),
                         reads=[ps_b[pb], pp_b], writes=[cv_b[k_]])
                    S.op("act", lambda e: e.copy(out=fcar[:, qi, :], in_=uext[:, k_, T:T + 2]), reads=[ue_b[k_]],
                         writes=[fcar_b])
                    if qi > 0:
                        ffn_tail(qi - 1)
                    for tap in (1, 0):
                        S.op("dve", lambda e, tap=tap: e.scalar_tensor_tensor(
                            out=cv[:, k_, :], in0=uext[:, k_, tap:tap + T], scalar=pcol(l, "fdw", qi * 3 + tap),
                            in1=cv[:, k_, :], op0=ALU.mult, op1=ALU.add), reads=[ue_b[k_], pp_b, cv_b[k_]],
                            writes=[cv_b[k_]])
                ffn_tail(43)
                if last:
                    fc5 = fcar[:, :, :].rearrange("p (s hf jj) r -> p s hf jj r", s=11, hf=2, jj=2)
                    S.dma("sp", lambda e: [e.dma_start(
                        out=o_ffn[l, r_, hf * 2816:(hf + 1) * 2816].rearrange("(s jj p) -> p s jj", s=11, jj=2, p=128)[:, :, jj],
                        in_=fc5[:, :, hf, jj, r_], allow_slow_non_contiguous=True)
                        for hf in range(2) for r_ in range(2) for jj in range(2)],
                        "offn", reads=[fcar_b], n=8, out=True)
                S.barrier()
            if STOP == "ffn_in":
                raise _Stop()
            with scope():
                sq, _ = lt("fsq2", [128, 2, T])
                sq_b = [Buf("fsq20"), Buf("fsq21")]
                tmp, tmp_b = lt("ftmp2", [128, 1, T])
                es_local["sq"], es_local["sq_b"] = sq, sq_b
                for oc in range(8):
                    slot = next_slab(SL_FOUT + oc)
                    Wf = W(slot, 22, 128)
                    for kc in range(22):
                        mm(ps[oc % 2][:, :], Wf[:, kc, :], actT[:, kc, :], kc == 0, kc == 21, [ring_b[slot], actT_b],
                           [ps_b[oc % 2]], kc == 21)
                    S.op("dve", lambda e, oc=oc: e.scalar_tensor_tensor(out=pre[:, oc, :], in0=xT32[:, oc, :],
                                                                        scalar=float(ALPHA), in1=ps[oc % 2][:, :],
                                                                        op0=ALU.mult, op1=ALU.add),
                         reads=[xT32_b, ps_b[oc % 2]], writes=[pre_b])
                if STOP == "ffn_out":
                    raise _Stop()
                ln_fm(l, pre, pre_b, 8, mean_d, "l2g", "l2b", LN_EPS, [(pre, pre_b)], tmp, tmp_b)
                if l < DEPTH - 1:
                    S.dma("sp", lambda e: e.dma_start(out=xs_d[:, :, i * T:(i + 1) * T], in_=pre[:]), "xsst",
                          reads=[pre_b])
                else:
                    for blk in range(4):
                        for hf in range(2):
                            fm2tm_store(lambda c, blk=blk, hf=hf: (pre[:, hf * 4 + c, blk * 128:(blk + 1) * 128], pre_b), 4,
                                        128, lambda stg, blk=blk, hf=hf: (lambda e: e.dma_start(
                                            out=y_p[i * T + blk * 128:i * T + (blk + 1) * 128, hf * 512:(hf + 1) * 512],
                                            in_=stg[:, :])), "yout")
                S.barrier()


    if NS:
        xs32 = sb("xs32", [128, 8, NS])
        xs32_b = Buf("xs32")
        sel = cf[0:NS, 1408:1408 + NS * 128]
        ones_f64 = cf[:, 1920:1984]
        id64r = cf[:, 1344:1408]
        S.dma("sp", lambda e: e.dma_start(out=xs32[:], in_=xs_fm[:, :, :]), "xsload", writes=[xs32_b], local=False)
        for l in range(DEPTH):
            S.dma("sp", lambda e, l=l: e.dma_start(out=o_conv_s[l, :, 0:CONV_K - 2, :], in_=cv_nat[l, :, 1:CONV_K - 1, :]),
                  "d2d", out=True, local=False)
            S.dma("sp", lambda e, l=l: e.dma_start(out=o_ffn_s[l, :, 0, :], in_=ff_nat[l, :, 1, :]), "d2d", out=True,
                  local=False)
            for g in range(3):
                nb = NBUF[g]
                for s_ in range(NS):
                    S.dma("sp", lambda e, l=l, g=g, s_=s_, nb=nb: e.dma_start(out=o_swa_s[g][l, s_, 0:nb - 1, :],
                                                                           in_=cache_d[g][l, s_, 1:nb, :]),
                          "d2d", out=True, local=False)

    def sample_layer(l):
        N = NS
        with scope():
            def lt(name, shape, dt=F32):
                return AR.alloc(list(shape), dt), Buf(name)
            xsb, xsb_b = lt("xsb", [128, 8, N], BF16)
            mgs, mgs_b = lt("mgs", [128, 8, N])
            shs, shs_b = lt("shs", [128, 14, N])
            zrs, zrs_b = lt("zrs", [128, 14, N])
            S.op("act", lambda e: e.copy(out=xsb[:], in_=xs32[:]), reads=[xs32_b], writes=[xsb_b])
            S.dma("sp", lambda e: e.dma_start(out=shs[:], in_=sh_fm[l, :, :, :]), "sld0", writes=[shs_b])

            def sshift(pi, och, dst, dst_b):
                S.op("act", lambda e: e.copy(out=zrs[:, och, :], in_=ps[pi][:, 0:N]), reads=[ps_b[pi]], writes=[zrs_b])
                S.op("dve", lambda e: e.tensor_scalar(out=dst, in0=ps[pi][:, 0:N], scalar1=pcol(l, "omm", och), scalar2=None,
                                                      op0=ALU.mult), reads=[ps_b[pi], pp_b], writes=[dst_b])
                S.op("dve", lambda e: e.scalar_tensor_tensor(out=dst, in0=shs[:, och, :], scalar=pcol(l, "mu", och), in1=dst,
                                                             op0=ALU.mult, op1=ALU.add), reads=[shs_b, pp_b, dst_b],
                     writes=[dst_b])

            def sproj(pi, slot, col0, wcols=512):
                proj_fm(pi, slot, col0, xsb, xsb_b, ncol=N, wcols=wcols, rhs_cols=slice(0, N))

            with scope():
                lora, lora_b = lt("slora", [128, 2, N])
                ozs, ozs_b = lt("ozs", [128, 4, N], BF16)
                Hs, Hs_b = lt("Hs", [128, N, 4, 64])
                S.dma("sp", lambda e: e.dma_start(out=Hs[:], in_=wkv_fm[l, :, :, :, :]), "sld1", writes=[Hs_b])
                slot = next_slab(SL_LORA)
                for c in range(2):
                    sproj(c, slot, c * 128, wcols=256)
                    sshift(c, 12 + c, lora[:, c, :], lora_b)
                S.op("act", lambda e: e.activation(out=lora[0:64, 0, :], in_=lora[0:64, 0, :], func=AF.Tanh),
                     reads=[lora_b], writes=[lora_b])
                S.op("act", lambda e: e.activation(out=lora[:, 1, :], in_=lora[:, 1, :], func=AF.Sigmoid),
                     reads=[lora_b], writes=[lora_b])
                for p in range(4):
                    with scope():
                        rT, r_b = lt("srT", [128, N])
                        kT, k_b = lt("skT", [128, N])
                        vT, v_b = lt("svT", [128, N])
                        wT, w_b = lt("swT", [128, N])
                        aT, a_b = lt("saT", [128, N])
                        gT, g_b = lt("sgT", [128, N])
                        kkT, kk_b = lt("skkT", [128, N])
                        bT, b_b = lt("sbT", [128, N])
                        eT, e_b = lt("seT", [128, N])
                        bon, bon_b = lt("sbon", [128, N])
                        OT, OT_b = lt("sOT", [128, N])
                        t1, t1_b = lt("st1", [128, 64])
                        t2, t2_b = lt("st2", [128, 64])
                        slot = next_slab(SL_RP + p)
                        for c, (dst, db, och) in enumerate(((rT, r_b, p), (kT, k_b, 4 + p), (vT, v_b, 8 + p))):
                            sproj(c % 2, slot, c * 128, wcols=384)
                            sshift(c % 2, och, dst[:, :], db)
                        cs128 = slice(p * 128, (p + 1) * 128)
                        mm(ps[2][:, 0:N], lw[0:64, l, cs128], lora[0:64, 0, :], True, True, [lw_b, lora_b], [ps_b[2]], True)
                        mm(ps[3][:, 0:N], lw[64:128, l, cs128], lora[64:128, 0, :], True, True, [lw_b, lora_b], [ps_b[3]], True)
                        mm(ps[4][:, 0:N], lw[:, l, 512 + p * 128:512 + (p + 1) * 128], lora[:, 1, :], True, True,
                           [lw_b, lora_b], [ps_b[4]], True)
                        S.op("act", lambda e: e.activation(out=eT[:, :], in_=ps[2][:, 0:N], func=AF.Exp, scale=-1.0,
                                                           bias=pcol(l, "nw0", p)), reads=[ps_b[2], pp_b], writes=[e_b])
                        S.op("act", lambda e: e.activation(out=eT[:, :], in_=eT[:, :], func=AF.Ln, bias=1.0), reads=[e_b],
                             writes=[e_b])
                        S.op("act", lambda e: e.activation(out=eT[:, :], in_=eT[:, :], func=AF.Exp, scale=-1.0, bias=-0.5),
                             reads=[e_b], writes=[e_b])
                        S.op("act", lambda e: e.activation(out=wT[:, :], in_=eT[:, :], func=AF.Exp, scale=-1.0),
                             reads=[e_b], writes=[w_b])
                        S.op("act", lambda e: e.activation(out=aT[:, :], in_=ps[3][:, 0:N], func=AF.Sigmoid,
                                                           bias=pcol(l, "a0", p)), reads=[ps_b[3], pp_b], writes=[a_b])
                        S.op("act", lambda e: e.copy(out=gT[:, :], in_=ps[4][:, 0:N]), reads=[ps_b[4]], writes=[g_b])
                        S.op("dve", lambda e: e.tensor_scalar(out=kkT[:, :], in0=kT[:, :], scalar1=pcol(l, "kk", p),
                                                              scalar2=None, op0=ALU.mult), reads=[k_b, pp_b], writes=[kk_b])
                        S.op("dve", lambda e: e.tensor_mul(out=eT[:, :], in0=kkT[:, :], in1=kkT[:, :]), reads=[kk_b],
                             writes=[e_b])
                        mm(ps[2][:, 0:N], ones_blk, eT[:, :], True, True, [cf_b, e_b], [ps_b[2]], True)
                        rsqrt(eT[:, :], ps[2][:, 0:N], 1e-24, [ps_b[2]], e_b)
                        S.op("dve", lambda e: e.tensor_mul(out=kkT[:, :], in0=kkT[:, :], in1=eT[:, :]), reads=[kk_b, e_b],
                             writes=[kk_b])
                        S.op("dve", lambda e: e.tensor_mul(out=bT[:, :], in0=kkT[:, :], in1=aT[:, :]), reads=[kk_b, a_b],
                             writes=[b_b])
                        S.op("dve", lambda e: e.tensor_scalar(out=aT[:, :], in0=aT[:, :], scalar1=-1.0,
                                                              scalar2=pcol(l, "ka", p), op0=ALU.add, op1=ALU.mult),
                             reads=[a_b, pp_b], writes=[a_b])
                        S.op("dve", lambda e: e.scalar_tensor_tensor(out=kT[:, :], in0=aT[:, :], scalar=1.0, in1=kT[:, :],
                                                                     op0=ALU.add, op1=ALU.mult), reads=[a_b, k_b],
                             writes=[k_b])
                        S.op("dve", lambda e: e.scalar_tensor_tensor(out=eT[:, :], in0=rT[:, :], scalar=pcol(l, "rk", p),
                                                                     in1=kT[:, :], op0=ALU.mult, op1=ALU.mult),
                             reads=[r_b, k_b, pp_b], writes=[e_b])
                        mm(ps[3][:, 0:N], ones_blk, eT[:, :], True, True, [cf_b, e_b], [ps_b[3]], True)
                        S.op("dve", lambda e: e.tensor_mul(out=bon[:, :], in0=ps[3][:, 0:N], in1=vT[:, :]),
                             reads=[ps_b[3], v_b], writes=[bon_b])
                        S.op("dve", lambda e: e.tensor_scalar(out=bT[:, :], in0=bT[:, :], scalar1=-1.0, scalar2=None,
                                                              op0=ALU.mult), reads=[b_b], writes=[b_b])
                        for s_ in range(N):
                            H0 = Hs[:, s_, p, :]
                            sc = slice(s_, s_ + 1)
                            S.op("dve", lambda e, H0=H0, sc=sc: e.tensor_scalar(out=t1[:, :], in0=H0, scalar1=kkT[:, sc],
                                                                               scalar2=None, op0=ALU.mult),
                                 reads=[Hs_b, kk_b], writes=[t1_b])
                            S.op("dve", lambda e, sc=sc: e.tensor_scalar(out=t2[:, :], in0=id64r, scalar1=vT[:, sc],
                                                                        scalar2=None, op0=ALU.mult),
                                 reads=[cf_b, v_b], writes=[t2_b])
                            mm(ps[0][:, 0:64], ones_blk, t1[:, :], True, True, [cf_b, t1_b], [ps_b[0]], True)
                            mm(ps[1][:, 0:64], ones_blk, t2[:, :], True, True, [cf_b, t2_b], [ps_b[1]], True)
                            S.op("dve", lambda e, H0=H0, sc=sc: e.tensor_scalar(out=H0, in0=H0, scalar1=wT[:, sc], scalar2=None,
                                                                               op0=ALU.mult), reads=[Hs_b, w_b],
                                 writes=[Hs_b])
                            S.op("dve", lambda e, H0=H0, sc=sc: e.scalar_tensor_tensor(out=H0, in0=ps[0][:, 0:64],
                                                                                      scalar=bT[:, sc], in1=H0,
                                                                                      op0=ALU.mult, op1=ALU.add),
                                 reads=[ps_b[0], b_b, Hs_b], writes=[Hs_b])
                            S.op("dve", lambda e, H0=H0, sc=sc: e.scalar_tensor_tensor(out=H0, in0=ps[1][:, 0:64],
                                                                                      scalar=kT[:, sc], in1=H0,
                                                                                      op0=ALU.mult, op1=ALU.add),
                                 reads=[ps_b[1], k_b, Hs_b], writes=[Hs_b])
                            for hl in range(2):
                                hp = slice(hl * 64, (hl + 1) * 64)
                                mm(ps[4 + hl][hp, s_:s_ + 1], Hs[hp, s_, p, :], rT[hp, sc], True, True, [Hs_b, r_b],
                                   [ps_b[4 + hl]], True)
                        for hl in range(2):
                            hp = slice(hl * 64, (hl + 1) * 64)
                            S.op("act", lambda e, hp=hp, hl=hl: e.copy(out=OT[hp, :], in_=ps[4 + hl][hp, 0:N]),
                                 reads=[ps_b[4 + hl]], writes=[OT_b])
                        mm(ps[0][:, 0:N], mean_blk, OT[:, :], True, True, [cf_b, OT_b], [ps_b[0]], True)
                        S.op("dve", lambda e: e.tensor_sub(out=OT[:, :], in0=OT[:, :], in1=ps[0][:, 0:N]),
                             reads=[OT_b, ps_b[0]], writes=[OT_b])
                        S.op("act", lambda e: e.activation(out=eT[:, :], in_=OT[:, :], func=AF.Square), reads=[OT_b],
                             writes=[e_b])
                        mm(ps[1][:, 0:N], mean_blk, eT[:, :], True, True, [cf_b, e_b], [ps_b[1]], True)
                        rsqrt(eT[:, :], ps[1][:, 0:N], float(LNX_EPS), [ps_b[1]], e_b)
                        S.op("dve", lambda e: e.tensor_mul(out=OT[:, :], in0=OT[:, :], in1=eT[:, :]), reads=[OT_b, e_b],
                             writes=[OT_b])
                        S.op("dve", lambda e: e.tensor_scalar(out=OT[:, :], in0=OT[:, :], scalar1=pcol(l, "lng", p),
                                                              scalar2=pcol(l, "lnb", p), op0=ALU.mult, op1=ALU.add),
                             reads=[OT_b, pp_b], writes=[OT_b])
                        S.op("dve", lambda e: e.tensor_add(out=OT[:, :], in0=OT[:, :], in1=bon[:, :]), reads=[OT_b, bon_b],
                             writes=[OT_b])
                        S.op("dve", lambda e: e.tensor_mul(out=ozs[:, p, :], in0=OT[:, :], in1=gT[:, :]), reads=[OT_b, g_b],
                             writes=[ozs_b])
                        S.barrier()
                s_out = next_slab(SL_ROUT)
                g0 = next_slab(SL_G0)
                g1 = next_slab(SL_G0 + 1)
                merge_branch(l, 0, s_out, 4, ozs, ozs_b, (g0, g1), N=N, xb=xsb, xb_b=xsb_b, mg=mgs, mg_b=mgs_b)
                S.dma("sp", lambda e: [e.dma_start(out=o_shift_s[l, s_, :].rearrange("(c p) -> p c", p=128), in_=zrs[:, :, s_],
                                                   allow_slow_non_contiguous=True) for s_ in range(N)], "sosh",
                      reads=[zrs_b], n=N, out=True)
                for s_ in range(N):
                    with scope_alloc([64, 8, 64], F32) as stg:
                        stg_b = Buf("swkvst")
                        for h in range(8):
                            pr_, hl = h // 2, h % 2
                            S.op("pe", lambda e, pr_=pr_, hl=hl, s_=s_: e.matmul(
                                ps[4 + hl][0:64, pr_ * 64:(pr_ + 1) * 64], lhsT=Hs[hl * 64:(hl + 1) * 64, s_, pr_, :],
                                rhs=cf[hl * 64:(hl + 1) * 64, 1344:1408], start=True, stop=True), reads=[Hs_b, cf_b],
                                writes=[ps_b[4 + hl]])
                        stg4 = stg[:].rearrange("p (pr hl) k -> p pr hl k", hl=2)
                        for hl in range(2):
                            S.op("act", lambda e, hl=hl: e.copy(out=stg4[:, :, hl, :],
                                                                in_=ps[4 + hl][0:64, 0:256].rearrange("p (pr k) -> p pr k", k=64)),
                                 reads=[ps_b[4 + hl]], writes=[stg_b])
                        S.dma("sp", lambda e, s_=s_: e.dma_start(out=o_wkv_s[l, s_].rearrange("h v k -> v h k"), in_=stg[:]),
                              "sowkv", reads=[stg_b], out=True)
                        S.barrier()
                S.barrier()

            with scope():
                stc, stc_b = lt("stc", [128, 4, N, 30])
                uS, uS_b = lt("uS", [128, 4, N])
                acc, acc_b = lt("sacc", [128, 4, N])
                prod, prod_b = lt("sprod", [128, N, 30])
                sg, sg_b = lt("ssg", [128, N])
                cTb, cT_b = lt("scTb", [128, 4, N], BF16)
                sq, _ = lt("ssq", [128, 2, N])
                sq_b = [Buf("ssq0"), Buf("ssq1")]
                tmp, tmp_b = lt("stmp", [128, 1, N])
                es_local["sq"], es_local["sq_b"] = sq, sq_b
                S.dma("sp", lambda e: e.dma_start(out=stc[:], in_=cv_fm[l, :, :, :, :]), "sld2", writes=[stc_b])
                su = next_slab(SL_CU)
                sgt = next_slab(SL_CG)
                for c in range(4):
                    sproj(0, su, c * 128)
                    sproj(1, sgt, c * 128)
                    S.op("act", lambda e: e.activation(out=sg[:, :], in_=ps[1][:, 0:N], func=AF.Sigmoid), reads=[ps_b[1]],
                         writes=[sg_b])
                    S.op("dve", lambda e, c=c: e.tensor_mul(out=uS[:, c, :], in0=ps[0][:, 0:N], in1=sg[:, :]),
                         reads=[ps_b[0], sg_b], writes=[uS_b])
                    wv = pp[:, l, PC["cdw"] + c * 31:PC["cdw"] + c * 31 + 30]
                    S.op("dve", lambda e, c=c, wv=wv: e.tensor_mul(out=prod[:], in0=stc[:, c, :, :],
                                                                   in1=wv.unsqueeze(1).broadcast_to([128, N, 30])),
                         reads=[stc_b, pp_b], writes=[prod_b])
                    S.op("dve", lambda e, c=c: e.tensor_reduce(out=acc[:, c, :], in_=prod[:], axis=AX.X, op=ALU.add),
                         reads=[prod_b], writes=[acc_b])
                    S.op("dve", lambda e, c=c: e.scalar_tensor_tensor(out=acc[:, c, :], in0=uS[:, c, :],
                                                                      scalar=pcol(l, "cdw", c * 31 + 30), in1=acc[:, c, :],
                                                                      op0=ALU.mult, op1=ALU.add),
                         reads=[uS_b, pp_b, acc_b], writes=[acc_b])
                    S.op("dve", lambda e, c=c: e.tensor_scalar(out=acc[:, c, :], in0=acc[:, c, :], scalar1=pcol(l, "cdb", c),
                                                               scalar2=None, op0=ALU.add), reads=[acc_b, pp_b],
                         writes=[acc_b])
                ln_fm(l, acc, acc_b, 4, mean_c, "clg", "clb", LN_EPS, [(acc, acc_b)], tmp, tmp_b, N=N)
                for c in range(4):
                    S.op("act", lambda e, c=c: e.activation(out=cTb[:, c, :], in_=acc[:, c, :], func=AF.Silu),
                         reads=[acc_b], writes=[cT_b])
                s_out = next_slab(SL_COUT)
                g0 = next_slab(SL_G1)
                g1 = next_slab(SL_G1 + 1)
                merge_branch(l, 1, s_out, 4, cTb, cT_b, (g0, g1), N=N, xb=xsb, xb_b=xsb_b, mg=mgs, mg_b=mgs_b)
                fm2tm_store(lambda c: (uS[:, c, :], uS_b), 4, N,
                            lambda stg: (lambda e: e.dma_start(out=o_conv_s[l, :, CONV_K - 2, :], in_=stg[0:N, 0:512])),
                            "soconv")
                S.barrier()

            with scope():
                qT, q_b = lt("sqT", [128, 2, N])
                kT, k_b = lt("skT2", [128, 2, N])
                vT, v_b = lt("svT2", [128, 2, N])
                Qtm, Qtm_b = lt("sQtm", [N, 256])
                Kt, _ = lt("sKt", [128, 2, 256])
                Kt_b = [Buf("sKt0"), Buf("sKt1")]
                Vt, _ = lt("sVt", [128, 2, 256])
                Vt_b = [Buf("sVt0"), Buf("sVt1")]
                prod, prod_b = lt("saprod", [128, 256])
                sc4, sc4_b = lt("ssc4", [128, 4])
                pself, pself_b = lt("spself", [128, 2, N])
                numS, numS_b = lt("snumS", [128, 2, N])
                denS, denS_b = lt("sdenS", [128, 2, N])
                oTs, oTs_b = lt("soTs", [128, 2, N], BF16)
                S.op("dve", lambda e: e.memset(numS[:], 0.0), writes=[numS_b])
                S.op("dve", lambda e: e.memset(denS[:], 0.0), writes=[denS_b])
                started = set()
                kvn = 0
                for g in range(3):
                    win, dil = SWA_GROUPS[g]
                    nb = NBUF[g]
                    sA = next_slab(SL_ATT + 2 * g)
                    sB = next_slab(SL_ATT + 2 * g + 1)
                    for c in range(4):
                        sproj(c % 2, sA, c * 128)
                        dst, db = (qT, q_b) if c < 2 else (kT, k_b)
                        S.op("act", lambda e, c=c, dst=dst: e.copy(out=dst[:, c % 2, :], in_=ps[c % 2][:, 0:N]),
                             reads=[ps_b[c % 2]], writes=[db])
                    for c in range(2):
                        sproj(2 + c, sB, 256 + c * 128)
                        S.op("act", lambda e, c=c: e.copy(out=vT[:, c, :], in_=ps[2 + c][:, 0:N]), reads=[ps_b[2 + c]],
                             writes=[v_b])
                    fm2tm_store(lambda c: ((kT[:, c, :], k_b) if c < 2 else (vT[:, c - 2, :], v_b)), 4, N,
                                lambda stg, g=g, nb=nb: (lambda e: e.dma_start(out=o_swa_s[g][l, :, nb - 1, :],
                                                                                in_=stg[0:N, 0:512])), "soswa")
                    S.op("dve", lambda e: e.tensor_mul(out=pself[:], in0=qT[:], in1=kT[:]), reads=[q_b, k_b],
                         writes=[pself_b])
                    mm(ps[0][:, 0:2 * N], ones_blk, pself[:].rearrange("p a n -> p (a n)"), True, True, [cf_b, pself_b],
                       [ps_b[0]], True)
                    S.op("act", lambda e: e.activation(out=pself[:].rearrange("p a n -> p (a n)"), in_=ps[0][:, 0:2 * N],
                                                       func=AF.Exp, scale=0.125), reads=[ps_b[0]], writes=[pself_b])
                    S.op("dve", lambda e: e.tensor_add(out=denS[:], in0=denS[:], in1=pself[:]), reads=[denS_b, pself_b],
                         writes=[denS_b])
                    S.op("dve", lambda e: e.tensor_mul(out=pself[:], in0=pself[:], in1=vT[:]), reads=[pself_b, v_b],
                         writes=[pself_b])
                    S.op("dve", lambda e: e.tensor_add(out=numS[:], in0=numS[:], in1=pself[:]), reads=[numS_b, pself_b],
                         writes=[numS_b])
                    for c in range(2):
                        S.op("pe", lambda e, c=c: e.matmul(ps[1][0:N, c * 128:(c + 1) * 128], lhsT=qT[:, c, :], rhs=ident,
                                                           start=True, stop=True), reads=[q_b, cf_b], writes=[ps_b[1]])
                    S.op("act", lambda e: e.copy(out=Qtm[:, :], in_=ps[1][0:N, 0:256]), reads=[ps_b[1]], writes=[Qtm_b])
                    for s_ in range(N):
                        k_ = kvn % 2
                        kvn += 1
                        S.dma("sp", lambda e, k_=k_, s_=s_, g=g, dil=dil, nb=nb: e.dma_start(
                            out=Kt[:, k_, :], in_=cache_d[g][l, s_, slice(0, nb, dil), 0:256]), f"sk{k_}",
                            writes=[Kt_b[k_]])
                        S.dma("sp", lambda e, k_=k_, s_=s_, g=g, dil=dil, nb=nb: e.dma_start(
                            out=Vt[:, k_, :], in_=cache_d[g][l, s_, slice(0, nb, dil), 256:512]), f"sv{k_}",
                            writes=[Vt_b[k_]])
                        mm(ps[2][:, 0:256], sel[:, s_ * 128:(s_ + 1) * 128], Qtm[:, :], True, True, [cf_b, Qtm_b],
                           [ps_b[2]], True)
                        S.op("dve", lambda e, k_=k_: e.tensor_mul(out=prod[:, :], in0=Kt[:, k_, :], in1=ps[2][:, 0:256]),
                             reads=[Kt_b[k_], ps_b[2]], writes=[prod_b])
                        S.op("dve", lambda e: e.tensor_reduce(out=sc4[:, :], in_=prod[:, :].rearrange("p (h e) -> p h e", e=64),
                                                              axis=AX.X, op=ALU.add), reads=[prod_b], writes=[sc4_b])
                        S.op("act", lambda e: e.activation(out=sc4[:, :], in_=sc4[:, :], func=AF.Exp, scale=0.125),
                             reads=[sc4_b], writes=[sc4_b])
                        for h in range(4):
                            pair, hl = h // 2, h % 2
                            hp = slice(hl * 64, (hl + 1) * 64)
                            col = pair * N + s_
                            first = (hl,) not in started
                            started.add((hl,))
                            mm(ps[4][hp, col:col + 1], Vt[:, k_, h * 64:(h + 1) * 64], sc4[:, h:h + 1], first, False,
                               [Vt_b[k_], sc4_b], [ps_b[4]], False)
                            mm(ps[5][hp, col:col + 1], ones_f64, sc4[:, h:h + 1], first, False, [cf_b, sc4_b], [ps_b[5]],
                               True)
                S.op("dve", lambda e: e.tensor_add(out=denS[:].rearrange("p a n -> p (a n)"),
                                                   in0=denS[:].rearrange("p a n -> p (a n)"), in1=ps[5][:, 0:2 * N]),
                     reads=[denS_b, ps_b[5]], writes=[denS_b])
                S.op("dve", lambda e: e.tensor_add(out=numS[:].rearrange("p a n -> p (a n)"),
                                                   in0=numS[:].rearrange("p a n -> p (a n)"), in1=ps[4][:, 0:2 * N]),
                     reads=[numS_b, ps_b[4]], writes=[numS_b])
                S.op("dve", lambda e: e.reciprocal(out=denS[:], in_=denS[:]), reads=[denS_b], writes=[denS_b])
                S.op("dve", lambda e: e.tensor_mul(out=oTs[:], in0=numS[:], in1=denS[:]), reads=[numS_b, denS_b],
                     writes=[oTs_b])
                s_out = next_slab(SL_AOUT)
                g0 = next_slab(SL_G2)
                g1 = next_slab(SL_G2 + 1)
                merge_branch(l, 2, s_out, 2, oTs, oTs_b, (g0, g1), N=N, xb=xsb, xb_b=xsb_b, mg=mgs, mg_b=mgs_b)
                S.barrier()

            with scope():
                hb, hb_b = lt("shb", [128, 8, N], BF16)
                mbf, mbf_b = lt("smbf", [128, 8, N], BF16)
                sq, _ = lt("sfsq", [128, 2, N])
                sq_b = [Buf("sfsq0"), Buf("sfsq1")]
                tmp, tmp_b = lt("sftmp", [128, 1, N])
                stf, stf_b = lt("stf", [128, 44, N, 2])
                urw, urw_b = lt("surw", [128, 44, N])
                cv, _ = lt("sfcv", [128, 2, N])
                cv_b = [Buf("sfcv0"), Buf("sfcv1")]
                sl, _ = lt("sfsl", [128, 2, N])
                sl_b = [Buf("sfsl0"), Buf("sfsl1")]
                actT, actT_b = lt("sactT", [128, 22, N], BF16)
                es_local["sq"], es_local["sq_b"] = sq, sq_b
                S.dma("sp", lambda e: e.dma_start(out=stf[:], in_=ff_fm[l, :, :, :, :]), "sld3", writes=[stf_b])
                S.op("act", lambda e: e.copy(out=mbf[:], in_=mgs[:]), reads=[mgs_b], writes=[mbf_b])
                wo = [next_slab(SL_WO), next_slab(SL_WO + 1)]
                for oc in range(8):
                    proj_fm(oc % 2, wo[oc // 4], (oc % 4) * 128, mbf, mbf_b, ncol=N, rhs_cols=slice(0, N))
                    S.op("dve", lambda e, oc=oc: e.scalar_tensor_tensor(out=mgs[:, oc, :], in0=xs32[:, oc, :],
                                                                        scalar=float(ALPHA), in1=ps[oc % 2][:, 0:N],
                                                                        op0=ALU.mult, op1=ALU.add),
                         reads=[xs32_b, ps_b[oc % 2], mgs_b], writes=[mgs_b])
                ln_fm(l, mgs, mgs_b, 8, mean_d, "l1g", "l1b", LN_EPS, [(xs32, xs32_b), (hb, hb_b)], tmp, tmp_b, N=N)
                for s in range(11):
                    slot = next_slab(SL_FIN + s)
                    for jj in range(4):
                        qi = s * 4 + jj
                        k_ = qi % 2
                        proj_fm(k_, slot, jj * 128, hb, hb_b, ncol=N, rhs_cols=slice(0, N))
                        S.op("act", lambda e, k_=k_, qi=qi: e.copy(out=urw[:, qi, :], in_=ps[k_][:, 0:N]), reads=[ps_b[k_]],
                             writes=[urw_b])
                        S.op("dve", lambda e, k_=k_, qi=qi: e.tensor_scalar(out=cv[:, k_, :], in0=ps[k_][:, 0:N],
                                                                            scalar1=pcol(l, "fdw", qi * 3 + 2),
                                                                            scalar2=pcol(l, "fdb", qi), op0=ALU.mult,
                                                                            op1=ALU.add),
                             reads=[ps_b[k_], pp_b], writes=[cv_b[k_]])
                        for tap in (1, 0):
                            S.op("dve", lambda e, k_=k_, qi=qi, tap=tap: e.scalar_tensor_tensor(
                                out=cv[:, k_, :], in0=stf[:, qi, :, tap], scalar=pcol(l, "fdw", qi * 3 + tap),
                                in1=cv[:, k_, :], op0=ALU.mult, op1=ALU.add), reads=[stf_b, pp_b, cv_b[k_]],
                                writes=[cv_b[k_]])
                        if jj < 2:
                            S.op("act", lambda e, k_=k_, jj=jj: e.activation(out=sl[:, jj, :], in_=cv[:, k_, :], func=AF.Silu),
                                 reads=[cv_b[k_]], writes=[sl_b[jj]])
                        else:
                            S.op("dve", lambda e, k_=k_, jj=jj, s=s: e.tensor_mul(out=actT[:, 2 * s + jj - 2, :],
                                                                                   in0=sl[:, jj - 2, :], in1=cv[:, k_, :]),
                                 reads=[sl_b[jj - 2], cv_b[k_]], writes=[actT_b])
                ur5 = urw[:, :, :].rearrange("p (s hf jj) n -> p s hf jj n", s=11, hf=2, jj=2)
                S.dma("sp", lambda e: [e.dma_start(
                    out=o_ffn_s[l, s_, 1, hf * 2816:(hf + 1) * 2816].rearrange("(s jj p) -> p s jj", s=11, jj=2, p=128)[:, :, jj],
                    in_=ur5[:, :, hf, jj, s_], allow_slow_non_contiguous=True)
                    for s_ in range(N) for hf in range(2) for jj in range(2)], "soffn", reads=[urw_b], n=4 * N, out=True)
                for oc in range(8):
                    slot = next_slab(SL_FOUT + oc)
                    Wf = W(slot, 22, 128)
                    for kc in range(22):
                        mm(ps[oc % 2][:, 0:N], Wf[:, kc, :], actT[:, kc, :], kc == 0, kc == 21, [ring_b[slot], actT_b],
                           [ps_b[oc % 2]], kc == 21)
                    S.op("dve", lambda e, oc=oc: e.scalar_tensor_tensor(out=mgs[:, oc, :], in0=xs32[:, oc, :],
                                                                        scalar=float(ALPHA), in1=ps[oc % 2][:, 0:N],
                                                                        op0=ALU.mult, op1=ALU.add),
                         reads=[xs32_b, ps_b[oc % 2], mgs_b], writes=[mgs_b])
                ln_fm(l, mgs, mgs_b, 8, mean_d, "l2g", "l2b", LN_EPS, [(xs32, xs32_b)], tmp, tmp_b, N=N)
                if l == DEPTH - 1:
                    for hf in range(2):
                        fm2tm_store(lambda c, hf=hf: (xs32[:, hf * 4 + c, :], xs32_b), 4, N,
                                    lambda stg, hf=hf: (lambda e: e.dma_start(out=y_s[:, hf * 512:(hf + 1) * 512],
                                                                              in_=stg[0:N, 0:512])), "sy")
                S.barrier()
            S.barrier()

    for l in range(DEPTH):
        S.op("dve", lambda e: e.memset(Hst[:], 0.0), writes=[Hst_b])
        S.op("dve", lambda e: e.memset(zcar[:], 0.0), writes=[zcar_b])
        S.op("dve", lambda e: e.memset(uhist[:], 0.0), writes=[uhist_b])
        S.op("dve", lambda e: e.memset(fcar[:], 0.0), writes=[fcar_b])
        if l > 0 and not STOP:
            S.wait_only("sp", [(S.dkeys["xsst"][0], S.dkeys["xsst"][1], "dma")])
        cur_layer["l"] = l
        if NS:
            sample_layer(l)
        for i in range(NT):
            if STOP and (l, i) != (0, 0):
                continue
            try:
                tile_prog(l, i)
            except _Stop:
                pass

    S.wait_only("sp", list(S.out_toks))
    with nc.Block() as block:
        S.emit(block)
    es.close()
    return nc


_NC_CACHE = {}


def kernel(**inputs):
    inp = {k: np.asarray(v) for k, v in inputs.items()}
    BATCH, SEQ, _ = inp["x_prompt"].shape
    NSAMP_ALL = inp["x_sample"].shape[0]
    NCORE = 8
    NS = NSAMP_ALL // NCORE
    key = (SEQ, NS)
    if key not in _NC_CACHE:
        _NC_CACHE[key] = build(SEQ, NSAMP=NS)
    nc = _NC_CACHE[key]
    wp, pp, lw = pack_weights(inp)
    cf, cb = make_consts()
    in_maps = []
    for c in range(NCORE):
        b = c % BATCH
        x = inp["x_prompt"][b]
        x_fm = np.ascontiguousarray(x.T.reshape(8, 128, SEQ).transpose(1, 0, 2))
        im = {"x_fm": x_fm, "wpack": wp, "ppack": pp, "lorapack": lw, "constf": cf, "constb": cb}
        im.update(pack_samples(inp, c * NS, NS))
        in_maps.append(im)
    res = run_bass_kernel_spmd(nc, in_maps, core_ids=list(range(NCORE)))
    R_ = res.results
    f32 = np.float32

    def pstack(name, shape_tail):
        return np.stack([np.asarray(R_[b][name], f32).reshape((DEPTH,) + shape_tail) for b in range(BATCH)], axis=1)

    def sstack(name, shape_tail):
        return np.concatenate([np.asarray(R_[c][name], f32).reshape((DEPTH, NS) + shape_tail) for c in range(NCORE)],
                              axis=1)
    y_prompt = np.stack([np.asarray(R_[b]["y_p"], f32) for b in range(BATCH)], axis=0)
    y_sample = np.concatenate([np.asarray(R_[c]["y_s"], f32) for c in range(NCORE)], axis=0)[:, None, :]
    keep = [min(w, SEQ) for w, _ in SWA_GROUPS]
    outs = [y_prompt, y_sample,
            pstack("o_shift_p", (1, R_COLS)), sstack("o_shift_s", (1, R_COLS)),
            pstack("o_wkv_p", (R_HEADS, 64, 64)), sstack("o_wkv_s", (R_HEADS, 64, 64)),
            pstack("o_conv_p", (CONV_K - 1, CONV_CH)), sstack("o_conv_s", (CONV_K - 1, CONV_CH))]
    for g in range(3):
        outs.append(pstack(f"o_swa{g}_p", (keep[g], 2, G_HEADS, HEAD_DIM)))
        outs.append(sstack(f"o_swa{g}_s", (NBUF[g], 2, G_HEADS, HEAD_DIM)))
    outs.append(pstack("o_ffn_p", (2, 2 * D_FF)))
    outs.append(sstack("o_ffn_s", (2, 2 * D_FF)))
    return tuple(np.ascontiguousarray(o, dtype=f32) for o in outs)
```

```python
import numpy as np
from contextlib import ExitStack, contextmanager
import concourse.bass as bass
import concourse.mybir as mybir
from concourse.bass_utils import run_bass_kernel_spmd

F32 = mybir.dt.float32
BF16 = mybir.dt.bfloat16
AF = mybir.ActivationFunctionType
ALU = mybir.AluOpType
AX = mybir.AxisListType

D_MODEL = 1024
DEPTH = 2
HEAD_DIM = 64
R_HEADS = 8
R_WIDTH = 512
R_COLS = 1792
LNX_EPS = 64e-5
CONV_CH = 512
CONV_K = 31
SWA_GROUPS = ((128, 1), (512, 4), (2048, 16))
G_HEADS = 4
A_WIDTH = 768
D_FF = 2816
OFF_R = 0
OFF_C = 1792
OFF_Q = OFF_C + 1024
OFF_K = OFF_Q + 768
OFF_V = OFF_K + 768
OFF_GATE = OFF_V + 768
IN_COLS = 8192
ALPHA = (2 * DEPTH) ** 0.25
LN_EPS = 1e-5
NEG = -30000.0
NBUF = (128, 512, 2048)

P = 128
T = 512
C = 64
NCH = T // C
NSLOT = 5
SLAB = 4096
NSLAB = 42
SL_LORA = 0
SL_RP = 1
SL_ROUT = 5
SL_G0 = 6
SL_CU = 8
SL_CG = 9
SL_COUT = 10
SL_G1 = 11
SL_ATT = 13
SL_AOUT = 19
SL_G2 = 20
SL_WO = 22
SL_FIN = 24
SL_FOUT = 35
NSLAB = 43

PC = {}
_pc = 0
for _n, _w in (("mu", 14), ("omm", 14), ("w0", 4), ("nw0", 4), ("a0", 4), ("kk", 4), ("ka", 4), ("rk", 4),
               ("lng", 4), ("lnb", 4), ("cdw", 4 * 31), ("cdb", 4), ("clg", 4), ("clb", 4), ("gb", 24),
               ("l1g", 8), ("l1b", 8), ("fdw", 44 * 3), ("fdb", 44), ("l2g", 8), ("l2b", 8)):
    PC[_n] = _pc
    _pc += _w
NPC = _pc


class Buf:
    __slots__ = ("name", "w", "r")

    def __init__(self, name):
        self.name = name
        self.w = None
        self.r = []


class _Rec:
    def __init__(self):
        self.calls = []

    def __getattr__(self, name):
        def f(*a, **k):
            self.calls.append((name, a, k))
            return self
        return f


def _freeze(fn):
    rec = _Rec()
    fn(rec)
    calls = rec.calls

    def replay(e):
        out = [getattr(e, name)(*a, **k) for (name, a, k) in calls]
        return out
    return replay, len(calls)


class Sched:
    ENGS = ("pe", "act", "dve", "pool", "sp")
    LIMIT = 60000

    def __init__(self, nc, es):
        self.nc = nc
        self.es = es
        self.sems = []
        self.ops = {e: [] for e in self.ENGS}
        self.cnt = {e: 0 for e in self.ENGS}
        self.semi = {e: self._newsem(e) for e in self.ENGS}
        self.waited = {e: {} for e in self.ENGS}
        self.dkeys = {}
        self.pending = []
        self.out_toks = []
        self.lastsig = {}

    def _newsem(self, name):
        s = self.es.enter_context(self.nc.semaphore(f"s{len(self.sems)}_{name}"))
        self.sems.append(s)
        return len(self.sems) - 1

    def _force_sig(self, te):
        ops = self.ops[te]
        k = len(ops) - 1
        while ops[k][0] is None:
            k -= 1
        assert ops[k][2] is None
        self.cnt[te] += 1
        ops[k][2] = (self.semi[te], 1)
        self.lastsig[te] = True

    def _resolve(self, eng, tok):
        si, val, te = tok
        if te != "dma" and te != eng and si == self.semi[te] and val > self.cnt[te]:
            assert val == self.cnt[te] + 1
            self._force_sig(te)

    def _waits(self, eng, reads, writes, is_dma):
        toks = set()
        for b in reads:
            if b.w is not None:
                toks.add(b.w)
        for b in writes:
            if b.w is not None:
                toks.add(b.w)
            toks.update(b.r)
        out = []
        for (si, val, te) in toks:
            if te == eng and not is_dma and eng == "pe":
                continue
            if te == eng and is_dma:
                if te != "dma" and si == self.semi[te] and val > self.cnt[te]:
                    self._force_sig(te)
            self._resolve(eng, (si, val, te))
            if self.waited[eng].get(si, 0) >= val:
                continue
            out.append((si, val))
        best = {}
        for si, val in out:
            best[si] = max(best.get(si, 0), val)
        for si, val in best.items():
            self.waited[eng][si] = val
        return list(best.items())

    def op(self, eng, fn, reads=(), writes=(), sig=True):
        fn, ncalls = _freeze(fn)
        assert ncalls == 1
        waits = self._waits(eng, reads, writes, False)
        if self.cnt[eng] >= self.LIMIT:
            self.semi[eng] = self._newsem(eng)
            self.cnt[eng] = 0
        if sig:
            self.cnt[eng] += 1
            tok = (self.semi[eng], self.cnt[eng], eng)
            inc = (self.semi[eng], 1)
        else:
            tok = (self.semi[eng], self.cnt[eng] + 1, eng)
            inc = None
        self.ops[eng].append([fn, waits, inc, 1])
        self.lastsig[eng] = sig
        for b in reads:
            b.r.append(tok)
        for b in writes:
            b.w = tok
            b.r = []
        return tok

    def dma(self, eng, fn, key, reads=(), writes=(), n=1, local=True, out=False):
        fn, ncalls = _freeze(fn)
        assert ncalls == n, (ncalls, n)
        waits = self._waits(eng, reads, writes, True)
        if key not in self.dkeys or self.dkeys[key][1] + 16 * n > self.LIMIT:
            self.dkeys[key] = [self._newsem("d" + key), 0]
        d = self.dkeys[key]
        d[1] += 16 * n
        tok = (d[0], d[1], "dma")
        self.ops[eng].append([fn, waits, (d[0], 16), n])
        for b in reads:
            b.r.append(tok)
        for b in writes:
            b.w = tok
            b.r = []
        if local:
            self.pending.append(tok)
        if out:
            self.out_toks.append(tok)
        return tok

    def wait_only(self, eng, toks):
        out = []
        best = {}
        for (si, val, te) in toks:
            self._resolve(eng, (si, val, te))
            if self.waited[eng].get(si, 0) >= val:
                continue
            best[si] = max(best.get(si, 0), val)
        for si, val in best.items():
            self.waited[eng][si] = val
            out.append((si, val))
        if out:
            self.ops[eng].append([None, out, None, 0])

    def last_tok(self, eng):
        return (self.semi[eng], self.cnt[eng], eng)

    def barrier(self, engs=("pe", "act", "dve")):
        for e in engs:
            if not self.lastsig.get(e, True):
                self._force_sig(e)
        toks = [self.last_tok(e) for e in engs if self.cnt[e] > 0] + list(self.pending)
        for e in engs:
            self.wait_only(e, toks)
        self.pending = []

    def emit(self, block):
        nc = self.nc
        table = {"pe": block.tensor, "act": block.scalar, "dve": block.vector, "pool": block.gpsimd,
                 "sp": block.sync}
        for eng in self.ENGS:
            ops = self.ops[eng]
            sems = self.sems

            def body(e, ops=ops):
                for fn, waits, inc, n in ops:
                    for si, val in waits:
                        e.wait_ge(sems[si], val)
                    if fn is None:
                        continue
                    r = fn(e)
                    if inc is not None:
                        assert len(r) == n
                        for ins in r:
                            ins.then_inc(sems[inc[0]], inc[1])
            table[eng](body)


def _slab_k(Wc):
    K, n = Wc.shape
    kc = K // 128
    a = np.ascontiguousarray(Wc.reshape(kc, 128, n).transpose(1, 0, 2)).reshape(128, kc * n)
    out = np.zeros((128, SLAB), np.float32)
    out[:, :kc * n] = a
    return out


def _cols(v):
    return np.ascontiguousarray(np.asarray(v, np.float32).reshape(-1, 128).T)


def _ffn_chunk_order():
    order = []
    for s in range(11):
        for j in range(4):
            if j < 2:
                order.append(2 * s + j)
            else:
                order.append(22 + 2 * s + (j - 2))
    return order


def pack_weights(inp):
    wp = np.zeros((DEPTH, NSLAB, 128, SLAB), np.float32)
    pp = np.zeros((DEPTH, 128, NPC), np.float32)
    lw = np.zeros((DEPTH, 128, 1024), np.float32)
    forder = _ffn_chunk_order()
    for l in range(DEPTH):
        w_in = inp["w_in"][l]
        wp[l, SL_LORA] = _slab_k(w_in[:, 1536:1792])
        for p in range(4):
            cols = np.concatenate([w_in[:, p * 128:(p + 1) * 128], w_in[:, 512 + p * 128:512 + (p + 1) * 128],
                                   w_in[:, 1024 + p * 128:1024 + (p + 1) * 128]], axis=1)
            wp[l, SL_RP + p] = _slab_k(cols)
        wp[l, SL_ROUT] = _slab_k(inp["w_rwkv_out"][l])
        for bi, sl in enumerate((SL_G0, SL_G1, SL_G2)):
            for hf in range(2):
                c0 = OFF_GATE + bi * 1024 + hf * 512
                wp[l, sl + hf] = _slab_k(w_in[:, c0:c0 + 512])
        wp[l, SL_CU] = _slab_k(w_in[:, OFF_C:OFF_C + 512])
        wp[l, SL_CG] = _slab_k(w_in[:, OFF_C + 512:OFF_C + 1024])
        wp[l, SL_COUT] = _slab_k(inp["w_conv_out"][l])
        for g in range(3):
            q = w_in[:, OFF_Q + g * 256:OFF_Q + (g + 1) * 256]
            k = w_in[:, OFF_K + g * 256:OFF_K + (g + 1) * 256]
            v = w_in[:, OFF_V + g * 256:OFF_V + (g + 1) * 256]
            wp[l, SL_ATT + 2 * g] = _slab_k(np.concatenate([q, k], axis=1))
            wp[l, SL_ATT + 2 * g + 1] = _slab_k(np.concatenate([k, v], axis=1))
        wp[l, SL_AOUT] = _slab_k(inp["w_attn_out"][l])
        wp[l, SL_WO] = _slab_k(inp["w_o"][l][:, 0:512])
        wp[l, SL_WO + 1] = _slab_k(inp["w_o"][l][:, 512:1024])
        wfi = inp["w_ffn_in"][l]
        for s in range(11):
            cols = np.concatenate([wfi[:, ch * 128:(ch + 1) * 128] for ch in forder[4 * s:4 * s + 4]], axis=1)
            wp[l, SL_FIN + s] = _slab_k(cols)
        wfo = inp["w_ffn_out"][l]
        for oc in range(8):
            wp[l, SL_FOUT + oc] = _slab_k(wfo[:, oc * 128:(oc + 1) * 128])
        pr = pp[l]
        pr[:, PC["mu"]:PC["mu"] + 14] = _cols(inp["rwkv_mu"][l])
        pr[:, PC["w0"]:PC["w0"] + 4] = _cols(inp["rwkv_w0"][l])
        pr[:, PC["a0"]:PC["a0"] + 4] = _cols(inp["rwkv_a0"][l])
        pr[:, PC["kk"]:PC["kk"] + 4] = _cols(inp["rwkv_k_k"][l])
        pr[:, PC["ka"]:PC["ka"] + 4] = _cols(inp["rwkv_k_a"][l])
        pr[:, PC["rk"]:PC["rk"] + 4] = _cols(inp["rwkv_r_k"][l].reshape(-1))
        pr[:, PC["lng"]:PC["lng"] + 4] = _cols(inp["rwkv_ln_g"][l])
        pr[:, PC["lnb"]:PC["lnb"] + 4] = _cols(inp["rwkv_ln_b"][l])
        cdw = inp["conv_dw"][l]
        for c in range(4):
            pr[:, PC["cdw"] + c * 31:PC["cdw"] + (c + 1) * 31] = cdw[:, c * 128:(c + 1) * 128].T
        pr[:, PC["cdb"]:PC["cdb"] + 4] = _cols(inp["conv_dw_b"][l])
        pr[:, PC["clg"]:PC["clg"] + 4] = _cols(inp["conv_ln_g"][l])
        pr[:, PC["clb"]:PC["clb"] + 4] = _cols(inp["conv_ln_b"][l])
        pr[:, PC["gb"]:PC["gb"] + 24] = _cols(inp["gate_b"][l].reshape(-1))
        pr[:, PC["l1g"]:PC["l1g"] + 8] = _cols(inp["ln1_g"][l])
        pr[:, PC["l1b"]:PC["l1b"] + 8] = _cols(inp["ln1_b"][l])
        fdw = inp["ffn_dw"][l]
        fdb = inp["ffn_dw_b"][l]
        for qi, ch in enumerate(forder):
            pr[:, PC["fdw"] + qi * 3:PC["fdw"] + qi * 3 + 3] = fdw[:, ch * 128:(ch + 1) * 128].T
            pr[:, PC["fdb"] + qi] = fdb[ch * 128:(ch + 1) * 128]
        pr[:, PC["l2g"]:PC["l2g"] + 8] = _cols(inp["ln2_g"][l])
        pr[:, PC["l2b"]:PC["l2b"] + 8] = _cols(inp["ln2_b"][l])
        lw[l, 0:64, 0:512] = inp["rwkv_w_up"][l]
        lw[l, 64:128, 0:512] = inp["rwkv_a_up"][l]
        lw[l, :, 512:1024] = inp["rwkv_g_up"][l]
    return wp, pp, lw


def pack_samples(inp, s0, ns):
    forder = _ffn_chunk_order()
    x = inp["x_sample"][s0:s0 + ns, 0, :]
    xs_fm = np.ascontiguousarray(x.reshape(ns, 8, 128).transpose(2, 1, 0))
    sh = inp["state_shift"][:, s0:s0 + ns, 0, :]
    sh_fm = np.ascontiguousarray(sh.reshape(DEPTH, ns, 14, 128).transpose(0, 3, 2, 1))
    wk = inp["state_wkv"][:, s0:s0 + ns]
    wk = wk.reshape(DEPTH, ns, 4, 2, 64, 64)
    wkv_fm = np.ascontiguousarray(wk.transpose(0, 3, 5, 1, 2, 4)).reshape(DEPTH, 128, ns, 4, 64)
    cv = inp["state_conv"][:, s0:s0 + ns]
    cv_fm = np.ascontiguousarray(cv.reshape(DEPTH, ns, 30, 4, 128).transpose(0, 4, 3, 1, 2))
    ff = inp["state_ffn"][:, s0:s0 + ns]
    ff4 = ff.reshape(DEPTH, ns, 2, 44, 128)[:, :, :, forder, :]
    ff_fm = np.ascontiguousarray(ff4.transpose(0, 4, 3, 1, 2))
    out = {"xs_fm": xs_fm, "sh_fm": sh_fm, "wkv_fm": wkv_fm, "cv_fm": cv_fm, "cv_nat": np.ascontiguousarray(cv),
           "ff_fm": ff_fm, "ff_nat": np.ascontiguousarray(ff)}
    for g, nm in enumerate(("cache_swa_a", "cache_swa_b", "cache_swa_c")):
        c = inp[nm][:, s0:s0 + ns]
        out[f"cache{g}"] = np.ascontiguousarray(c.reshape(DEPTH, ns, c.shape[2], 512))
    return out


def make_consts():
    cf = np.zeros((128, 2048), np.float32)
    cf[:, 0:128] = np.eye(128, dtype=np.float32)
    blk = np.zeros((128, 128), np.float32)
    blk[:64, :64] = 1.0
    blk[64:, 64:] = 1.0
    cf[:, 128:256] = blk
    cf[:, 256:384] = blk / 64.0
    cf[:, 384:512] = 1.0 / 1024.0
    cf[:, 512:640] = 1.0 / 512.0
    rm = np.ones((128, 512), np.float32)
    rm[:, ::64] = 0.0
    cf[:, 640:1152] = rm
    s = np.arange(64)[:, None]
    t = np.arange(64)[None, :]
    m1 = np.concatenate([(s < t), (s <= t)], axis=1).astype(np.float32)
    cf[0:64, 1152:1280] = m1
    cf[64:128, 1152:1280] = m1
    m2 = (t < s).astype(np.float32)
    cf[0:64, 1280:1344] = m2
    cf[64:128, 1280:1344] = m2
    cf[0:64, 1344:1408] = np.eye(64, dtype=np.float32)
    cf[64:128, 1344:1408] = np.eye(64, dtype=np.float32)
    for s_ in range(4):
        cf[s_, 1408 + s_ * 128:1408 + (s_ + 1) * 128] = 1.0
    cf[:, 1920:1984] = 1.0
    cb = np.zeros((128, 1536), np.float32)
    j = np.arange(128)[:, None]
    q = np.arange(128)[None, :]
    same = np.where(j <= q, 0.0, NEG).astype(np.float32)
    prev = np.where(j >= q, 0.0, NEG).astype(np.float32)
    cb[:, 0:128] = np.eye(128, dtype=np.float32)
    cb[:, 128:256] = same
    cb[:, 256:384] = prev
    cb[:, 384:448] = 1.0
    off = 448
    for o in range(4):
        for kb, m in enumerate((same, prev)):
            for rep in range(4):
                cb[:, off:off + 32] = m[:, o * 32:(o + 1) * 32]
                off += 32
    assert off == 448 + 1024
    return cf, cb


STOP = None
DEBUG = False
DBGSTATE = {"tile": -1}


class _Stop(Exception):
    pass


def build(SEQ, NSAMP=0):
    NT = SEQ // T
    nc = bass.Bass("TRN2", target_bir_lowering=False)
    es = ExitStack()
    dram = {}

    def din(name, shape, dt=F32):
        dram[name] = nc.dram_tensor(name, list(shape), dt, kind="ExternalInput").ap()
        return dram[name]

    def dout(name, shape):
        dram[name] = nc.dram_tensor(name, list(shape), F32, kind="ExternalOutput").ap()
        return dram[name]

    x_fm = din("x_fm", [128, 8, SEQ])
    wpk = din("wpack", [DEPTH, NSLAB, 128, SLAB])
    ppk = din("ppack", [DEPTH, 128, NPC])
    lwk = din("lorapack", [DEPTH, 128, 1024])
    cfk = din("constf", [128, 2048])
    cbk = din("constb", [128, 1536])
    xs_d = nc.dram_tensor("xs_scratch", [128, 8, SEQ], F32, kind="Internal").ap()
    y_p = dout("y_p", [SEQ, D_MODEL])
    o_shift = dout("o_shift_p", [DEPTH, R_COLS])
    o_wkv = dout("o_wkv_p", [DEPTH, R_HEADS, 64, 64])
    o_conv = dout("o_conv_p", [DEPTH, CONV_K - 1, CONV_CH])
    keep = [min(w, SEQ) for w, _ in SWA_GROUPS]
    o_swa = [dout(f"o_swa{g}_p", [DEPTH, keep[g], 512]) for g in range(3)]
    o_ffn = dout("o_ffn_p", [DEPTH, 2, 2 * D_FF])
    dbg_d = dout("dbg", [16, 128, T]) if DEBUG else None

    NS = NSAMP
    if NS:
        xs_fm = din("xs_fm", [128, 8, NS])
        sh_fm = din("sh_fm", [DEPTH, 128, 14, NS])
        wkv_fm = din("wkv_fm", [DEPTH, 128, NS, 4, 64])
        cv_fm = din("cv_fm", [DEPTH, 128, 4, NS, 30])
        cv_nat = din("cv_nat", [DEPTH, NS, 30, 512])
        ff_fm = din("ff_fm", [DEPTH, 128, 44, NS, 2])
        ff_nat = din("ff_nat", [DEPTH, NS, 2, 2 * D_FF])
        cache_d = [din(f"cache{g}", [DEPTH, NS, NBUF[g], 512]) for g in range(3)]
        y_s = dout("y_s", [NS, D_MODEL])
        o_shift_s = dout("o_shift_s", [DEPTH, NS, R_COLS])
        o_wkv_s = dout("o_wkv_s", [DEPTH, NS, R_HEADS, 64, 64])
        o_conv_s = dout("o_conv_s", [DEPTH, NS, CONV_K - 1, CONV_CH])
        o_swa_s = [dout(f"o_swa{g}_s", [DEPTH, NS, NBUF[g], 512]) for g in range(3)]
        o_ffn_s = dout("o_ffn_s", [DEPTH, NS, 2, 2 * D_FF])

    S = Sched(nc, es)

    def sb(name, shape, dt=F32):
        return es.enter_context(nc.sbuf_tensor(name, list(shape), dt))

    ring = sb("ring", [128, NSLOT, SLAB], BF16)
    ring_b = [Buf(f"ring{i}") for i in range(NSLOT)]
    cf = sb("cf", [128, 2048])
    cb = sb("cb", [128, 1536], BF16)
    cf_b, cb_b = Buf("cf"), Buf("cb")
    pp = sb("pp", [128, DEPTH, NPC])
    pp_b = Buf("pp")
    lw = sb("lw", [128, DEPTH, 1024])
    lw_b = Buf("lw")
    xT32 = sb("xT32", [128, 8, T])
    xTb = sb("xTb", [128, 8, T], BF16)
    xT32_b, xTb_b = Buf("xT32"), Buf("xTb")
    merged = sb("merged", [128, 8, T])
    merged_b = Buf("merged")
    kT_h = [sb("kTa", [128, 2, 2 * T], BF16), sb("kTb", [128, 2, 2 * T], BF16), sb("kTc", [128, 2, max(SEQ, 4096)], BF16)]
    kT_hb = [Buf("kTa"), Buf("kTb"), Buf("kTc")]
    Vtm = [sb("Va", [128, 8, 256], BF16), sb("Vb", [128, 8, 256], BF16), sb("Vc", [128, 32, 256], BF16)]
    Vtm_b = [Buf("Va"), Buf("Vb"), Buf("Vc")]
    Hst = sb("Hst", [128, 4, 64])
    Hst_b = Buf("Hst")
    zcar = sb("zcar", [128, 14])
    zcar_b = Buf("zcar")
    uhist = sb("uhist", [128, 4, 30])
    uhist_b = Buf("uhist")
    fcar = sb("fcar", [128, 44, 2])
    fcar_b = Buf("fcar")
    ps = [es.enter_context(nc.psum_tensor(f"ps{i}", [128, 512], F32)) for i in range(8)]
    ps_b = [Buf(f"ps{i}") for i in range(8)]

    ARENA_N = 12288
    arena_t = sb("arena", [128, ARENA_N])

    class Arena:
        def __init__(self):
            self.top = 0
            self.stack = []

        def push(self):
            self.stack.append(self.top)

        def pop(self):
            self.top = self.stack.pop()

        def alloc(self, shape, dt=F32):
            nfree = int(np.prod(shape[1:]))
            n32 = nfree if dt == F32 else (nfree + 1) // 2
            ap = arena_t[0:shape[0], self.top:self.top + n32]
            self.top += n32
            assert self.top <= ARENA_N, ("arena overflow", self.top)
            if dt != F32:
                ap = ap.bitcast(dt)
            if len(shape) == 3:
                ap = ap.rearrange("p (a b) -> p a b", a=shape[1])
            elif len(shape) == 4:
                ap = ap.rearrange("p (a b c) -> p a b c", a=shape[1], b=shape[2])
            return ap

    AR = Arena()

    @contextmanager
    def scope():
        AR.push()
        try:
            yield None
        finally:
            AR.pop()

    @contextmanager
    def scope_alloc(shape, dt=F32):
        AR.push()
        try:
            yield AR.alloc(list(shape), dt)
        finally:
            AR.pop()

    ident = cf[:, 0:128]
    ones_blk = cf[:, 128:256]
    mean_blk = cf[:, 256:384]
    mean_d = cf[:, 384:512]
    mean_c = cf[:, 512:640]
    rmask = cf[:, 640:1152]
    identb = cb[:, 0:128]
    mb_same = cb[:, 128:256]
    mb_prev = cb[:, 256:384]
    ones_b64 = cb[:, 384:448]

    def pcol(l, name, i=0, rows=slice(0, 128)):
        c = PC[name] + i
        return pp[rows, l, c:c + 1]

    S.dma("sp", lambda e: e.dma_start(out=cf[:], in_=cfk[:, :]), "cf", writes=[cf_b], local=False)
    S.dma("pool", lambda e: e.dma_start(out=cb[:], in_=cbk[:, :]), "cb", writes=[cb_b], local=False)
    S.dma("sp", lambda e: [e.dma_start(out=pp[:, l, :], in_=ppk[l, :, :]) for l in range(DEPTH)], "pp",
          writes=[pp_b], n=DEPTH, local=False)
    S.dma("sp", lambda e: [e.dma_start(out=lw[:, l, :], in_=lwk[l, :, :]) for l in range(DEPTH)], "lw",
          writes=[lw_b], n=DEPTH, local=False)
    for l in range(DEPTH):
        S.op("dve", lambda e, l=l: e.tensor_scalar(out=pp[:, l, PC["omm"]:PC["omm"] + 14],
                                                   in0=pp[:, l, PC["mu"]:PC["mu"] + 14], scalar1=-1.0, scalar2=1.0,
                                                   op0=ALU.mult, op1=ALU.add), reads=[pp_b], writes=[pp_b])
        S.op("dve", lambda e, l=l: e.tensor_scalar(out=pp[:, l, PC["nw0"]:PC["nw0"] + 4],
                                                   in0=pp[:, l, PC["w0"]:PC["w0"] + 4], scalar1=-1.0, scalar2=None,
                                                   op0=ALU.mult), reads=[pp_b], writes=[pp_b])
    for g in range(3):
        S.op("dve", lambda e, g=g: e.memset(kT_h[g][:], 0.0), writes=[kT_hb[g]])
        S.op("dve", lambda e, g=g: e.memset(Vtm[g][:], 0.0), writes=[Vtm_b[g]])

    wstate = {"n": 0}

    def wload(l, slab):
        i = wstate["n"] % NSLOT
        wstate["n"] += 1
        S.dma("pool", lambda e: e.dma_start(out=ring[:, i, :], in_=wpk[l, slab, :, :]), f"ring{i}",
              writes=[ring_b[i]], local=False)
        return i

    class WQ:
        def __init__(self):
            self.plan = []
            self.issued = 0
            self.slots = {}

        def extend(self, items):
            self.plan.extend(items)

        def get(self, idx, ahead=NSLOT - 1):
            while self.issued < len(self.plan) and self.issued <= idx + ahead:
                l, slab = self.plan[self.issued]
                self.slots[self.issued] = wload(l, slab)
                self.issued += 1
            return self.slots[idx]

    wq = WQ()
    for l in range(DEPTH):
        if NSAMP:
            for sl in range(NSLAB):
                wq.extend([(l, sl)])
        for i in range(NT):
            if STOP and ((l, i) != (0, 0) or STOP == "load"):
                continue
            for sl in range(NSLAB):
                wq.extend([(l, sl)])
    wctr = {"i": 0}

    cur_layer = {"l": 0}

    def next_slab(expect):
        idx = wctr["i"]
        assert wq.plan[idx] == (cur_layer["l"], expect), (wq.plan[idx], cur_layer["l"], expect)
        slot = wq.get(idx, ahead=NSLOT - 3)
        wctr["i"] += 1
        return slot

    def W(slot, kc_n, ncols):
        return ring[:, slot, 0:kc_n * ncols].rearrange("p (k n) -> p k n", n=ncols)

    def dbg(idx, ap, buf):
        if DEBUG:
            S.dma("sp", lambda e: e.dma_start(out=dbg_d[idx, 0:ap.shape[0], 0:ap.shape[1]], in_=ap), "dbg", reads=[buf], out=True)

    epsc = sb("epsc", [128, 4])
    epsc_b = Buf("epsc")
    for ci, ev in enumerate((LN_EPS, LNX_EPS, 1e-24)):
        S.op("dve", lambda e, ci=ci, ev=ev: e.memset(epsc[:, ci:ci + 1], float(ev)), writes=[epsc_b])
    eps_col = {float(LN_EPS): 0, float(LNX_EPS): 1, 1e-24: 2}

    def rsqrt(out, in_, eps, reads, wbuf):
        ci = eps_col[float(eps)]
        S.op("act", lambda e: e.activation(out=out, in_=in_, func=AF.Sqrt, bias=epsc[0:out.shape[0], ci:ci + 1]),
             reads=list(reads) + [epsc_b], writes=[wbuf])
        S.op("dve", lambda e: e.reciprocal(out=out, in_=out), reads=[wbuf], writes=[wbuf])

    def mm(out, lhsT, rhs, start, stop, reads, writes, sig):
        S.op("pe", lambda e: e.matmul(out, lhsT=lhsT, rhs=rhs, start=start, stop=stop), reads=reads,
             writes=writes, sig=sig)

    def proj_fm(pi, slot, col0, rhs_t, rhs_b, ncol=T, kc_n=8, wcols=512, rhs_cols=slice(0, T)):
        Wv = W(slot, kc_n, wcols)
        for kc in range(kc_n):
            mm(ps[pi][:, 0:ncol], Wv[:, kc, col0:col0 + 128], rhs_t[:, kc, rhs_cols], kc == 0, kc == kc_n - 1,
               [ring_b[slot], rhs_b], [ps_b[pi]], kc == kc_n - 1)

    def ln_fm(l, src, src_b, nch, mean_mat, gname, bname, eps, outs, tmp, tmp_b, pA=6, pB=7, N=T):
        for c in range(nch):
            mm(ps[pA][:, 0:N], mean_mat, src[:, c, :], c == 0, c == nch - 1, [cf_b, src_b], [ps_b[pA]], c == nch - 1)
        mean_sb = tmp[:, 0, 0:N]
        S.op("act", lambda e: e.copy(out=mean_sb, in_=ps[pA][:, 0:N]), reads=[ps_b[pA]], writes=[tmp_b])
        sq = es_local["sq"]
        sq_b = es_local["sq_b"]
        for c in range(nch):
            S.op("dve", lambda e, c=c: e.tensor_sub(out=src[:, c, :], in0=src[:, c, :], in1=mean_sb),
                 reads=[src_b, tmp_b], writes=[src_b])
            S.op("act", lambda e, c=c: e.activation(out=sq[:, c % 2, 0:N], in_=src[:, c, :], func=AF.Square),
                 reads=[src_b], writes=[sq_b[c % 2]])
            mm(ps[pB][:, 0:N], mean_mat, sq[:, c % 2, 0:N], c == 0, c == nch - 1, [cf_b, sq_b[c % 2]], [ps_b[pB]], True)
        rsqrt(mean_sb, ps[pB][:, 0:N], float(eps), [ps_b[pB]], tmp_b)
        for c in range(nch):
            S.op("dve", lambda e, c=c: e.tensor_mul(out=src[:, c, :], in0=src[:, c, :], in1=mean_sb),
                 reads=[src_b, tmp_b], writes=[src_b])
            for (ot, ob) in outs:
                S.op("dve", lambda e, c=c, ot=ot: e.tensor_scalar(out=ot[:, c, :], in0=src[:, c, :],
                                                                   scalar1=pcol(l, gname, c), scalar2=pcol(l, bname, c),
                                                                   op0=ALU.mult, op1=ALU.add),
                     reads=[src_b, pp_b], writes=[ob])

    es_local = {}

    def merge_branch(l, bidx, slot_out, kc_n, actT, act_b, gslots, N=T, xb=None, xb_b=None, mg=None, mg_b=None):
        xb = xTb if xb is None else xb
        xb_b = xTb_b if xb_b is None else xb_b
        mg = merged if mg is None else mg
        mg_b = merged_b if mg_b is None else mg_b
        Wo = W(slot_out, kc_n, 1024)
        with scope_alloc([128, 2, N], F32) as gsb:
            gsb_b = [Buf("gsb0"), Buf("gsb1")]
            for oc in range(8):
                py, pg = oc % 2, 2 + oc % 2
                for kc in range(kc_n):
                    mm(ps[py][:, 0:N], Wo[:, kc, oc * 128:(oc + 1) * 128], actT[:, kc, :], kc == 0, kc == kc_n - 1,
                       [ring_b[slot_out], act_b], [ps_b[py]], kc == kc_n - 1)
                proj_fm(pg, gslots[oc // 4], (oc % 4) * 128, xb, xb_b, ncol=N, rhs_cols=slice(0, N))
                g_ = gsb[:, oc % 2, :]
                S.op("act", lambda e, g_=g_, pg=pg, oc=oc: e.activation(out=g_, in_=ps[pg][:, 0:N], func=AF.Sigmoid,
                                                                         bias=pcol(l, "gb", bidx * 8 + oc)),
                     reads=[ps_b[pg], pp_b], writes=[gsb_b[oc % 2]])
                if bidx == 0:
                    S.op("dve", lambda e, g_=g_, py=py, oc=oc: e.tensor_mul(out=mg[:, oc, :], in0=ps[py][:, 0:N], in1=g_),
                         reads=[ps_b[py], gsb_b[oc % 2]], writes=[mg_b])
                else:
                    S.op("dve", lambda e, g_=g_, py=py: e.tensor_mul(out=g_, in0=ps[py][:, 0:N], in1=g_),
                         reads=[ps_b[py], gsb_b[oc % 2]], writes=[gsb_b[oc % 2]])
                    S.op("dve", lambda e, g_=g_, oc=oc: e.tensor_add(out=mg[:, oc, :], in0=mg[:, oc, :], in1=g_),
                         reads=[mg_b, gsb_b[oc % 2]], writes=[mg_b])
            if DEBUG and N == T and l == 0 and DBGSTATE["tile"] == 0:
                dbg(10 + bidx, mg[:, 0, :], mg_b)
            S.barrier()

    def fm2tm_store(src_cols_fn, nchunk, nrows, dst_fn, key):
        with scope_alloc([128, 512], F32) as stg:
            stg_b = Buf("stg")
            for c in range(nchunk):
                ap, bf = src_cols_fn(c)
                S.op("pe", lambda e, ap=ap, c=c: e.matmul(ps[4][0:nrows, c * 128:(c + 1) * 128], lhsT=ap, rhs=ident,
                                                          start=True, stop=True),
                     reads=[bf, cf_b], writes=[ps_b[4]])
            S.op("act", lambda e: e.copy(out=stg[0:nrows, 0:nchunk * 128], in_=ps[4][0:nrows, 0:nchunk * 128]),
                 reads=[ps_b[4]], writes=[stg_b])
            S.dma("sp", dst_fn(stg), key, reads=[stg_b], out=True)
            S.barrier()

    def tile_prog(l, i):
        DBGSTATE["tile"] = i
        par = i % 2
        last = (i == NT - 1)
        src = x_fm if l == 0 else xs_d
        S.dma("sp", lambda e: e.dma_start(out=xT32[:], in_=src[:, :, i * T:(i + 1) * T]), "xload",
              writes=[xT32_b], local=False)
        S.op("act", lambda e: e.copy(out=xTb[:], in_=xT32[:]), reads=[xT32_b], writes=[xTb_b])

        if STOP == "load":
            return
        rwkv_phase(l, i, last)
        if STOP == "rwkv":
            return
        conv_phase(l, i, last)
        if STOP == "conv":
            return
        attn_phase(l, i, par, last)
        if STOP == "attn":
            return
        ffn_phase(l, i, last)

    def rwkv_phase(l, i, last):
        with scope():
            def lt(name, shape, dt=F32):
                return AR.alloc(list(shape), dt)
            lora = merged[:, 0:2, :]
            lora_b = Buf("lora")
            zraw = merged[:, 2:5, :].rearrange("p a t -> p (a t)")[:, 0:2 * (T + 1)].rearrange("p (a t) -> p a t", a=2)
            zraw_b = [Buf("zraw0"), Buf("zraw1")]
            ozT = lt("ozT", [128, 4, T], BF16)
            ozT_b = Buf("ozT")
            zctr = {"n": 0}

            def shift_chunk(pi, och, dst, dst_b):
                zi = zctr["n"] % 2
                zctr["n"] += 1
                zr = zraw[:, zi, :]
                S.op("act", lambda e: e.copy(out=zr[:, 1:T + 1], in_=ps[pi][:, :]), reads=[ps_b[pi]], writes=[zraw_b[zi]])
                S.op("dve", lambda e: e.tensor_copy(out=zr[:, 0:1], in_=zcar[:, och:och + 1]), reads=[zcar_b],
                     writes=[zraw_b[zi]])
                S.op("dve", lambda e: e.tensor_scalar(out=dst, in0=ps[pi][:, :], scalar1=pcol(l, "omm", och), scalar2=None,
                                                      op0=ALU.mult), reads=[ps_b[pi], pp_b], writes=[dst_b])
                S.op("dve", lambda e: e.scalar_tensor_tensor(out=dst, in0=zr[:, 0:T], scalar=pcol(l, "mu", och), in1=dst,
                                                             op0=ALU.mult, op1=ALU.add),
                     reads=[zraw_b[zi], pp_b, dst_b], writes=[dst_b])
                S.op("dve", lambda e: e.tensor_copy(out=zcar[:, och:och + 1], in_=zr[:, T:T + 1]), reads=[zraw_b[zi]],
                     writes=[zcar_b])

            slot = next_slab(SL_LORA)
            for c in range(2):
                proj_fm(c, slot, c * 128, xTb, xTb_b, wcols=256)
                shift_chunk(c, 12 + c, lora[:, c, :], lora_b)
            S.op("act", lambda e: e.activation(out=lora[0:64, 0, :], in_=lora[0:64, 0, :], func=AF.Tanh),
                 reads=[lora_b], writes=[lora_b])
            S.op("act", lambda e: e.activation(out=lora[:, 1, :], in_=lora[:, 1, :], func=AF.Sigmoid),
                 reads=[lora_b], writes=[lora_b])

            if STOP == "rw_lora":
                raise _Stop()
            for p in range(4):
                rwkv_pair(l, i, p, lora, lora_b, shift_chunk, ozT, ozT_b, None)

            s_out = next_slab(SL_ROUT)
            g0 = next_slab(SL_G0)
            g1 = next_slab(SL_G0 + 1)
            merge_branch(l, 0, s_out, 4, ozT, ozT_b, (g0, g1))
            if last:
                S.dma("sp", lambda e: e.dma_start(out=o_shift[l].rearrange("(c p) -> p c", p=128), in_=zcar[:, :],
                                                  allow_slow_non_contiguous=True), "oshift", reads=[zcar_b], out=True)
                with scope_alloc([64, 8, 64], F32) as stg:
                    stg_b = Buf("wkvst")
                    for h in range(8):
                        pr_, hl = h // 2, h % 2
                        S.op("pe", lambda e, pr_=pr_, hl=hl, h=h: e.matmul(
                            ps[4 + hl][0:64, pr_ * 64:(pr_ + 1) * 64], lhsT=Hst[hl * 64:(hl + 1) * 64, pr_, :],
                            rhs=cf[hl * 64:(hl + 1) * 64, 1344:1408], start=True, stop=True), reads=[Hst_b, cf_b],
                            writes=[ps_b[4 + hl]])
                    stg4 = stg[:].rearrange("p (pr hl) k -> p pr hl k", hl=2)
                    for hl in range(2):
                        S.op("act", lambda e, hl=hl: e.copy(out=stg4[:, :, hl, :],
                                                            in_=ps[4 + hl][0:64, 0:256].rearrange("p (pr k) -> p pr k", k=64)),
                             reads=[ps_b[4 + hl]], writes=[stg_b])
                    S.dma("sp", lambda e: e.dma_start(out=o_wkv[l].rearrange("h v k -> v h k"), in_=stg[:]), "owkv",
                          reads=[stg_b], out=True)
                    S.barrier()
            S.barrier()

    def rwkv_pair(l, i, p, lora, lora_b, shift_chunk, ozT, ozT_b, st_outer):
        with scope():
            cnt = {"n": 0}

            def lt(name, shape, dt=F32):
                cnt["n"] += 1
                return AR.alloc(list(shape), dt), Buf(name)
            rT, r_b = lt("rT", [128, T])
            kT, k_b = lt("kT", [128, T])
            vT, v_b = lt("vT", [128, T])
            lwT, lw_b2 = lt("lwT", [128, T])
            aT, a_b = lt("aT", [128, T])
            gT, g_b = lt("gT", [128, T])
            kkT, kk_b = lt("kkT", [128, T])
            bT, b_b = lt("bT", [128, T])
            csT, cs_b = lt("csT", [128, T])
            eT, e_b = lt("eT", [128, T])
            e2T, e2_b = lt("e2T", [128, T])
            KR, KR_b = lt("KR", [128, NCH, 2, C])
            bon, bon_b = lt("bon", [128, T])
            gam, gam_b = lt("gam", [128, NCH])
            Kt, Kt_b = kT, k_b
            Bt, Bt_b = bT, b_b
            Kh, Kh_b = rT, r_b
            Bh, Bh_b = kkT, kk_b
            OT, OT_b = aT, a_b
            slot = next_slab(SL_RP + p)
            for c, (dst, db, och) in enumerate(((rT, r_b, p), (kT, k_b, 4 + p), (vT, v_b, 8 + p))):
                proj_fm(c % 2, slot, c * 128, xTb, xTb_b, wcols=384)
                shift_chunk(c % 2, och, dst[:, :], db)
            cs128 = slice(p * 128, (p + 1) * 128)
            mm(ps[2][:, :], lw[0:64, l, cs128], lora[0:64, 0, :], True, True, [lw_b, lora_b], [ps_b[2]], True)
            mm(ps[3][:, :], lw[64:128, l, cs128], lora[64:128, 0, :], True, True, [lw_b, lora_b], [ps_b[3]], True)
            mm(ps[4][:, :], lw[:, l, 512 + p * 128:512 + (p + 1) * 128], lora[:, 1, :], True, True, [lw_b, lora_b],
               [ps_b[4]], True)
            S.op("act", lambda e: e.activation(out=eT[:, :], in_=ps[2][:, :], func=AF.Exp, scale=-1.0,
                                               bias=pcol(l, "nw0", p)), reads=[ps_b[2], pp_b], writes=[e_b])
            S.op("act", lambda e: e.activation(out=eT[:, :], in_=eT[:, :], func=AF.Ln, bias=1.0), reads=[e_b], writes=[e_b])
            S.op("act", lambda e: e.activation(out=eT[:, :], in_=eT[:, :], func=AF.Exp, scale=-1.0, bias=-0.5),
                 reads=[e_b], writes=[e_b])
            S.op("dve", lambda e: e.tensor_scalar(out=lwT[:, :], in0=eT[:, :], scalar1=-1.0, scalar2=None, op0=ALU.mult),
                 reads=[e_b], writes=[lw_b2])
            S.op("act", lambda e: e.activation(out=aT[:, :], in_=ps[3][:, :], func=AF.Sigmoid, bias=pcol(l, "a0", p)),
                 reads=[ps_b[3], pp_b], writes=[a_b])
            S.op("act", lambda e: e.copy(out=gT[:, :], in_=ps[4][:, :]), reads=[ps_b[4]], writes=[g_b])
            S.op("dve", lambda e: e.tensor_scalar(out=kkT[:, :], in0=kT[:, :], scalar1=pcol(l, "kk", p), scalar2=None,
                                                  op0=ALU.mult), reads=[k_b, pp_b], writes=[kk_b])
            S.op("dve", lambda e: e.tensor_mul(out=e2T[:, :], in0=kkT[:, :], in1=kkT[:, :]), reads=[kk_b], writes=[e2_b])
            mm(ps[2][:, :], ones_blk, e2T[:, :], True, True, [cf_b, e2_b], [ps_b[2]], True)
            rsqrt(e2T[:, :], ps[2][:, :], 1e-24, [ps_b[2]], e2_b)
            S.op("dve", lambda e: e.tensor_mul(out=kkT[:, :], in0=kkT[:, :], in1=e2T[:, :]), reads=[kk_b, e2_b],
                 writes=[kk_b])
            S.op("dve", lambda e: e.tensor_mul(out=bT[:, :], in0=kkT[:, :], in1=aT[:, :]), reads=[kk_b, a_b], writes=[b_b])
            S.op("dve", lambda e: e.tensor_scalar(out=aT[:, :], in0=aT[:, :], scalar1=-1.0, scalar2=pcol(l, "ka", p),
                                                  op0=ALU.add, op1=ALU.mult), reads=[a_b, pp_b], writes=[a_b])
            S.op("dve", lambda e: e.scalar_tensor_tensor(out=kT[:, :], in0=aT[:, :], scalar=1.0, in1=kT[:, :], op0=ALU.add,
                                                         op1=ALU.mult), reads=[a_b, k_b], writes=[k_b])
            S.op("dve", lambda e: e.scalar_tensor_tensor(out=e2T[:, :], in0=rT[:, :], scalar=pcol(l, "rk", p), in1=kT[:, :],
                                                         op0=ALU.mult, op1=ALU.mult), reads=[r_b, k_b, pp_b], writes=[e2_b])
            mm(ps[3][:, :], ones_blk, e2T[:, :], True, True, [cf_b, e2_b], [ps_b[3]], True)
            S.op("dve", lambda e: e.tensor_mul(out=bon[:, :], in0=ps[3][:, :], in1=vT[:, :]), reads=[ps_b[3], v_b],
                 writes=[bon_b])
            if (l, i, p) == (0, 0, 0):
                for di, (ap_, bf_) in enumerate(((rT, r_b), (kT, k_b), (vT, v_b), (lwT, lw_b2), (kkT, kk_b), (bT, b_b),
                                                 (gT, g_b), (bon, bon_b))):
                    dbg(di, ap_[:, :], bf_)
            S.op("dve", lambda e: e.tensor_tensor_scan(out=csT[:, :], data0=rmask, data1=lwT[:, :], initial=0.0,
                                                       op0=ALU.mult, op1=ALU.add), reads=[cf_b, lw_b2], writes=[cs_b])
            cs3 = csT[:, :].rearrange("p (j t) -> p j t", t=C)
            S.op("act", lambda e: e.activation(out=eT[:, :], in_=csT[:, :], func=AF.Exp), reads=[cs_b], writes=[e_b])
            S.op("dve", lambda e: e.tensor_mul(out=KR[:, :, 1, :], in0=rT[:, :].rearrange("p (j t) -> p j t", t=C),
                                               in1=eT[:, :].rearrange("p (j t) -> p j t", t=C)),
                 reads=[r_b, e_b], writes=[KR_b])
            S.op("dve", lambda e: e.tensor_copy(out=gam[:, :], in_=eT[:, :].rearrange("p (j t) -> p j t", t=C)[:, :, C - 1]),
                 reads=[e_b], writes=[gam_b])
            S.op("dve", lambda e: e.tensor_sub(out=OT[:, :], in0=csT[:, :], in1=lwT[:, :]), reads=[cs_b, lw_b2], writes=[OT_b])
            S.op("act", lambda e: e.activation(out=OT[:, :], in_=OT[:, :], func=AF.Exp), reads=[OT_b], writes=[OT_b])
            S.op("dve", lambda e: e.tensor_mul(out=KR[:, :, 0, :], in0=kkT[:, :].rearrange("p (j t) -> p j t", t=C),
                                               in1=OT[:, :].rearrange("p (j t) -> p j t", t=C)),
                 reads=[kk_b, OT_b], writes=[KR_b])
            S.op("dve", lambda e: e.tensor_sub(out=OT[:, :].rearrange("p (j t) -> p j t", t=C),
                                               in0=cs3[:, :, C - 1:C].broadcast_to([128, NCH, C]), in1=cs3),
                 reads=[cs_b], writes=[OT_b])
            S.op("act", lambda e: e.activation(out=OT[:, :], in_=OT[:, :], func=AF.Exp), reads=[OT_b], writes=[OT_b])
            S.op("dve", lambda e: e.tensor_mul(out=Kh[:, :], in0=kT[:, :], in1=OT[:, :]), reads=[k_b, OT_b], writes=[Kh_b])
            S.op("dve", lambda e: e.tensor_mul(out=Bh[:, :], in0=bT[:, :], in1=OT[:, :]), reads=[b_b, OT_b], writes=[Bh_b])

            S.op("act", lambda e: e.activation(out=eT[:, :], in_=csT[:, :], func=AF.Exp, scale=-1.0), reads=[cs_b],
                 writes=[e_b])
            S.op("dve", lambda e: e.tensor_mul(out=Kt[:, :], in0=kT[:, :], in1=eT[:, :]), reads=[k_b, e_b], writes=[Kt_b])
            S.op("dve", lambda e: e.tensor_mul(out=Bt[:, :], in0=bT[:, :], in1=eT[:, :]), reads=[b_b, e_b], writes=[Bt_b])
            if STOP == "rw_prep":
                raise _Stop()
            GQ = 4
            A1, A1_b = lt("A1", [128, GQ, 2, 128])
            Ab, Ab_b = lt("Ab", [128, GQ, 64])
            Y, Y_b = lt("Y", [128, GQ, 2, 64])
            Pm, Pm_b = lt("Pm", [128, GQ, 2, 64])
            TMx, _ = lt("TM", [128, 2, 3, 128])
            TMx_b = [Buf("TM0"), Buf("TM1")]
            W1, W1_b = lt("W1", [128, 64])
            U, U_b = lt("U", [128, 64])
            dup, dup_b = lt("dup", [128, 3, 128])
            mask1 = cf[:, 1152:1280]
            mask2 = cf[:, 1280:1344]
            id64 = cf[:, 1344:1408]
            HP = [slice(0, 64), slice(64, 128)]
            BK_W, BK_O, BK_H = (1, 3), (4, 5), (6, 7)

            def group_stage(jg):
                for jj in range(GQ):
                    j_ = jg * GQ + jj
                    tcs_ = slice(j_ * C, (j_ + 1) * C)
                    for hl in range(2):
                        hp = HP[hl]
                        krj = KR[hp, j_, :, :].rearrange("p a t -> p (a t)")
                        bA = 2 * hl + jj // 2
                        c0 = (jj % 2) * 256
                        bB = 4 + hl
                        mm(ps[bA][hp, c0:c0 + 128], Kt[hp, tcs_], krj, True, True, [Kt_b, KR_b], [ps_b[bA]], False)
                        mm(ps[bA][hp, c0 + 128:c0 + 256], Bt[hp, tcs_], krj, True, True, [Bt_b, KR_b], [ps_b[bA]], False)
                        mm(ps[bB][hp, jj * 64:(jj + 1) * 64], KR[hp, j_, 0, :], Bt[hp, tcs_], True, True, [KR_b, Bt_b],
                           [ps_b[bB]], True)
                for hl in range(2):
                    hp = HP[hl]
                    for hb_ in range(2):
                        bA = 2 * hl + hb_
                        S.op("dve", lambda e, hp=hp, bA=bA, hb_=hb_: e.tensor_mul(
                            out=A1[hp, 2 * hb_:2 * hb_ + 2, :, :].rearrange("p c x t -> p (c x) t"),
                            in0=ps[bA][hp, 0:512].rearrange("p (x t) -> p x t", t=128),
                            in1=mask1[hp, :].unsqueeze(1).broadcast_to([64, 4, 128])), reads=[ps_b[bA], cf_b],
                            writes=[A1_b])
                    S.op("dve", lambda e, hp=hp, hl=hl: e.tensor_mul(
                        out=Ab[hp, :, :], in0=ps[4 + hl][hp, 0:GQ * 64].rearrange("p (c t) -> p c t", t=64),
                        in1=mask2[hp, :].unsqueeze(1).broadcast_to([64, GQ, 64])), reads=[ps_b[4 + hl], cf_b],
                        writes=[Ab_b])
                S.op("dve", lambda e: e.tensor_scalar(out=Y[:, :, 0, :], in0=A1[:, :, 1, 0:64], scalar1=-1.0, scalar2=None,
                                                      op0=ALU.mult), reads=[A1_b], writes=[Y_b])
                S.op("dve", lambda e: e.tensor_scalar(out=Y[:, :, 1, :], in0=Ab[:, :, :], scalar1=-1.0, scalar2=None,
                                                      op0=ALU.mult), reads=[Ab_b], writes=[Y_b])
                S.op("dve", lambda e: e.tensor_add(out=Pm[:].rearrange("p c a t -> p (c a) t"),
                                                   in0=Y[:].rearrange("p c a t -> p (c a) t"),
                                                   in1=id64.unsqueeze(1).broadcast_to([128, 2 * GQ, 64])),
                     reads=[Y_b, cf_b], writes=[Pm_b])
                for lev in range(5):
                    for jj in range(GQ):
                        for hl in range(2):
                            hp = HP[hl]
                            b_ = 6 + hl
                            mm(ps[b_][hp, jj * 128:jj * 128 + 64], Y[hp, jj, 1, :], Y[hp, jj, 0, :], True, True, [Y_b],
                               [ps_b[b_]], False)
                            mm(ps[b_][hp, jj * 128 + 64:jj * 128 + 128], Y[hp, jj, 0, :], Y[hp, jj, 1, :], True, True,
                               [Y_b], [ps_b[b_]], True)
                    S.op("act", lambda e: e.copy(out=Y[HP[0], :, :, :].rearrange("p c a t -> p (c a t)"),
                                                 in_=ps[6][HP[0], 0:GQ * 128]), reads=[ps_b[6]], writes=[Y_b])
                    S.op("act", lambda e: e.copy(out=Y[HP[1], :, :, :].rearrange("p c a t -> p (c a t)"),
                                                 in_=ps[7][HP[1], 0:GQ * 128]), reads=[ps_b[7]], writes=[Y_b])
                    for jj in range(GQ):
                        for hl in range(2):
                            hp = HP[hl]
                            b_ = 4 + hl
                            mm(ps[b_][hp, jj * 128:jj * 128 + 64], Pm[hp, jj, 1, :], Y[hp, jj, 0, :], True, True,
                               [Pm_b, Y_b], [ps_b[b_]], False)
                            mm(ps[b_][hp, jj * 128 + 64:jj * 128 + 128], Y[hp, jj, 0, :], Pm[hp, jj, 1, :], True, True,
                               [Pm_b, Y_b], [ps_b[b_]], True)
                    for hl in range(2):
                        hp = HP[hl]
                        S.op("dve", lambda e, hp=hp, hl=hl: e.tensor_add(
                            out=Pm[hp, :, :, :].rearrange("p c a t -> p (c a t)"),
                            in0=Pm[hp, :, :, :].rearrange("p c a t -> p (c a t)"), in1=ps[4 + hl][hp, 0:GQ * 128]),
                            reads=[ps_b[4 + hl], Pm_b], writes=[Pm_b])

            def emit_TM(j_):
                tcs_ = slice(j_ * C, (j_ + 1) * C)
                for ti, (srcT, sbf) in enumerate(((vT, v_b), (Kh, Kh_b), (Bh, Bh_b))):
                    for a_ in range(2):
                        S.op("pe", lambda e, ti=ti, srcT=srcT, a_=a_: e.matmul(
                            ps[0][a_ * 64:(a_ + 1) * 64, ti * 128:(ti + 1) * 128], lhsT=srcT[:, tcs_], rhs=ident,
                            start=True, stop=True), reads=[sbf, cf_b], writes=[ps_b[0]], sig=(ti == 2 and a_ == 1))
                S.op("act", lambda e: e.copy(out=TMx[:, j_ % 2, :, :].rearrange("p a f -> p (a f)"), in_=ps[0][:, 0:384]),
                     reads=[ps_b[0]], writes=[TMx_b[j_ % 2]])

            for j in range(NCH):
                tc0 = j * C
                tcs = slice(tc0, tc0 + C)
                jj = j % GQ
                if jj == 0:
                    group_stage(j // GQ)
                if STOP == "rw_inv":
                    raise _Stop()
                if jj == 0:
                    emit_TM(j)
                TM = TMx[:, j % 2, :, :]
                TM_b = TMx_b[j % 2]
                if STOP == "rw_tm":
                    raise _Stop()
                for hl in range(2):
                    hp = HP[hl]
                    b_ = BK_W[hl]
                    o_ = ps[b_][hp, 0:64]
                    mm(o_, KR[hp, j, 0, :], Hst[hp, p, :], True, False, [KR_b, Hst_b], [ps_b[b_]], False)
                    mm(o_, A1[hp, jj, 0, 0:64], TM[hp, 0, hp], False, True, [A1_b, TM_b], [ps_b[b_]], True)
                for hl in range(2):
                    hp = HP[hl]
                    b_ = BK_W[hl]
                    S.op("act", lambda e, hp=hp, b_=b_: e.copy(out=W1[hp, :], in_=ps[b_][hp, 0:64]), reads=[ps_b[b_]],
                         writes=[W1_b])
                if j + 1 < NCH and (j + 1) % GQ != 0:
                    emit_TM(j + 1)
                if STOP == "rw_w1":
                    raise _Stop()
                for hl in range(2):
                    hp = HP[hl]
                    b_ = BK_W[hl]
                    mm(ps[b_][hp, 128:192], Pm[hp, jj, 0, :], W1[hp, :], True, True, [Pm_b, W1_b], [ps_b[b_]], True)
                for hl in range(2):
                    hp = HP[hl]
                    b_ = BK_W[hl]
                    S.op("act", lambda e, hp=hp, b_=b_: e.mul(out=U[hp, :], in_=ps[b_][hp, 128:192], mul=-1.0),
                         reads=[ps_b[b_]], writes=[U_b])
                if STOP == "rw_u":
                    raise _Stop()
                for hl in range(2):
                    hp = HP[hl]
                    b_ = BK_H[hl]
                    o_ = ps[b_][hp, 0:64]
                    mm(o_, TM[hp, 1, hp], TM[hp, 0, hp], True, False, [TM_b], [ps_b[b_]], False)
                    mm(o_, TM[hp, 2, hp], U[hp, :], False, True, [TM_b, U_b], [ps_b[b_]], True)
                for hl in range(2):
                    hp = HP[hl]
                    b_ = BK_O[hl]
                    o_ = ps[b_][hp, 0:64]
                    mm(o_, Hst[hp, p, :], KR[hp, j, 1, :], True, False, [Hst_b, KR_b], [ps_b[b_]], False)
                    mm(o_, TM[hp, 0, hp], A1[hp, jj, 0, 64:128], False, False, [TM_b, A1_b], [ps_b[b_]], False)
                    mm(o_, U[hp, :], A1[hp, jj, 1, 64:128], False, True, [U_b, A1_b], [ps_b[b_]], True)
                for hl in range(2):
                    hp = HP[hl]
                    b_ = BK_O[hl]
                    S.op("act", lambda e, hp=hp, b_=b_: e.copy(out=OT[hp, tcs], in_=ps[b_][hp, 0:64]), reads=[ps_b[b_]],
                         writes=[OT_b])
                if STOP == "rw_o":
                    raise _Stop()
                for hl in range(2):
                    hp = HP[hl]
                    b_ = BK_H[hl]
                    S.op("dve", lambda e, hp=hp, b_=b_: e.scalar_tensor_tensor(
                        out=Hst[hp, p, :], in0=Hst[hp, p, :], scalar=gam[hp, j:j + 1], in1=ps[b_][hp, 0:64],
                        op0=ALU.mult, op1=ALU.add), reads=[Hst_b, gam_b, ps_b[b_]], writes=[Hst_b])
            if STOP == "rw_chunk":
                raise _Stop()
            if (l, i, p) == (0, 0, 0):
                dbg(8, OT[:, :], OT_b)
            mm(ps[0][:, :], mean_blk, OT[:, :], True, True, [cf_b, OT_b], [ps_b[0]], True)
            S.op("dve", lambda e: e.tensor_sub(out=OT[:, :], in0=OT[:, :], in1=ps[0][:, :]), reads=[OT_b, ps_b[0]],
                 writes=[OT_b])
            S.op("act", lambda e: e.activation(out=eT[:, :], in_=OT[:, :], func=AF.Square), reads=[OT_b], writes=[e_b])
            mm(ps[1][:, :], mean_blk, eT[:, :], True, True, [cf_b, e_b], [ps_b[1]], True)
            rsqrt(eT[:, :], ps[1][:, :], float(LNX_EPS), [ps_b[1]], e_b)
            S.op("dve", lambda e: e.tensor_mul(out=OT[:, :], in0=OT[:, :], in1=eT[:, :]), reads=[OT_b, e_b], writes=[OT_b])
            S.op("dve", lambda e: e.tensor_scalar(out=OT[:, :], in0=OT[:, :], scalar1=pcol(l, "lng", p),
                                                  scalar2=pcol(l, "lnb", p), op0=ALU.mult, op1=ALU.add),
                 reads=[OT_b, pp_b], writes=[OT_b])
            S.op("dve", lambda e: e.tensor_add(out=OT[:, :], in0=OT[:, :], in1=bon[:, :]), reads=[OT_b, bon_b], writes=[OT_b])
            if (l, i, p) == (0, 0, 0):
                dbg(9, OT[:, :], OT_b)
            S.op("dve", lambda e: e.tensor_mul(out=ozT[:, p, :], in0=OT[:, :], in1=gT[:, :]), reads=[OT_b, g_b],
                 writes=[ozT_b])
            S.barrier()

    def conv_phase(l, i, last):
        with scope():
            def lt(name, shape, dt=F32):
                return AR.alloc(list(shape), dt), Buf(name)
            uext, ue_b = lt("uext", [128, 4, T + 30])
            acc, acc_b = lt("cacc", [128, 4, T])
            acc2, _ = lt("cacc2", [128, 2, T])
            acc2_b = [Buf("cacc20"), Buf("cacc21")]
            sq, _ = lt("csq", [128, 2, T])
            sq_b = [Buf("csq0"), Buf("csq1")]
            sg, sg_b = lt("csg", [128, 2, T])
            cTb, cT_b = lt("cTb", [128, 4, T], BF16)
            tmp, tmp_b = lt("ctmp", [128, 1, T])
            es_local["sq"], es_local["sq_b"] = sq, sq_b
            su = next_slab(SL_CU)
            sgt = next_slab(SL_CG)
            for c in range(4):
                proj_fm(0, su, c * 128, xTb, xTb_b)
                proj_fm(1, sgt, c * 128, xTb, xTb_b)
                S.op("act", lambda e, c=c: e.activation(out=sg[:, c % 2, :], in_=ps[1][:, :], func=AF.Sigmoid),
                     reads=[ps_b[1]], writes=[sg_b])
                S.op("dve", lambda e, c=c: e.tensor_copy(out=uext[:, c, 0:30], in_=uhist[:, c, :]), reads=[uhist_b],
                     writes=[ue_b])
                S.op("dve", lambda e, c=c: e.tensor_mul(out=uext[:, c, 30:30 + T], in0=ps[0][:, :], in1=sg[:, c % 2, :]),
                     reads=[ps_b[0], sg_b], writes=[ue_b])
                S.op("dve", lambda e, c=c: e.tensor_copy(out=uhist[:, c, :], in_=uext[:, c, T:T + 30]), reads=[ue_b],
                     writes=[uhist_b])
                S.op("dve", lambda e, c=c: e.tensor_scalar(out=acc[:, c, :], in0=uext[:, c, 0:T],
                                                           scalar1=pcol(l, "cdw", c * 31), scalar2=pcol(l, "cdb", c),
                                                           op0=ALU.mult, op1=ALU.add), reads=[ue_b, pp_b], writes=[acc_b])
                a2 = acc2[:, c % 2, :]
                a2_b = acc2_b[c % 2]
                S.op("dve", lambda e, c=c, a2=a2: e.tensor_scalar(out=a2, in0=uext[:, c, 1:1 + T],
                                                                  scalar1=pcol(l, "cdw", c * 31 + 1), scalar2=None,
                                                                  op0=ALU.mult), reads=[ue_b, pp_b], writes=[a2_b])
                for jj in range(2, 31):
                    if jj % 2 == 0:
                        S.op("dve", lambda e, c=c, jj=jj: e.scalar_tensor_tensor(out=acc[:, c, :], in0=uext[:, c, jj:jj + T],
                                                                                 scalar=pcol(l, "cdw", c * 31 + jj),
                                                                                 in1=acc[:, c, :], op0=ALU.mult, op1=ALU.add),
                             reads=[ue_b, pp_b, acc_b], writes=[acc_b])
                    else:
                        S.op("dve", lambda e, c=c, jj=jj, a2=a2: e.scalar_tensor_tensor(out=a2, in0=uext[:, c, jj:jj + T],
                                                                                         scalar=pcol(l, "cdw", c * 31 + jj),
                                                                                         in1=a2, op0=ALU.mult, op1=ALU.add),
                             reads=[ue_b, pp_b, a2_b], writes=[a2_b])
                S.op("dve", lambda e, c=c, a2=a2: e.tensor_add(out=acc[:, c, :], in0=acc[:, c, :], in1=a2),
                     reads=[acc_b, a2_b], writes=[acc_b])
            ln_fm(l, acc, acc_b, 4, mean_c, "clg", "clb", LN_EPS, [(acc, acc_b)], tmp, tmp_b)
            for c in range(4):
                S.op("act", lambda e, c=c: e.activation(out=cTb[:, c, :], in_=acc[:, c, :], func=AF.Silu), reads=[acc_b],
                     writes=[cT_b])
            s_out = next_slab(SL_COUT)
            g0 = next_slab(SL_G1)
            g1 = next_slab(SL_G1 + 1)
            merge_branch(l, 1, s_out, 4, cTb, cT_b, (g0, g1))
            if last:
                n30 = CONV_K - 1
                fm2tm_store(lambda c: (uhist[:, c, :], uhist_b), 4, n30,
                            lambda stg: (lambda e: e.dma_start(out=o_conv[l, :, :], in_=stg[0:n30, 0:512])), "oconv")
            S.barrier()

    def attn_phase(l, i, par, last):
        with scope():
            def lt(name, shape, dt=F32):
                return AR.alloc(list(shape), dt), Buf(name)
            qT, q_b = lt("qT", [128, 3, 2, T], BF16)
            pt, _ = lt("pt", [128, 3, 256], BF16)
            pt_b = [Buf("pt0"), Buf("pt1"), Buf("pt2")]
            vstg, vstg_b = lt("vstg", [32, 16, 256], BF16)
            kvo, _ = lt("kvo", [128, 2, 512])
            kvo_b = [Buf("kvo0"), Buf("kvo1")]
            oTb, oT_b = lt("oTb", [128, 2, T], BF16)
            rec, rec_b = lt("rec", [128, T])
            B_c = i // 4
            o_c = (i % 4) * 32
            for g in range(3):
                win, dil = SWA_GROUPS[g]
                sA = next_slab(SL_ATT + 2 * g)
                sB = next_slab(SL_ATT + 2 * g + 1)
                kcol0 = (par * T) if g < 2 else i * T
                for c in range(4):
                    proj_fm(c % 2, sA, c * 128, xTb, xTb_b)
                    if c < 2:
                        S.op("act", lambda e, c=c, g=g: e.copy(out=qT[:, g, c, :], in_=ps[c % 2][:, :]),
                             reads=[ps_b[c % 2]], writes=[q_b])
                    else:
                        S.op("act", lambda e, c=c, g=g: e.copy(out=kT_h[g][:, c - 2, kcol0:kcol0 + T], in_=ps[c % 2][:, :]),
                             reads=[ps_b[c % 2]], writes=[kT_hb[g]])
                WB = W(sB, 8, 512)
                if g < 2:
                    for blk in range(4):
                        cols = slice(blk * 128, (blk + 1) * 128) if g == 0 else slice(blk, T, 4)
                        vb = (par * 4 + blk) if g == 0 else (blk * 2 + par)
                        pi = 2 + blk % 2
                        for kc in range(8):
                            mm(ps[pi][:, 0:256], xTb[:, kc, cols], WB[:, kc, 256:512], kc == 0, kc == 7,
                               [xTb_b, ring_b[sB]], [ps_b[pi]], kc == 7)
                        S.op("dve", lambda e, pi=pi, vb=vb, g=g: e.tensor_copy(out=Vtm[g][:, vb, :], in_=ps[pi][:, 0:256]),
                             reads=[ps_b[pi]], writes=[Vtm_b[g]])
                else:
                    for rho in range(16):
                        pi = 2 + rho % 2
                        for kc in range(8):
                            mm(ps[pi][0:32, 0:256], xTb[:, kc, slice(rho, T, 16)], WB[:, kc, 256:512], kc == 0, kc == 7,
                               [xTb_b, ring_b[sB]], [ps_b[pi]], kc == 7)
                        S.op("dve", lambda e, pi=pi, rho=rho: e.tensor_copy(out=vstg[:, rho, :], in_=ps[pi][0:32, 0:256]),
                             reads=[ps_b[pi]], writes=[vstg_b])
                    vdst = Vtm[2][o_c:o_c + 32, :, :].rearrange("p (r b) f -> p r b f", b=2)[:, :, B_c, :]
                    S.dma("sp", lambda e: e.dma_start(out=vdst, in_=vstg[:, :, :]), "vcdma", reads=[vstg_b],
                          writes=[Vtm_b[2]])
                kp = keep[g]
                for blk in range(4):
                    t0 = i * T + blk * 128
                    if t0 + 128 <= SEQ - kp:
                        continue
                    r0 = t0 - (SEQ - kp)
                    pi = 2 + blk % 2
                    for kc in range(8):
                        mm(ps[pi][:, :], xTb[:, kc, blk * 128:(blk + 1) * 128], WB[:, kc, :], kc == 0, kc == 7,
                           [xTb_b, ring_b[sB]], [ps_b[pi]], kc == 7)
                    S.op("act", lambda e, pi=pi, blk=blk: e.copy(out=kvo[:, blk % 2, :], in_=ps[pi][:, :]),
                         reads=[ps_b[pi]], writes=[kvo_b[blk % 2]])
                    S.dma("sp", lambda e, g=g, r0=r0, blk=blk: e.dma_start(out=o_swa[g][l, r0:r0 + 128, :],
                                                                           in_=kvo[:, blk % 2, :]), f"okv{blk % 2}",
                          reads=[kvo_b[blk % 2]], out=True)
            started = set()
            ptc = {"n": 0}

            def pv(pair, hl, vblk_ap, pt_ap, ptb, out_cols, vbuf):
                hp = slice(hl * 64, (hl + 1) * 64)
                first = (pair, hl) not in started
                started.add((pair, hl))
                mm(ps[4 + pair][hp, out_cols], vblk_ap, pt_ap, first, False, [vbuf, ptb], [ps_b[4 + pair]], False)
                mm(ps[6 + pair][hp, out_cols], ones_b64, pt_ap, first, False, [cb_b, ptb], [ps_b[6 + pair]], True)

            NPB = 3

            def mk_ab(h, g, qb):
                pair, hl = h // 2, h % 2
                hp = slice(hl * 64, (hl + 1) * 64)
                if g == 0:
                    qcols = slice(qb * 128, (qb + 1) * 128)
                    kbs = [(par * T + qb * 128, 1, par * 4 + qb, mb_same)]
                    if qb > 0:
                        kbs.append((par * T + (qb - 1) * 128, 1, par * 4 + qb - 1, mb_prev))
                    elif i > 0:
                        kbs.append(((1 - par) * T + 384, 1, (1 - par) * 4 + 3, mb_prev))
                else:
                    qcols = slice(qb, T, 4)
                    kbs = [(par * T + qb, 4, qb * 2 + par, mb_same)]
                    if i > 0:
                        kbs.append(((1 - par) * T + qb, 4, qb * 2 + 1 - par, mb_prev))
                nb = len(kbs)

                def scores(pi):
                    for bi, (kc0, kst, vb, mbias) in enumerate(kbs):
                        kcols = slice(kc0, kc0 + 127 * kst + 1, kst)
                        mm(ps[pi][:, bi * 128:(bi + 1) * 128], kT_h[g][hp, pair, kcols], qT[hp, g, pair, qcols], True,
                           False, [kT_hb[g], q_b], [ps_b[pi]], False)
                        mm(ps[pi][:, bi * 128:(bi + 1) * 128], identb, mbias, False, True, [cb_b], [ps_b[pi]], bi == nb - 1)

                def pvs(k_):
                    for bi, (kc0, kst, vb, mbias) in enumerate(kbs):
                        pv(pair, hl, Vtm[g][:, vb, h * 64:(h + 1) * 64], pt[:, k_, bi * 128:(bi + 1) * 128], pt_b[k_],
                           qcols, Vtm_b[g])
                return scores, nb * 128, pvs

            def mk_c(h, rg):
                pair, hl = h // 2, h % 2
                hp = slice(hl * 64, (hl + 1) * 64)
                nkb = 2 if B_c >= 1 else 1

                def scores(pi):
                    for kb in range(nkb):
                        Bk = B_c - kb
                        moff = 448 + (o_c // 32) * 256 + kb * 128
                        mm(ps[pi][:, kb * 128:(kb + 1) * 128], identb, cb[:, moff:moff + 128], True, False, [cb_b],
                           [ps_b[pi]], False)
                        for r4 in range(4):
                            rho = rg * 4 + r4
                            kc0 = 2048 * Bk + rho
                            kcols = slice(kc0, kc0 + 127 * 16 + 1, 16)
                            cc = kb * 128 + r4 * 32
                            mm(ps[pi][:, cc:cc + 32], kT_h[2][hp, pair, kcols], qT[hp, 2, pair, slice(rho, T, 16)], False,
                               r4 == 3, [kT_hb[2], q_b], [ps_b[pi]], r4 == 3 and kb == nkb - 1)

                def pvs(k_):
                    for kb in range(nkb):
                        Bk = B_c - kb
                        for r4 in range(4):
                            rho = rg * 4 + r4
                            cc = kb * 128 + r4 * 32
                            pv(pair, hl, Vtm[2][:, rho * 2 + Bk, h * 64:(h + 1) * 64], pt[:, k_, cc:cc + 32], pt_b[k_],
                               slice(rho, T, 16), Vtm_b[2])
                return scores, nkb * 128, pvs

            blocks = []
            for h in range(4):
                for g in range(2):
                    for qb in range(4):
                        blocks.append(mk_ab(h, g, qb))
                for rg in range(4):
                    blocks.append(mk_c(h, rg))
            blocks[0][0](0)
            for n, (sc_fn, ncols, pv_fn) in enumerate(blocks):
                k_ = n % NPB
                if n + 1 < len(blocks):
                    blocks[n + 1][0]((n + 1) % NPB)
                S.op("act", lambda e: e.activation(out=pt[:, k_, 0:ncols], in_=ps[k_][:, 0:ncols], func=AF.Exp, scale=0.125),
                     reads=[ps_b[k_]], writes=[pt_b[k_]])
                pv_fn(k_)
            for pair in range(2):
                S.op("dve", lambda e, pair=pair: e.reciprocal(out=rec[:, :], in_=ps[6 + pair][:, :]), reads=[ps_b[6 + pair]],
                     writes=[rec_b])
                S.op("dve", lambda e, pair=pair: e.tensor_mul(out=oTb[:, pair, :], in0=ps[4 + pair][:, :], in1=rec[:, :]),
                     reads=[ps_b[4 + pair], rec_b], writes=[oT_b])
            s_out = next_slab(SL_AOUT)
            g0 = next_slab(SL_G2)
            g1 = next_slab(SL_G2 + 1)
            merge_branch(l, 2, s_out, 2, oTb, oT_b, (g0, g1))
            S.barrier()

    def ffn_phase(l, i, last):
        with scope():
            def lt(name, shape, dt=F32):
                return AR.alloc(list(shape), dt), Buf(name)
            hb, hb_b = lt("hb", [128, 8, T], BF16)
            pre, pre_b = merged, merged_b
            with scope():
                mbf, mbf_b = lt("mbf", [128, 8, T], BF16)
                sq, _ = lt("fsq", [128, 2, T])
                sq_b = [Buf("fsq0"), Buf("fsq1")]
                tmp, tmp_b = lt("ftmp", [128, 1, T])
                es_local["sq"], es_local["sq_b"] = sq, sq_b
                S.op("act", lambda e: e.copy(out=mbf[:], in_=merged[:]), reads=[merged_b], writes=[mbf_b])
                wo = [next_slab(SL_WO), next_slab(SL_WO + 1)]
                for oc in range(8):
                    proj_fm(oc % 2, wo[oc // 4], (oc % 4) * 128, mbf, mbf_b)
                    S.op("dve", lambda e, oc=oc: e.scalar_tensor_tensor(out=pre[:, oc, :], in0=xT32[:, oc, :],
                                                                        scalar=float(ALPHA), in1=ps[oc % 2][:, :],
                                                                        op0=ALU.mult, op1=ALU.add),
                         reads=[xT32_b, ps_b[oc % 2]], writes=[pre_b])
                if STOP == "ffn_wo":
                    raise _Stop()
                ln_fm(l, pre, pre_b, 8, mean_d, "l1g", "l1b", LN_EPS, [(xT32, xT32_b), (hb, hb_b)], tmp, tmp_b)
                S.barrier()
                if STOP == "ffn_ln1":
                    raise _Stop()
            actT, actT_b = lt("actT", [128, 22, T], BF16)
            with scope():
                NB3 = 3
                uext, _ = lt("fuext", [128, NB3, T + 2])
                ue_b = [Buf(f"fue{k}") for k in range(NB3)]
                cv, _ = lt("fcv", [128, NB3, T])
                cv_b = [Buf(f"fcv{k}") for k in range(NB3)]
                sl, _ = lt("fsl", [128, 2, T])
                sl_b = [Buf("fsl0"), Buf("fsl1")]

                def ffn_tail(qi):
                    s_, jj = divmod(qi, 4)
                    k_ = qi % NB3
                    if jj < 2:
                        S.op("act", lambda e: e.activation(out=sl[:, jj, :], in_=cv[:, k_, :], func=AF.Silu),
                             reads=[cv_b[k_]], writes=[sl_b[jj]])
                    else:
                        S.op("dve", lambda e: e.tensor_mul(out=actT[:, 2 * s_ + jj - 2, :], in0=sl[:, jj - 2, :],
                                                           in1=cv[:, k_, :]),
                             reads=[sl_b[jj - 2], cv_b[k_]], writes=[actT_b])

                slot = None
                for qi in range(44):
                    s_, jj = divmod(qi, 4)
                    if jj == 0:
                        slot = next_slab(SL_FIN + s_)
                    k_ = qi % NB3
                    pb = qi % 4
                    proj_fm(pb, slot, jj * 128, hb, hb_b)
                    S.op("act", lambda e: e.copy(out=uext[:, k_, 2:T + 2], in_=ps[pb][:, :]), reads=[ps_b[pb]],
                         writes=[ue_b[k_]])
                    S.op("act", lambda e: e.copy(out=uext[:, k_, 0:2], in_=fcar[:, qi, :]), reads=[fcar_b],
                         writes=[ue_b[k_]])
                    S.op("act", lambda e: e.activation(out=cv[:, k_, :], in_=ps[pb][:, :], func=AF.Identity,
                                                       scale=pcol(l, "fdw", qi * 3 + 2), bias=pcol(l, "fdb", qi)),
                         reads=[ps_b[pb], pp_b], writes=[cv_b[k_]])
                    S.op("act", lambda e: e.copy(out=fcar[:, qi, :], in_=uext[:, k_, T:T + 2]), reads=[ue_b[k_]],
                         writes=[fcar_b])
                    if qi > 0:
                        ffn_tail(qi - 1)
                    for tap in (1, 0):
                        S.op("dve", lambda e, tap=tap: e.scalar_tensor_tensor(
                            out=cv[:, k_, :], in0=uext[:, k_, tap:tap + T], scalar=pcol(l, "fdw", qi * 3 + tap),
                            in1=cv[:, k_, :], op0=ALU.mult, op1=ALU.add), reads=[ue_b[k_], pp_b, cv_b[k_]],
                            writes=[cv_b[k_]])
                ffn_tail(43)
                if last:
                    fc5 = fcar[:, :, :].rearrange("p (s hf jj) r -> p s hf jj r", s=11, hf=2, jj=2)
                    S.dma("sp", lambda e: [e.dma_start(
                        out=o_ffn[l, r_, hf * 2816:(hf + 1) * 2816].rearrange("(s jj p) -> p s jj", s=11, jj=2, p=128)[:, :, jj],
                        in_=fc5[:, :, hf, jj, r_], allow_slow_non_contiguous=True)
                        for hf in range(2) for r_ in range(2) for jj in range(2)],
                        "offn", reads=[fcar_b], n=8, out=True)
                S.barrier()
            if STOP == "ffn_in":
                raise _Stop()
            with scope():
                sq, _ = lt("fsq2", [128, 2, T])
                sq_b = [Buf("fsq20"), Buf("fsq21")]
                tmp, tmp_b = lt("ftmp2", [128, 1, T])
                es_local["sq"], es_local["sq_b"] = sq, sq_b
                for oc in range(8):
                    slot = next_slab(SL_FOUT + oc)
                    Wf = W(slot, 22, 128)
                    for kc in range(22):
                        mm(ps[oc % 2][:, :], Wf[:, kc, :], actT[:, kc, :], kc == 0, kc == 21, [ring_b[slot], actT_b],
                           [ps_b[oc % 2]], kc == 21)
                    S.op("dve", lambda e, oc=oc: e.scalar_tensor_tensor(out=pre[:, oc, :], in0=xT32[:, oc, :],
                                                                        scalar=float(ALPHA), in1=ps[oc % 2][:, :],
                                                                        op0=ALU.mult, op1=ALU.add),
                         reads=[xT32_b, ps_b[oc % 2]], writes=[pre_b])
                if STOP == "ffn_out":
                    raise _Stop()
                ln_fm(l, pre, pre_b, 8, mean_d, "l2g", "l2b", LN_EPS, [(pre, pre_b)], tmp, tmp_b)
                if l < DEPTH - 1:
                    S.dma("sp", lambda e: e.dma_start(out=xs_d[:, :, i * T:(i + 1) * T], in_=pre[:]), "xsst",
                          reads=[pre_b])
                else:
                    for blk in range(4):
                        for hf in range(2):
                            fm2tm_store(lambda c, blk=blk, hf=hf: (pre[:, hf * 4 + c, blk * 128:(blk + 1) * 128], pre_b), 4,
                                        128, lambda stg, blk=blk, hf=hf: (lambda e: e.dma_start(
                                            out=y_p[i * T + blk * 128:i * T + (blk + 1) * 128, hf * 512:(hf + 1) * 512],
                                            in_=stg[:, :])), "yout")
                S.barrier()


    if NS:
        xs32 = sb("xs32", [128, 8, NS])
        xs32_b = Buf("xs32")
        sel = cf[0:NS, 1408:1408 + NS * 128]
        ones_f64 = cf[:, 1920:1984]
        id64r = cf[:, 1344:1408]
        S.dma("sp", lambda e: e.dma_start(out=xs32[:], in_=xs_fm[:, :, :]), "xsload", writes=[xs32_b], local=False)
        for l in range(DEPTH):
            S.dma("sp", lambda e, l=l: e.dma_start(out=o_conv_s[l, :, 0:CONV_K - 2, :], in_=cv_nat[l, :, 1:CONV_K - 1, :]),
                  "d2d", out=True, local=False)
            S.dma("sp", lambda e, l=l: e.dma_start(out=o_ffn_s[l, :, 0, :], in_=ff_nat[l, :, 1, :]), "d2d", out=True,
                  local=False)
            for g in range(3):
                nb = NBUF[g]
                for s_ in range(NS):
                    S.dma("sp", lambda e, l=l, g=g, s_=s_, nb=nb: e.dma_start(out=o_swa_s[g][l, s_, 0:nb - 1, :],
                                                                           in_=cache_d[g][l, s_, 1:nb, :]),
                          "d2d", out=True, local=False)

    def sample_layer(l):
        N = NS
        with scope():
            def lt(name, shape, dt=F32):
                return AR.alloc(list(shape), dt), Buf(name)
            xsb, xsb_b = lt("xsb", [128, 8, N], BF16)
            mgs, mgs_b = lt("mgs", [128, 8, N])
            shs, shs_b = lt("shs", [128, 14, N])
            zrs, zrs_b = lt("zrs", [128, 14, N])
            S.op("act", lambda e: e.copy(out=xsb[:], in_=xs32[:]), reads=[xs32_b], writes=[xsb_b])
            S.dma("sp", lambda e: e.dma_start(out=shs[:], in_=sh_fm[l, :, :, :]), "sld0", writes=[shs_b])

            def sshift(pi, och, dst, dst_b):
                S.op("act", lambda e: e.copy(out=zrs[:, och, :], in_=ps[pi][:, 0:N]), reads=[ps_b[pi]], writes=[zrs_b])
                S.op("dve", lambda e: e.tensor_scalar(out=dst, in0=ps[pi][:, 0:N], scalar1=pcol(l, "omm", och), scalar2=None,
                                                      op0=ALU.mult), reads=[ps_b[pi], pp_b], writes=[dst_b])
                S.op("dve", lambda e: e.scalar_tensor_tensor(out=dst, in0=shs[:, och, :], scalar=pcol(l, "mu", och), in1=dst,
                                                             op0=ALU.mult, op1=ALU.add), reads=[shs_b, pp_b, dst_b],
                     writes=[dst_b])

            def sproj(pi, slot, col0, wcols=512):
                proj_fm(pi, slot, col0, xsb, xsb_b, ncol=N, wcols=wcols, rhs_cols=slice(0, N))

            with scope():
                lora, lora_b = lt("slora", [128, 2, N])
                ozs, ozs_b = lt("ozs", [128, 4, N], BF16)
                Hs, Hs_b = lt("Hs", [128, N, 4, 64])
                S.dma("sp", lambda e: e.dma_start(out=Hs[:], in_=wkv_fm[l, :, :, :, :]), "sld1", writes=[Hs_b])
                slot = next_slab(SL_LORA)
                for c in range(2):
                    sproj(c, slot, c * 128, wcols=256)
                    sshift(c, 12 + c, lora[:, c, :], lora_b)
                S.op("act", lambda e: e.activation(out=lora[0:64, 0, :], in_=lora[0:64, 0, :], func=AF.Tanh),
                     reads=[lora_b], writes=[lora_b])
                S.op("act", lambda e: e.activation(out=lora[:, 1, :], in_=lora[:, 1, :], func=AF.Sigmoid),
                     reads=[lora_b], writes=[lora_b])
                for p in range(4):
                    with scope():
                        rT, r_b = lt("srT", [128, N])
                        kT, k_b = lt("skT", [128, N])
                        vT, v_b = lt("svT", [128, N])
                        wT, w_b = lt("swT", [128, N])
                        aT, a_b = lt("saT", [128, N])
                        gT, g_b = lt("sgT", [128, N])
                        kkT, kk_b = lt("skkT", [128, N])
                        bT, b_b = lt("sbT", [128, N])
                        eT, e_b = lt("seT", [128, N])
                        bon, bon_b = lt("sbon", [128, N])
                        OT, OT_b = lt("sOT", [128, N])
                        t1, t1_b = lt("st1", [128, 64])
                        t2, t2_b = lt("st2", [128, 64])
                        slot = next_slab(SL_RP + p)
                        for c, (dst, db, och) in enumerate(((rT, r_b, p), (kT, k_b, 4 + p), (vT, v_b, 8 + p))):
                            sproj(c % 2, slot, c * 128, wcols=384)
                            sshift(c % 2, och, dst[:, :], db)
                        cs128 = slice(p * 128, (p + 1) * 128)
                        mm(ps[2][:, 0:N], lw[0:64, l, cs128], lora[0:64, 0, :], True, True, [lw_b, lora_b], [ps_b[2]], True)
                        mm(ps[3][:, 0:N], lw[64:128, l, cs128], lora[64:128, 0, :], True, True, [lw_b, lora_b], [ps_b[3]], True)
                        mm(ps[4][:, 0:N], lw[:, l, 512 + p * 128:512 + (p + 1) * 128], lora[:, 1, :], True, True,
                           [lw_b, lora_b], [ps_b[4]], True)
                        S.op("act", lambda e: e.activation(out=eT[:, :], in_=ps[2][:, 0:N], func=AF.Exp, scale=-1.0,
                                                           bias=pcol(l, "nw0", p)), reads=[ps_b[2], pp_b], writes=[e_b])
                        S.op("act", lambda e: e.activation(out=eT[:, :], in_=eT[:, :], func=AF.Ln, bias=1.0), reads=[e_b],
                             writes=[e_b])
                        S.op("act", lambda e: e.activation(out=eT[:, :], in_=eT[:, :], func=AF.Exp, scale=-1.0, bias=-0.5),
                             reads=[e_b], writes=[e_b])
                        S.op("act", lambda e: e.activation(out=wT[:, :], in_=eT[:, :], func=AF.Exp, scale=-1.0),
                             reads=[e_b], writes=[w_b])
                        S.op("act", lambda e: e.activation(out=aT[:, :], in_=ps[3][:, 0:N], func=AF.Sigmoid,
                                                           bias=pcol(l, "a0", p)), reads=[ps_b[3], pp_b], writes=[a_b])
                        S.op("act", lambda e: e.copy(out=gT[:, :], in_=ps[4][:, 0:N]), reads=[ps_b[4]], writes=[g_b])
                        S.op("dve", lambda e: e.tensor_scalar(out=kkT[:, :], in0=kT[:, :], scalar1=pcol(l, "kk", p),
                                                              scalar2=None, op0=ALU.mult), reads=[k_b, pp_b], writes=[kk_b])
                        S.op("dve", lambda e: e.tensor_mul(out=eT[:, :], in0=kkT[:, :], in1=kkT[:, :]), reads=[kk_b],
                             writes=[e_b])
                        mm(ps[2][:, 0:N], ones_blk, eT[:, :], True, True, [cf_b, e_b], [ps_b[2]], True)
                        rsqrt(eT[:, :], ps[2][:, 0:N], 1e-24, [ps_b[2]], e_b)
                        S.op("dve", lambda e: e.tensor_mul(out=kkT[:, :], in0=kkT[:, :], in1=eT[:, :]), reads=[kk_b, e_b],
                             writes=[kk_b])
                        S.op("dve", lambda e: e.tensor_mul(out=bT[:, :], in0=kkT[:, :], in1=aT[:, :]), reads=[kk_b, a_b],
                             writes=[b_b])
                        S.op("dve", lambda e: e.tensor_scalar(out=aT[:, :], in0=aT[:, :], scalar1=-1.0,
                                                              scalar2=pcol(l, "ka", p), op0=ALU.add, op1=ALU.mult),
                             reads=[a_b, pp_b], writes=[a_b])
                        S.op("dve", lambda e: e.scalar_tensor_tensor(out=kT[:, :], in0=aT[:, :], scalar=1.0, in1=kT[:, :],
                                                                     op0=ALU.add, op1=ALU.mult), reads=[a_b, k_b],
                             writes=[k_b])
                        S.op("dve", lambda e: e.scalar_tensor_tensor(out=eT[:, :], in0=rT[:, :], scalar=pcol(l, "rk", p),
                                                                     in1=kT[:, :], op0=ALU.mult, op1=ALU.mult),
                             reads=[r_b, k_b, pp_b], writes=[e_b])
                        mm(ps[3][:, 0:N], ones_blk, eT[:, :], True, True, [cf_b, e_b], [ps_b[3]], True)
                        S.op("dve", lambda e: e.tensor_mul(out=bon[:, :], in0=ps[3][:, 0:N], in1=vT[:, :]),
                             reads=[ps_b[3], v_b], writes=[bon_b])
                        S.op("dve", lambda e: e.tensor_scalar(out=bT[:, :], in0=bT[:, :], scalar1=-1.0, scalar2=None,
                                                              op0=ALU.mult), reads=[b_b], writes=[b_b])
                        for s_ in range(N):
                            H0 = Hs[:, s_, p, :]
                            sc = slice(s_, s_ + 1)
                            S.op("dve", lambda e, H0=H0, sc=sc: e.tensor_scalar(out=t1[:, :], in0=H0, scalar1=kkT[:, sc],
                                                                               scalar2=None, op0=ALU.mult),
                                 reads=[Hs_b, kk_b], writes=[t1_b])
                            S.op("dve", lambda e, sc=sc: e.tensor_scalar(out=t2[:, :], in0=id64r, scalar1=vT[:, sc],
                                                                        scalar2=None, op0=ALU.mult),
                                 reads=[cf_b, v_b], writes=[t2_b])
                            mm(ps[0][:, 0:64], ones_blk, t1[:, :], True, True, [cf_b, t1_b], [ps_b[0]], True)
                            mm(ps[1][:, 0:64], ones_blk, t2[:, :], True, True, [cf_b, t2_b], [ps_b[1]], True)
                            S.op("dve", lambda e, H0=H0, sc=sc: e.tensor_scalar(out=H0, in0=H0, scalar1=wT[:, sc], scalar2=None,
                                                                               op0=ALU.mult), reads=[Hs_b, w_b],
                                 writes=[Hs_b])
                            S.op("dve", lambda e, H0=H0, sc=sc: e.scalar_tensor_tensor(out=H0, in0=ps[0][:, 0:64],
                                                                                      scalar=bT[:, sc], in1=H0,
                                                                                      op0=ALU.mult, op1=ALU.add),
                                 reads=[ps_b[0], b_b, Hs_b], writes=[Hs_b])
                            S.op("dve", lambda e, H0=H0, sc=sc: e.scalar_tensor_tensor(out=H0, in0=ps[1][:, 0:64],
                                                                                      scalar=kT[:, sc], in1=H0,
                                                                                      op0=ALU.mult, op1=ALU.add),
                                 reads=[ps_b[1], k_b, Hs_b], writes=[Hs_b])
                            for hl in range(2):
                                hp = slice(hl * 64, (hl + 1) * 64)
                                mm(ps[4 + hl][hp, s_:s_ + 1], Hs[hp, s_, p, :], rT[hp, sc], True, True, [Hs_b, r_b],
                                   [ps_b[4 + hl]], True)
                        for hl in range(2):
                            hp = slice(hl * 64, (hl + 1) * 64)
                            S.op("act", lambda e, hp=hp, hl=hl: e.copy(out=OT[hp, :], in_=ps[4 + hl][hp, 0:N]),
                                 reads=[ps_b[4 + hl]], writes=[OT_b])
                        mm(ps[0][:, 0:N], mean_blk, OT[:, :], True, True, [cf_b, OT_b], [ps_b[0]], True)
                        S.op("dve", lambda e: e.tensor_sub(out=OT[:, :], in0=OT[:, :], in1=ps[0][:, 0:N]),
                             reads=[OT_b, ps_b[0]], writes=[OT_b])
                        S.op("act", lambda e: e.activation(out=eT[:, :], in_=OT[:, :], func=AF.Square), reads=[OT_b],
                             writes=[e_b])
                        mm(ps[1][:, 0:N], mean_blk, eT[:, :], True, True, [cf_b, e_b], [ps_b[1]], True)
                        rsqrt(eT[:, :], ps[1][:, 0:N], float(LNX_EPS), [ps_b[1]], e_b)
                        S.op("dve", lambda e: e.tensor_mul(out=OT[:, :], in0=OT[:, :], in1=eT[:, :]), reads=[OT_b, e_b],
                             writes=[OT_b])
                        S.op("dve", lambda e: e.tensor_scalar(out=OT[:, :], in0=OT[:, :], scalar1=pcol(l, "lng", p),
                                                              scalar2=pcol(l, "lnb", p), op0=ALU.mult, op1=ALU.add),
                             reads=[OT_b, pp_b], writes=[OT_b])
                        S.op("dve", lambda e: e.tensor_add(out=OT[:, :], in0=OT[:, :], in1=bon[:, :]), reads=[OT_b, bon_b],
                             writes=[OT_b])
                        S.op("dve", lambda e: e.tensor_mul(out=ozs[:, p, :], in0=OT[:, :], in1=gT[:, :]), reads=[OT_b, g_b],
                             writes=[ozs_b])
                        S.barrier()
                s_out = next_slab(SL_ROUT)
                g0 = next_slab(SL_G0)
                g1 = next_slab(SL_G0 + 1)
                merge_branch(l, 0, s_out, 4, ozs, ozs_b, (g0, g1), N=N, xb=xsb, xb_b=xsb_b, mg=mgs, mg_b=mgs_b)
                S.dma("sp", lambda e: [e.dma_start(out=o_shift_s[l, s_, :].rearrange("(c p) -> p c", p=128), in_=zrs[:, :, s_],
                                                   allow_slow_non_contiguous=True) for s_ in range(N)], "sosh",
                      reads=[zrs_b], n=N, out=True)
                for s_ in range(N):
                    with scope_alloc([64, 8, 64], F32) as stg:
                        stg_b = Buf("swkvst")
                        for h in range(8):
                            pr_, hl = h // 2, h % 2
                            S.op("pe", lambda e, pr_=pr_, hl=hl, s_=s_: e.matmul(
                                ps[4 + hl][0:64, pr_ * 64:(pr_ + 1) * 64], lhsT=Hs[hl * 64:(hl + 1) * 64, s_, pr_, :],
                                rhs=cf[hl * 64:(hl + 1) * 64, 1344:1408], start=True, stop=True), reads=[Hs_b, cf_b],
                                writes=[ps_b[4 + hl]])
                        stg4 = stg[:].rearrange("p (pr hl) k -> p pr hl k", hl=2)
                        for hl in range(2):
                            S.op("act", lambda e, hl=hl: e.copy(out=stg4[:, :, hl, :],
                                                                in_=ps[4 + hl][0:64, 0:256].rearrange("p (pr k) -> p pr k", k=64)),
                                 reads=[ps_b[4 + hl]], writes=[stg_b])
                        S.dma("sp", lambda e, s_=s_: e.dma_start(out=o_wkv_s[l, s_].rearrange("h v k -> v h k"), in_=stg[:]),
                              "sowkv", reads=[stg_b], out=True)
                        S.barrier()
                S.barrier()

            with scope():
                stc, stc_b = lt("stc", [128, 4, N, 30])
                uS, uS_b = lt("uS", [128, 4, N])
                acc, acc_b = lt("sacc", [128, 4, N])
                prod, prod_b = lt("sprod", [128, N, 30])
                sg, sg_b = lt("ssg", [128, N])
                cTb, cT_b = lt("scTb", [128, 4, N], BF16)
                sq, _ = lt("ssq", [128, 2, N])
                sq_b = [Buf("ssq0"), Buf("ssq1")]
                tmp, tmp_b = lt("stmp", [128, 1, N])
                es_local["sq"], es_local["sq_b"] = sq, sq_b
                S.dma("sp", lambda e: e.dma_start(out=stc[:], in_=cv_fm[l, :, :, :, :]), "sld2", writes=[stc_b])
                su = next_slab(SL_CU)
                sgt = next_slab(SL_CG)
                for c in range(4):
                    sproj(0, su, c * 128)
                    sproj(1, sgt, c * 128)
                    S.op("act", lambda e: e.activation(out=sg[:, :], in_=ps[1][:, 0:N], func=AF.Sigmoid), reads=[ps_b[1]],
                         writes=[sg_b])
                    S.op("dve", lambda e, c=c: e.tensor_mul(out=uS[:, c, :], in0=ps[0][:, 0:N], in1=sg[:, :]),
                         reads=[ps_b[0], sg_b], writes=[uS_b])
                    wv = pp[:, l, PC["cdw"] + c * 31:PC["cdw"] + c * 31 + 30]
                    S.op("dve", lambda e, c=c, wv=wv: e.tensor_mul(out=prod[:], in0=stc[:, c, :, :],
                                                                   in1=wv.unsqueeze(1).broadcast_to([128, N, 30])),
                         reads=[stc_b, pp_b], writes=[prod_b])
                    S.op("dve", lambda e, c=c: e.tensor_reduce(out=acc[:, c, :], in_=prod[:], axis=AX.X, op=ALU.add),
                         reads=[prod_b], writes=[acc_b])
                    S.op("dve", lambda e, c=c: e.scalar_tensor_tensor(out=acc[:, c, :], in0=uS[:, c, :],
                                                                      scalar=pcol(l, "cdw", c * 31 + 30), in1=acc[:, c, :],
                                                                      op0=ALU.mult, op1=ALU.add),
                         reads=[uS_b, pp_b, acc_b], writes=[acc_b])
                    S.op("dve", lambda e, c=c: e.tensor_scalar(out=acc[:, c, :], in0=acc[:, c, :], scalar1=pcol(l, "cdb", c),
                                                               scalar2=None, op0=ALU.add), reads=[acc_b, pp_b],
                         writes=[acc_b])
                ln_fm(l, acc, acc_b, 4, mean_c, "clg", "clb", LN_EPS, [(acc, acc_b)], tmp, tmp_b, N=N)
                for c in range(4):
                    S.op("act", lambda e, c=c: e.activation(out=cTb[:, c, :], in_=acc[:, c, :], func=AF.Silu),
                         reads=[acc_b], writes=[cT_b])
                s_out = next_slab(SL_COUT)
                g0 = next_slab(SL_G1)
                g1 = next_slab(SL_G1 + 1)
                merge_branch(l, 1, s_out, 4, cTb, cT_b, (g0, g1), N=N, xb=xsb, xb_b=xsb_b, mg=mgs, mg_b=mgs_b)
                fm2tm_store(lambda c: (uS[:, c, :], uS_b), 4, N,
                            lambda stg: (lambda e: e.dma_start(out=o_conv_s[l, :, CONV_K - 2, :], in_=stg[0:N, 0:512])),
                            "soconv")
                S.barrier()

            with scope():
                qT, q_b = lt("sqT", [128, 2, N])
                kT, k_b = lt("skT2", [128, 2, N])
                vT, v_b = lt("svT2", [128, 2, N])
                Qtm, Qtm_b = lt("sQtm", [N, 256])
                Kt, _ = lt("sKt", [128, 2, 256])
                Kt_b = [Buf("sKt0"), Buf("sKt1")]
                Vt, _ = lt("sVt", [128, 2, 256])
                Vt_b = [Buf("sVt0"), Buf("sVt1")]
                prod, prod_b = lt("saprod", [128, 256])
                sc4, sc4_b = lt("ssc4", [128, 4])
                pself, pself_b = lt("spself", [128, 2, N])
                numS, numS_b = lt("snumS", [128, 2, N])
                denS, denS_b = lt("sdenS", [128, 2, N])
                oTs, oTs_b = lt("soTs", [128, 2, N], BF16)
                S.op("dve", lambda e: e.memset(numS[:], 0.0), writes=[numS_b])
                S.op("dve", lambda e: e.memset(denS[:], 0.0), writes=[denS_b])
                started = set()
                kvn = 0
                for g in range(3):
                    win, dil = SWA_GROUPS[g]
                    nb = NBUF[g]
                    sA = next_slab(SL_ATT + 2 * g)
                    sB = next_slab(SL_ATT + 2 * g + 1)
                    for c in range(4):
                        sproj(c % 2, sA, c * 128)
                        dst, db = (qT, q_b) if c < 2 else (kT, k_b)
                        S.op("act", lambda e, c=c, dst=dst: e.copy(out=dst[:, c % 2, :], in_=ps[c % 2][:, 0:N]),
                             reads=[ps_b[c % 2]], writes=[db])
                    for c in range(2):
                        sproj(2 + c, sB, 256 + c * 128)
                        S.op("act", lambda e, c=c: e.copy(out=vT[:, c, :], in_=ps[2 + c][:, 0:N]), reads=[ps_b[2 + c]],
                             writes=[v_b])
                    fm2tm_store(lambda c: ((kT[:, c, :], k_b) if c < 2 else (vT[:, c - 2, :], v_b)), 4, N,
                                lambda stg, g=g, nb=nb: (lambda e: e.dma_start(out=o_swa_s[g][l, :, nb - 1, :],
                                                                                in_=stg[0:N, 0:512])), "soswa")
                    S.op("dve", lambda e: e.tensor_mul(out=pself[:], in0=qT[:], in1=kT[:]), reads=[q_b, k_b],
                         writes=[pself_b])
                    mm(ps[0][:, 0:2 * N], ones_blk, pself[:].rearrange("p a n -> p (a n)"), True, True, [cf_b, pself_b],
                       [ps_b[0]], True)
                    S.op("act", lambda e: e.activation(out=pself[:].rearrange("p a n -> p (a n)"), in_=ps[0][:, 0:2 * N],
                                                       func=AF.Exp, scale=0.125), reads=[ps_b[0]], writes=[pself_b])
                    S.op("dve", lambda e: e.tensor_add(out=denS[:], in0=denS[:], in1=pself[:]), reads=[denS_b, pself_b],
                         writes=[denS_b])
                    S.op("dve", lambda e: e.tensor_mul(out=pself[:], in0=pself[:], in1=vT[:]), reads=[pself_b, v_b],
                         writes=[pself_b])
                    S.op("dve", lambda e: e.tensor_add(out=numS[:], in0=numS[:], in1=pself[:]), reads=[numS_b, pself_b],
                         writes=[numS_b])
                    for c in range(2):
                        S.op("pe", lambda e, c=c: e.matmul(ps[1][0:N, c * 128:(c + 1) * 128], lhsT=qT[:, c, :], rhs=ident,
                                                           start=True, stop=True), reads=[q_b, cf_b], writes=[ps_b[1]])
                    S.op("act", lambda e: e.copy(out=Qtm[:, :], in_=ps[1][0:N, 0:256]), reads=[ps_b[1]], writes=[Qtm_b])
                    for s_ in range(N):
                        k_ = kvn % 2
                        kvn += 1
                        S.dma("sp", lambda e, k_=k_, s_=s_, g=g, dil=dil, nb=nb: e.dma_start(
                            out=Kt[:, k_, :], in_=cache_d[g][l, s_, slice(0, nb, dil), 0:256]), f"sk{k_}",
                            writes=[Kt_b[k_]])
                        S.dma("sp", lambda e, k_=k_, s_=s_, g=g, dil=dil, nb=nb: e.dma_start(
                            out=Vt[:, k_, :], in_=cache_d[g][l, s_, slice(0, nb, dil), 256:512]), f"sv{k_}",
                            writes=[Vt_b[k_]])
                        mm(ps[2][:, 0:256], sel[:, s_ * 128:(s_ + 1) * 128], Qtm[:, :], True, True, [cf_b, Qtm_b],
                           [ps_b[2]], True)
                        S.op("dve", lambda e, k_=k_: e.tensor_mul(out=prod[:, :], in0=Kt[:, k_, :], in1=ps[2][:, 0:256]),
                             reads=[Kt_b[k_], ps_b[2]], writes=[prod_b])
                        S.op("dve", lambda e: e.tensor_reduce(out=sc4[:, :], in_=prod[:, :].rearrange("p (h e) -> p h e", e=64),
                                                              axis=AX.X, op=ALU.add), reads=[prod_b], writes=[sc4_b])
                        S.op("act", lambda e: e.activation(out=sc4[:, :], in_=sc4[:, :], func=AF.Exp, scale=0.125),
                             reads=[sc4_b], writes=[sc4_b])
                        for h in range(4):
                            pair, hl = h // 2, h % 2
                            hp = slice(hl * 64, (hl + 1) * 64)
                            col = pair * N + s_
                            first = (hl,) not in started
                            started.add((hl,))
                            mm(ps[4][hp, col:col + 1], Vt[:, k_, h * 64:(h + 1) * 64], sc4[:, h:h + 1], first, False,
                               [Vt_b[k_], sc4_b], [ps_b[4]], False)
                            mm(ps[5][hp, col:col + 1], ones_f64, sc4[:, h:h + 1], first, False, [cf_b, sc4_b], [ps_b[5]],
                               True)
                S.op("dve", lambda e: e.tensor_add(out=denS[:].rearrange("p a n -> p (a n)"),
                                                   in0=denS[:].rearrange("p a n -> p (a n)"), in1=ps[5][:, 0:2 * N]),
                     reads=[denS_b, ps_b[5]], writes=[denS_b])
                S.op("dve", lambda e: e.tensor_add(out=numS[:].rearrange("p a n -> p (a n)"),
                                                   in0=numS[:].rearrange("p a n -> p (a n)"), in1=ps[4][:, 0:2 * N]),
                     reads=[numS_b, ps_b[4]], writes=[numS_b])
                S.op("dve", lambda e: e.reciprocal(out=denS[:], in_=denS[:]), reads=[denS_b], writes=[denS_b])
                S.op("dve", lambda e: e.tensor_mul(out=oTs[:], in0=numS[:], in1=denS[:]), reads=[numS_b, denS_b],
                     writes=[oTs_b])
                s_out = next_slab(SL_AOUT)
                g0 = next_slab(SL_G2)
                g1 = next_slab(SL_G2 + 1)
                merge_branch(l, 2, s_out, 2, oTs, oTs_b, (g0, g1), N=N, xb=xsb, xb_b=xsb_b, mg=mgs, mg_b=mgs_b)
                S.barrier()

            with scope():
                hb, hb_b = lt("shb", [128, 8, N], BF16)
                mbf, mbf_b = lt("smbf", [128, 8, N], BF16)
                sq, _ = lt("sfsq", [128, 2, N])
                sq_b = [Buf("sfsq0"), Buf("sfsq1")]
                tmp, tmp_b = lt("sftmp", [128, 1, N])
                stf, stf_b = lt("stf", [128, 44, N, 2])
                urw, urw_b = lt("surw", [128, 44, N])
                cv, _ = lt("sfcv", [128, 2, N])
                cv_b = [Buf("sfcv0"), Buf("sfcv1")]
                sl, _ = lt("sfsl", [128, 2, N])
                sl_b = [Buf("sfsl0"), Buf("sfsl1")]
                actT, actT_b = lt("sactT", [128, 22, N], BF16)
                es_local["sq"], es_local["sq_b"] = sq, sq_b
                S.dma("sp", lambda e: e.dma_start(out=stf[:], in_=ff_fm[l, :, :, :, :]), "sld3", writes=[stf_b])
                S.op("act", lambda e: e.copy(out=mbf[:], in_=mgs[:]), reads=[mgs_b], writes=[mbf_b])
                wo = [next_slab(SL_WO), next_slab(SL_WO + 1)]
                for oc in range(8):
                    proj_fm(oc % 2, wo[oc // 4], (oc % 4) * 128, mbf, mbf_b, ncol=N, rhs_cols=slice(0, N))
                    S.op("dve", lambda e, oc=oc: e.scalar_tensor_tensor(out=mgs[:, oc, :], in0=xs32[:, oc, :],
                                                                        scalar=float(ALPHA), in1=ps[oc % 2][:, 0:N],
                                                                        op0=ALU.mult, op1=ALU.add),
                         reads=[xs32_b, ps_b[oc % 2], mgs_b], writes=[mgs_b])
                ln_fm(l, mgs, mgs_b, 8, mean_d, "l1g", "l1b", LN_EPS, [(xs32, xs32_b), (hb, hb_b)], tmp, tmp_b, N=N)
                for s in range(11):
                    slot = next_slab(SL_FIN + s)
                    for jj in range(4):
                        qi = s * 4 + jj
                        k_ = qi % 2
                        proj_fm(k_, slot, jj * 128, hb, hb_b, ncol=N, rhs_cols=slice(0, N))
                        S.op("act", lambda e, k_=k_, qi=qi: e.copy(out=urw[:, qi, :], in_=ps[k_][:, 0:N]), reads=[ps_b[k_]],
                             writes=[urw_b])
                        S.op("dve", lambda e, k_=k_, qi=qi: e.tensor_scalar(out=cv[:, k_, :], in0=ps[k_][:, 0:N],
                                                                            scalar1=pcol(l, "fdw", qi * 3 + 2),
                                                                            scalar2=pcol(l, "fdb", qi), op0=ALU.mult,
                                                                            op1=ALU.add),
                             reads=[ps_b[k_], pp_b], writes=[cv_b[k_]])
                        for tap in (1, 0):
                            S.op("dve", lambda e, k_=k_, qi=qi, tap=tap: e.scalar_tensor_tensor(
                                out=cv[:, k_, :], in0=stf[:, qi, :, tap], scalar=pcol(l, "fdw", qi * 3 + tap),
                                in1=cv[:, k_, :], op0=ALU.mult, op1=ALU.add), reads=[stf_b, pp_b, cv_b[k_]],
                                writes=[cv_b[k_]])
                        if jj < 2:
                            S.op("act", lambda e, k_=k_, jj=jj: e.activation(out=sl[:, jj, :], in_=cv[:, k_, :], func=AF.Silu),
                                 reads=[cv_b[k_]], writes=[sl_b[jj]])
                        else:
                            S.op("dve", lambda e, k_=k_, jj=jj, s=s: e.tensor_mul(out=actT[:, 2 * s + jj - 2, :],
                                                                                   in0=sl[:, jj - 2, :], in1=cv[:, k_, :]),
                                 reads=[sl_b[jj - 2], cv_b[k_]], writes=[actT_b])
                ur5 = urw[:, :, :].rearrange("p (s hf jj) n -> p s hf jj n", s=11, hf=2, jj=2)
                S.dma("sp", lambda e: [e.dma_start(
                    out=o_ffn_s[l, s_, 1, hf * 2816:(hf + 1) * 2816].rearrange("(s jj p) -> p s jj", s=11, jj=2, p=128)[:, :, jj],
                    in_=ur5[:, :, hf, jj, s_], allow_slow_non_contiguous=True)
                    for s_ in range(N) for hf in range(2) for jj in range(2)], "soffn", reads=[urw_b], n=4 * N, out=True)
                for oc in range(8):
                    slot = next_slab(SL_FOUT + oc)
                    Wf = W(slot, 22, 128)
                    for kc in range(22):
                        mm(ps[oc % 2][:, 0:N], Wf[:, kc, :], actT[:, kc, :], kc == 0, kc == 21, [ring_b[slot], actT_b],
                           [ps_b[oc % 2]], kc == 21)
                    S.op("dve", lambda e, oc=oc: e.scalar_tensor_tensor(out=mgs[:, oc, :], in0=xs32[:, oc, :],
                                                                        scalar=float(ALPHA), in1=ps[oc % 2][:, 0:N],
                                                                        op0=ALU.mult, op1=ALU.add),
                         reads=[xs32_b, ps_b[oc % 2], mgs_b], writes=[mgs_b])
                ln_fm(l, mgs, mgs_b, 8, mean_d, "l2g", "l2b", LN_EPS, [(xs32, xs32_b)], tmp, tmp_b, N=N)
                if l == DEPTH - 1:
                    for hf in range(2):
                        fm2tm_store(lambda c, hf=hf: (xs32[:, hf * 4 + c, :], xs32_b), 4, N,
                                    lambda stg, hf=hf: (lambda e: e.dma_start(out=y_s[:, hf * 512:(hf + 1) * 512],
                                                                              in_=stg[0:N, 0:512])), "sy")
                S.barrier()
            S.barrier()

    for l in range(DEPTH):
        S.op("dve", lambda e: e.memset(Hst[:], 0.0), writes=[Hst_b])
        S.op("dve", lambda e: e.memset(zcar[:], 0.0), writes=[zcar_b])
        S.op("dve", lambda e: e.memset(uhist[:], 0.0), writes=[uhist_b])
        S.op("dve", lambda e: e.memset(fcar[:], 0.0), writes=[fcar_b])
        if l > 0 and not STOP:
            S.wait_only("sp", [(S.dkeys["xsst"][0], S.dkeys["xsst"][1], "dma")])
        cur_layer["l"] = l
        if NS:
            sample_layer(l)
        for i in range(NT):
            if STOP and (l, i) != (0, 0):
                continue
            try:
                tile_prog(l, i)
            except _Stop:
                pass

    S.wait_only("sp", list(S.out_toks))
    with nc.Block() as block:
        S.emit(block)
    es.close()
    return nc


_NC_CACHE = {}


def kernel(**inputs):
    inp = {k: np.asarray(v) for k, v in inputs.items()}
    BATCH, SEQ, _ = inp["x_prompt"].shape
    NSAMP_ALL = inp["x_sample"].shape[0]
    NCORE = 8
    NS = NSAMP_ALL // NCORE
    key = (SEQ, NS)
    if key not in _NC_CACHE:
        _NC_CACHE[key] = build(SEQ, NSAMP=NS)
    nc = _NC_CACHE[key]
    wp, pp, lw = pack_weights(inp)
    cf, cb = make_consts()
    in_maps = []
    for c in range(NCORE):
        b = c % BATCH
        x = inp["x_prompt"][b]
        x_fm = np.ascontiguousarray(x.T.reshape(8, 128, SEQ).transpose(1, 0, 2))
        im = {"x_fm": x_fm, "wpack": wp, "ppack": pp, "lorapack": lw, "constf": cf, "constb": cb}
        im.update(pack_samples(inp, c * NS, NS))
        in_maps.append(im)
    res = run_bass_kernel_spmd(nc, in_maps, core_ids=list(range(NCORE)))
    R_ = res.results
    f32 = np.float32

    def pstack(name, shape_tail):
        return np.stack([np.asarray(R_[b][name], f32).reshape((DEPTH,) + shape_tail) for b in range(BATCH)], axis=1)

    def sstack(name, shape_tail):
        return np.concatenate([np.asarray(R_[c][name], f32).reshape((DEPTH, NS) + shape_tail) for c in range(NCORE)],
                              axis=1)
    y_prompt = np.stack([np.asarray(R_[b]["y_p"], f32) for b in range(BATCH)], axis=0)
    y_sample = np.concatenate([np.asarray(R_[c]["y_s"], f32) for c in range(NCORE)], axis=0)[:, None, :]
    keep = [min(w, SEQ) for w, _ in SWA_GROUPS]
    outs = [y_prompt, y_sample,
            pstack("o_shift_p", (1, R_COLS)), sstack("o_shift_s", (1, R_COLS)),
            pstack("o_wkv_p", (R_HEADS, 64, 64)), sstack("o_wkv_s", (R_HEADS, 64, 64)),
            pstack("o_conv_p", (CONV_K - 1, CONV_CH)), sstack("o_conv_s", (CONV_K - 1, CONV_CH))]
    for g in range(3):
        outs.append(pstack(f"o_swa{g}_p", (keep[g], 2, G_HEADS, HEAD_DIM)))
        outs.append(sstack(f"o_swa{g}_s", (NBUF[g], 2, G_HEADS, HEAD_DIM)))
    outs.append(pstack("o_ffn_p", (2, 2 * D_FF)))
    outs.append(sstack("o_ffn_s", (2, 2 * D_FF)))
    return tuple(np.ascontiguousarray(o, dtype=f32) for o in outs)
```
